# Optimizing a Trainium2 kernel written in Bass

```python
import math
import jax, jax.numpy as jnp
from jax import lax
import numpy as np

D_MODEL = 1024
BATCH = 4
SEQ = 8192
DEPTH = 2

HEAD_DIM = 64
N_MOBA_HEADS = 6
N_DIFF_HEADS = 6
N_SB_HEADS = 6
DIFF_QK_DIM = HEAD_DIM // 2
MOBA_BLOCK = 256
MOBA_TOPK = 3
MOBA_Q_CHUNK = 64
Q_BLOCK = 128
D_FF = 2816
N_BRANCH = 3
LN_EPS = 1e-5
SUBLN_EPS = 1e-5
MOBA_W = N_MOBA_HEADS * HEAD_DIM
DIFF_QK_W = N_DIFF_HEADS * 2 * DIFF_QK_DIM
DIFF_V_W = N_DIFF_HEADS * HEAD_DIM
SB_W = N_SB_HEADS * HEAD_DIM
IN_SIZES = (MOBA_W, MOBA_W, MOBA_W, DIFF_QK_W, DIFF_QK_W, DIFF_V_W, SB_W, SB_W, SB_W, N_BRANCH * D_MODEL)
N_IN = 3 * MOBA_W + 2 * DIFF_QK_W + DIFF_V_W + 3 * SB_W + N_BRANCH * D_MODEL

kernel_name = "hybrid_moba_diff_stickbreak_deepnorm"

F32 = jnp.float32


def _alibi_slopes(n):
    return (2.0 ** (-8.0 * np.arange(1, n + 1, dtype=np.float32) / n)).astype(np.float32)


def layer_norm(x, g, b):
    xf = x.astype(F32)
    mu = jnp.mean(xf, axis=-1, keepdims=True)
    var = jnp.mean(jnp.square(xf - mu), axis=-1, keepdims=True)
    return ((xf - mu) * lax.rsqrt(var + LN_EPS) * g + b).astype(x.dtype)


def swiglu(x, w_gate, w_up, w_down):
    return (jax.nn.silu(x @ w_gate) * (x @ w_up)) @ w_down


def split_heads(t, n_heads):
    B, S, _ = t.shape
    return t.reshape(B, S, n_heads, -1).transpose(0, 2, 1, 3)


def merge_heads(t):
    B, H, S, d = t.shape
    return t.transpose(0, 2, 1, 3).reshape(B, S, H * d)


def moba_attention(q, k, v, slopes):
    B, H, S, Dh = q.shape
    nb = -(-S // MOBA_BLOCK)
    pad = nb * MOBA_BLOCK - S
    kp = jnp.pad(k, ((0, 0), (0, 0), (0, pad), (0, 0)))
    vp = jnp.pad(v, ((0, 0), (0, 0), (0, pad), (0, 0)))
    kb = kp.reshape(B, H, nb, MOBA_BLOCK, Dh)
    vb = vp.reshape(B, H, nb, MOBA_BLOCK, Dh)
    kmean = jnp.mean(kb.astype(F32), axis=3)
    topk = min(MOBA_TOPK, nb)
    scale = Dh ** -0.5
    bi = jnp.arange(B)[:, None, None, None]
    hi = jnp.arange(H)[None, :, None, None]
    slope = slopes.reshape(1, H, 1, 1)
    blk_off = jnp.arange(MOBA_BLOCK)
    blk_ids = jnp.arange(nb)
    n_chunks = S // MOBA_Q_CHUNK

    def chunk(i):
        t0 = i * MOBA_Q_CHUNK
        qc = lax.dynamic_slice_in_dim(q, t0, MOBA_Q_CHUNK, axis=2)
        tpos = t0 + jnp.arange(MOBA_Q_CHUNK)
        own = t0 // MOBA_BLOCK
        gate = jnp.einsum('bhcd,bhnd->bhcn', qc.astype(F32), kmean)
        gate = jnp.where(blk_ids < own, gate, -jnp.inf)
        _, sel = lax.top_k(gate, topk)
        valid = sel < own
        ksel = kb[bi, hi, sel]
        vsel = vb[bi, hi, sel]
        s_sel = jnp.einsum('bhcd,bhcnkd->bhcnk', qc, ksel).astype(F32) * scale
        dist_sel = tpos[None, None, :, None, None] - (sel[..., None] * MOBA_BLOCK + blk_off)
        s_sel = jnp.where(valid[..., None], s_sel - slope[..., None] * dist_sel, -jnp.inf)
        kown = lax.dynamic_index_in_dim(kb, own, axis=2, keepdims=False)
        vown = lax.dynamic_index_in_dim(vb, own, axis=2, keepdims=False)
        s_own = jnp.einsum('bhcd,bhkd->bhck', qc, kown).astype(F32) * scale
        dist_own = tpos[:, None] - (own * MOBA_BLOCK + blk_off)[None, :]
        s_own = jnp.where(dist_own >= 0, s_own - slope * dist_own, -jnp.inf)
        scores = jnp.concatenate([s_sel.reshape(B, H, MOBA_Q_CHUNK, topk * MOBA_BLOCK), s_own], axis=-1)
        p = jax.nn.softmax(scores, axis=-1).astype(v.dtype)
        p_sel = p[..., :topk * MOBA_BLOCK].reshape(B, H, MOBA_Q_CHUNK, topk, MOBA_BLOCK)
        p_own = p[..., topk * MOBA_BLOCK:]
        return (jnp.einsum('bhcnk,bhcnkd->bhcd', p_sel, vsel)
                + jnp.einsum('bhck,bhkd->bhcd', p_own, vown))

    out = lax.map(chunk, jnp.arange(n_chunks))
    return out.transpose(1, 2, 0, 3, 4).reshape(B, H, S, Dh)


def diff_attention(q, k, v, slopes, lam, subln_g, lambda_init):
    B, H, _, S, dq = q.shape
    dv = v.shape[-1]
    scale = dq ** -0.5
    kpos = jnp.arange(S)
    slope = slopes.reshape(1, H, 1, 1, 1)

    def block(i):
        t0 = i * Q_BLOCK
        qc = lax.dynamic_slice_in_dim(q, t0, Q_BLOCK, axis=3)
        dist = (t0 + jnp.arange(Q_BLOCK))[:, None] - kpos[None, :]
        s = jnp.einsum('bhmcd,bhmsd->bhmcs', qc, k).astype(F32) * scale
        s = jnp.where(dist >= 0, s - slope * dist, -jnp.inf)
        p = jax.nn.softmax(s, axis=-1)
        w = p[:, :, 0] - lam * p[:, :, 1]
        return jnp.einsum('bhcs,bhsd->bhcd', w.astype(v.dtype), v)

    o = lax.map(block, jnp.arange(S // Q_BLOCK))
    o = o.transpose(1, 2, 0, 3, 4).reshape(B, H, S, dv).astype(F32)
    o = o * lax.rsqrt(jnp.mean(jnp.square(o), axis=-1, keepdims=True) + SUBLN_EPS) * subln_g
    return (o * (1.0 - lambda_init)).astype(v.dtype)


def stick_breaking_attention(q, k, v):
    B, H, S, Dh = q.shape
    scale = Dh ** -0.5
    kpos = jnp.arange(S)

    def block(i):
        t0 = i * Q_BLOCK
        qc = lax.dynamic_slice_in_dim(q, t0, Q_BLOCK, axis=2)
        before = kpos[None, :] < (t0 + jnp.arange(Q_BLOCK))[:, None]
        z = jnp.einsum('bhcd,bhsd->bhcs', qc, k).astype(F32) * scale
        log_1m = jnp.where(before, jax.nn.log_sigmoid(-z), 0.0)
        tail = lax.cumsum(log_1m, axis=3, reverse=True) - log_1m
        w = jnp.where(before, jnp.exp(jax.nn.log_sigmoid(z) + tail), 0.0)
        return jnp.einsum('bhcs,bhsd->bhcd', w.astype(v.dtype), v)

    o = lax.map(block, jnp.arange(S // Q_BLOCK))
    return o.transpose(1, 2, 0, 3, 4).reshape(B, H, S, Dh)


def hybrid_mixer(x, w_in, b_gate, diff_lambda, diff_subln_g, w_br_moba, w_br_diff, w_br_sb, w_out, layer_idx):
    B, S, D = x.shape
    h = x @ w_in
    offs = []
    acc = 0
    for n in IN_SIZES[:-1]:
        acc += n
        offs.append(acc)
    q_m, k_m, v_m, q_d, k_d, v_d, q_s, k_s, v_s, g = jnp.split(h, offs, axis=-1)

    slopes = jnp.asarray(_alibi_slopes(N_MOBA_HEADS + N_DIFF_HEADS))
    o_m = moba_attention(split_heads(q_m, N_MOBA_HEADS), split_heads(k_m, N_MOBA_HEADS),
                         split_heads(v_m, N_MOBA_HEADS), slopes[0::2])

    lambda_init = 0.8 - 0.6 * math.exp(-0.3 * layer_idx)
    lf = diff_lambda.astype(F32)
    lam = jnp.exp(jnp.sum(lf[0] * lf[1])) - jnp.exp(jnp.sum(lf[2] * lf[3])) + lambda_init
    qd = q_d.reshape(B, S, N_DIFF_HEADS, 2, DIFF_QK_DIM).transpose(0, 2, 3, 1, 4)
    kd = k_d.reshape(B, S, N_DIFF_HEADS, 2, DIFF_QK_DIM).transpose(0, 2, 3, 1, 4)
    o_d = diff_attention(qd, kd, split_heads(v_d, N_DIFF_HEADS), slopes[1::2], lam, diff_subln_g, lambda_init)

    o_s = stick_breaking_attention(split_heads(q_s, N_SB_HEADS), split_heads(k_s, N_SB_HEADS),
                                   split_heads(v_s, N_SB_HEADS))

    gates = jax.nn.sigmoid((g.reshape(B, S, N_BRANCH, D) + b_gate).astype(F32)).astype(x.dtype)
    merged = (gates[:, :, 0] * (merge_heads(o_m) @ w_br_moba)
              + gates[:, :, 1] * (merge_heads(o_d) @ w_br_diff)
              + gates[:, :, 2] * (merge_heads(o_s) @ w_br_sb))
    return merged @ w_out


def setup_inputs(seed: int = 0) -> dict:
    key = jax.random.key(seed)
    ks = jax.random.split(key, 14)
    D = D_MODEL
    beta = (8.0 * DEPTH) ** -0.25
    nrm = jax.random.normal
    segs = [(MOBA_W, 1.0), (MOBA_W, 1.0), (MOBA_W, beta), (DIFF_QK_W, 1.0), (DIFF_QK_W, 1.0),
            (DIFF_V_W, beta), (SB_W, 1.0), (SB_W, 1.0), (SB_W, beta), (N_BRANCH * D, 1.0)]
    col_scale = jnp.asarray(np.concatenate([np.full(n, s, np.float32) for n, s in segs]))
    return {
        "x": nrm(ks[0], (BATCH, SEQ, D), F32),
        "ln_g": 1.0 + 0.02 * nrm(ks[1], (DEPTH, 3, D), F32),
        "ln_b": 0.02 * nrm(ks[2], (DEPTH, 3, D), F32),
        "ffn_w_gate": nrm(ks[3], (DEPTH, 2, D, D_FF), F32) * D ** -0.5,
        "ffn_w_up": nrm(ks[4], (DEPTH, 2, D, D_FF), F32) * (D ** -0.5 * beta),
        "ffn_w_down": nrm(ks[5], (DEPTH, 2, D_FF, D), F32) * (D_FF ** -0.5 * beta),
        "w_in": nrm(ks[6], (DEPTH, D, N_IN), F32) * D ** -0.5 * col_scale,
        "b_gate": 0.01 * nrm(ks[7], (DEPTH, N_BRANCH, D), F32),
        "diff_lambda": 0.1 * nrm(ks[8], (DEPTH, 4, DIFF_QK_DIM), F32),
        "diff_subln_g": 1.0 + 0.02 * nrm(ks[9], (DEPTH, HEAD_DIM), F32),
        "w_br_moba": nrm(ks[10], (DEPTH, MOBA_W, D), F32) * MOBA_W ** -0.5,
        "w_br_diff": nrm(ks[11], (DEPTH, DIFF_V_W, D), F32) * DIFF_V_W ** -0.5,
        "w_br_sb": nrm(ks[12], (DEPTH, SB_W, D), F32) * SB_W ** -0.5,
        "w_out": nrm(ks[13], (DEPTH, D, D), F32) * (D ** -0.5 * beta),
    }


def reference(x, ln_g, ln_b, ffn_w_gate, ffn_w_up, ffn_w_down, w_in, b_gate, diff_lambda,
              diff_subln_g, w_br_moba, w_br_diff, w_br_sb, w_out):
    alpha = (2.0 * DEPTH) ** 0.25
    for l in range(DEPTH):
        x = layer_norm(alpha * x + 0.5 * swiglu(x, ffn_w_gate[l, 0], ffn_w_up[l, 0], ffn_w_down[l, 0]),
                       ln_g[l, 0], ln_b[l, 0])
        x = layer_norm(alpha * x + hybrid_mixer(x, w_in[l], b_gate[l], diff_lambda[l], diff_subln_g[l],
                                                w_br_moba[l], w_br_diff[l], w_br_sb[l], w_out[l], l),
                       ln_g[l, 1], ln_b[l, 1])
        x = layer_norm(alpha * x + 0.5 * swiglu(x, ffn_w_gate[l, 1], ffn_w_up[l, 1], ffn_w_down[l, 1]),
                       ln_g[l, 2], ln_b[l, 2])
    return x
```

```python
import math
from contextlib import contextmanager, ExitStack
import numpy as np
import ml_dtypes
import concourse.bass as bass
import concourse.mybir as mybir
from concourse.bass_utils import run_bass_kernel_spmd

F32 = mybir.dt.float32
BF16 = mybir.dt.bfloat16
AF = mybir.ActivationFunctionType
ALU = mybir.AluOpType
AX = mybir.AxisListType

D = 1024
DFF = 2816
NF = DFF // 128
NTOK = 4096
SEQ = 8192
NIN = 6528
ALPHA = 4.0 ** 0.25
LN_EPS = 1e-5
NCORES = 8


@contextmanager
def my_fori(nc, e, regs, start, end):
    loop_id = nc.next_id()
    name = f"myfori_{loop_id}"
    ls, le = name + "_loop", name + "_end"
    engines = bass.OrderedEngineSet([e.engine])
    nc.regs_mov(regs, start)
    nc.br(ls, engines=engines)
    with nc.body(ls, valid_engines=engines):
        yield nc.snap(regs, min_val=start, max_val=end - 1)
        nc.regs_alu(regs, regs, 1, op=mybir.AluOpType.add)
        nc.br_lt(regs, end, on_true=ls, on_false=le, engines=engines)
    nc.switch_bb(le)


class Prog:
    def __init__(self):
        self.root = []
        self.stack = [self.root]
        self.nloops = 0

    def op(self, eng, fn, chain=False, waits=()):
        self.stack[-1].append(['op', eng, fn, chain, tuple(waits), 'c'])

    def dma(self, eng, fn, waits=()):
        self.stack[-1].append(['op', eng, fn, False, tuple(waits), 'd'])

    @contextmanager
    def loop(self, n, clear=()):
        body = []
        lid = self.nloops
        self.nloops += 1
        self.stack[-1].append(['loop', n, body, lid, tuple(clear)])
        self.stack.append(body)
        try:
            yield lid
        finally:
            self.stack.pop()

    @staticmethod
    def _size(node):
        if node[0] == 'op':
            return 1
        return node[1] * sum(Prog._size(b) for b in node[2])

    @staticmethod
    def _has(node, ename):
        if node[0] == 'op':
            return node[1] == ename
        return any(Prog._has(b, ename) for b in node[2])

    def total(self):
        return sum(self._size(n) for n in self.root)

    def emit(self, ename, e, G, semfn, nc):
        k = 0
        lregs = nc.alloc_registers(f"lr_{ename}", engines=bass.OrderedEngineSet([e.engine]))
        for node in self.root:
            if node[0] == 'op' or node[1] == 1:
                subs = [node] if node[0] == 'op' else node[2]
                iv = {} if node[0] == 'op' else {node[3]: 0}
                for sub in subs:
                    _, eng, fn, chain, waits, kind = sub
                    if eng == ename:
                        e.wait_ge(G, k)
                        for (sem, vf) in waits:
                            e.wait_ge(sem, vf(iv) if callable(vf) else vf)
                        if kind == 'c':
                            fn(e, iv).then_inc(G, 1)
                        else:
                            fn(e, iv)
                            e.sem_inc(G, 1)
                    k += 1
                continue
            _, n, body, lid, clr = node
            E, S2 = semfn(lid)
            N = len(body)

            def release(gwait):
                e.wait_ge(G, gwait)
                e.wait_ge(E, 5)
                e.sem_clear(G)
                e.sem_clear(E)
                for cs in clr:
                    e.sem_clear(cs)
                e.sem_inc(S2, 1)
            e.sem_inc(E, 1)
            if ename == 'sp':
                release(k)
            with my_fori(nc, e, lregs, 1, n + 1) as i:
                e.wait_ge(S2, i)
                iv = {lid: i - 1}
                prev = None
                for j, sub in enumerate(body):
                    assert sub[0] == 'op'
                    _, eng, fn, chain, waits, kind = sub
                    if eng == ename:
                        if j > 0 and not (chain and prev is not None and prev[1] == ename):
                            e.wait_ge(G, j)
                        for (sem, vf) in waits:
                            e.wait_ge(sem, vf({lid: 0}) if callable(vf) else vf)
                        if kind == 'c':
                            fn(e, iv).then_inc(G, 1)
                        else:
                            fn(e, iv)
                            e.sem_inc(G, 1)
                    prev = sub
                e.sem_inc(E, 1)
                if ename == 'sp':
                    release(N)
            e.wait_ge(S2, n + 1)
            if ename == 'sp':
                e.sem_inc(G, k + 1)
            k += 1


def run_prog(nc, P, G):
    engs = {'pe': 'tensor', 'act': 'scalar', 'dve': 'vector', 'pool': 'gpsimd', 'sp': 'sync'}
    with ExitStack() as sctx:
        lsems = {}
        for node in P.root:
            if node[0] == 'loop' and node[1] > 1:
                lid = node[3]
                lsems[lid] = tuple(sctx.enter_context(nc.semaphore(f"L{lid}_{nm}")) for nm in ("E", "S"))
        with nc.Block() as block:
            for ename, bname in engs.items():
                def mk(ename=ename):
                    def f(e):
                        P.emit(ename, e, G, lambda lid: lsems[lid], nc)
                    return f
                getattr(block, bname)(mk())


class RL:
    pass


def rl_alloc(nc, ctx):
    R = RL()
    R.nc = nc
    R.WBUF = ctx.enter_context(nc.sbuf_tensor("WBUF", [128, 67584], BF16))
    R.SCRF = ctx.enter_context(nc.sbuf_tensor("SCRF", [128, 8704], F32))
    R.SCRB = ctx.enter_context(nc.sbuf_tensor("SCRB", [128, 15360], BF16))
    R.identf = ctx.enter_context(nc.sbuf_tensor("identf", [128, 128], F32))
    R.identb = ctx.enter_context(nc.sbuf_tensor("identb", [128, 128], BF16))
    R.stats = ctx.enter_context(nc.sbuf_tensor("stats", [128, 12], F32))
    R.mv = ctx.enter_context(nc.sbuf_tensor("mv", [128, 2], F32))
    R.rstd = ctx.enter_context(nc.sbuf_tensor("rstd", [128, 1], F32))
    R.ps = [ctx.enter_context(nc.psum_tensor(f"ps{i}", [128, 512], F32)) for i in range(6)]
    R.psb = [ctx.enter_context(nc.psum_tensor(f"psb{i}", [128, 1024], BF16)) for i in range(2)]
    R.G = ctx.enter_context(nc.semaphore("G"))
    R.csem = ctx.enter_context(nc.semaphore("csem"))
    R.nsem = 0
    R.ctx = ctx
    return R


def new_sem(R, name):
    R.nsem += 1
    return R.ctx.enter_context(R.nc.semaphore(f"{name}_{R.nsem}"))


def rl_consts(P, R, identf_d, identb_d):
    def f(e, iv):
        e.dma_start(out=R.identf[:], in_=identf_d[:, :]).then_inc(R.csem, 16)
        e.dma_start(out=R.identb[:], in_=identb_d[:, :]).then_inc(R.csem, 16)
    P.dma('sp', f)
    P.dma('sp', lambda e, iv: None, waits=[(R.csem, 32)])


def ln_ops(P, R, u, gbc, bbc, yo, pre_waits=()):
    for h in range(2):
        P.op('dve', lambda e, iv, h=h: e.bn_stats(out=R.stats[:, h * 6:(h + 1) * 6], in_=u[:, h * 512:(h + 1) * 512]))
    P.op('dve', lambda e, iv: e.bn_aggr(out=R.mv[:], in_=R.stats[:]))
    P.op('dve', lambda e, iv: e.tensor_scalar(out=R.rstd[:], in0=R.mv[:, 1:2], scalar1=LN_EPS, scalar2=None, op0=ALU.add))
    P.op('act', lambda e, iv: e.activation(out=R.rstd[:], in_=R.rstd[:], func=AF.Sqrt))
    P.op('dve', lambda e, iv: e.reciprocal(out=R.rstd[:], in_=R.rstd[:]))
    P.op('dve', lambda e, iv: e.tensor_scalar(out=u, in0=u, scalar1=R.mv[:, 0:1], scalar2=R.rstd[:, 0:1],
                                              op0=ALU.subtract, op1=ALU.mult))
    P.op('pool', lambda e, iv: e.tensor_tensor(out=u, in0=u, in1=gbc, op=ALU.mult))
    P.op('pool', lambda e, iv: e.tensor_tensor(out=yo, in0=u, in1=bbc, op=ALU.add), waits=pre_waits)


def load_ln_params(P, R, g_d, b_d, gbc, bbc):
    sem = new_sem(R, "lnp")

    def f(e, iv):
        e.dma_start(out=gbc, in_=g_d.partition_broadcast(128)).then_inc(sem, 16)
        e.dma_start(out=bbc, in_=b_d.partition_broadcast(128)).then_inc(sem, 16)
    P.dma('sp', f)
    P.dma('sp', lambda e, iv: None, waits=[(sem, 32)])


def transposes_to_xT(P, R, xs4_loader, xT, nj=4):
    for j in range(nj):
        xs = xs4_loader(j)
        for c2 in range(2):
            for c in range(4):
                cc = c2 * 4 + c
                P.op('pe', lambda e, iv, cc=cc, c=c, xs=xs: e.transpose(
                    out=R.ps[0][:, c * 128:(c + 1) * 128], in_=xs[:, cc * 128:(cc + 1) * 128], identity=R.identf[:]),
                    chain=(c > 0))
            P.op('act', lambda e, iv, c2=c2, j=j: e.activation(
                out=xT[:, c2 * 4:(c2 + 1) * 4, j * 128:(j + 1) * 128],
                in_=R.ps[0][:, :].rearrange("p (c t) -> p c t", c=4), func=AF.Copy))


def stage_ffnln(P, R, x_in, x_out, wg_d, wu_d, wd_d, g_d, b_d, nsb=16):
    ntiles = nsb * 4
    wg = R.WBUF[:, 0:22528].rearrange("p (c f) -> p c f", c=8)
    wu = R.WBUF[:, 22528:45056].rearrange("p (c f) -> p c f", c=8)
    wd = R.WBUF[:, 45056:67584].rearrange("p (c n) -> p c n", c=NF)
    xs = R.SCRF[:, 0:1024]
    u = R.SCRF[:, 1024:2048]
    xa = R.SCRF[:, 2048:3072]
    yo = R.SCRF[:, 3072:4096]
    sg = R.SCRF[:, 4096:4608]
    gbc = R.SCRF[:, 4608:5632]
    bbc = R.SCRF[:, 5632:6656]
    xT = R.SCRB[:, 0:4096].rearrange("p (c t) -> p c t", c=8)
    aT = R.SCRB[:, 4096:15360].rearrange("p (f t) -> p f t", f=NF)
    wsem = new_sem(R, "w")
    dsx = new_sem(R, "dsx")
    dso = new_sem(R, "dso")

    def loadw(e, iv):
        for c in range(8):
            e.dma_start(out=wg[:, c, :].rearrange("p (h n) -> p h n", h=2),
                        in_=wg_d[c * 128:(c + 1) * 128, :].rearrange("p (h n) -> p h n", h=2)).then_inc(wsem, 16)
            e.dma_start(out=wu[:, c, :].rearrange("p (h n) -> p h n", h=2),
                        in_=wu_d[c * 128:(c + 1) * 128, :].rearrange("p (h n) -> p h n", h=2)).then_inc(wsem, 16)
        for f in range(NF):
            e.dma_start(out=wd[:, f, :], in_=wd_d[f * 128:(f + 1) * 128, :]).then_inc(wsem, 16)
    P.dma('pool', loadw)
    P.dma('pool', lambda e, iv: None, waits=[(wsem, 16 * (16 + NF))])
    load_ln_params(P, R, g_d, b_d, gbc, bbc)

    with P.loop(ntiles, clear=(dsx, dso)) as L:
        def loader(j):
            P.dma('sp', lambda e, iv: e.dma_start(out=xs, in_=x_in[bass.ts(iv[L], 128), :]).then_inc(dsx, 16))
            P.dma('sp', lambda e, iv: None, waits=[(dsx, 16)])
            return xs
        transposes_to_xT(P, R, loader, xT, nj=1)
        for g0 in range(0, NF, 4):
            gw = min(4, NF - g0)
            first = True
            for gi in range(gw):
                f = g0 + gi
                for c in range(8):
                    P.op('pe', lambda e, iv, f=f, c=c, gi=gi: e.matmul(
                        R.ps[1][:, gi * 128:(gi + 1) * 128], lhsT=wg[:, c, f * 128:(f + 1) * 128], rhs=xT[:, c, 0:128],
                        start=(c == 0), stop=(c == 7)), chain=not first)
                    first = False
            for gi in range(gw):
                f = g0 + gi
                for c in range(8):
                    P.op('pe', lambda e, iv, f=f, c=c, gi=gi: e.matmul(
                        R.ps[2][:, gi * 128:(gi + 1) * 128], lhsT=wu[:, c, f * 128:(f + 1) * 128], rhs=xT[:, c, 0:128],
                        start=(c == 0), stop=(c == 7)), chain=True)
            P.op('act', lambda e, iv, gw=gw: e.activation(out=sg[:, 0:gw * 128], in_=R.ps[1][:, 0:gw * 128], func=AF.Silu))
            P.op('dve', lambda e, iv, gw=gw, g0=g0: e.tensor_tensor(
                out=aT[:, g0:g0 + gw, 0:128], in0=sg[:, 0:gw * 128].rearrange("p (g t) -> p g t", g=gw),
                in1=R.ps[2][:, 0:gw * 128].rearrange("p (g t) -> p g t", g=gw), op=ALU.mult))
        first = True
        for h in range(2):
            for f in range(NF):
                P.op('pe', lambda e, iv, f=f, h=h: e.matmul(
                    R.ps[3 + h][:], lhsT=aT[:, f, 0:128], rhs=wd[:, f, h * 512:(h + 1) * 512],
                    start=(f == 0), stop=(f == NF - 1)), chain=not first)
                first = False
        P.op('act', lambda e, iv: e.mul(out=xa, in_=xs, mul=ALPHA))
        for h in range(2):
            P.op('dve', lambda e, iv, h=h: e.scalar_tensor_tensor(
                out=u[:, h * 512:(h + 1) * 512], in0=R.ps[3 + h][:], scalar=0.5, in1=xa[:, h * 512:(h + 1) * 512],
                op0=ALU.mult, op1=ALU.add))
        ln_ops(P, R, u, gbc, bbc, yo)
        P.dma('sp', lambda e, iv: e.dma_start(out=x_out[bass.ts(iv[L], 128), :], in_=yo).then_inc(dso, 16))
        P.dma('sp', lambda e, iv: None, waits=[(dso, 16)])


QK_COLS = [0, 128, 256, 384, 512, 640,
           1152, 1280, 1408, 1536, 1664, 1792,
           2304, 2432, 2560, 2688, 2816, 2944]
V_COLS = [768, 1920, 3072]
G_COL = 3456


def stage_win(P, R, x_in, win_d, bgate_d, qkT_out, qm32_out, kmean_out, v_out, gates_out, nsb=16):
    ntiles = nsb * 4
    win = R.WBUF[:, 0:52224].rearrange("p (c n) -> p c n", c=8)
    xs = R.SCRF[:, 0:1024]
    gt = R.SCRF[:, 1024:4096]
    q32 = R.SCRF[:, 4096:4480].rearrange("p (r t) -> p r t", r=3)
    bgb = R.SCRF[:, 5632:8704]
    xT = R.SCRB[:, 0:4096].rearrange("p (c t) -> p c t", c=8)
    qkst = R.SCRB[:, 4096:6400].rearrange("p (r t) -> p r t", r=18)
    vst = R.SCRB[:, 13312:14464]
    km = R.stats[:, 0:3]
    wsem = new_sem(R, "w")
    dsx = new_sem(R, "dsx")
    dsq = new_sem(R, "dsq")

    def loadw(e, iv):
        for c in range(8):
            e.dma_start(out=win[:, c, :].rearrange("p (h n) -> p h n", h=4),
                        in_=win_d[c * 128:(c + 1) * 128, :].rearrange("p (h n) -> p h n", h=4)).then_inc(wsem, 16)
    P.dma('pool', loadw)
    P.dma('pool', lambda e, iv: None, waits=[(wsem, 16 * 8)])
    bsem = new_sem(R, "bg")
    P.dma('sp', lambda e, iv: e.dma_start(
        out=bgb, in_=bgate_d.rearrange("a d -> (a d)").partition_broadcast(128)).then_inc(bsem, 16))
    P.dma('sp', lambda e, iv: None, waits=[(bsem, 16)])

    with P.loop(ntiles, clear=(dsx, dsq)) as L:
        def loader(j):
            P.dma('act', lambda e, iv: e.dma_start(out=xs, in_=x_in[bass.ts(iv[L], 128), :]).then_inc(dsx, 16))
            P.dma('act', lambda e, iv: None, waits=[(dsx, 16)])
            return xs
        transposes_to_xT(P, R, loader, xT, nj=1)
        for r in range(18):
            co = QK_COLS[r]
            for c in range(8):
                P.op('pe', lambda e, iv, c=c, co=co: e.matmul(R.ps[1][:, 0:128], lhsT=win[:, c, co:co + 128], rhs=xT[:, c, 0:128],
                                                              start=(c == 0), stop=(c == 7)), chain=(c > 0))
            P.op('act', lambda e, iv, r=r: e.copy(out=qkst[:, r, :], in_=R.ps[1][:, 0:128]))
            if r < 3:
                P.op('dve', lambda e, iv, r=r: e.tensor_copy(out=q32[:, r, :], in_=R.ps[1][:, 0:128]))
            if 3 <= r < 6:
                P.op('dve', lambda e, iv, r=r: e.tensor_reduce(out=km[:, r - 3:r - 2], in_=R.ps[1][:, 0:128], axis=AX.X, op=ALU.add))

        def stq(e, iv):
            e.dma_start(out=qkT_out[:, bass.ts(iv[L], 128)].rearrange("(r p) t -> p r t", p=128),
                        in_=qkst).then_inc(dsq, 16)
            e.dma_start(out=qm32_out[:, bass.ts(iv[L], 128)].rearrange("(r p) t -> p r t", p=128),
                        in_=q32).then_inc(dsq, 16)
            e.dma_start(out=kmean_out[:, bass.ts(iv[L], 1)].rearrange("(r p) b -> p r b", p=128),
                        in_=km.rearrange("p (r b) -> p r b", b=1), allow_slow_non_contiguous=True).then_inc(dsq, 16)
        P.dma('act', stq)
        for vi, co in enumerate(V_COLS):
            for c in range(8):
                P.op('pe', lambda e, iv, c=c, co=co: e.matmul(
                    R.ps[2][:, 0:384], lhsT=xT[:, c, 0:128], rhs=win[:, c, co:co + 384],
                    start=(c == 0), stop=(c == 7)), chain=(c > 0))
            P.op('act', lambda e, iv, vi=vi: e.copy(out=vst[:, vi * 384:(vi + 1) * 384], in_=R.ps[2][:, 0:384]))
        for gi in range(6):
            co = G_COL + gi * 512
            for c in range(8):
                P.op('pe', lambda e, iv, c=c, co=co: e.matmul(
                    R.ps[3][:], lhsT=xT[:, c, 0:128], rhs=win[:, c, co:co + 512],
                    start=(c == 0), stop=(c == 7)), chain=(c > 0))
            P.op('dve', lambda e, iv, gi=gi: e.tensor_tensor(out=gt[:, gi * 512:(gi + 1) * 512], in0=R.ps[3][:],
                                                             in1=bgb[:, gi * 512:(gi + 1) * 512], op=ALU.add))
        P.op('act', lambda e, iv: e.activation(out=gt, in_=gt, func=AF.Sigmoid))
        P.dma('act', lambda e, iv: e.dma_start(out=v_out[bass.ts(iv[L], 128), :], in_=vst).then_inc(dsq, 16))
        P.dma('act', lambda e, iv: e.dma_start(out=gates_out[bass.ts(iv[L], 128), :], in_=gt).then_inc(dsq, 16))
        P.dma('act', lambda e, iv: None, waits=[(dsq, 80)])


BIG = 30000.0
SLOPES = (2.0 ** (-8.0 * np.arange(1, 13, dtype=np.float32) / 12)).astype(np.float32)
SC_M = 64 ** -0.5
SC_D = 32 ** -0.5
SC_S = 64 ** -0.5
SUBLN_EPS = 1e-5


def att_tables():
    bf = ml_dtypes.bfloat16
    i = np.arange(SEQ)
    ib = (i // 128).astype(np.float32)
    il = (i % 128).astype(np.float32)
    one = np.ones(SEQ, np.float32)

    def hl(v):
        hi = np.float32(np.asarray(v, np.float32).astype(bf).astype(np.float32))
        lo = np.float32(np.asarray(v - hi, np.float32).astype(bf).astype(np.float32))
        return hi, lo
    qaug = np.zeros((12, 8, SEQ), np.float32)
    kaug = np.zeros((12, 8, SEQ), np.float32)
    for slot in range(12):
        scale = SC_M if slot % 2 == 0 else SC_D
        sig = float(SLOPES[slot]) / scale
        Sh, Sl = hl(128.0 * sig)
        sh, sl = hl(sig)
        qaug[slot] = np.stack([Sh * one, Sl * one, sh * one, sl * one, -ib, -ib, -il, -il])
        kaug[slot] = np.stack([ib, ib, il, il, Sh * one, Sl * one, sh * one, sl * one])
    onehot = (np.arange(32)[:, None] == (i // 256)[None, :]).astype(np.float32)
    p = np.arange(128)[:, None]
    c = np.arange(512)[None, :]
    maskT = np.concatenate([np.where((r * 128 + p) > c, -BIG, 0.0) for r in range(4)], axis=1)
    maskS = np.concatenate([np.where(c >= (r * 128 + p), -BIG, 0.0) for r in range(4)], axis=1)
    o = np.arange(32)[:, None]
    n = np.arange(32)[None, :]
    pb = np.stack([np.where(n < o, 0.0, -BIG), np.where(n == o, 0.0, -BIG), np.where(n <= o, 0.0, -BIG)]).astype(np.float32)
    return dict(qaug=qaug.reshape(96, SEQ).astype(bf), kaug=kaug.reshape(96, SEQ).astype(bf), onehot=onehot.astype(bf),
                maskT=maskT.astype(bf), maskS=maskS.astype(bf), pb=pb.reshape(-1))


def stage_att(P, R, T, qkT_d, qm32_d, kmean_d, v_d, o_d, lam_d, subg_d, one_m_linit, nheads=6, do=('m', 'd', 's')):
    W = R.WBUF
    QA = [W[:, 0:8192], W[:, 8192:16384]]
    KA = [W[:, 16384:24576], W[:, 24576:32768]]
    VA = W[:, 32768:40960].rearrange("p (k d) -> p k d", k=64)
    VAf = W[:, 32768:40960]
    maskT = W[:, 40960:43008]
    maskS = W[:, 43008:45056]
    pt = [W[:, 45056:45568], W[:, 45568:46080]]
    wb = W[:, 46080:46592]
    wT = W[:, 46592:47104].rearrange("p (k q) -> p k q", k=4)
    FW = W[:, 47360:67584].bitcast(F32)
    pbt = FW[:, 0:3072].rearrange("p (a o n) -> p a o n", a=3, o=32)
    osb = [FW[:, 3072:3584], FW[:, 3584:4096]]
    gm = FW[:, 4096:4128]
    sel = FW[:, 4128:4160]
    m8 = FW[:, 4160:4168]
    rinv = [FW[:, 4168:4172], FW[:, 4172:4176]]
    ss = FW[:, 4176:4180]
    carry = FW[:, 4180:4181]
    negc = FW[:, 4181:4182]
    tot = FW[:, 4182:4183]
    lamt = FW[:, 4183:4184]
    lsc = FW[:, 4184:4186]
    t1 = FW[:, 4192:4256]
    od = FW[:, 4256:4512].rearrange("p (j d) -> p j d", j=4)
    gsc = FW[:, 4512:4576]
    lamw = FW[:, 4576:4704]
    lamw2 = FW[:, 4704:4768]
    ones = FW[:, 4768:5280]
    tt = FW[:, 5280:5792]
    spb = FW[:, 5792:6305]
    E1 = FW[:, 6336:6848]
    aa = FW[:, 6848:7360]
    kme = FW[:, 7360:7392]
    kmr = FW[:, 9472:9536]
    ost_all = FW[:, 7424:9472].bitcast(BF16).rearrange("p (t d) -> p t d", t=64)
    qm32 = R.SCRF[:, 0:8192]
    ps_s = [R.ps[0], R.ps[1]]
    ps_o = [R.ps[2], R.ps[3]]
    ps_t = [R.ps[4], R.ps[5]]
    psb = R.psb[0]
    tsem = new_sem(R, "att_t")
    hsem = new_sem(R, "att_h")
    osem = new_sem(R, "att_o")
    cnt = {'h': 0, 'o': 0}

    def ldc(e, iv):
        e.dma_start(out=maskT, in_=T['maskT'][:, :]).then_inc(tsem, 16)
        e.dma_start(out=maskS, in_=T['maskS'][:, :]).then_inc(tsem, 16)
        e.dma_start(out=FW[:, 0:3072], in_=T['pb'].partition_broadcast(128)).then_inc(tsem, 16)
        e.dma_start(out=gsc, in_=subg_d.partition_broadcast(128)).then_inc(tsem, 16)
        e.dma_start(out=lamw, in_=lam_d.rearrange("a d -> (a d)").partition_broadcast(128)).then_inc(tsem, 16)
    P.dma('sp', ldc)
    P.dma('sp', lambda e, iv: None, waits=[(tsem, 80)])
    P.op('dve', lambda e, iv: e.memset(ones, 1.0))
    P.op('dve', lambda e, iv: e.memset(spb[:, 0:1], 0.0))
    P.op('dve', lambda e, iv: e.memset(VA[:, :, 64:65], 1.0))
    linit = 1.0 - one_m_linit
    P.op('dve', lambda e, iv: e.tensor_tensor(out=lamw2[:, 0:32], in0=lamw[:, 0:32], in1=lamw[:, 32:64], op=ALU.mult))
    P.op('dve', lambda e, iv: e.tensor_tensor(out=lamw2[:, 32:64], in0=lamw[:, 64:96], in1=lamw[:, 96:128], op=ALU.mult))
    P.op('dve', lambda e, iv: e.tensor_reduce(out=lsc, in_=lamw2.rearrange("p (a d) -> p a d", a=2), axis=AX.X, op=ALU.add))
    P.op('act', lambda e, iv: e.activation(out=lsc, in_=lsc, func=AF.Exp))
    P.op('dve', lambda e, iv: e.tensor_tensor(out=lamt, in0=lsc[:, 0:1], in1=lsc[:, 1:2], op=ALU.subtract))
    P.op('dve', lambda e, iv: e.tensor_scalar(out=lamt, in0=lamt, scalar1=linit, scalar2=-1.0, op0=ALU.add, op1=ALU.mult))
    P.op('act', lambda e, iv: e.mul(out=gsc, in_=gsc, mul=one_m_linit))

    def dense_tile(s, Kd, scale, kb_ap_K, kb_ap_V, Q, mask_r, first):
        P.op('pe', lambda e, iv: e.matmul(ps_s[s][:], lhsT=kb_ap_K(iv), rhs=QA[s][0:Kd, Q * 512:(Q + 1) * 512],
                                          start=True, stop=(mask_r is None)))
        if mask_r is not None:
            P.op('pe', lambda e, iv: e.matmul(ps_s[s][:], lhsT=R.identb[:], rhs=maskT[:, mask_r * 512:(mask_r + 1) * 512],
                                              start=False, stop=True), chain=True)
        P.op('act', lambda e, iv: e.activation(out=pt[s], in_=ps_s[s][:], func=AF.Exp, scale=scale))
        P.op('pe', lambda e, iv: e.matmul(ps_o[s][0:65, :], lhsT=kb_ap_V(iv), rhs=pt[s], start=first, stop=True))

    def dense_Q(streams, Q):
        for r in range(4):
            kb = 4 * Q + r
            for (s, Kd, scale) in streams:
                dense_tile(s, Kd, scale, lambda iv, s=s, Kd=Kd, kb=kb: KA[s][0:Kd, kb * 128:(kb + 1) * 128],
                           lambda iv, kb=kb: VA[:, kb, 0:65], Q, r, r == 0)
        for kb in range(4 * Q):
            for (s, Kd, scale) in streams:
                dense_tile(s, Kd, scale, lambda iv, s=s, Kd=Kd, kb=kb: KA[s][0:Kd, kb * 128:(kb + 1) * 128],
                           lambda iv, kb=kb: VA[:, kb, 0:65], Q, None, False)
        for (s, Kd, scale) in streams:
            P.op('act', lambda e, iv, s=s: e.copy(out=osb[s][0:65, :], in_=ps_o[s][0:65, :]))
            for j in range(4):
                P.op('pe', lambda e, iv, s=s, j=j: e.transpose(out=ps_t[s][:, j * 65:(j + 1) * 65],
                                                               in_=osb[s][0:65, j * 128:(j + 1) * 128],
                                                               identity=R.identf[0:65, 0:65]), chain=(j > 0))
            P.op('dve', lambda e, iv, s=s: e.reciprocal(
                out=rinv[s], in_=ps_t[s][:, 0:260].rearrange("p (j d) -> p j d", j=4)[:, :, 64]))

    def store_head(hfn, dq):
        P.dma(dq, lambda e, iv: e.dma_start(
            out=o_d[bass.ts(hfn(iv), 1), :, :].rearrange("h (t p) d -> p (h t) d", p=128),
            in_=ost_all).then_inc(osem, 16))
        P.dma(dq, lambda e, iv: None, waits=[(osem, 16)])

    if 'm' in do:
        with P.loop(nheads, clear=(hsem, osem)) as LHM:
            hbaseM = 0

            def ldm(e, iv):
                h = iv[LHM]
                e.dma_start(out=QA[0][32:96, :], in_=qkT_d[0:384, :][bass.ts(h, 64), :]).then_inc(hsem, 16)
                e.dma_start(out=KA[0][32:96, :], in_=qkT_d[384:768, :][bass.ts(h, 64), :]).then_inc(hsem, 16)
                e.dma_start(out=KA[0][0:32, :], in_=T['onehot'][:, :]).then_inc(hsem, 16)
                e.dma_start(out=QA[0][96:104, :], in_=T['qaug'][bass.ts(h * 2, 8), :]).then_inc(hsem, 16)
                e.dma_start(out=KA[0][96:104, :], in_=T['kaug'][bass.ts(h * 2, 8), :]).then_inc(hsem, 16)
                e.dma_start(out=VA[:, :, 0:64],
                            in_=v_d[:, 0:384][:, bass.ts(h, 64)].rearrange("(k p) d -> p k d", p=128)).then_inc(hsem, 16)
                e.dma_start(out=qm32[0:64, :], in_=qm32_d[bass.ts(h, 64), :]).then_inc(hsem, 16)
                e.dma_start(out=kmr[0:64, :], in_=kmean_d[bass.ts(h, 64), :]).then_inc(hsem, 16)
            P.dma('sp', ldm)
            P.dma('sp', lambda e, iv: None, waits=[(hsem, lambda iv: (hbaseM + (iv[LHM] + 1) * 8) * 16)])
            P.op('dve', lambda e, iv: e.tensor_reduce(out=kme[0:64, :], in_=kmr[0:64, :].rearrange("p (n b) -> p n b", b=2),
                                                      axis=AX.X, op=ALU.add))
            P.op('dve', lambda e, iv: e.tensor_scalar(out=kme[0:64, :], in0=kme[0:64, :], scalar1=1.0 / 256.0, scalar2=None,
                                                      op0=ALU.mult))
            for t4 in range(16):
                for tj in range(4):
                    t = t4 * 4 + tj
                    own = t // 2
                    P.op('pe', lambda e, iv, t=t: e.matmul(ps_s[1][:, 0:32], lhsT=qm32[0:64, t * 128:(t + 1) * 128],
                                                           rhs=kme[0:64, :], start=True, stop=True))
                    P.op('dve', lambda e, iv, own=own: e.tensor_tensor(out=gm, in0=ps_s[1][:, 0:32], in1=pbt[:, 0, own, :],
                                                                       op=ALU.add))
                    P.op('dve', lambda e, iv: e.max(out=m8, in_=gm))
                    P.op('dve', lambda e, iv: e.tensor_scalar(out=sel, in0=gm, scalar1=m8[:, 2:3], scalar2=1.0,
                                                              op0=ALU.is_ge, op1=ALU.subtract))
                    P.op('dve', lambda e, iv, own=own: e.scalar_tensor_tensor(out=sel, in0=sel, scalar=BIG,
                                                                              in1=pbt[:, 1, own, :], op0=ALU.mult, op1=ALU.max))
                    P.op('dve', lambda e, iv, own=own: e.tensor_tensor(out=sel, in0=sel, in1=pbt[:, 2, own, :], op=ALU.add))
                    P.op('pe', lambda e, iv, tj=tj: e.transpose(out=ps_t[0][0:32, tj * 128:(tj + 1) * 128], in_=sel,
                                                                identity=R.identf[:]))
                P.op('act', lambda e, iv, t4=t4: e.copy(out=QA[0][0:32, t4 * 512:(t4 + 1) * 512], in_=ps_t[0][0:32, :]))
            for Q in range(16):
                dense_Q([(0, 104, SC_M)], Q)
                for j in range(4):
                    P.op('dve', lambda e, iv, j=j, Q=Q: e.tensor_scalar(
                        out=ost_all[:, Q * 4 + j, :], in0=ps_t[0][:, j * 65:j * 65 + 64], scalar1=rinv[0][:, j:j + 1], scalar2=None,
                        op0=ALU.mult))
            store_head(lambda iv: iv[LHM], 'sp')
    obase_d = 0

    if 'd' in do:
        with P.loop(nheads, clear=(hsem, osem)) as LHD:
            hbaseD = 0

            def ldd(e, iv):
                h = iv[LHD]
                for m in range(2):
                    e.dma_start(out=QA[m][0:32, :], in_=qkT_d[768:1152, :][bass.ts(h * 2 + m, 32), :]).then_inc(hsem, 16)
                    e.dma_start(out=KA[m][0:32, :], in_=qkT_d[1152:1536, :][bass.ts(h * 2 + m, 32), :]).then_inc(hsem, 16)
                    e.dma_start(out=QA[m][32:40, :], in_=T['qaug'][bass.ts(h * 2 + 1, 8), :]).then_inc(hsem, 16)
                    e.dma_start(out=KA[m][32:40, :], in_=T['kaug'][bass.ts(h * 2 + 1, 8), :]).then_inc(hsem, 16)
                e.dma_start(out=VA[:, :, 0:64],
                            in_=v_d[:, 384:768][:, bass.ts(h, 64)].rearrange("(k p) d -> p k d", p=128)).then_inc(hsem, 16)
            P.dma('act', ldd)
            P.dma('act', lambda e, iv: None, waits=[(hsem, lambda iv: (hbaseD + (iv[LHD] + 1) * 9) * 16)])
            for Q in range(16):
                dense_Q([(0, 40, SC_D), (1, 40, SC_D)], Q)
                P.op('dve', lambda e, iv: e.tensor_scalar(out=rinv[1], in0=rinv[1], scalar1=lamt[:, 0:1], scalar2=None,
                                                          op0=ALU.mult))
                for j in range(4):
                    P.op('dve', lambda e, iv, j=j: e.tensor_scalar(
                        out=t1, in0=ps_t[0][:, j * 65:j * 65 + 64], scalar1=rinv[0][:, j:j + 1], scalar2=None, op0=ALU.mult))
                    P.op('dve', lambda e, iv, j=j: e.scalar_tensor_tensor(
                        out=od[:, j, :], in0=ps_t[1][:, j * 65:j * 65 + 64], scalar=rinv[1][:, j:j + 1], in1=t1,
                        op0=ALU.mult, op1=ALU.add))
                    P.op('dve', lambda e, iv, j=j: e.tensor_tensor(out=t1, in0=od[:, j, :], in1=od[:, j, :], op=ALU.mult))
                    P.op('dve', lambda e, iv, j=j: e.tensor_reduce(out=ss[:, j:j + 1], in_=t1, axis=AX.X, op=ALU.add))
                P.op('dve', lambda e, iv: e.tensor_scalar(out=ss, in0=ss, scalar1=1.0 / 64.0, scalar2=SUBLN_EPS,
                                                          op0=ALU.mult, op1=ALU.add))
                P.op('act', lambda e, iv: e.activation(out=ss, in_=ss, func=AF.Sqrt))
                P.op('dve', lambda e, iv: e.reciprocal(out=ss, in_=ss))
                for j in range(4):
                    P.op('dve', lambda e, iv, j=j, Q=Q: e.scalar_tensor_tensor(
                        out=ost_all[:, Q * 4 + j, :], in0=od[:, j, :], scalar=ss[:, j:j + 1], in1=gsc, op0=ALU.mult, op1=ALU.mult))
            store_head(lambda iv: iv[LHD] + 6, 'act')
    obase_s = 0

    if 's' in do:
        with P.loop(nheads, clear=(hsem, osem)) as LHS:
            hbaseS = 0

            def lds(e, iv):
                h = iv[LHS]
                e.dma_start(out=QA[0][0:64, :], in_=qkT_d[1536:1920, :][bass.ts(h, 64), :]).then_inc(hsem, 16)
                e.dma_start(out=KA[0][0:64, :], in_=qkT_d[1920:2304, :][bass.ts(h, 64), :]).then_inc(hsem, 16)
                e.dma_start(out=VA[:, :, 0:64],
                            in_=v_d[:, 768:1152][:, bass.ts(h, 64)].rearrange("(k p) d -> p k d", p=128)).then_inc(hsem, 16)
            P.dma('sp', lds)
            P.dma('sp', lambda e, iv: None, waits=[(hsem, lambda iv: (hbaseS + (iv[LHS] + 1) * 3) * 16)])

            def sb_tile(t, kt_K, kt_V, mask_r, first):
                P.op('pe', lambda e, iv: e.matmul(ps_s[0][:], lhsT=QA[0][0:64, t * 128:(t + 1) * 128], rhs=kt_K(iv),
                                                  start=True, stop=(mask_r is None)))
                if mask_r is not None:
                    P.op('pe', lambda e, iv: e.matmul(ps_s[0][:], lhsT=R.identb[:], rhs=maskS[:, mask_r * 512:(mask_r + 1) * 512],
                                                      start=False, stop=True), chain=True)
                P.op('act', lambda e, iv: e.activation(out=tt, in_=ps_s[0][:], func=AF.Exp, scale=SC_S))
                P.op('act', lambda e, iv: e.activation(out=spb[:, 1:513], in_=tt, func=AF.Ln, bias=1.0))
                P.op('dve', lambda e, iv: e.tensor_tensor_scan(out=E1, data0=ones, data1=spb[:, 0:512], initial=0.0,
                                                               op0=ALU.mult, op1=ALU.add))
                P.op('dve', lambda e, iv: e.scalar_tensor_tensor(out=aa, in0=ps_s[0][:], scalar=SC_S, in1=E1,
                                                                 op0=ALU.mult, op1=ALU.add))
                P.op('dve', lambda e, iv: e.tensor_tensor(out=tot, in0=E1[:, 511:512], in1=spb[:, 512:513], op=ALU.add))
                if first:
                    P.op('dve', lambda e, iv: e.tensor_scalar(out=carry, in0=tot, scalar1=1.0, scalar2=None, op0=ALU.mult))
                else:
                    P.op('dve', lambda e, iv: e.tensor_tensor(out=carry, in0=carry, in1=tot, op=ALU.add))
                P.op('dve', lambda e, iv: e.tensor_scalar(out=negc, in0=carry, scalar1=-1.0, scalar2=None, op0=ALU.mult))
                P.op('act', lambda e, iv: e.activation(out=wb, in_=aa, func=AF.Exp, bias=negc[:, 0:1]))
                for k in range(4):
                    P.op('pe', lambda e, iv, k=k: e.transpose(out=psb[:, k * 128:(k + 1) * 128], in_=wb[:, k * 128:(k + 1) * 128],
                                                              identity=R.identb[:]), chain=(k > 0))
                P.op('act', lambda e, iv: e.copy(out=wT, in_=psb[:, 0:512].rearrange("p (k q) -> p k q", k=4)))
                for k in range(4):
                    P.op('pe', lambda e, iv, k=k: e.matmul(ps_o[0][:, 0:64], lhsT=wT[:, k, :], rhs=kt_V(iv, k),
                                                           start=(first and k == 0), stop=True), chain=(k > 0))

            for t in range(64):
                ktd = t // 4
                sb_tile(t, lambda iv, ktd=ktd: KA[0][0:64, ktd * 512:(ktd + 1) * 512],
                        lambda iv, k, ktd=ktd: VA[:, ktd * 4 + k, 0:64], t % 4, True)
                for kt in range(ktd - 1, -1, -1):
                    sb_tile(t, lambda iv, kt=kt: KA[0][0:64, kt * 512:(kt + 1) * 512],
                            lambda iv, k, kt=kt: VA[:, kt * 4 + k, 0:64], None, False)
                Q, j = t // 4, t % 4
                P.op('act', lambda e, iv, t=t: e.copy(out=ost_all[:, t, :], in_=ps_o[0][:, 0:64]))
            store_head(lambda iv: iv[LHS] + 12, 'sp')


def stage_mixpost(P, R, x_in, o_d, gates_d, x_out, wbm_d, wbd_d, wbs_d, wout_d, g_d, b_d, ntiles=64):
    W = R.WBUF
    wbr = W[:, 0:9216].rearrange("p (k n) -> p k n", k=9)
    wout = W[:, 9216:17408].rearrange("p (c n) -> p c n", c=8)
    gts = W[:, 17408:23552].bitcast(F32)
    xs = R.SCRF[:, 0:1024]
    u = R.SCRF[:, 1024:2048]
    xa = R.SCRF[:, 2048:3072]
    yo = R.SCRF[:, 3072:4096]
    gbc = R.SCRF[:, 4608:5632]
    bbc = R.SCRF[:, 5632:6656]
    merged = R.SCRF[:, 6656:7680]
    tmp = R.SCRF[:, 7680:8704]
    xT = R.SCRB[:, 0:4096].rearrange("p (c t) -> p c t", c=8)
    ot = R.SCRB[:, 4096:5248]
    oT = R.SCRB[:, 5248:6400].rearrange("p (k t) -> p k t", k=9)
    wsem = new_sem(R, "w")
    dsi = new_sem(R, "dsi")
    dso = new_sem(R, "dso")

    def loadw(e, iv):
        for bi, wd_ in enumerate((wbm_d, wbd_d, wbs_d)):
            for i in range(3):
                e.dma_start(out=wbr[:, bi * 3 + i, :], in_=wd_[i * 128:(i + 1) * 128, :]).then_inc(wsem, 16)
        for c in range(8):
            e.dma_start(out=wout[:, c, :], in_=wout_d[c * 128:(c + 1) * 128, :]).then_inc(wsem, 16)
    P.dma('pool', loadw)
    P.dma('pool', lambda e, iv: None, waits=[(wsem, 16 * 17)])
    load_ln_params(P, R, g_d, b_d, gbc, bbc)
    with P.loop(ntiles, clear=(dsi, dso)) as L:
        def ld(e, iv):
            e.dma_start(out=gts, in_=gates_d[bass.ts(iv[L], 128), :]).then_inc(dsi, 16)
            e.dma_start(out=xs, in_=x_in[bass.ts(iv[L], 128), :]).then_inc(dsi, 16)
        P.dma('sp', lambda e, iv: e.dma_start(out=ot.rearrange("p (h d) -> p h d", h=18),
                                              in_=o_d.rearrange("h s d -> s h d")[bass.ts(iv[L], 128), :, :]).then_inc(dsi, 16))
        P.dma('act', ld)
        P.dma('act', lambda e, iv: None, waits=[(dsi, lambda iv: (iv[L] + 1) * 48)])
        for k in range(9):
            P.op('pe', lambda e, iv, k=k: e.transpose(
                out=(R.psb[0][:, k * 128:(k + 1) * 128] if k < 8 else R.psb[1][:, 0:128]),
                in_=ot[:, k * 128:(k + 1) * 128], identity=R.identb[:]), chain=(k > 0))
        P.op('act', lambda e, iv: e.copy(out=oT[:, 0:8, :], in_=R.psb[0][:, :].rearrange("p (k t) -> p k t", k=8)))
        P.op('act', lambda e, iv: e.copy(out=oT[:, 8, :], in_=R.psb[1][:, 0:128]))
        for br in range(3):
            for h in range(2):
                for i in range(3):
                    P.op('pe', lambda e, iv, br=br, h=h, i=i: e.matmul(
                        R.ps[1][:], lhsT=oT[:, br * 3 + i, :], rhs=wbr[:, br * 3 + i, h * 512:(h + 1) * 512],
                        start=(i == 0), stop=(i == 2)), chain=(i > 0))
                gsl = gts[:, br * 1024 + h * 512: br * 1024 + (h + 1) * 512]
                if br == 0:
                    P.op('dve', lambda e, iv, h=h, gsl=gsl: e.tensor_tensor(out=merged[:, h * 512:(h + 1) * 512], in0=gsl,
                                                                            in1=R.ps[1][:], op=ALU.mult))
                else:
                    P.op('dve', lambda e, iv, h=h, gsl=gsl: e.tensor_tensor(out=tmp[:, h * 512:(h + 1) * 512], in0=gsl,
                                                                            in1=R.ps[1][:], op=ALU.mult))
                    P.op('dve', lambda e, iv, h=h: e.tensor_tensor(out=merged[:, h * 512:(h + 1) * 512],
                                                                   in0=merged[:, h * 512:(h + 1) * 512],
                                                                   in1=tmp[:, h * 512:(h + 1) * 512], op=ALU.add))
        transposes_to_xT(P, R, lambda j: merged, xT, nj=1)
        first = True
        for h in range(2):
            for c in range(8):
                P.op('pe', lambda e, iv, h=h, c=c: e.matmul(R.ps[3 + h][:], lhsT=xT[:, c, 0:128],
                                                            rhs=wout[:, c, h * 512:(h + 1) * 512],
                                                            start=(c == 0), stop=(c == 7)), chain=not first)
                first = False
        P.op('act', lambda e, iv: e.mul(out=xa, in_=xs, mul=ALPHA))
        for h in range(2):
            P.op('dve', lambda e, iv, h=h: e.tensor_tensor(out=u[:, h * 512:(h + 1) * 512], in0=xa[:, h * 512:(h + 1) * 512],
                                                           in1=R.ps[3 + h][:], op=ALU.add))
        ln_ops(P, R, u, gbc, bbc, yo, pre_waits=[(dso, lambda iv: iv[L] * 16)])
        P.dma('act', lambda e, iv: e.dma_start(out=x_out[bass.ts(iv[L], 128), :], in_=yo).then_inc(dso, 16))
        P.dma('act', lambda e, iv: None, waits=[(dso, 16)])


NACT = 4


def build_layer(tb, l):
    bf = ml_dtypes.bfloat16
    nc = bass.Bass("TRN2", target_bir_lowering=False)

    def dt(n, s, d=F32, k="ExternalInput"):
        return nc.dram_tensor(n, list(s), d, kind=k).ap()
    x = dt("x", [SEQ, D])
    ln_g = dt("ln_g", [3, D])
    ln_b = dt("ln_b", [3, D])
    wg = dt("ffn_w_gate", [2, D, DFF])
    wu = dt("ffn_w_up", [2, D, DFF])
    wd = dt("ffn_w_down", [2, DFF, D])
    win = dt("w_in", [D, NIN])
    bg = dt("b_gate", [3, D])
    dl = dt("diff_lambda", [4, 32])
    dg = dt("diff_subln_g", [64])
    wbm = dt("w_br_moba", [384, D])
    wbd = dt("w_br_diff", [384, D])
    wbs = dt("w_br_sb", [384, D])
    wo = dt("w_out", [D, D])
    idf = dt("idf", [128, 128])
    idb = dt("idb", [128, 128], BF16)
    T = {k: dt("t_" + k, a.shape, BF16 if a.dtype == bf else F32) for k, a in tb.items()}
    y = dt("y", [SEQ, D], F32, "ExternalOutput")
    xa_d = dt("xa_d", [SEQ, D], F32, "Internal")
    xb_d = dt("xb_d", [SEQ, D], F32, "Internal")
    qkT = dt("qkT_d", [2304, SEQ], BF16, "Internal")
    qm32 = dt("qm32_d", [384, SEQ], F32, "Internal")
    kmean = dt("kmean_d", [384, 64], F32, "Internal")
    v = dt("v_d", [SEQ, 1152], BF16, "Internal")
    gates = dt("gates_d", [SEQ, 3072], F32, "Internal")
    o = dt("o_d", [18, SEQ, 64], BF16, "Internal")
    with ExitStack() as ctx:
        R = rl_alloc(nc, ctx)
        P = Prog()
        rl_consts(P, R, idf, idb)
        linit = 0.8 - 0.6 * math.exp(-0.3 * l)
        stage_ffnln(P, R, x, xa_d, wg[0], wu[0], wd[0], ln_g[0], ln_b[0], nsb=16)
        stage_win(P, R, xa_d, win, bg, qkT, qm32, kmean, v, gates, nsb=16)
        stage_att(P, R, T, qkT, qm32, kmean, v, o, dl, dg, 1.0 - linit)
        stage_mixpost(P, R, xa_d, o, gates, xb_d, wbm, wbd, wbs, wo, ln_g[1], ln_b[1])
        stage_ffnln(P, R, xb_d, y, wg[1], wu[1], wd[1], ln_g[2], ln_b[2], nsb=16)
        run_prog(nc, P, R.G)
    return nc


def kernel(x, ln_g, ln_b, ffn_w_gate, ffn_w_up, ffn_w_down, w_in, b_gate, diff_lambda, diff_subln_g,
           w_br_moba, w_br_diff, w_br_sb, w_out):
    bf = ml_dtypes.bfloat16
    tb = att_tables()
    f = lambda a: np.ascontiguousarray(np.asarray(a, dtype=np.float32))
    cur = f(x)
    for l in range(2):
        nc = build_layer(tb, l)
        shared = dict(ln_g=f(ln_g[l]), ln_b=f(ln_b[l]), ffn_w_gate=f(ffn_w_gate[l]), ffn_w_up=f(ffn_w_up[l]),
                      ffn_w_down=f(ffn_w_down[l]), w_in=f(w_in[l]), b_gate=f(b_gate[l]), diff_lambda=f(diff_lambda[l]),
                      diff_subln_g=f(diff_subln_g[l]), w_br_moba=f(w_br_moba[l]), w_br_diff=f(w_br_diff[l]),
                      w_br_sb=f(w_br_sb[l]), w_out=f(w_out[l]),
                      idf=np.eye(128, dtype=np.float32), idb=np.eye(128).astype(bf))
        for k, a in tb.items():
            shared["t_" + k] = a
        in_maps = [dict(shared, x=cur[b]) for b in range(NACT)]
        res = run_bass_kernel_spmd(nc, in_maps, core_ids=list(range(NACT)))
        cur = np.stack([np.asarray(r["y"], dtype=np.float32) for r in res.results], axis=0)
    return cur
```

```python
import math
from contextlib import contextmanager, ExitStack
import numpy as np
import ml_dtypes
import concourse.bass as bass
import concourse.mybir as mybir
from concourse.bass_utils import run_bass_kernel_spmd

F32 = mybir.dt.float32
BF16 = mybir.dt.bfloat16
AF = mybir.ActivationFunctionType
ALU = mybir.AluOpType
AX = mybir.AxisListType

D = 1024
DFF = 2816
NF = DFF // 128
NTOK = 4096
SEQ = 8192
NIN = 6528
ALPHA = 4.0 ** 0.25
LN_EPS = 1e-5
NCORES = 8


@contextmanager
def my_fori(nc, e, regs, start, end):
    loop_id = nc.next_id()
    name = f"myfori_{loop_id}"
    ls, le = name + "_loop", name + "_end"
    engines = bass.OrderedEngineSet([e.engine])
    nc.regs_mov(regs, start)
    nc.br(ls, engines=engines)
    with nc.body(ls, valid_engines=engines):
        yield nc.snap(regs, min_val=start, max_val=end - 1)
        nc.regs_alu(regs, regs, 1, op=mybir.AluOpType.add)
        nc.br_lt(regs, end, on_true=ls, on_false=le, engines=engines)
    nc.switch_bb(le)


class Prog:
    def __init__(self):
        self.root = []
        self.stack = [self.root]
        self.nloops = 0

    def op(self, eng, fn, chain=False, waits=(), thread=0):
        self.stack[-1].append(['op', eng, fn, chain, tuple(waits), 'c', thread])

    def dma(self, eng, fn, waits=(), thread=0):
        self.stack[-1].append(['op', eng, fn, False, tuple(waits), 'd', thread])

    def join(self, eng, thread, other):
        self.stack[-1].append(['op', eng, None, False, (), 'j', thread, other])

    @contextmanager
    def loop(self, n, clear=()):
        body = []
        lid = self.nloops
        self.nloops += 1
        self.stack[-1].append(['loop', n, body, lid, tuple(clear)])
        self.stack.append(body)
        try:
            yield lid
        finally:
            self.stack.pop()

    @staticmethod
    def _size(node):
        if node[0] == 'op':
            return 1
        return node[1] * sum(Prog._size(b) for b in node[2])

    @staticmethod
    def _has(node, ename):
        if node[0] == 'op':
            return node[1] == ename
        return any(Prog._has(b, ename) for b in node[2])

    def total(self):
        return sum(self._size(n) for n in self.root)

    def emit(self, ename, e, G, semfn, nc, G2):
        k = 0
        lregs = nc.alloc_registers(f"lr_{ename}", engines=bass.OrderedEngineSet([e.engine]))
        GS = (G, G2)
        for node in self.root:
            if node[0] == 'op' or node[1] == 1:
                subs = [node] if node[0] == 'op' else node[2]
                iv = {} if node[0] == 'op' else {node[3]: 0}
                for sub in subs:
                    eng, fn, chain, waits, kind = sub[1:6]
                    assert kind != 'j' and sub[6] == 0
                    if eng == ename:
                        e.wait_ge(G, k)
                        for (sem, vf) in waits:
                            e.wait_ge(sem, vf(iv) if callable(vf) else vf)
                        if kind == 'c':
                            fn(e, iv).then_inc(G, 1)
                        else:
                            fn(e, iv)
                            e.sem_inc(G, 1)
                    k += 1
                continue
            _, n, body, lid, clr = node
            E, S2 = semfn(lid)
            NT = [sum(1 for sub in body if sub[6] == th) for th in (0, 1)]

            def release(first):
                if first:
                    e.wait_ge(G, k)
                else:
                    e.wait_ge(G, NT[0])
                    if NT[1]:
                        e.wait_ge(G2, NT[1])
                e.wait_ge(E, 5)
                e.sem_clear(G)
                e.sem_clear(G2)
                e.sem_clear(E)
                for cs in clr:
                    e.sem_clear(cs)
                e.sem_inc(S2, 1)
            e.sem_inc(E, 1)
            if ename == 'sp':
                release(True)
            with my_fori(nc, e, lregs, 1, n + 1) as i:
                e.wait_ge(S2, i)
                iv = {lid: i - 1}
                prev = [None, None]
                cnt = [0, 0]
                for sub in body:
                    assert sub[0] == 'op'
                    eng, fn, chain, waits, kind, th = sub[1:7]
                    j = cnt[th]
                    if eng == ename:
                        if j > 0 and not (chain and prev[th] is not None and prev[th][1] == ename):
                            e.wait_ge(GS[th], j)
                        if kind == 'j':
                            e.wait_ge(GS[sub[7]], cnt[sub[7]])
                            e.sem_inc(GS[th], 1)
                        else:
                            for (sem, vf) in waits:
                                e.wait_ge(sem, vf({lid: 0}) if callable(vf) else vf)
                            if kind == 'c':
                                fn(e, iv).then_inc(GS[th], 1)
                            else:
                                fn(e, iv)
                                e.sem_inc(GS[th], 1)
                    prev[th] = sub
                    cnt[th] += 1
                e.sem_inc(E, 1)
                if ename == 'sp':
                    release(False)
            e.wait_ge(S2, n + 1)
            if ename == 'sp':
                e.sem_inc(G, k + 1)
            k += 1


def run_prog(nc, P, G):
    engs = {'pe': 'tensor', 'act': 'scalar', 'dve': 'vector', 'pool': 'gpsimd', 'sp': 'sync'}
    with ExitStack() as sctx:
        G2 = sctx.enter_context(nc.semaphore("G2"))
        lsems = {}
        for node in P.root:
            if node[0] == 'loop' and node[1] > 1:
                lid = node[3]
                lsems[lid] = tuple(sctx.enter_context(nc.semaphore(f"L{lid}_{nm}")) for nm in ("E", "S"))
        with nc.Block() as block:
            for ename, bname in engs.items():
                def mk(ename=ename):
                    def f(e):
                        P.emit(ename, e, G, lambda lid: lsems[lid], nc, G2)
                    return f
                getattr(block, bname)(mk())


class RL:
    pass


def rl_alloc(nc, ctx):
    R = RL()
    R.nc = nc
    R.WBUF = ctx.enter_context(nc.sbuf_tensor("WBUF", [128, 67584], BF16))
    R.SCRF = ctx.enter_context(nc.sbuf_tensor("SCRF", [128, 8704], F32))
    R.SCRB = ctx.enter_context(nc.sbuf_tensor("SCRB", [128, 15360], BF16))
    R.identf = ctx.enter_context(nc.sbuf_tensor("identf", [128, 128], F32))
    R.identb = ctx.enter_context(nc.sbuf_tensor("identb", [128, 128], BF16))
    R.stats = ctx.enter_context(nc.sbuf_tensor("stats", [128, 12], F32))
    R.mv = ctx.enter_context(nc.sbuf_tensor("mv", [128, 2], F32))
    R.rstd = ctx.enter_context(nc.sbuf_tensor("rstd", [128, 1], F32))
    R.ps = [ctx.enter_context(nc.psum_tensor(f"ps{i}", [128, 512], F32)) for i in range(6)]
    R.psb = [ctx.enter_context(nc.psum_tensor(f"psb{i}", [128, 1024], BF16)) for i in range(2)]
    R.G = ctx.enter_context(nc.semaphore("G"))
    R.csem = ctx.enter_context(nc.semaphore("csem"))
    R.nsem = 0
    R.ctx = ctx
    return R


def new_sem(R, name):
    R.nsem += 1
    return R.ctx.enter_context(R.nc.semaphore(f"{name}_{R.nsem}"))


def rl_consts(P, R, identf_d, identb_d):
    def f(e, iv):
        e.dma_start(out=R.identf[:], in_=identf_d[:, :]).then_inc(R.csem, 16)
        e.dma_start(out=R.identb[:], in_=identb_d[:, :]).then_inc(R.csem, 16)
    P.dma('sp', f)
    P.dma('sp', lambda e, iv: None, waits=[(R.csem, 32)])


def ln_ops(P, R, u, gbc, bbc, yo, pre_waits=()):
    for h in range(2):
        P.op('dve', lambda e, iv, h=h: e.bn_stats(out=R.stats[:, h * 6:(h + 1) * 6], in_=u[:, h * 512:(h + 1) * 512]))
    P.op('dve', lambda e, iv: e.bn_aggr(out=R.mv[:], in_=R.stats[:]))
    P.op('dve', lambda e, iv: e.tensor_scalar(out=R.rstd[:], in0=R.mv[:, 1:2], scalar1=LN_EPS, scalar2=None, op0=ALU.add))
    P.op('act', lambda e, iv: e.activation(out=R.rstd[:], in_=R.rstd[:], func=AF.Sqrt))
    P.op('dve', lambda e, iv: e.reciprocal(out=R.rstd[:], in_=R.rstd[:]))
    P.op('dve', lambda e, iv: e.tensor_scalar(out=u, in0=u, scalar1=R.mv[:, 0:1], scalar2=R.rstd[:, 0:1],
                                              op0=ALU.subtract, op1=ALU.mult))
    P.op('pool', lambda e, iv: e.tensor_tensor(out=u, in0=u, in1=gbc, op=ALU.mult))
    P.op('pool', lambda e, iv: e.tensor_tensor(out=yo, in0=u, in1=bbc, op=ALU.add), waits=pre_waits)


def load_ln_params(P, R, g_d, b_d, gbc, bbc):
    sem = new_sem(R, "lnp")

    def f(e, iv):
        e.dma_start(out=gbc, in_=g_d.partition_broadcast(128)).then_inc(sem, 16)
        e.dma_start(out=bbc, in_=b_d.partition_broadcast(128)).then_inc(sem, 16)
    P.dma('sp', f)
    P.dma('sp', lambda e, iv: None, waits=[(sem, 32)])


def transposes_to_xT(P, R, xs4_loader, xT, nj=4):
    for j in range(nj):
        xs = xs4_loader(j)
        for c2 in range(2):
            for c in range(4):
                cc = c2 * 4 + c
                P.op('pe', lambda e, iv, cc=cc, c=c, xs=xs: e.transpose(
                    out=R.ps[0][:, c * 128:(c + 1) * 128], in_=xs[:, cc * 128:(cc + 1) * 128], identity=R.identf[:]),
                    chain=(c > 0))
            P.op('act', lambda e, iv, c2=c2, j=j: e.activation(
                out=xT[:, c2 * 4:(c2 + 1) * 4, j * 128:(j + 1) * 128],
                in_=R.ps[0][:, :].rearrange("p (c t) -> p c t", c=4), func=AF.Copy))


def stage_ffnln(P, R, x_in, x_out, wg_d, wu_d, wd_d, g_d, b_d, nsb=16):
    ntiles = nsb * 4
    wg = R.WBUF[:, 0:22528].rearrange("p (c f) -> p c f", c=8)
    wu = R.WBUF[:, 22528:45056].rearrange("p (c f) -> p c f", c=8)
    wd = R.WBUF[:, 45056:67584].rearrange("p (c n) -> p c n", c=NF)
    xs = R.SCRF[:, 0:1024]
    u = R.SCRF[:, 1024:2048]
    xa = R.SCRF[:, 2048:3072]
    yo = R.SCRF[:, 3072:4096]
    sg = R.SCRF[:, 4096:4608]
    gbc = R.SCRF[:, 4608:5632]
    bbc = R.SCRF[:, 5632:6656]
    xT = R.SCRB[:, 0:4096].rearrange("p (c t) -> p c t", c=8)
    aT = R.SCRB[:, 4096:15360].rearrange("p (f t) -> p f t", f=NF)
    wsem = new_sem(R, "w")
    dsx = new_sem(R, "dsx")
    dso = new_sem(R, "dso")

    def loadw(e, iv):
        for c in range(8):
            e.dma_start(out=wg[:, c, :].rearrange("p (h n) -> p h n", h=2),
                        in_=wg_d[c * 128:(c + 1) * 128, :].rearrange("p (h n) -> p h n", h=2)).then_inc(wsem, 16)
            e.dma_start(out=wu[:, c, :].rearrange("p (h n) -> p h n", h=2),
                        in_=wu_d[c * 128:(c + 1) * 128, :].rearrange("p (h n) -> p h n", h=2)).then_inc(wsem, 16)
        for f in range(NF):
            e.dma_start(out=wd[:, f, :], in_=wd_d[f * 128:(f + 1) * 128, :]).then_inc(wsem, 16)
    P.dma('pool', loadw)
    P.dma('pool', lambda e, iv: None, waits=[(wsem, 16 * (16 + NF))])
    load_ln_params(P, R, g_d, b_d, gbc, bbc)

    with P.loop(ntiles, clear=(dsx, dso)) as L:
        def loader(j):
            P.dma('sp', lambda e, iv: e.dma_start(out=xs, in_=x_in[bass.ts(iv[L], 128), :]).then_inc(dsx, 16))
            P.dma('sp', lambda e, iv: None, waits=[(dsx, 16)])
            return xs
        transposes_to_xT(P, R, loader, xT, nj=1)
        for g0 in range(0, NF, 4):
            gw = min(4, NF - g0)
            first = True
            for gi in range(gw):
                f = g0 + gi
                for c in range(8):
                    P.op('pe', lambda e, iv, f=f, c=c, gi=gi: e.matmul(
                        R.ps[1][:, gi * 128:(gi + 1) * 128], lhsT=wg[:, c, f * 128:(f + 1) * 128], rhs=xT[:, c, 0:128],
                        start=(c == 0), stop=(c == 7)), chain=not first)
                    first = False
            for gi in range(gw):
                f = g0 + gi
                for c in range(8):
                    P.op('pe', lambda e, iv, f=f, c=c, gi=gi: e.matmul(
                        R.ps[2][:, gi * 128:(gi + 1) * 128], lhsT=wu[:, c, f * 128:(f + 1) * 128], rhs=xT[:, c, 0:128],
                        start=(c == 0), stop=(c == 7)), chain=True)
            P.op('act', lambda e, iv, gw=gw: e.activation(out=sg[:, 0:gw * 128], in_=R.ps[1][:, 0:gw * 128], func=AF.Silu))
            P.op('dve', lambda e, iv, gw=gw, g0=g0: e.tensor_tensor(
                out=aT[:, g0:g0 + gw, 0:128], in0=sg[:, 0:gw * 128].rearrange("p (g t) -> p g t", g=gw),
                in1=R.ps[2][:, 0:gw * 128].rearrange("p (g t) -> p g t", g=gw), op=ALU.mult))
        first = True
        for h in range(2):
            for f in range(NF):
                P.op('pe', lambda e, iv, f=f, h=h: e.matmul(
                    R.ps[3 + h][:], lhsT=aT[:, f, 0:128], rhs=wd[:, f, h * 512:(h + 1) * 512],
                    start=(f == 0), stop=(f == NF - 1)), chain=not first)
                first = False
        P.op('act', lambda e, iv: e.mul(out=xa, in_=xs, mul=ALPHA))
        for h in range(2):
            P.op('dve', lambda e, iv, h=h: e.scalar_tensor_tensor(
                out=u[:, h * 512:(h + 1) * 512], in0=R.ps[3 + h][:], scalar=0.5, in1=xa[:, h * 512:(h + 1) * 512],
                op0=ALU.mult, op1=ALU.add))
        ln_ops(P, R, u, gbc, bbc, yo)
        P.dma('sp', lambda e, iv: e.dma_start(out=x_out[bass.ts(iv[L], 128), :], in_=yo).then_inc(dso, 16))
        P.dma('sp', lambda e, iv: None, waits=[(dso, 16)])


QK_COLS = [0, 128, 256, 384, 512, 640,
           1152, 1280, 1408, 1536, 1664, 1792,
           2304, 2432, 2560, 2688, 2816, 2944]
V_COLS = [768, 1920, 3072]
G_COL = 3456


def stage_win(P, R, x_in, win_d, bgate_d, qkT_out, qm32_out, kmean_out, v_out, gates_out, nsb=16):
    ntiles = nsb * 4
    win = R.WBUF[:, 0:52224].rearrange("p (c n) -> p c n", c=8)
    xs = R.SCRF[:, 0:1024]
    gt = R.SCRF[:, 1024:4096]
    q32 = R.SCRF[:, 4096:4480].rearrange("p (r t) -> p r t", r=3)
    bgb = R.SCRF[:, 5632:8704]
    xT = R.SCRB[:, 0:4096].rearrange("p (c t) -> p c t", c=8)
    qkst = R.SCRB[:, 4096:6400].rearrange("p (r t) -> p r t", r=18)
    vst = R.SCRB[:, 13312:14464]
    km = R.stats[:, 0:3]
    wsem = new_sem(R, "w")
    dsx = new_sem(R, "dsx")
    dsq = new_sem(R, "dsq")

    def loadw(e, iv):
        for c in range(8):
            e.dma_start(out=win[:, c, :].rearrange("p (h n) -> p h n", h=4),
                        in_=win_d[c * 128:(c + 1) * 128, :].rearrange("p (h n) -> p h n", h=4)).then_inc(wsem, 16)
    P.dma('pool', loadw)
    P.dma('pool', lambda e, iv: None, waits=[(wsem, 16 * 8)])
    bsem = new_sem(R, "bg")
    P.dma('sp', lambda e, iv: e.dma_start(
        out=bgb, in_=bgate_d.rearrange("a d -> (a d)").partition_broadcast(128)).then_inc(bsem, 16))
    P.dma('sp', lambda e, iv: None, waits=[(bsem, 16)])

    with P.loop(ntiles, clear=(dsx, dsq)) as L:
        def loader(j):
            P.dma('act', lambda e, iv: e.dma_start(out=xs, in_=x_in[bass.ts(iv[L], 128), :]).then_inc(dsx, 16))
            P.dma('act', lambda e, iv: None, waits=[(dsx, 16)])
            return xs
        transposes_to_xT(P, R, loader, xT, nj=1)
        for r in range(18):
            co = QK_COLS[r]
            for c in range(8):
                P.op('pe', lambda e, iv, c=c, co=co: e.matmul(R.ps[1][:, 0:128], lhsT=win[:, c, co:co + 128], rhs=xT[:, c, 0:128],
                                                              start=(c == 0), stop=(c == 7)), chain=(c > 0))
            P.op('act', lambda e, iv, r=r: e.copy(out=qkst[:, r, :], in_=R.ps[1][:, 0:128]))
            if r < 3:
                P.op('dve', lambda e, iv, r=r: e.tensor_copy(out=q32[:, r, :], in_=R.ps[1][:, 0:128]))
            if 3 <= r < 6:
                P.op('dve', lambda e, iv, r=r: e.tensor_reduce(out=km[:, r - 3:r - 2], in_=R.ps[1][:, 0:128], axis=AX.X, op=ALU.add))

        def stq(e, iv):
            e.dma_start(out=qkT_out[:, bass.ts(iv[L], 128)].rearrange("(r p) t -> p r t", p=128),
                        in_=qkst).then_inc(dsq, 16)
            e.dma_start(out=qm32_out[:, bass.ts(iv[L], 128)].rearrange("(r p) t -> p r t", p=128),
                        in_=q32).then_inc(dsq, 16)
            e.dma_start(out=kmean_out[:, bass.ts(iv[L], 1)].rearrange("(r p) b -> p r b", p=128),
                        in_=km.rearrange("p (r b) -> p r b", b=1), allow_slow_non_contiguous=True).then_inc(dsq, 16)
        P.dma('act', stq)
        for vi, co in enumerate(V_COLS):
            for c in range(8):
                P.op('pe', lambda e, iv, c=c, co=co: e.matmul(
                    R.ps[2][:, 0:384], lhsT=xT[:, c, 0:128], rhs=win[:, c, co:co + 384],
                    start=(c == 0), stop=(c == 7)), chain=(c > 0))
            P.op('act', lambda e, iv, vi=vi: e.copy(out=vst[:, vi * 384:(vi + 1) * 384], in_=R.ps[2][:, 0:384]))
        for gi in range(6):
            co = G_COL + gi * 512
            for c in range(8):
                P.op('pe', lambda e, iv, c=c, co=co: e.matmul(
                    R.ps[3][:], lhsT=xT[:, c, 0:128], rhs=win[:, c, co:co + 512],
                    start=(c == 0), stop=(c == 7)), chain=(c > 0))
            P.op('dve', lambda e, iv, gi=gi: e.tensor_tensor(out=gt[:, gi * 512:(gi + 1) * 512], in0=R.ps[3][:],
                                                             in1=bgb[:, gi * 512:(gi + 1) * 512], op=ALU.add))
        P.op('act', lambda e, iv: e.activation(out=gt, in_=gt, func=AF.Sigmoid))
        P.dma('act', lambda e, iv: e.dma_start(out=v_out[bass.ts(iv[L], 128), :], in_=vst).then_inc(dsq, 16))
        P.dma('act', lambda e, iv: e.dma_start(out=gates_out[bass.ts(iv[L], 128), :], in_=gt).then_inc(dsq, 16))
        P.dma('act', lambda e, iv: None, waits=[(dsq, 80)])


BIG = 30000.0
SLOPES = (2.0 ** (-8.0 * np.arange(1, 13, dtype=np.float32) / 12)).astype(np.float32)
SC_M = 64 ** -0.5
SC_D = 32 ** -0.5
SC_S = 64 ** -0.5
SUBLN_EPS = 1e-5


def att_tables():
    bf = ml_dtypes.bfloat16
    i = np.arange(SEQ)
    ib = (i // 128).astype(np.float32)
    il = (i % 128).astype(np.float32)
    one = np.ones(SEQ, np.float32)

    def hl(v):
        hi = np.float32(np.asarray(v, np.float32).astype(bf).astype(np.float32))
        lo = np.float32(np.asarray(v - hi, np.float32).astype(bf).astype(np.float32))
        return hi, lo
    qaug = np.zeros((12, 8, SEQ), np.float32)
    kaug = np.zeros((12, 8, SEQ), np.float32)
    for slot in range(12):
        scale = SC_M if slot % 2 == 0 else SC_D
        sig = float(SLOPES[slot]) / scale
        Sh, Sl = hl(128.0 * sig)
        sh, sl = hl(sig)
        qaug[slot] = np.stack([Sh * one, Sl * one, sh * one, sl * one, -ib, -ib, -il, -il])
        kaug[slot] = np.stack([ib, ib, il, il, Sh * one, Sl * one, sh * one, sl * one])
    onehot = (np.arange(32)[:, None] == (i // 256)[None, :]).astype(np.float32)
    p = np.arange(128)[:, None]
    c = np.arange(512)[None, :]
    maskT = np.concatenate([np.where((r * 128 + p) > c, -BIG, 0.0) for r in range(4)], axis=1)
    maskS = np.concatenate([np.where(c >= (r * 128 + p), -BIG, 0.0) for r in range(4)], axis=1)
    o = np.arange(32)[:, None]
    n = np.arange(32)[None, :]
    pb = np.stack([np.where(n < o, 0.0, -BIG), np.where(n == o, 0.0, -BIG), np.where(n <= o, 0.0, -BIG)]).astype(np.float32)
    return dict(qaug=qaug.reshape(96, SEQ).astype(bf), kaug=kaug.reshape(96, SEQ).astype(bf), onehot=onehot.astype(bf),
                maskT=maskT.astype(bf), maskS=maskS.astype(bf), pb=pb.reshape(-1))


def stage_att(P, R, T, qkT_d, qm32_d, kmean_d, v_d, o_d, lam_d, subg_d, one_m_linit, nheads=6, do=('m', 'd', 's')):
    W = R.WBUF
    QA = [W[:, 0:8192], W[:, 8192:16384]]
    KA = [W[:, 16384:24576], W[:, 24576:32768]]
    VA = W[:, 32768:40960].rearrange("p (k d) -> p k d", k=64)
    VAf = W[:, 32768:40960]
    maskT = W[:, 40960:43008]
    maskS = W[:, 43008:45056]
    pt = [W[:, 45056:45568], W[:, 45568:46080]]
    wb = W[:, 46080:46592]
    wT = W[:, 46592:47104].rearrange("p (k q) -> p k q", k=4)
    FW = W[:, 47360:67584].bitcast(F32)
    pbt = FW[:, 0:3072].rearrange("p (a o n) -> p a o n", a=3, o=32)
    osb = [FW[:, 3072:3584], FW[:, 3584:4096]]
    gm = FW[:, 4096:4128]
    sel = FW[:, 4128:4160]
    m8 = FW[:, 4160:4168]
    rinv = [FW[:, 4168:4172], FW[:, 4172:4176]]
    ss = FW[:, 4176:4180]
    carry = FW[:, 4180:4181]
    negc = FW[:, 4181:4182]
    tot = FW[:, 4182:4183]
    lamt = FW[:, 4183:4184]
    lsc = FW[:, 4184:4186]
    t1 = FW[:, 4192:4256]
    od = FW[:, 4256:4512].rearrange("p (j d) -> p j d", j=4)
    gsc = FW[:, 4512:4576]
    lamw = FW[:, 4576:4704]
    lamw2 = FW[:, 4704:4768]
    ones = FW[:, 4768:5280]
    tt = FW[:, 5280:5792]
    spb = FW[:, 5792:6305]
    E1 = FW[:, 6336:6848]
    aa = FW[:, 6848:7360]
    kme = FW[:, 7360:7392]
    kmr = FW[:, 9472:9536]
    ost_all = FW[:, 7424:9472].bitcast(BF16).rearrange("p (t d) -> p t d", t=64)
    qm32 = R.SCRF[:, 0:8192]
    ps_s = [R.ps[0], R.ps[1]]
    ps_o = [R.ps[2], R.ps[3]]
    ps_t = [R.ps[4], R.ps[5]]
    psb = R.psb[0]
    tsem = new_sem(R, "att_t")
    hsem = new_sem(R, "att_h")
    osem = new_sem(R, "att_o")
    cnt = {'h': 0, 'o': 0}

    def ldc(e, iv):
        e.dma_start(out=maskT, in_=T['maskT'][:, :]).then_inc(tsem, 16)
        e.dma_start(out=maskS, in_=T['maskS'][:, :]).then_inc(tsem, 16)
        e.dma_start(out=FW[:, 0:3072], in_=T['pb'].partition_broadcast(128)).then_inc(tsem, 16)
        e.dma_start(out=gsc, in_=subg_d.partition_broadcast(128)).then_inc(tsem, 16)
        e.dma_start(out=lamw, in_=lam_d.rearrange("a d -> (a d)").partition_broadcast(128)).then_inc(tsem, 16)
    P.dma('sp', ldc)
    P.dma('sp', lambda e, iv: None, waits=[(tsem, 80)])
    P.op('dve', lambda e, iv: e.memset(ones, 1.0))
    P.op('dve', lambda e, iv: e.memset(spb[:, 0:1], 0.0))
    P.op('dve', lambda e, iv: e.memset(VA[:, :, 64:65], 1.0))
    linit = 1.0 - one_m_linit
    P.op('dve', lambda e, iv: e.tensor_tensor(out=lamw2[:, 0:32], in0=lamw[:, 0:32], in1=lamw[:, 32:64], op=ALU.mult))
    P.op('dve', lambda e, iv: e.tensor_tensor(out=lamw2[:, 32:64], in0=lamw[:, 64:96], in1=lamw[:, 96:128], op=ALU.mult))
    P.op('dve', lambda e, iv: e.tensor_reduce(out=lsc, in_=lamw2.rearrange("p (a d) -> p a d", a=2), axis=AX.X, op=ALU.add))
    P.op('act', lambda e, iv: e.activation(out=lsc, in_=lsc, func=AF.Exp))
    P.op('dve', lambda e, iv: e.tensor_tensor(out=lamt, in0=lsc[:, 0:1], in1=lsc[:, 1:2], op=ALU.subtract))
    P.op('dve', lambda e, iv: e.tensor_scalar(out=lamt, in0=lamt, scalar1=linit, scalar2=-1.0, op0=ALU.add, op1=ALU.mult))
    P.op('act', lambda e, iv: e.mul(out=gsc, in_=gsc, mul=one_m_linit))

    def dense_tile(s, Kd, scale, kb_ap_K, kb_ap_V, Q, mask_r, first):
        P.op('pe', lambda e, iv: e.matmul(ps_s[s][:], lhsT=kb_ap_K(iv), rhs=QA[s][0:Kd, Q * 512:(Q + 1) * 512],
                                          start=True, stop=(mask_r is None)))
        if mask_r is not None:
            P.op('pe', lambda e, iv: e.matmul(ps_s[s][:], lhsT=R.identb[:], rhs=maskT[:, mask_r * 512:(mask_r + 1) * 512],
                                              start=False, stop=True), chain=True)
        P.op('act', lambda e, iv: e.activation(out=pt[s], in_=ps_s[s][:], func=AF.Exp, scale=scale))
        P.op('pe', lambda e, iv: e.matmul(ps_o[s][0:65, :], lhsT=kb_ap_V(iv), rhs=pt[s], start=first, stop=True))

    def dense_Q(streams, Q):
        for r in range(4):
            kb = 4 * Q + r
            for (s, Kd, scale) in streams:
                dense_tile(s, Kd, scale, lambda iv, s=s, Kd=Kd, kb=kb: KA[s][0:Kd, kb * 128:(kb + 1) * 128],
                           lambda iv, kb=kb: VA[:, kb, 0:65], Q, r, r == 0)
        for kb in range(4 * Q):
            for (s, Kd, scale) in streams:
                dense_tile(s, Kd, scale, lambda iv, s=s, Kd=Kd, kb=kb: KA[s][0:Kd, kb * 128:(kb + 1) * 128],
                           lambda iv, kb=kb: VA[:, kb, 0:65], Q, None, False)
        for (s, Kd, scale) in streams:
            P.op('act', lambda e, iv, s=s: e.copy(out=osb[s][0:65, :], in_=ps_o[s][0:65, :]))
            for j in range(4):
                P.op('pe', lambda e, iv, s=s, j=j: e.transpose(out=ps_t[s][:, j * 65:(j + 1) * 65],
                                                               in_=osb[s][0:65, j * 128:(j + 1) * 128],
                                                               identity=R.identf[0:65, 0:65]), chain=(j > 0))
            P.op('dve', lambda e, iv, s=s: e.reciprocal(
                out=rinv[s], in_=ps_t[s][:, 0:260].rearrange("p (j d) -> p j d", j=4)[:, :, 64]))

    def store_head(hfn, dq):
        P.dma(dq, lambda e, iv: e.dma_start(
            out=o_d[bass.ts(hfn(iv), 1), :, :].rearrange("h (t p) d -> p (h t) d", p=128),
            in_=ost_all).then_inc(osem, 16))
        P.dma(dq, lambda e, iv: None, waits=[(osem, 16)])

    if 'm' in do:
        with P.loop(nheads, clear=(hsem, osem)) as LHM:
            hbaseM = 0

            def ldm(e, iv):
                h = iv[LHM]
                e.dma_start(out=QA[0][32:96, :], in_=qkT_d[0:384, :][bass.ts(h, 64), :]).then_inc(hsem, 16)
                e.dma_start(out=KA[0][32:96, :], in_=qkT_d[384:768, :][bass.ts(h, 64), :]).then_inc(hsem, 16)
                e.dma_start(out=KA[0][0:32, :], in_=T['onehot'][:, :]).then_inc(hsem, 16)
                e.dma_start(out=QA[0][96:104, :], in_=T['qaug'][bass.ts(h * 2, 8), :]).then_inc(hsem, 16)
                e.dma_start(out=KA[0][96:104, :], in_=T['kaug'][bass.ts(h * 2, 8), :]).then_inc(hsem, 16)
                e.dma_start(out=VA[:, :, 0:64],
                            in_=v_d[:, 0:384][:, bass.ts(h, 64)].rearrange("(k p) d -> p k d", p=128)).then_inc(hsem, 16)
                e.dma_start(out=qm32[0:64, :], in_=qm32_d[bass.ts(h, 64), :]).then_inc(hsem, 16)
                e.dma_start(out=kmr[0:64, :], in_=kmean_d[bass.ts(h, 64), :]).then_inc(hsem, 16)
            P.dma('sp', ldm)
            P.dma('sp', lambda e, iv: None, waits=[(hsem, lambda iv: (hbaseM + (iv[LHM] + 1) * 8) * 16)])
            P.op('dve', lambda e, iv: e.tensor_reduce(out=kme[0:64, :], in_=kmr[0:64, :].rearrange("p (n b) -> p n b", b=2),
                                                      axis=AX.X, op=ALU.add))
            P.op('dve', lambda e, iv: e.tensor_scalar(out=kme[0:64, :], in0=kme[0:64, :], scalar1=1.0 / 256.0, scalar2=None,
                                                      op0=ALU.mult))
            for t4 in range(16):
                for tj in range(4):
                    t = t4 * 4 + tj
                    own = t // 2
                    P.op('pe', lambda e, iv, t=t: e.matmul(ps_s[1][:, 0:32], lhsT=qm32[0:64, t * 128:(t + 1) * 128],
                                                           rhs=kme[0:64, :], start=True, stop=True))
                    P.op('dve', lambda e, iv, own=own: e.tensor_tensor(out=gm, in0=ps_s[1][:, 0:32], in1=pbt[:, 0, own, :],
                                                                       op=ALU.add))
                    P.op('dve', lambda e, iv: e.max(out=m8, in_=gm))
                    P.op('dve', lambda e, iv: e.tensor_scalar(out=sel, in0=gm, scalar1=m8[:, 2:3], scalar2=1.0,
                                                              op0=ALU.is_ge, op1=ALU.subtract))
                    P.op('dve', lambda e, iv, own=own: e.scalar_tensor_tensor(out=sel, in0=sel, scalar=BIG,
                                                                              in1=pbt[:, 1, own, :], op0=ALU.mult, op1=ALU.max))
                    P.op('dve', lambda e, iv, own=own: e.tensor_tensor(out=sel, in0=sel, in1=pbt[:, 2, own, :], op=ALU.add))
                    P.op('pe', lambda e, iv, tj=tj: e.transpose(out=ps_t[0][0:32, tj * 128:(tj + 1) * 128], in_=sel,
                                                                identity=R.identf[:]))
                P.op('act', lambda e, iv, t4=t4: e.copy(out=QA[0][0:32, t4 * 512:(t4 + 1) * 512], in_=ps_t[0][0:32, :]))
            for Q in range(16):
                dense_Q([(0, 104, SC_M)], Q)
                for j in range(4):
                    P.op('dve', lambda e, iv, j=j, Q=Q: e.tensor_scalar(
                        out=ost_all[:, Q * 4 + j, :], in0=ps_t[0][:, j * 65:j * 65 + 64], scalar1=rinv[0][:, j:j + 1], scalar2=None,
                        op0=ALU.mult))
            store_head(lambda iv: iv[LHM], 'sp')
    obase_d = 0

    if 'd' in do:
        with P.loop(nheads, clear=(hsem, osem)) as LHD:
            hbaseD = 0

            def ldd(e, iv):
                h = iv[LHD]
                for m in range(2):
                    e.dma_start(out=QA[m][0:32, :], in_=qkT_d[768:1152, :][bass.ts(h * 2 + m, 32), :]).then_inc(hsem, 16)
                    e.dma_start(out=KA[m][0:32, :], in_=qkT_d[1152:1536, :][bass.ts(h * 2 + m, 32), :]).then_inc(hsem, 16)
                    e.dma_start(out=QA[m][32:40, :], in_=T['qaug'][bass.ts(h * 2 + 1, 8), :]).then_inc(hsem, 16)
                    e.dma_start(out=KA[m][32:40, :], in_=T['kaug'][bass.ts(h * 2 + 1, 8), :]).then_inc(hsem, 16)
                e.dma_start(out=VA[:, :, 0:64],
                            in_=v_d[:, 384:768][:, bass.ts(h, 64)].rearrange("(k p) d -> p k d", p=128)).then_inc(hsem, 16)
            P.dma('act', ldd)
            P.dma('act', lambda e, iv: None, waits=[(hsem, lambda iv: (hbaseD + (iv[LHD] + 1) * 9) * 16)])
            for Q in range(16):
                dense_Q([(0, 40, SC_D), (1, 40, SC_D)], Q)
                P.op('dve', lambda e, iv: e.tensor_scalar(out=rinv[1], in0=rinv[1], scalar1=lamt[:, 0:1], scalar2=None,
                                                          op0=ALU.mult))
                for j in range(4):
                    P.op('dve', lambda e, iv, j=j: e.tensor_scalar(
                        out=t1, in0=ps_t[0][:, j * 65:j * 65 + 64], scalar1=rinv[0][:, j:j + 1], scalar2=None, op0=ALU.mult))
                    P.op('dve', lambda e, iv, j=j: e.scalar_tensor_tensor(
                        out=od[:, j, :], in0=ps_t[1][:, j * 65:j * 65 + 64], scalar=rinv[1][:, j:j + 1], in1=t1,
                        op0=ALU.mult, op1=ALU.add))
                    P.op('dve', lambda e, iv, j=j: e.tensor_tensor(out=t1, in0=od[:, j, :], in1=od[:, j, :], op=ALU.mult))
                    P.op('dve', lambda e, iv, j=j: e.tensor_reduce(out=ss[:, j:j + 1], in_=t1, axis=AX.X, op=ALU.add))
                P.op('dve', lambda e, iv: e.tensor_scalar(out=ss, in0=ss, scalar1=1.0 / 64.0, scalar2=SUBLN_EPS,
                                                          op0=ALU.mult, op1=ALU.add))
                P.op('act', lambda e, iv: e.activation(out=ss, in_=ss, func=AF.Sqrt))
                P.op('dve', lambda e, iv: e.reciprocal(out=ss, in_=ss))
                for j in range(4):
                    P.op('dve', lambda e, iv, j=j, Q=Q: e.scalar_tensor_tensor(
                        out=ost_all[:, Q * 4 + j, :], in0=od[:, j, :], scalar=ss[:, j:j + 1], in1=gsc, op0=ALU.mult, op1=ALU.mult))
            store_head(lambda iv: iv[LHD] + 6, 'act')
    obase_s = 0

    if 's' in do:
        ones13 = FW[:, 4768:5281]
        tts = [FW[:, 5312:5824], R.SCRF[:, 0:512]]
        spbs = [FW[:, 5824:6337], R.SCRF[:, 512:1025]]
        Efs = [FW[:, 6368:6881], R.SCRF[:, 1056:1569]]
        aas = [FW[:, 6912:7424], R.SCRF[:, 1600:2112]]
        negcs = [FW[:, 4181:4182], R.SCRF[:, 2112:2113]]
        wbs_ = [wb, R.SCRB[:, 0:512]]
        wTs = [wT, R.SCRB[:, 512:1024].rearrange("p (k q) -> p k q", k=4)]
        psbs = [R.psb[0], R.psb[1]]
        P.op('dve', lambda e, iv: e.memset(ones13, 1.0))
        for th in range(2):
            P.op('dve', lambda e, iv, th=th: e.memset(spbs[th][:, 0:1], 0.0))
        with P.loop(nheads, clear=(hsem, osem)) as LHS:
            def lds(e, iv):
                h = iv[LHS]
                e.dma_start(out=QA[0][0:64, :], in_=qkT_d[1536:1920, :][bass.ts(h, 64), :]).then_inc(hsem, 16)
                e.dma_start(out=KA[0][0:64, :], in_=qkT_d[1920:2304, :][bass.ts(h, 64), :]).then_inc(hsem, 16)
                e.dma_start(out=VA[:, :, 0:64],
                            in_=v_d[:, 768:1152][:, bass.ts(h, 64)].rearrange("(k p) d -> p k d", p=128)).then_inc(hsem, 16)
            P.dma('sp', lds)
            P.dma('sp', lambda e, iv: None, waits=[(hsem, 48)])
            P.join('pe', 1, 0)

            def sb_tile_ops(th, t, kt, mask_r, first):
                ops = []
                ops.append(('pe', lambda e, iv: e.matmul(ps_s[th][:], lhsT=QA[0][0:64, t * 128:(t + 1) * 128],
                                                         rhs=KA[0][0:64, kt * 512:(kt + 1) * 512],
                                                         start=True, stop=(mask_r is None)), False))
                if mask_r is not None:
                    ops.append(('pe', lambda e, iv: e.matmul(ps_s[th][:], lhsT=R.identb[:],
                                                             rhs=maskS[:, mask_r * 512:(mask_r + 1) * 512],
                                                             start=False, stop=True), True))
                ops.append(('act', lambda e, iv: e.activation(out=tts[th], in_=ps_s[th][:], func=AF.Exp, scale=SC_S), False))
                ops.append(('act', lambda e, iv: e.activation(out=spbs[th][:, 1:513], in_=tts[th], func=AF.Ln, bias=1.0), False))
                ops.append(('dve', lambda e, iv: e.tensor_tensor_scan(out=Efs[th], data0=ones13, data1=spbs[th][:, 0:513],
                                                                      initial=0.0, op0=ALU.mult, op1=ALU.add), False))
                ops.append(('dve', lambda e, iv: e.scalar_tensor_tensor(out=aas[th], in0=ps_s[th][:], scalar=SC_S,
                                                                        in1=Efs[th][:, 0:512], op0=ALU.mult, op1=ALU.add), False))
                if first:
                    ops.append(('dve', lambda e, iv: e.tensor_scalar(out=negcs[th], in0=Efs[th][:, 512:513], scalar1=-1.0,
                                                                     scalar2=None, op0=ALU.mult), False))
                else:
                    ops.append(('dve', lambda e, iv: e.tensor_tensor(out=negcs[th], in0=negcs[th], in1=Efs[th][:, 512:513],
                                                                     op=ALU.subtract), False))
                ops.append(('act', lambda e, iv: e.activation(out=wbs_[th], in_=aas[th], func=AF.Exp, bias=negcs[th][:, 0:1]), False))
                for k4 in range(4):
                    ops.append(('pe', lambda e, iv, k4=k4: e.transpose(out=psbs[th][:, k4 * 128:(k4 + 1) * 128],
                                                                       in_=wbs_[th][:, k4 * 128:(k4 + 1) * 128],
                                                                       identity=R.identb[:]), k4 > 0))
                ops.append(('act', lambda e, iv: e.copy(out=wTs[th], in_=psbs[th][:, 0:512].rearrange("p (k q) -> p k q", k=4)), False))
                for k4 in range(4):
                    ops.append(('pe', lambda e, iv, k4=k4: e.matmul(ps_o[th][:, 0:64], lhsT=wTs[th][:, k4, :],
                                                                    rhs=VA[:, kt * 4 + k4, 0:64],
                                                                    start=(first and k4 == 0), stop=True), k4 > 0))
                return ops

            for m in range(32):
                ktd = (2 * m) // 4
                steps = [(ktd, 'diag', True)] + [(kt, None, False) for kt in range(ktd - 1, -1, -1)]
                for (kt, dg_, first) in steps:
                    lists = []
                    for th in range(2):
                        t = 2 * m + th
                        lists.append(sb_tile_ops(th, t, kt, (t % 4) if dg_ else None, first))
                    for idx in range(max(len(lists[0]), len(lists[1]))):
                        for th in range(2):
                            if idx < len(lists[th]):
                                eng, fn, ch = lists[th][idx]
                                P.op(eng, fn, chain=ch, thread=th)
                for th in range(2):
                    t = 2 * m + th
                    P.op('act', lambda e, iv, t=t, th=th: e.copy(out=ost_all[:, t, :], in_=ps_o[th][:, 0:64]), thread=th)
            P.join('sp', 0, 1)
            store_head(lambda iv: iv[LHS] + 12, 'sp')
def stage_mixpost(P, R, x_in, o_d, gates_d, x_out, wbm_d, wbd_d, wbs_d, wout_d, g_d, b_d, ntiles=64):
    W = R.WBUF
    wbr = W[:, 0:9216].rearrange("p (k n) -> p k n", k=9)
    wout = W[:, 9216:17408].rearrange("p (c n) -> p c n", c=8)
    gts = W[:, 17408:23552].bitcast(F32)
    xs = R.SCRF[:, 0:1024]
    u = R.SCRF[:, 1024:2048]
    xa = R.SCRF[:, 2048:3072]
    yo = R.SCRF[:, 3072:4096]
    gbc = R.SCRF[:, 4608:5632]
    bbc = R.SCRF[:, 5632:6656]
    merged = R.SCRF[:, 6656:7680]
    tmp = R.SCRF[:, 7680:8704]
    xT = R.SCRB[:, 0:4096].rearrange("p (c t) -> p c t", c=8)
    ot = R.SCRB[:, 4096:5248]
    oT = R.SCRB[:, 5248:6400].rearrange("p (k t) -> p k t", k=9)
    wsem = new_sem(R, "w")
    dsi = new_sem(R, "dsi")
    dso = new_sem(R, "dso")

    def loadw(e, iv):
        for bi, wd_ in enumerate((wbm_d, wbd_d, wbs_d)):
            for i in range(3):
                e.dma_start(out=wbr[:, bi * 3 + i, :], in_=wd_[i * 128:(i + 1) * 128, :]).then_inc(wsem, 16)
        for c in range(8):
            e.dma_start(out=wout[:, c, :], in_=wout_d[c * 128:(c + 1) * 128, :]).then_inc(wsem, 16)
    P.dma('pool', loadw)
    P.dma('pool', lambda e, iv: None, waits=[(wsem, 16 * 17)])
    load_ln_params(P, R, g_d, b_d, gbc, bbc)
    with P.loop(ntiles, clear=(dsi, dso)) as L:
        def ld(e, iv):
            e.dma_start(out=gts, in_=gates_d[bass.ts(iv[L], 128), :]).then_inc(dsi, 16)
            e.dma_start(out=xs, in_=x_in[bass.ts(iv[L], 128), :]).then_inc(dsi, 16)
        P.dma('sp', lambda e, iv: e.dma_start(out=ot.rearrange("p (h d) -> p h d", h=18),
                                              in_=o_d.rearrange("h s d -> s h d")[bass.ts(iv[L], 128), :, :]).then_inc(dsi, 16))
        P.dma('act', ld)
        P.dma('act', lambda e, iv: None, waits=[(dsi, lambda iv: (iv[L] + 1) * 48)])
        for k in range(9):
            P.op('pe', lambda e, iv, k=k: e.transpose(
                out=(R.psb[0][:, k * 128:(k + 1) * 128] if k < 8 else R.psb[1][:, 0:128]),
                in_=ot[:, k * 128:(k + 1) * 128], identity=R.identb[:]), chain=(k > 0))
        P.op('act', lambda e, iv: e.copy(out=oT[:, 0:8, :], in_=R.psb[0][:, :].rearrange("p (k t) -> p k t", k=8)))
        P.op('act', lambda e, iv: e.copy(out=oT[:, 8, :], in_=R.psb[1][:, 0:128]))
        for br in range(3):
            for h in range(2):
                for i in range(3):
                    P.op('pe', lambda e, iv, br=br, h=h, i=i: e.matmul(
                        R.ps[1][:], lhsT=oT[:, br * 3 + i, :], rhs=wbr[:, br * 3 + i, h * 512:(h + 1) * 512],
                        start=(i == 0), stop=(i == 2)), chain=(i > 0))
                gsl = gts[:, br * 1024 + h * 512: br * 1024 + (h + 1) * 512]
                if br == 0:
                    P.op('dve', lambda e, iv, h=h, gsl=gsl: e.tensor_tensor(out=merged[:, h * 512:(h + 1) * 512], in0=gsl,
                                                                            in1=R.ps[1][:], op=ALU.mult))
                else:
                    P.op('dve', lambda e, iv, h=h, gsl=gsl: e.tensor_tensor(out=tmp[:, h * 512:(h + 1) * 512], in0=gsl,
                                                                            in1=R.ps[1][:], op=ALU.mult))
                    P.op('dve', lambda e, iv, h=h: e.tensor_tensor(out=merged[:, h * 512:(h + 1) * 512],
                                                                   in0=merged[:, h * 512:(h + 1) * 512],
                                                                   in1=tmp[:, h * 512:(h + 1) * 512], op=ALU.add))
        transposes_to_xT(P, R, lambda j: merged, xT, nj=1)
        first = True
        for h in range(2):
            for c in range(8):
                P.op('pe', lambda e, iv, h=h, c=c: e.matmul(R.ps[3 + h][:], lhsT=xT[:, c, 0:128],
                                                            rhs=wout[:, c, h * 512:(h + 1) * 512],
                                                            start=(c == 0), stop=(c == 7)), chain=not first)
                first = False
        P.op('act', lambda e, iv: e.mul(out=xa, in_=xs, mul=ALPHA))
        for h in range(2):
            P.op('dve', lambda e, iv, h=h: e.tensor_tensor(out=u[:, h * 512:(h + 1) * 512], in0=xa[:, h * 512:(h + 1) * 512],
                                                           in1=R.ps[3 + h][:], op=ALU.add))
        ln_ops(P, R, u, gbc, bbc, yo, pre_waits=[(dso, lambda iv: iv[L] * 16)])
        P.dma('act', lambda e, iv: e.dma_start(out=x_out[bass.ts(iv[L], 128), :], in_=yo).then_inc(dso, 16))
        P.dma('act', lambda e, iv: None, waits=[(dso, 16)])


NACT = 4


def build_layer(tb, l):
    bf = ml_dtypes.bfloat16
    nc = bass.Bass("TRN2", target_bir_lowering=False)

    def dt(n, s, d=F32, k="ExternalInput"):
        return nc.dram_tensor(n, list(s), d, kind=k).ap()
    x = dt("x", [SEQ, D])
    ln_g = dt("ln_g", [3, D])
    ln_b = dt("ln_b", [3, D])
    wg = dt("ffn_w_gate", [2, D, DFF])
    wu = dt("ffn_w_up", [2, D, DFF])
    wd = dt("ffn_w_down", [2, DFF, D])
    win = dt("w_in", [D, NIN])
    bg = dt("b_gate", [3, D])
    dl = dt("diff_lambda", [4, 32])
    dg = dt("diff_subln_g", [64])
    wbm = dt("w_br_moba", [384, D])
    wbd = dt("w_br_diff", [384, D])
    wbs = dt("w_br_sb", [384, D])
    wo = dt("w_out", [D, D])
    idf = dt("idf", [128, 128])
    idb = dt("idb", [128, 128], BF16)
    T = {k: dt("t_" + k, a.shape, BF16 if a.dtype == bf else F32) for k, a in tb.items()}
    y = dt("y", [SEQ, D], F32, "ExternalOutput")
    xa_d = dt("xa_d", [SEQ, D], F32, "Internal")
    xb_d = dt("xb_d", [SEQ, D], F32, "Internal")
    qkT = dt("qkT_d", [2304, SEQ], BF16, "Internal")
    qm32 = dt("qm32_d", [384, SEQ], F32, "Internal")
    kmean = dt("kmean_d", [384, 64], F32, "Internal")
    v = dt("v_d", [SEQ, 1152], BF16, "Internal")
    gates = dt("gates_d", [SEQ, 3072], F32, "Internal")
    o = dt("o_d", [18, SEQ, 64], BF16, "Internal")
    with ExitStack() as ctx:
        R = rl_alloc(nc, ctx)
        P = Prog()
        rl_consts(P, R, idf, idb)
        linit = 0.8 - 0.6 * math.exp(-0.3 * l)
        stage_ffnln(P, R, x, xa_d, wg[0], wu[0], wd[0], ln_g[0], ln_b[0], nsb=16)
        stage_win(P, R, xa_d, win, bg, qkT, qm32, kmean, v, gates, nsb=16)
        stage_att(P, R, T, qkT, qm32, kmean, v, o, dl, dg, 1.0 - linit)
        stage_mixpost(P, R, xa_d, o, gates, xb_d, wbm, wbd, wbs, wo, ln_g[1], ln_b[1])
        stage_ffnln(P, R, xb_d, y, wg[1], wu[1], wd[1], ln_g[2], ln_b[2], nsb=16)
        run_prog(nc, P, R.G)
    return nc


def kernel(x, ln_g, ln_b, ffn_w_gate, ffn_w_up, ffn_w_down, w_in, b_gate, diff_lambda, diff_subln_g,
           w_br_moba, w_br_diff, w_br_sb, w_out):
    bf = ml_dtypes.bfloat16
    tb = att_tables()
    f = lambda a: np.ascontiguousarray(np.asarray(a, dtype=np.float32))
    cur = f(x)
    for l in range(2):
        nc = build_layer(tb, l)
        shared = dict(ln_g=f(ln_g[l]), ln_b=f(ln_b[l]), ffn_w_gate=f(ffn_w_gate[l]), ffn_w_up=f(ffn_w_up[l]),
                      ffn_w_down=f(ffn_w_down[l]), w_in=f(w_in[l]), b_gate=f(b_gate[l]), diff_lambda=f(diff_lambda[l]),
                      diff_subln_g=f(diff_subln_g[l]), w_br_moba=f(w_br_moba[l]), w_br_diff=f(w_br_diff[l]),
                      w_br_sb=f(w_br_sb[l]), w_out=f(w_out[l]),
                      idf=np.eye(128, dtype=np.float32), idb=np.eye(128).astype(bf))
        for k, a in tb.items():
            shared["t_" + k] = a
        in_maps = [dict(shared, x=cur[b]) for b in range(NACT)]
        res = run_bass_kernel_spmd(nc, in_maps, core_ids=list(range(NACT)))
        cur = np.stack([np.asarray(r["y"], dtype=np.float32) for r in res.results], axis=0)
    return cur
```

```python
import math
from contextlib import contextmanager, ExitStack
import numpy as np
import ml_dtypes
import concourse.bass as bass
import concourse.mybir as mybir
from concourse.bass_utils import run_bass_kernel_spmd

F32 = mybir.dt.float32
BF16 = mybir.dt.bfloat16
AF = mybir.ActivationFunctionType
ALU = mybir.AluOpType
AX = mybir.AxisListType

D = 1024
DFF = 2816
NF = DFF // 128
NTOK = 4096
SEQ = 8192
NIN = 6528
ALPHA = 4.0 ** 0.25
LN_EPS = 1e-5
NCORES = 8


@contextmanager
def my_fori(nc, e, regs, start, end):
    loop_id = nc.next_id()
    name = f"myfori_{loop_id}"
    ls, le = name + "_loop", name + "_end"
    engines = bass.OrderedEngineSet([e.engine])
    nc.regs_mov(regs, start)
    nc.br(ls, engines=engines)
    with nc.body(ls, valid_engines=engines):
        yield nc.snap(regs, min_val=start, max_val=end - 1)
        nc.regs_alu(regs, regs, 1, op=mybir.AluOpType.add)
        nc.br_lt(regs, end, on_true=ls, on_false=le, engines=engines)
    nc.switch_bb(le)


class Prog:
    def __init__(self):
        self.root = []
        self.stack = [self.root]
        self.nloops = 0

    def op(self, eng, fn, chain=False, waits=(), thread=0):
        self.stack[-1].append(['op', eng, fn, chain, tuple(waits), 'c', thread])

    def dma(self, eng, fn, waits=(), thread=0):
        self.stack[-1].append(['op', eng, fn, False, tuple(waits), 'd', thread])

    def join(self, eng, thread, other):
        self.stack[-1].append(['op', eng, None, False, (), 'j', thread, other])

    @contextmanager
    def loop(self, n, clear=()):
        body = []
        lid = self.nloops
        self.nloops += 1
        self.stack[-1].append(['loop', n, body, lid, tuple(clear)])
        self.stack.append(body)
        try:
            yield lid
        finally:
            self.stack.pop()

    @staticmethod
    def _size(node):
        if node[0] == 'op':
            return 1
        return node[1] * sum(Prog._size(b) for b in node[2])

    @staticmethod
    def _has(node, ename):
        if node[0] == 'op':
            return node[1] == ename
        return any(Prog._has(b, ename) for b in node[2])

    def total(self):
        return sum(self._size(n) for n in self.root)

    def emit(self, ename, e, G, semfn, nc, GX):
        k = 0
        lregs = nc.alloc_registers(f"lr_{ename}", engines=bass.OrderedEngineSet([e.engine]))
        GS = (G,) + tuple(GX)
        NTH = len(GS)
        for node in self.root:
            if node[0] == 'op' or node[1] == 1:
                subs = [node] if node[0] == 'op' else node[2]
                iv = {} if node[0] == 'op' else {node[3]: 0}
                for sub in subs:
                    eng, fn, chain, waits, kind = sub[1:6]
                    assert kind != 'j' and sub[6] == 0
                    if eng == ename:
                        e.wait_ge(G, k)
                        for (sem, vf) in waits:
                            e.wait_ge(sem, vf(iv) if callable(vf) else vf)
                        if kind == 'c':
                            fn(e, iv).then_inc(G, 1)
                        else:
                            fn(e, iv)
                            e.sem_inc(G, 1)
                    k += 1
                continue
            _, n, body, lid, clr = node
            E, S2 = semfn(lid)
            NT = [sum(1 for sub in body if sub[6] == th) for th in range(NTH)]

            def release(first):
                if first:
                    e.wait_ge(G, k)
                else:
                    for th in range(NTH):
                        if NT[th]:
                            e.wait_ge(GS[th], NT[th])
                e.wait_ge(E, 5)
                for gs in GS:
                    e.sem_clear(gs)
                e.sem_clear(E)
                for cs in clr:
                    e.sem_clear(cs)
                e.sem_inc(S2, 1)
            e.sem_inc(E, 1)
            if ename == 'sp':
                release(True)
            with my_fori(nc, e, lregs, 1, n + 1) as i:
                e.wait_ge(S2, i)
                iv = {lid: i - 1}
                prev = [None] * NTH
                cnt = [0] * NTH
                for sub in body:
                    assert sub[0] == 'op'
                    eng, fn, chain, waits, kind, th = sub[1:7]
                    j = cnt[th]
                    if eng == ename:
                        if j > 0 and not (chain and prev[th] is not None and prev[th][1] == ename):
                            e.wait_ge(GS[th], j)
                        if kind == 'j':
                            e.wait_ge(GS[sub[7]], cnt[sub[7]])
                            e.sem_inc(GS[th], 1)
                        else:
                            for (sem, vf) in waits:
                                e.wait_ge(sem, vf({lid: 0}) if callable(vf) else vf)
                            if kind == 'c':
                                fn(e, iv).then_inc(GS[th], 1)
                            else:
                                fn(e, iv)
                                e.sem_inc(GS[th], 1)
                    prev[th] = sub
                    cnt[th] += 1
                e.sem_inc(E, 1)
                if ename == 'sp':
                    release(False)
            e.wait_ge(S2, n + 1)
            if ename == 'sp':
                e.sem_inc(G, k + 1)
            k += 1


def run_prog(nc, P, G):
    engs = {'pe': 'tensor', 'act': 'scalar', 'dve': 'vector', 'pool': 'gpsimd', 'sp': 'sync'}
    with ExitStack() as sctx:
        GX = [sctx.enter_context(nc.semaphore(f"GX{i}")) for i in range(3)]
        lsems = {}
        for node in P.root:
            if node[0] == 'loop' and node[1] > 1:
                lid = node[3]
                lsems[lid] = tuple(sctx.enter_context(nc.semaphore(f"L{lid}_{nm}")) for nm in ("E", "S"))
        with nc.Block() as block:
            for ename, bname in engs.items():
                def mk(ename=ename):
                    def f(e):
                        P.emit(ename, e, G, lambda lid: lsems[lid], nc, GX)
                    return f
                getattr(block, bname)(mk())


class RL:
    pass


def rl_alloc(nc, ctx):
    R = RL()
    R.nc = nc
    R.WBUF = ctx.enter_context(nc.sbuf_tensor("WBUF", [128, 67584], BF16))
    R.SCRF = ctx.enter_context(nc.sbuf_tensor("SCRF", [128, 8704], F32))
    R.SCRB = ctx.enter_context(nc.sbuf_tensor("SCRB", [128, 15360], BF16))
    R.identf = ctx.enter_context(nc.sbuf_tensor("identf", [128, 128], F32))
    R.identb = ctx.enter_context(nc.sbuf_tensor("identb", [128, 128], BF16))
    R.stats = ctx.enter_context(nc.sbuf_tensor("stats", [128, 12], F32))
    R.mv = ctx.enter_context(nc.sbuf_tensor("mv", [128, 2], F32))
    R.rstd = ctx.enter_context(nc.sbuf_tensor("rstd", [128, 1], F32))
    R.ps = [ctx.enter_context(nc.psum_tensor(f"ps{i}", [128, 512], F32)) for i in range(6)]
    R.psb = [ctx.enter_context(nc.psum_tensor(f"psb{i}", [128, 1024], BF16)) for i in range(2)]
    R.G = ctx.enter_context(nc.semaphore("G"))
    R.csem = ctx.enter_context(nc.semaphore("csem"))
    R.nsem = 0
    R.ctx = ctx
    return R


def new_sem(R, name):
    R.nsem += 1
    return R.ctx.enter_context(R.nc.semaphore(f"{name}_{R.nsem}"))


def rl_consts(P, R, identf_d, identb_d):
    def f(e, iv):
        e.dma_start(out=R.identf[:], in_=identf_d[:, :]).then_inc(R.csem, 16)
        e.dma_start(out=R.identb[:], in_=identb_d[:, :]).then_inc(R.csem, 16)
    P.dma('sp', f)
    P.dma('sp', lambda e, iv: None, waits=[(R.csem, 32)])


def ln_ops(P, R, u, gbc, bbc, yo, pre_waits=()):
    for h in range(2):
        P.op('dve', lambda e, iv, h=h: e.bn_stats(out=R.stats[:, h * 6:(h + 1) * 6], in_=u[:, h * 512:(h + 1) * 512]))
    P.op('dve', lambda e, iv: e.bn_aggr(out=R.mv[:], in_=R.stats[:]))
    P.op('dve', lambda e, iv: e.tensor_scalar(out=R.rstd[:], in0=R.mv[:, 1:2], scalar1=LN_EPS, scalar2=None, op0=ALU.add))
    P.op('act', lambda e, iv: e.activation(out=R.rstd[:], in_=R.rstd[:], func=AF.Sqrt))
    P.op('dve', lambda e, iv: e.reciprocal(out=R.rstd[:], in_=R.rstd[:]))
    P.op('dve', lambda e, iv: e.tensor_scalar(out=u, in0=u, scalar1=R.mv[:, 0:1], scalar2=R.rstd[:, 0:1],
                                              op0=ALU.subtract, op1=ALU.mult))
    P.op('pool', lambda e, iv: e.tensor_tensor(out=u, in0=u, in1=gbc, op=ALU.mult))
    P.op('pool', lambda e, iv: e.tensor_tensor(out=yo, in0=u, in1=bbc, op=ALU.add), waits=pre_waits)


def load_ln_params(P, R, g_d, b_d, gbc, bbc):
    sem = new_sem(R, "lnp")

    def f(e, iv):
        e.dma_start(out=gbc, in_=g_d.partition_broadcast(128)).then_inc(sem, 16)
        e.dma_start(out=bbc, in_=b_d.partition_broadcast(128)).then_inc(sem, 16)
    P.dma('sp', f)
    P.dma('sp', lambda e, iv: None, waits=[(sem, 32)])


def transposes_to_xT(P, R, xs4_loader, xT, nj=4):
    for j in range(nj):
        xs = xs4_loader(j)
        for c2 in range(2):
            for c in range(4):
                cc = c2 * 4 + c
                P.op('pe', lambda e, iv, cc=cc, c=c, xs=xs: e.transpose(
                    out=R.ps[0][:, c * 128:(c + 1) * 128], in_=xs[:, cc * 128:(cc + 1) * 128], identity=R.identf[:]),
                    chain=(c > 0))
            P.op('act', lambda e, iv, c2=c2, j=j: e.activation(
                out=xT[:, c2 * 4:(c2 + 1) * 4, j * 128:(j + 1) * 128],
                in_=R.ps[0][:, :].rearrange("p (c t) -> p c t", c=4), func=AF.Copy))


def stage_ffnln(P, R, x_in, x_out, wg_d, wu_d, wd_d, g_d, b_d, nsb=16):
    ntiles = nsb * 4
    wg = R.WBUF[:, 0:22528].rearrange("p (c f) -> p c f", c=8)
    wu = R.WBUF[:, 22528:45056].rearrange("p (c f) -> p c f", c=8)
    wd = R.WBUF[:, 45056:67584].rearrange("p (c n) -> p c n", c=NF)
    xs = R.SCRF[:, 0:1024]
    u = R.SCRF[:, 1024:2048]
    xa = R.SCRF[:, 2048:3072]
    yo = R.SCRF[:, 3072:4096]
    sg = R.SCRF[:, 4096:4608]
    gbc = R.SCRF[:, 4608:5632]
    bbc = R.SCRF[:, 5632:6656]
    xT = R.SCRB[:, 0:4096].rearrange("p (c t) -> p c t", c=8)
    aT = R.SCRB[:, 4096:15360].rearrange("p (f t) -> p f t", f=NF)
    wsem = new_sem(R, "w")
    dsx = new_sem(R, "dsx")
    dso = new_sem(R, "dso")

    def loadw(e, iv):
        for c in range(8):
            e.dma_start(out=wg[:, c, :].rearrange("p (h n) -> p h n", h=2),
                        in_=wg_d[c * 128:(c + 1) * 128, :].rearrange("p (h n) -> p h n", h=2)).then_inc(wsem, 16)
            e.dma_start(out=wu[:, c, :].rearrange("p (h n) -> p h n", h=2),
                        in_=wu_d[c * 128:(c + 1) * 128, :].rearrange("p (h n) -> p h n", h=2)).then_inc(wsem, 16)
        for f in range(NF):
            e.dma_start(out=wd[:, f, :], in_=wd_d[f * 128:(f + 1) * 128, :]).then_inc(wsem, 16)
    P.dma('pool', loadw)
    P.dma('pool', lambda e, iv: None, waits=[(wsem, 16 * (16 + NF))])
    load_ln_params(P, R, g_d, b_d, gbc, bbc)

    with P.loop(ntiles, clear=(dsx, dso)) as L:
        def loader(j):
            P.dma('sp', lambda e, iv: e.dma_start(out=xs, in_=x_in[bass.ts(iv[L], 128), :]).then_inc(dsx, 16))
            P.dma('sp', lambda e, iv: None, waits=[(dsx, 16)])
            return xs
        transposes_to_xT(P, R, loader, xT, nj=1)
        for g0 in range(0, NF, 4):
            gw = min(4, NF - g0)
            first = True
            for gi in range(gw):
                f = g0 + gi
                for c in range(8):
                    P.op('pe', lambda e, iv, f=f, c=c, gi=gi: e.matmul(
                        R.ps[1][:, gi * 128:(gi + 1) * 128], lhsT=wg[:, c, f * 128:(f + 1) * 128], rhs=xT[:, c, 0:128],
                        start=(c == 0), stop=(c == 7)), chain=not first)
                    first = False
            for gi in range(gw):
                f = g0 + gi
                for c in range(8):
                    P.op('pe', lambda e, iv, f=f, c=c, gi=gi: e.matmul(
                        R.ps[2][:, gi * 128:(gi + 1) * 128], lhsT=wu[:, c, f * 128:(f + 1) * 128], rhs=xT[:, c, 0:128],
                        start=(c == 0), stop=(c == 7)), chain=True)
            P.op('act', lambda e, iv, gw=gw: e.activation(out=sg[:, 0:gw * 128], in_=R.ps[1][:, 0:gw * 128], func=AF.Silu))
            P.op('dve', lambda e, iv, gw=gw, g0=g0: e.tensor_tensor(
                out=aT[:, g0:g0 + gw, 0:128], in0=sg[:, 0:gw * 128].rearrange("p (g t) -> p g t", g=gw),
                in1=R.ps[2][:, 0:gw * 128].rearrange("p (g t) -> p g t", g=gw), op=ALU.mult))
        first = True
        for h in range(2):
            for f in range(NF):
                P.op('pe', lambda e, iv, f=f, h=h: e.matmul(
                    R.ps[3 + h][:], lhsT=aT[:, f, 0:128], rhs=wd[:, f, h * 512:(h + 1) * 512],
                    start=(f == 0), stop=(f == NF - 1)), chain=not first)
                first = False
        P.op('act', lambda e, iv: e.mul(out=xa, in_=xs, mul=ALPHA))
        for h in range(2):
            P.op('dve', lambda e, iv, h=h: e.scalar_tensor_tensor(
                out=u[:, h * 512:(h + 1) * 512], in0=R.ps[3 + h][:], scalar=0.5, in1=xa[:, h * 512:(h + 1) * 512],
                op0=ALU.mult, op1=ALU.add))
        ln_ops(P, R, u, gbc, bbc, yo)
        P.dma('sp', lambda e, iv: e.dma_start(out=x_out[bass.ts(iv[L], 128), :], in_=yo).then_inc(dso, 16))
        P.dma('sp', lambda e, iv: None, waits=[(dso, 16)])


QK_COLS = [0, 128, 256, 384, 512, 640,
           1152, 1280, 1408, 1536, 1664, 1792,
           2304, 2432, 2560, 2688, 2816, 2944]
V_COLS = [768, 1920, 3072]
G_COL = 3456


def stage_win(P, R, x_in, win_d, bgate_d, qkT_out, qm32_out, kmean_out, v_out, gates_out, nsb=16):
    ntiles = nsb * 4
    win = R.WBUF[:, 0:52224].rearrange("p (c n) -> p c n", c=8)
    xs = R.SCRF[:, 0:1024]
    gt = R.SCRF[:, 1024:4096]
    q32 = R.SCRF[:, 4096:4480].rearrange("p (r t) -> p r t", r=3)
    bgb = R.SCRF[:, 5632:8704]
    xT = R.SCRB[:, 0:4096].rearrange("p (c t) -> p c t", c=8)
    qkst = R.SCRB[:, 4096:6400].rearrange("p (r t) -> p r t", r=18)
    vst = R.SCRB[:, 13312:14464]
    km = R.stats[:, 0:3]
    wsem = new_sem(R, "w")
    dsx = new_sem(R, "dsx")
    dsq = new_sem(R, "dsq")

    def loadw(e, iv):
        for c in range(8):
            e.dma_start(out=win[:, c, :].rearrange("p (h n) -> p h n", h=4),
                        in_=win_d[c * 128:(c + 1) * 128, :].rearrange("p (h n) -> p h n", h=4)).then_inc(wsem, 16)
    P.dma('pool', loadw)
    P.dma('pool', lambda e, iv: None, waits=[(wsem, 16 * 8)])
    bsem = new_sem(R, "bg")
    P.dma('sp', lambda e, iv: e.dma_start(
        out=bgb, in_=bgate_d.rearrange("a d -> (a d)").partition_broadcast(128)).then_inc(bsem, 16))
    P.dma('sp', lambda e, iv: None, waits=[(bsem, 16)])

    with P.loop(ntiles, clear=(dsx, dsq)) as L:
        def loader(j):
            P.dma('act', lambda e, iv: e.dma_start(out=xs, in_=x_in[bass.ts(iv[L], 128), :]).then_inc(dsx, 16))
            P.dma('act', lambda e, iv: None, waits=[(dsx, 16)])
            return xs
        transposes_to_xT(P, R, loader, xT, nj=1)
        for r in range(18):
            co = QK_COLS[r]
            for c in range(8):
                P.op('pe', lambda e, iv, c=c, co=co: e.matmul(R.ps[1][:, 0:128], lhsT=win[:, c, co:co + 128], rhs=xT[:, c, 0:128],
                                                              start=(c == 0), stop=(c == 7)), chain=(c > 0))
            P.op('act', lambda e, iv, r=r: e.copy(out=qkst[:, r, :], in_=R.ps[1][:, 0:128]))
            if r < 3:
                P.op('dve', lambda e, iv, r=r: e.tensor_copy(out=q32[:, r, :], in_=R.ps[1][:, 0:128]))
            if 3 <= r < 6:
                P.op('dve', lambda e, iv, r=r: e.tensor_reduce(out=km[:, r - 3:r - 2], in_=R.ps[1][:, 0:128], axis=AX.X, op=ALU.add))

        def stq(e, iv):
            e.dma_start(out=qkT_out[:, bass.ts(iv[L], 128)].rearrange("(r p) t -> p r t", p=128),
                        in_=qkst).then_inc(dsq, 16)
            e.dma_start(out=qm32_out[:, bass.ts(iv[L], 128)].rearrange("(r p) t -> p r t", p=128),
                        in_=q32).then_inc(dsq, 16)
            e.dma_start(out=kmean_out[:, bass.ts(iv[L], 1)].rearrange("(r p) b -> p r b", p=128),
                        in_=km.rearrange("p (r b) -> p r b", b=1), allow_slow_non_contiguous=True).then_inc(dsq, 16)
        P.dma('act', stq)
        for vi, co in enumerate(V_COLS):
            for c in range(8):
                P.op('pe', lambda e, iv, c=c, co=co: e.matmul(
                    R.ps[2][:, 0:384], lhsT=xT[:, c, 0:128], rhs=win[:, c, co:co + 384],
                    start=(c == 0), stop=(c == 7)), chain=(c > 0))
            P.op('act', lambda e, iv, vi=vi: e.copy(out=vst[:, vi * 384:(vi + 1) * 384], in_=R.ps[2][:, 0:384]))
        for gi in range(6):
            co = G_COL + gi * 512
            for c in range(8):
                P.op('pe', lambda e, iv, c=c, co=co: e.matmul(
                    R.ps[3][:], lhsT=xT[:, c, 0:128], rhs=win[:, c, co:co + 512],
                    start=(c == 0), stop=(c == 7)), chain=(c > 0))
            P.op('dve', lambda e, iv, gi=gi: e.tensor_tensor(out=gt[:, gi * 512:(gi + 1) * 512], in0=R.ps[3][:],
                                                             in1=bgb[:, gi * 512:(gi + 1) * 512], op=ALU.add))
        P.op('act', lambda e, iv: e.activation(out=gt, in_=gt, func=AF.Sigmoid))
        P.dma('act', lambda e, iv: e.dma_start(out=v_out[bass.ts(iv[L], 128), :], in_=vst).then_inc(dsq, 16))
        P.dma('act', lambda e, iv: e.dma_start(out=gates_out[bass.ts(iv[L], 128), :], in_=gt).then_inc(dsq, 16))
        P.dma('act', lambda e, iv: None, waits=[(dsq, 80)])


BIG = 30000.0
SLOPES = (2.0 ** (-8.0 * np.arange(1, 13, dtype=np.float32) / 12)).astype(np.float32)
SC_M = 64 ** -0.5
SC_D = 32 ** -0.5
SC_S = 64 ** -0.5
SUBLN_EPS = 1e-5


def att_tables():
    bf = ml_dtypes.bfloat16
    i = np.arange(SEQ)
    ib = (i // 128).astype(np.float32)
    il = (i % 128).astype(np.float32)
    one = np.ones(SEQ, np.float32)

    def hl(v):
        hi = np.float32(np.asarray(v, np.float32).astype(bf).astype(np.float32))
        lo = np.float32(np.asarray(v - hi, np.float32).astype(bf).astype(np.float32))
        return hi, lo
    qaug = np.zeros((12, 8, SEQ), np.float32)
    kaug = np.zeros((12, 8, SEQ), np.float32)
    for slot in range(12):
        scale = SC_M if slot % 2 == 0 else SC_D
        sig = float(SLOPES[slot]) / scale
        Sh, Sl = hl(128.0 * sig)
        sh, sl = hl(sig)
        qaug[slot] = np.stack([Sh * one, Sl * one, sh * one, sl * one, -ib, -ib, -il, -il])
        kaug[slot] = np.stack([ib, ib, il, il, Sh * one, Sl * one, sh * one, sl * one])
    onehot = (np.arange(32)[:, None] == (i // 256)[None, :]).astype(np.float32)
    p = np.arange(128)[:, None]
    c = np.arange(512)[None, :]
    maskT = np.concatenate([np.where((r * 128 + p) > c, -BIG, 0.0) for r in range(4)], axis=1)
    maskS = np.concatenate([np.where(c >= (r * 128 + p), -BIG, 0.0) for r in range(4)], axis=1)
    o = np.arange(32)[:, None]
    n = np.arange(32)[None, :]
    pb = np.stack([np.where(n < o, 0.0, -BIG), np.where(n == o, 0.0, -BIG), np.where(n <= o, 0.0, -BIG)]).astype(np.float32)
    return dict(qaug=qaug.reshape(96, SEQ).astype(bf), kaug=kaug.reshape(96, SEQ).astype(bf), onehot=onehot.astype(bf),
                maskT=maskT.astype(bf), maskS=maskS.astype(bf), pb=pb.reshape(-1))


def stage_att(P, R, T, qkT_d, qm32_d, kmean_d, v_d, o_d, lam_d, subg_d, one_m_linit, nheads=6, do=('m', 'd', 's')):
    W = R.WBUF
    QA = [W[:, 0:8192], W[:, 8192:16384]]
    KA = [W[:, 16384:24576], W[:, 24576:32768]]
    VA = W[:, 32768:40960].rearrange("p (k d) -> p k d", k=64)
    VAf = W[:, 32768:40960]
    maskT = W[:, 40960:43008]
    maskS = W[:, 43008:45056]
    pt = [W[:, 45056:45568], W[:, 45568:46080]]
    wb = W[:, 46080:46592]
    wT = W[:, 46592:47104].rearrange("p (k q) -> p k q", k=4)
    FW = W[:, 47360:67584].bitcast(F32)
    pbt = FW[:, 0:3072].rearrange("p (a o n) -> p a o n", a=3, o=32)
    osb = [FW[:, 3072:3584], FW[:, 3584:4096]]
    gm = FW[:, 4096:4128]
    sel = FW[:, 4128:4160]
    m8 = FW[:, 4160:4168]
    rinv = [FW[:, 4168:4172], FW[:, 4172:4176]]
    ss = FW[:, 4176:4180]
    carry = FW[:, 4180:4181]
    negc = FW[:, 4181:4182]
    tot = FW[:, 4182:4183]
    lamt = FW[:, 4183:4184]
    lsc = FW[:, 4184:4186]
    t1 = FW[:, 4192:4256]
    od = FW[:, 4256:4512].rearrange("p (j d) -> p j d", j=4)
    gsc = FW[:, 4512:4576]
    lamw = FW[:, 4576:4704]
    lamw2 = FW[:, 4704:4768]
    ones = FW[:, 4768:5280]
    tt = FW[:, 5280:5792]
    spb = FW[:, 5792:6305]
    E1 = FW[:, 6336:6848]
    aa = FW[:, 6848:7360]
    kme = FW[:, 7360:7392]
    kmr = FW[:, 9472:9536]
    ost_all = FW[:, 7424:9472].bitcast(BF16).rearrange("p (t d) -> p t d", t=64)
    qm32 = R.SCRF[:, 0:8192]
    ps_s = [R.ps[0], R.ps[1]]
    ps_o = [R.ps[2], R.ps[3]]
    ps_t = [R.ps[4], R.ps[5]]
    psb = R.psb[0]
    tsem = new_sem(R, "att_t")
    hsem = new_sem(R, "att_h")
    osem = new_sem(R, "att_o")
    cnt = {'h': 0, 'o': 0}

    def ldc(e, iv):
        e.dma_start(out=maskT, in_=T['maskT'][:, :]).then_inc(tsem, 16)
        e.dma_start(out=maskS, in_=T['maskS'][:, :]).then_inc(tsem, 16)
        e.dma_start(out=FW[:, 0:3072], in_=T['pb'].partition_broadcast(128)).then_inc(tsem, 16)
        e.dma_start(out=gsc, in_=subg_d.partition_broadcast(128)).then_inc(tsem, 16)
        e.dma_start(out=lamw, in_=lam_d.rearrange("a d -> (a d)").partition_broadcast(128)).then_inc(tsem, 16)
    P.dma('sp', ldc)
    P.dma('sp', lambda e, iv: None, waits=[(tsem, 80)])
    P.op('dve', lambda e, iv: e.memset(ones, 1.0))
    P.op('dve', lambda e, iv: e.memset(spb[:, 0:1], 0.0))
    P.op('dve', lambda e, iv: e.memset(VA[:, :, 64:65], 1.0))
    linit = 1.0 - one_m_linit
    P.op('dve', lambda e, iv: e.tensor_tensor(out=lamw2[:, 0:32], in0=lamw[:, 0:32], in1=lamw[:, 32:64], op=ALU.mult))
    P.op('dve', lambda e, iv: e.tensor_tensor(out=lamw2[:, 32:64], in0=lamw[:, 64:96], in1=lamw[:, 96:128], op=ALU.mult))
    P.op('dve', lambda e, iv: e.tensor_reduce(out=lsc, in_=lamw2.rearrange("p (a d) -> p a d", a=2), axis=AX.X, op=ALU.add))
    P.op('act', lambda e, iv: e.activation(out=lsc, in_=lsc, func=AF.Exp))
    P.op('dve', lambda e, iv: e.tensor_tensor(out=lamt, in0=lsc[:, 0:1], in1=lsc[:, 1:2], op=ALU.subtract))
    P.op('dve', lambda e, iv: e.tensor_scalar(out=lamt, in0=lamt, scalar1=linit, scalar2=-1.0, op0=ALU.add, op1=ALU.mult))
    P.op('act', lambda e, iv: e.mul(out=gsc, in_=gsc, mul=one_m_linit))

    def dense_tile(s, Kd, scale, kb_ap_K, kb_ap_V, Q, mask_r, first):
        P.op('pe', lambda e, iv: e.matmul(ps_s[s][:], lhsT=kb_ap_K(iv), rhs=QA[s][0:Kd, Q * 512:(Q + 1) * 512],
                                          start=True, stop=(mask_r is None)))
        if mask_r is not None:
            P.op('pe', lambda e, iv: e.matmul(ps_s[s][:], lhsT=R.identb[:], rhs=maskT[:, mask_r * 512:(mask_r + 1) * 512],
                                              start=False, stop=True), chain=True)
        P.op('act', lambda e, iv: e.activation(out=pt[s], in_=ps_s[s][:], func=AF.Exp, scale=scale))
        P.op('pe', lambda e, iv: e.matmul(ps_o[s][0:65, :], lhsT=kb_ap_V(iv), rhs=pt[s], start=first, stop=True))

    def dense_Q(streams, Q):
        for r in range(4):
            kb = 4 * Q + r
            for (s, Kd, scale) in streams:
                dense_tile(s, Kd, scale, lambda iv, s=s, Kd=Kd, kb=kb: KA[s][0:Kd, kb * 128:(kb + 1) * 128],
                           lambda iv, kb=kb: VA[:, kb, 0:65], Q, r, r == 0)
        for kb in range(4 * Q):
            for (s, Kd, scale) in streams:
                dense_tile(s, Kd, scale, lambda iv, s=s, Kd=Kd, kb=kb: KA[s][0:Kd, kb * 128:(kb + 1) * 128],
                           lambda iv, kb=kb: VA[:, kb, 0:65], Q, None, False)
        for (s, Kd, scale) in streams:
            P.op('act', lambda e, iv, s=s: e.copy(out=osb[s][0:65, :], in_=ps_o[s][0:65, :]))
            for j in range(4):
                P.op('pe', lambda e, iv, s=s, j=j: e.transpose(out=ps_t[s][:, j * 65:(j + 1) * 65],
                                                               in_=osb[s][0:65, j * 128:(j + 1) * 128],
                                                               identity=R.identf[0:65, 0:65]), chain=(j > 0))
            P.op('dve', lambda e, iv, s=s: e.reciprocal(
                out=rinv[s], in_=ps_t[s][:, 0:260].rearrange("p (j d) -> p j d", j=4)[:, :, 64]))

    def store_head(hfn, dq):
        P.dma(dq, lambda e, iv: e.dma_start(
            out=o_d[bass.ts(hfn(iv), 1), :, :].rearrange("h (t p) d -> p (h t) d", p=128),
            in_=ost_all).then_inc(osem, 16))
        P.dma(dq, lambda e, iv: None, waits=[(osem, 16)])

    if 'm' in do:
        with P.loop(nheads, clear=(hsem, osem)) as LHM:
            hbaseM = 0

            def ldm(e, iv):
                h = iv[LHM]
                e.dma_start(out=QA[0][32:96, :], in_=qkT_d[0:384, :][bass.ts(h, 64), :]).then_inc(hsem, 16)
                e.dma_start(out=KA[0][32:96, :], in_=qkT_d[384:768, :][bass.ts(h, 64), :]).then_inc(hsem, 16)
                e.dma_start(out=KA[0][0:32, :], in_=T['onehot'][:, :]).then_inc(hsem, 16)
                e.dma_start(out=QA[0][96:104, :], in_=T['qaug'][bass.ts(h * 2, 8), :]).then_inc(hsem, 16)
                e.dma_start(out=KA[0][96:104, :], in_=T['kaug'][bass.ts(h * 2, 8), :]).then_inc(hsem, 16)
                e.dma_start(out=VA[:, :, 0:64],
                            in_=v_d[:, 0:384][:, bass.ts(h, 64)].rearrange("(k p) d -> p k d", p=128)).then_inc(hsem, 16)
                e.dma_start(out=qm32[0:64, :], in_=qm32_d[bass.ts(h, 64), :]).then_inc(hsem, 16)
                e.dma_start(out=kmr[0:64, :], in_=kmean_d[bass.ts(h, 64), :]).then_inc(hsem, 16)
            P.dma('sp', ldm)
            P.dma('sp', lambda e, iv: None, waits=[(hsem, lambda iv: (hbaseM + (iv[LHM] + 1) * 8) * 16)])
            P.op('dve', lambda e, iv: e.tensor_reduce(out=kme[0:64, :], in_=kmr[0:64, :].rearrange("p (n b) -> p n b", b=2),
                                                      axis=AX.X, op=ALU.add))
            P.op('dve', lambda e, iv: e.tensor_scalar(out=kme[0:64, :], in0=kme[0:64, :], scalar1=1.0 / 256.0, scalar2=None,
                                                      op0=ALU.mult))
            for t4 in range(16):
                for tj in range(4):
                    t = t4 * 4 + tj
                    own = t // 2
                    P.op('pe', lambda e, iv, t=t: e.matmul(ps_s[1][:, 0:32], lhsT=qm32[0:64, t * 128:(t + 1) * 128],
                                                           rhs=kme[0:64, :], start=True, stop=True))
                    P.op('dve', lambda e, iv, own=own: e.tensor_tensor(out=gm, in0=ps_s[1][:, 0:32], in1=pbt[:, 0, own, :],
                                                                       op=ALU.add))
                    P.op('dve', lambda e, iv: e.max(out=m8, in_=gm))
                    P.op('dve', lambda e, iv: e.tensor_scalar(out=sel, in0=gm, scalar1=m8[:, 2:3], scalar2=1.0,
                                                              op0=ALU.is_ge, op1=ALU.subtract))
                    P.op('dve', lambda e, iv, own=own: e.scalar_tensor_tensor(out=sel, in0=sel, scalar=BIG,
                                                                              in1=pbt[:, 1, own, :], op0=ALU.mult, op1=ALU.max))
                    P.op('dve', lambda e, iv, own=own: e.tensor_tensor(out=sel, in0=sel, in1=pbt[:, 2, own, :], op=ALU.add))
                    P.op('pe', lambda e, iv, tj=tj: e.transpose(out=ps_t[0][0:32, tj * 128:(tj + 1) * 128], in_=sel,
                                                                identity=R.identf[:]))
                P.op('act', lambda e, iv, t4=t4: e.copy(out=QA[0][0:32, t4 * 512:(t4 + 1) * 512], in_=ps_t[0][0:32, :]))
            for Q in range(16):
                dense_Q([(0, 104, SC_M)], Q)
                for j in range(4):
                    P.op('dve', lambda e, iv, j=j, Q=Q: e.tensor_scalar(
                        out=ost_all[:, Q * 4 + j, :], in0=ps_t[0][:, j * 65:j * 65 + 64], scalar1=rinv[0][:, j:j + 1], scalar2=None,
                        op0=ALU.mult))
            store_head(lambda iv: iv[LHM], 'sp')
    obase_d = 0

    if 'd' in do:
        with P.loop(nheads, clear=(hsem, osem)) as LHD:
            hbaseD = 0

            def ldd(e, iv):
                h = iv[LHD]
                for m in range(2):
                    e.dma_start(out=QA[m][0:32, :], in_=qkT_d[768:1152, :][bass.ts(h * 2 + m, 32), :]).then_inc(hsem, 16)
                    e.dma_start(out=KA[m][0:32, :], in_=qkT_d[1152:1536, :][bass.ts(h * 2 + m, 32), :]).then_inc(hsem, 16)
                    e.dma_start(out=QA[m][32:40, :], in_=T['qaug'][bass.ts(h * 2 + 1, 8), :]).then_inc(hsem, 16)
                    e.dma_start(out=KA[m][32:40, :], in_=T['kaug'][bass.ts(h * 2 + 1, 8), :]).then_inc(hsem, 16)
                e.dma_start(out=VA[:, :, 0:64],
                            in_=v_d[:, 384:768][:, bass.ts(h, 64)].rearrange("(k p) d -> p k d", p=128)).then_inc(hsem, 16)
            P.dma('act', ldd)
            P.dma('act', lambda e, iv: None, waits=[(hsem, lambda iv: (hbaseD + (iv[LHD] + 1) * 9) * 16)])
            for Q in range(16):
                dense_Q([(0, 40, SC_D), (1, 40, SC_D)], Q)
                P.op('dve', lambda e, iv: e.tensor_scalar(out=rinv[1], in0=rinv[1], scalar1=lamt[:, 0:1], scalar2=None,
                                                          op0=ALU.mult))
                for j in range(4):
                    P.op('dve', lambda e, iv, j=j: e.tensor_scalar(
                        out=t1, in0=ps_t[0][:, j * 65:j * 65 + 64], scalar1=rinv[0][:, j:j + 1], scalar2=None, op0=ALU.mult))
                    P.op('dve', lambda e, iv, j=j: e.scalar_tensor_tensor(
                        out=od[:, j, :], in0=ps_t[1][:, j * 65:j * 65 + 64], scalar=rinv[1][:, j:j + 1], in1=t1,
                        op0=ALU.mult, op1=ALU.add))
                    P.op('dve', lambda e, iv, j=j: e.tensor_tensor(out=t1, in0=od[:, j, :], in1=od[:, j, :], op=ALU.mult))
                    P.op('dve', lambda e, iv, j=j: e.tensor_reduce(out=ss[:, j:j + 1], in_=t1, axis=AX.X, op=ALU.add))
                P.op('dve', lambda e, iv: e.tensor_scalar(out=ss, in0=ss, scalar1=1.0 / 64.0, scalar2=SUBLN_EPS,
                                                          op0=ALU.mult, op1=ALU.add))
                P.op('act', lambda e, iv: e.activation(out=ss, in_=ss, func=AF.Sqrt))
                P.op('dve', lambda e, iv: e.reciprocal(out=ss, in_=ss))
                for j in range(4):
                    P.op('dve', lambda e, iv, j=j, Q=Q: e.scalar_tensor_tensor(
                        out=ost_all[:, Q * 4 + j, :], in0=od[:, j, :], scalar=ss[:, j:j + 1], in1=gsc, op0=ALU.mult, op1=ALU.mult))
            store_head(lambda iv: iv[LHD] + 6, 'act')
    obase_s = 0

    if 's' in do:
        ones13 = FW[:, 4768:5281]
        SF = R.SCRF
        tts = [FW[:, 5312:5824]] + [SF[:, 2176 * i:2176 * i + 512] for i in range(3)]
        spbs = [FW[:, 5824:6337]] + [SF[:, 2176 * i + 512:2176 * i + 1025] for i in range(3)]
        Efs = [FW[:, 6368:6881]] + [SF[:, 2176 * i + 1056:2176 * i + 1569] for i in range(3)]
        aas = [FW[:, 6912:7424]] + [SF[:, 2176 * i + 1600:2176 * i + 2112] for i in range(3)]
        negcs = [FW[:, 4181:4182]] + [SF[:, 2176 * i + 2112:2176 * i + 2113] for i in range(3)]
        wbs_ = [wb] + [R.SCRB[:, 1024 * i:1024 * i + 512] for i in range(3)]
        wTs = [wT] + [R.SCRB[:, 1024 * i + 512:1024 * i + 1024].rearrange("p (k q) -> p k q", k=4) for i in range(3)]
        psbs = [R.psb[0][:, 0:512], R.psb[0][:, 512:1024], R.psb[1][:, 0:512], R.psb[1][:, 512:1024]]
        pszs = [R.ps[0], R.ps[1], R.ps[2]]
        psos = [R.ps[3][:, 0:64], R.ps[4][:, 0:64], R.ps[5][:, 0:64]]
        NTS = 3
        P.op('dve', lambda e, iv: e.memset(ones13, 1.0))
        for th in range(NTS):
            P.op('dve', lambda e, iv, th=th: e.memset(spbs[th][:, 0:1], 0.0))
        with P.loop(nheads, clear=(hsem, osem)) as LHS:
            def lds(e, iv):
                h = iv[LHS]
                e.dma_start(out=QA[0][0:64, :], in_=qkT_d[1536:1920, :][bass.ts(h, 64), :]).then_inc(hsem, 16)
                e.dma_start(out=KA[0][0:64, :], in_=qkT_d[1920:2304, :][bass.ts(h, 64), :]).then_inc(hsem, 16)
                e.dma_start(out=VA[:, :, 0:64],
                            in_=v_d[:, 768:1152][:, bass.ts(h, 64)].rearrange("(k p) d -> p k d", p=128)).then_inc(hsem, 16)
            P.dma('sp', lds)
            P.dma('sp', lambda e, iv: None, waits=[(hsem, 48)])
            for th in range(1, NTS):
                P.join('pe', th, 0)

            def sb_tile_ops(th, t, kt, mask_r, first):
                ops = []
                ops.append(('pe', lambda e, iv: e.matmul(pszs[th][:], lhsT=QA[0][0:64, t * 128:(t + 1) * 128],
                                                         rhs=KA[0][0:64, kt * 512:(kt + 1) * 512],
                                                         start=True, stop=(mask_r is None)), False))
                if mask_r is not None:
                    ops.append(('pe', lambda e, iv: e.matmul(pszs[th][:], lhsT=R.identb[:],
                                                             rhs=maskS[:, mask_r * 512:(mask_r + 1) * 512],
                                                             start=False, stop=True), True))
                ops.append(('act', lambda e, iv: e.activation(out=tts[th], in_=pszs[th][:], func=AF.Exp, scale=SC_S), False))
                ops.append(('act', lambda e, iv: e.activation(out=spbs[th][:, 1:513], in_=tts[th], func=AF.Ln, bias=1.0), False))
                ops.append(('dve', lambda e, iv: e.tensor_tensor_scan(out=Efs[th], data0=ones13, data1=spbs[th][:, 0:513],
                                                                      initial=0.0, op0=ALU.mult, op1=ALU.add), False))
                ops.append(('dve', lambda e, iv: e.scalar_tensor_tensor(out=aas[th], in0=pszs[th][:], scalar=SC_S,
                                                                        in1=Efs[th][:, 0:512], op0=ALU.mult, op1=ALU.add), False))
                if first:
                    ops.append(('dve', lambda e, iv: e.tensor_scalar(out=negcs[th], in0=Efs[th][:, 512:513], scalar1=-1.0,
                                                                     scalar2=None, op0=ALU.mult), False))
                else:
                    ops.append(('dve', lambda e, iv: e.tensor_tensor(out=negcs[th], in0=negcs[th], in1=Efs[th][:, 512:513],
                                                                     op=ALU.subtract), False))
                ops.append(('act', lambda e, iv: e.activation(out=wbs_[th], in_=aas[th], func=AF.Exp, bias=negcs[th][:, 0:1]), False))
                for k4 in range(4):
                    ops.append(('pe', lambda e, iv, k4=k4: e.transpose(out=psbs[th][:, k4 * 128:(k4 + 1) * 128],
                                                                       in_=wbs_[th][:, k4 * 128:(k4 + 1) * 128],
                                                                       identity=R.identb[:]), k4 > 0))
                ops.append(('act', lambda e, iv: e.copy(out=wTs[th], in_=psbs[th].rearrange("p (k q) -> p k q", k=4)), False))
                for k4 in range(4):
                    ops.append(('pe', lambda e, iv, k4=k4: e.matmul(psos[th], lhsT=wTs[th][:, k4, :],
                                                                    rhs=VA[:, kt * 4 + k4, 0:64],
                                                                    start=(first and k4 == 0), stop=True), k4 > 0))
                return ops

            for base in range(0, 64, NTS):
                lists = []
                for th in range(NTS):
                    t = base + th
                    if t >= 64:
                        lists.append([])
                        continue
                    ktd = t // 4
                    ops = sb_tile_ops(th, t, ktd, t % 4, True)
                    for kt in range(ktd - 1, -1, -1):
                        ops += sb_tile_ops(th, t, kt, None, False)
                    ops.append(('act', (lambda e, iv, t=t, th=th: e.copy(out=ost_all[:, t, :], in_=psos[th])), False))
                    lists.append(ops)
                for idx in range(max(len(x) for x in lists)):
                    for th in range(NTS):
                        if idx < len(lists[th]):
                            eng, fn, ch = lists[th][idx]
                            P.op(eng, fn, chain=ch, thread=th)
            for th in range(1, NTS):
                P.join('sp', 0, th)
            store_head(lambda iv: iv[LHS] + 12, 'sp')
def stage_mixpost(P, R, x_in, o_d, gates_d, x_out, wbm_d, wbd_d, wbs_d, wout_d, g_d, b_d, ntiles=64):
    W = R.WBUF
    wbr = W[:, 0:9216].rearrange("p (k n) -> p k n", k=9)
    wout = W[:, 9216:17408].rearrange("p (c n) -> p c n", c=8)
    gts = W[:, 17408:23552].bitcast(F32)
    xs = R.SCRF[:, 0:1024]
    u = R.SCRF[:, 1024:2048]
    xa = R.SCRF[:, 2048:3072]
    yo = R.SCRF[:, 3072:4096]
    gbc = R.SCRF[:, 4608:5632]
    bbc = R.SCRF[:, 5632:6656]
    merged = R.SCRF[:, 6656:7680]
    tmp = R.SCRF[:, 7680:8704]
    xT = R.SCRB[:, 0:4096].rearrange("p (c t) -> p c t", c=8)
    ot = R.SCRB[:, 4096:5248]
    oT = R.SCRB[:, 5248:6400].rearrange("p (k t) -> p k t", k=9)
    wsem = new_sem(R, "w")
    dsi = new_sem(R, "dsi")
    dso = new_sem(R, "dso")

    def loadw(e, iv):
        for bi, wd_ in enumerate((wbm_d, wbd_d, wbs_d)):
            for i in range(3):
                e.dma_start(out=wbr[:, bi * 3 + i, :], in_=wd_[i * 128:(i + 1) * 128, :]).then_inc(wsem, 16)
        for c in range(8):
            e.dma_start(out=wout[:, c, :], in_=wout_d[c * 128:(c + 1) * 128, :]).then_inc(wsem, 16)
    P.dma('pool', loadw)
    P.dma('pool', lambda e, iv: None, waits=[(wsem, 16 * 17)])
    load_ln_params(P, R, g_d, b_d, gbc, bbc)
    with P.loop(ntiles, clear=(dsi, dso)) as L:
        def ld(e, iv):
            e.dma_start(out=gts, in_=gates_d[bass.ts(iv[L], 128), :]).then_inc(dsi, 16)
            e.dma_start(out=xs, in_=x_in[bass.ts(iv[L], 128), :]).then_inc(dsi, 16)
        P.dma('sp', lambda e, iv: e.dma_start(out=ot.rearrange("p (h d) -> p h d", h=18),
                                              in_=o_d.rearrange("h s d -> s h d")[bass.ts(iv[L], 128), :, :]).then_inc(dsi, 16))
        P.dma('act', ld)
        P.dma('act', lambda e, iv: None, waits=[(dsi, lambda iv: (iv[L] + 1) * 48)])
        for k in range(9):
            P.op('pe', lambda e, iv, k=k: e.transpose(
                out=(R.psb[0][:, k * 128:(k + 1) * 128] if k < 8 else R.psb[1][:, 0:128]),
                in_=ot[:, k * 128:(k + 1) * 128], identity=R.identb[:]), chain=(k > 0))
        P.op('act', lambda e, iv: e.copy(out=oT[:, 0:8, :], in_=R.psb[0][:, :].rearrange("p (k t) -> p k t", k=8)))
        P.op('act', lambda e, iv: e.copy(out=oT[:, 8, :], in_=R.psb[1][:, 0:128]))
        for br in range(3):
            for h in range(2):
                for i in range(3):
                    P.op('pe', lambda e, iv, br=br, h=h, i=i: e.matmul(
                        R.ps[1][:], lhsT=oT[:, br * 3 + i, :], rhs=wbr[:, br * 3 + i, h * 512:(h + 1) * 512],
                        start=(i == 0), stop=(i == 2)), chain=(i > 0))
                gsl = gts[:, br * 1024 + h * 512: br * 1024 + (h + 1) * 512]
                if br == 0:
                    P.op('dve', lambda e, iv, h=h, gsl=gsl: e.tensor_tensor(out=merged[:, h * 512:(h + 1) * 512], in0=gsl,
                                                                            in1=R.ps[1][:], op=ALU.mult))
                else:
                    P.op('dve', lambda e, iv, h=h, gsl=gsl: e.tensor_tensor(out=tmp[:, h * 512:(h + 1) * 512], in0=gsl,
                                                                            in1=R.ps[1][:], op=ALU.mult))
                    P.op('dve', lambda e, iv, h=h: e.tensor_tensor(out=merged[:, h * 512:(h + 1) * 512],
                                                                   in0=merged[:, h * 512:(h + 1) * 512],
                                                                   in1=tmp[:, h * 512:(h + 1) * 512], op=ALU.add))
        transposes_to_xT(P, R, lambda j: merged, xT, nj=1)
        first = True
        for h in range(2):
            for c in range(8):
                P.op('pe', lambda e, iv, h=h, c=c: e.matmul(R.ps[3 + h][:], lhsT=xT[:, c, 0:128],
                                                            rhs=wout[:, c, h * 512:(h + 1) * 512],
                                                            start=(c == 0), stop=(c == 7)), chain=not first)
                first = False
        P.op('act', lambda e, iv: e.mul(out=xa, in_=xs, mul=ALPHA))
        for h in range(2):
            P.op('dve', lambda e, iv, h=h: e.tensor_tensor(out=u[:, h * 512:(h + 1) * 512], in0=xa[:, h * 512:(h + 1) * 512],
                                                           in1=R.ps[3 + h][:], op=ALU.add))
        ln_ops(P, R, u, gbc, bbc, yo, pre_waits=[(dso, lambda iv: iv[L] * 16)])
        P.dma('act', lambda e, iv: e.dma_start(out=x_out[bass.ts(iv[L], 128), :], in_=yo).then_inc(dso, 16))
        P.dma('act', lambda e, iv: None, waits=[(dso, 16)])


NACT = 4


def build_layer(tb, l):
    bf = ml_dtypes.bfloat16
    nc = bass.Bass("TRN2", target_bir_lowering=False)

    def dt(n, s, d=F32, k="ExternalInput"):
        return nc.dram_tensor(n, list(s), d, kind=k).ap()
    x = dt("x", [SEQ, D])
    ln_g = dt("ln_g", [3, D])
    ln_b = dt("ln_b", [3, D])
    wg = dt("ffn_w_gate", [2, D, DFF])
    wu = dt("ffn_w_up", [2, D, DFF])
    wd = dt("ffn_w_down", [2, DFF, D])
    win = dt("w_in", [D, NIN])
    bg = dt("b_gate", [3, D])
    dl = dt("diff_lambda", [4, 32])
    dg = dt("diff_subln_g", [64])
    wbm = dt("w_br_moba", [384, D])
    wbd = dt("w_br_diff", [384, D])
    wbs = dt("w_br_sb", [384, D])
    wo = dt("w_out", [D, D])
    idf = dt("idf", [128, 128])
    idb = dt("idb", [128, 128], BF16)
    T = {k: dt("t_" + k, a.shape, BF16 if a.dtype == bf else F32) for k, a in tb.items()}
    y = dt("y", [SEQ, D], F32, "ExternalOutput")
    xa_d = dt("xa_d", [SEQ, D], F32, "Internal")
    xb_d = dt("xb_d", [SEQ, D], F32, "Internal")
    qkT = dt("qkT_d", [2304, SEQ], BF16, "Internal")
    qm32 = dt("qm32_d", [384, SEQ], F32, "Internal")
    kmean = dt("kmean_d", [384, 64], F32, "Internal")
    v = dt("v_d", [SEQ, 1152], BF16, "Internal")
    gates = dt("gates_d", [SEQ, 3072], F32, "Internal")
    o = dt("o_d", [18, SEQ, 64], BF16, "Internal")
    with ExitStack() as ctx:
        R = rl_alloc(nc, ctx)
        P = Prog()
        rl_consts(P, R, idf, idb)
        linit = 0.8 - 0.6 * math.exp(-0.3 * l)
        stage_ffnln(P, R, x, xa_d, wg[0], wu[0], wd[0], ln_g[0], ln_b[0], nsb=16)
        stage_win(P, R, xa_d, win, bg, qkT, qm32, kmean, v, gates, nsb=16)
        stage_att(P, R, T, qkT, qm32, kmean, v, o, dl, dg, 1.0 - linit)
        stage_mixpost(P, R, xa_d, o, gates, xb_d, wbm, wbd, wbs, wo, ln_g[1], ln_b[1])
        stage_ffnln(P, R, xb_d, y, wg[1], wu[1], wd[1], ln_g[2], ln_b[2], nsb=16)
        run_prog(nc, P, R.G)
    return nc


def kernel(x, ln_g, ln_b, ffn_w_gate, ffn_w_up, ffn_w_down, w_in, b_gate, diff_lambda, diff_subln_g,
           w_br_moba, w_br_diff, w_br_sb, w_out):
    bf = ml_dtypes.bfloat16
    tb = att_tables()
    f = lambda a: np.ascontiguousarray(np.asarray(a, dtype=np.float32))
    cur = f(x)
    for l in range(2):
        nc = build_layer(tb, l)
        shared = dict(ln_g=f(ln_g[l]), ln_b=f(ln_b[l]), ffn_w_gate=f(ffn_w_gate[l]), ffn_w_up=f(ffn_w_up[l]),
                      ffn_w_down=f(ffn_w_down[l]), w_in=f(w_in[l]), b_gate=f(b_gate[l]), diff_lambda=f(diff_lambda[l]),
                      diff_subln_g=f(diff_subln_g[l]), w_br_moba=f(w_br_moba[l]), w_br_diff=f(w_br_diff[l]),
                      w_br_sb=f(w_br_sb[l]), w_out=f(w_out[l]),
                      idf=np.eye(128, dtype=np.float32), idb=np.eye(128).astype(bf))
        for k, a in tb.items():
            shared["t_" + k] = a
        in_maps = [dict(shared, x=cur[b]) for b in range(NACT)]
        res = run_bass_kernel_spmd(nc, in_maps, core_ids=list(range(NACT)))
        cur = np.stack([np.asarray(r["y"], dtype=np.float32) for r in res.results], axis=0)
    return cur
```

```python
import math
from contextlib import contextmanager, ExitStack
import numpy as np
import ml_dtypes
import concourse.bass as bass
import concourse.mybir as mybir
from concourse.bass_utils import run_bass_kernel_spmd

F32 = mybir.dt.float32
BF16 = mybir.dt.bfloat16
AF = mybir.ActivationFunctionType
ALU = mybir.AluOpType
AX = mybir.AxisListType

D = 1024
DFF = 2816
NF = DFF // 128
NTOK = 4096
SEQ = 8192
NIN = 6528
ALPHA = 4.0 ** 0.25
LN_EPS = 1e-5
NCORES = 8


@contextmanager
def my_fori(nc, e, regs, start, end):
    loop_id = nc.next_id()
    name = f"myfori_{loop_id}"
    ls, le = name + "_loop", name + "_end"
    engines = bass.OrderedEngineSet([e.engine])
    nc.regs_mov(regs, start)
    nc.br(ls, engines=engines)
    with nc.body(ls, valid_engines=engines):
        yield nc.snap(regs, min_val=start, max_val=end - 1)
        nc.regs_alu(regs, regs, 1, op=mybir.AluOpType.add)
        nc.br_lt(regs, end, on_true=ls, on_false=le, engines=engines)
    nc.switch_bb(le)


class Prog:
    def __init__(self):
        self.root = []
        self.stack = [self.root]
        self.nloops = 0

    def op(self, eng, fn, chain=False, waits=(), thread=0):
        self.stack[-1].append(['op', eng, fn, chain, tuple(waits), 'c', thread])

    def dma(self, eng, fn, waits=(), thread=0):
        self.stack[-1].append(['op', eng, fn, False, tuple(waits), 'd', thread])

    def join(self, eng, thread, other):
        self.stack[-1].append(['op', eng, None, False, (), 'j', thread, other])

    @contextmanager
    def loop(self, n, clear=()):
        body = []
        lid = self.nloops
        self.nloops += 1
        self.stack[-1].append(['loop', n, body, lid, tuple(clear)])
        self.stack.append(body)
        try:
            yield lid
        finally:
            self.stack.pop()

    @staticmethod
    def _size(node):
        if node[0] == 'op':
            return 1
        return node[1] * sum(Prog._size(b) for b in node[2])

    @staticmethod
    def _has(node, ename):
        if node[0] == 'op':
            return node[1] == ename
        return any(Prog._has(b, ename) for b in node[2])

    def total(self):
        return sum(self._size(n) for n in self.root)

    def emit(self, ename, e, G, semfn, nc, GX):
        k = 0
        lregs = nc.alloc_registers(f"lr_{ename}", engines=bass.OrderedEngineSet([e.engine]))
        GS = (G,) + tuple(GX)
        NTH = len(GS)
        for node in self.root:
            if node[0] == 'op' or node[1] == 1:
                subs = [node] if node[0] == 'op' else node[2]
                iv = {} if node[0] == 'op' else {node[3]: 0}
                for sub in subs:
                    eng, fn, chain, waits, kind = sub[1:6]
                    assert kind != 'j' and sub[6] == 0
                    if eng == ename:
                        e.wait_ge(G, k)
                        for (sem, vf) in waits:
                            e.wait_ge(sem, vf(iv) if callable(vf) else vf)
                        if kind == 'c':
                            fn(e, iv).then_inc(G, 1)
                        else:
                            fn(e, iv)
                            e.sem_inc(G, 1)
                    k += 1
                continue
            _, n, body, lid, clr = node
            E, S2 = semfn(lid)
            NT = [sum(1 for sub in body if sub[6] == th) for th in range(NTH)]

            def release(first):
                if first:
                    e.wait_ge(G, k)
                else:
                    for th in range(NTH):
                        if NT[th]:
                            e.wait_ge(GS[th], NT[th])
                e.wait_ge(E, 5)
                for gs in GS:
                    e.sem_clear(gs)
                e.sem_clear(E)
                for cs in clr:
                    e.sem_clear(cs)
                e.sem_inc(S2, 1)
            e.sem_inc(E, 1)
            if ename == 'sp':
                release(True)
            with my_fori(nc, e, lregs, 1, n + 1) as i:
                e.wait_ge(S2, i)
                iv = {lid: i - 1}
                prev = [None] * NTH
                cnt = [0] * NTH
                for sub in body:
                    assert sub[0] == 'op'
                    eng, fn, chain, waits, kind, th = sub[1:7]
                    j = cnt[th]
                    if eng == ename:
                        if j > 0 and not (chain and prev[th] is not None and prev[th][1] == ename):
                            e.wait_ge(GS[th], j)
                        if kind == 'j':
                            e.wait_ge(GS[sub[7]], cnt[sub[7]])
                            e.sem_inc(GS[th], 1)
                        else:
                            for (sem, vf) in waits:
                                e.wait_ge(sem, vf({lid: 0}) if callable(vf) else vf)
                            if kind == 'c':
                                fn(e, iv).then_inc(GS[th], 1)
                            else:
                                fn(e, iv)
                                e.sem_inc(GS[th], 1)
                    prev[th] = sub
                    cnt[th] += 1
                e.sem_inc(E, 1)
                if ename == 'sp':
                    release(False)
            e.wait_ge(S2, n + 1)
            if ename == 'sp':
                e.sem_inc(G, k + 1)
            k += 1


def run_prog(nc, P, G):
    engs = {'pe': 'tensor', 'act': 'scalar', 'dve': 'vector', 'pool': 'gpsimd', 'sp': 'sync'}
    with ExitStack() as sctx:
        GX = [sctx.enter_context(nc.semaphore(f"GX{i}")) for i in range(3)]
        lsems = {}
        for node in P.root:
            if node[0] == 'loop' and node[1] > 1:
                lid = node[3]
                lsems[lid] = tuple(sctx.enter_context(nc.semaphore(f"L{lid}_{nm}")) for nm in ("E", "S"))
        with nc.Block() as block:
            for ename, bname in engs.items():
                def mk(ename=ename):
                    def f(e):
                        P.emit(ename, e, G, lambda lid: lsems[lid], nc, GX)
                    return f
                getattr(block, bname)(mk())


class RL:
    pass


def rl_alloc(nc, ctx):
    R = RL()
    R.nc = nc
    R.WBUF = ctx.enter_context(nc.sbuf_tensor("WBUF", [128, 67584], BF16))
    R.SCRF = ctx.enter_context(nc.sbuf_tensor("SCRF", [128, 8704], F32))
    R.SCRB = ctx.enter_context(nc.sbuf_tensor("SCRB", [128, 15360], BF16))
    R.identf = ctx.enter_context(nc.sbuf_tensor("identf", [128, 128], F32))
    R.identb = ctx.enter_context(nc.sbuf_tensor("identb", [128, 128], BF16))
    R.stats = ctx.enter_context(nc.sbuf_tensor("stats", [128, 12], F32))
    R.mv = ctx.enter_context(nc.sbuf_tensor("mv", [128, 2], F32))
    R.rstd = ctx.enter_context(nc.sbuf_tensor("rstd", [128, 1], F32))
    R.ps = [ctx.enter_context(nc.psum_tensor(f"ps{i}", [128, 512], F32)) for i in range(6)]
    R.psb = [ctx.enter_context(nc.psum_tensor(f"psb{i}", [128, 1024], BF16)) for i in range(2)]
    R.G = ctx.enter_context(nc.semaphore("G"))
    R.csem = ctx.enter_context(nc.semaphore("csem"))
    R.nsem = 0
    R.ctx = ctx
    return R


def new_sem(R, name):
    R.nsem += 1
    return R.ctx.enter_context(R.nc.semaphore(f"{name}_{R.nsem}"))


def rl_consts(P, R, identf_d, identb_d):
    def f(e, iv):
        e.dma_start(out=R.identf[:], in_=identf_d[:, :]).then_inc(R.csem, 16)
        e.dma_start(out=R.identb[:], in_=identb_d[:, :]).then_inc(R.csem, 16)
    P.dma('sp', f)
    P.dma('sp', lambda e, iv: None, waits=[(R.csem, 32)])


def ln_ops(P, R, u, gbc, bbc, yo, pre_waits=()):
    for h in range(2):
        P.op('dve', lambda e, iv, h=h: e.bn_stats(out=R.stats[:, h * 6:(h + 1) * 6], in_=u[:, h * 512:(h + 1) * 512]))
    P.op('dve', lambda e, iv: e.bn_aggr(out=R.mv[:], in_=R.stats[:]))
    P.op('dve', lambda e, iv: e.tensor_scalar(out=R.rstd[:], in0=R.mv[:, 1:2], scalar1=LN_EPS, scalar2=None, op0=ALU.add))
    P.op('act', lambda e, iv: e.activation(out=R.rstd[:], in_=R.rstd[:], func=AF.Sqrt))
    P.op('dve', lambda e, iv: e.reciprocal(out=R.rstd[:], in_=R.rstd[:]))
    P.op('dve', lambda e, iv: e.tensor_scalar(out=u, in0=u, scalar1=R.mv[:, 0:1], scalar2=R.rstd[:, 0:1],
                                              op0=ALU.subtract, op1=ALU.mult))
    P.op('pool', lambda e, iv: e.tensor_tensor(out=u, in0=u, in1=gbc, op=ALU.mult))
    P.op('pool', lambda e, iv: e.tensor_tensor(out=yo, in0=u, in1=bbc, op=ALU.add), waits=pre_waits)


def load_ln_params(P, R, g_d, b_d, gbc, bbc):
    sem = new_sem(R, "lnp")

    def f(e, iv):
        e.dma_start(out=gbc, in_=g_d.partition_broadcast(128)).then_inc(sem, 16)
        e.dma_start(out=bbc, in_=b_d.partition_broadcast(128)).then_inc(sem, 16)
    P.dma('sp', f)
    P.dma('sp', lambda e, iv: None, waits=[(sem, 32)])


def transposes_to_xT(P, R, xs4_loader, xT, nj=4):
    for j in range(nj):
        xs = xs4_loader(j)
        for c2 in range(2):
            for c in range(4):
                cc = c2 * 4 + c
                P.op('pe', lambda e, iv, cc=cc, c=c, xs=xs: e.transpose(
                    out=R.ps[0][:, c * 128:(c + 1) * 128], in_=xs[:, cc * 128:(cc + 1) * 128], identity=R.identf[:]),
                    chain=(c > 0))
            P.op('act', lambda e, iv, c2=c2, j=j: e.activation(
                out=xT[:, c2 * 4:(c2 + 1) * 4, j * 128:(j + 1) * 128],
                in_=R.ps[0][:, :].rearrange("p (c t) -> p c t", c=4), func=AF.Copy))


def stage_ffnln(P, R, x_in, x_out, wg_d, wu_d, wd_d, g_d, b_d, nsb=16):
    ntiles = nsb * 4
    wg = R.WBUF[:, 0:22528].rearrange("p (c f) -> p c f", c=8)
    wu = R.WBUF[:, 22528:45056].rearrange("p (c f) -> p c f", c=8)
    wd = R.WBUF[:, 45056:67584].rearrange("p (c n) -> p c n", c=NF)
    xs = R.SCRF[:, 0:1024]
    u = R.SCRF[:, 1024:2048]
    xa = R.SCRF[:, 2048:3072]
    yo = R.SCRF[:, 3072:4096]
    sg = R.SCRF[:, 4096:4608]
    gbc = R.SCRF[:, 4608:5632]
    bbc = R.SCRF[:, 5632:6656]
    xT = R.SCRB[:, 0:4096].rearrange("p (c t) -> p c t", c=8)
    aT = R.SCRB[:, 4096:15360].rearrange("p (f t) -> p f t", f=NF)
    wsem = new_sem(R, "w")
    dsx = new_sem(R, "dsx")
    dso = new_sem(R, "dso")

    def loadw(e, iv):
        for c in range(8):
            e.dma_start(out=wg[:, c, :].rearrange("p (h n) -> p h n", h=2),
                        in_=wg_d[c * 128:(c + 1) * 128, :].rearrange("p (h n) -> p h n", h=2)).then_inc(wsem, 16)
            e.dma_start(out=wu[:, c, :].rearrange("p (h n) -> p h n", h=2),
                        in_=wu_d[c * 128:(c + 1) * 128, :].rearrange("p (h n) -> p h n", h=2)).then_inc(wsem, 16)
        for f in range(NF):
            e.dma_start(out=wd[:, f, :], in_=wd_d[f * 128:(f + 1) * 128, :]).then_inc(wsem, 16)
    P.dma('pool', loadw)
    P.dma('pool', lambda e, iv: None, waits=[(wsem, 16 * (16 + NF))])
    load_ln_params(P, R, g_d, b_d, gbc, bbc)

    with P.loop(ntiles, clear=(dsx, dso)) as L:
        def loader(j):
            P.dma('sp', lambda e, iv: e.dma_start(out=xs, in_=x_in[bass.ts(iv[L], 128), :]).then_inc(dsx, 16))
            P.dma('sp', lambda e, iv: None, waits=[(dsx, 16)])
            return xs
        transposes_to_xT(P, R, loader, xT, nj=1)
        for g0 in range(0, NF, 4):
            gw = min(4, NF - g0)
            first = True
            for gi in range(gw):
                f = g0 + gi
                for c in range(8):
                    P.op('pe', lambda e, iv, f=f, c=c, gi=gi: e.matmul(
                        R.ps[1][:, gi * 128:(gi + 1) * 128], lhsT=wg[:, c, f * 128:(f + 1) * 128], rhs=xT[:, c, 0:128],
                        start=(c == 0), stop=(c == 7)), chain=not first)
                    first = False
            for gi in range(gw):
                f = g0 + gi
                for c in range(8):
                    P.op('pe', lambda e, iv, f=f, c=c, gi=gi: e.matmul(
                        R.ps[2][:, gi * 128:(gi + 1) * 128], lhsT=wu[:, c, f * 128:(f + 1) * 128], rhs=xT[:, c, 0:128],
                        start=(c == 0), stop=(c == 7)), chain=True)
            P.op('act', lambda e, iv, gw=gw: e.activation(out=sg[:, 0:gw * 128], in_=R.ps[1][:, 0:gw * 128], func=AF.Silu))
            P.op('dve', lambda e, iv, gw=gw, g0=g0: e.tensor_tensor(
                out=aT[:, g0:g0 + gw, 0:128], in0=sg[:, 0:gw * 128].rearrange("p (g t) -> p g t", g=gw),
                in1=R.ps[2][:, 0:gw * 128].rearrange("p (g t) -> p g t", g=gw), op=ALU.mult))
        first = True
        for h in range(2):
            for f in range(NF):
                P.op('pe', lambda e, iv, f=f, h=h: e.matmul(
                    R.ps[3 + h][:], lhsT=aT[:, f, 0:128], rhs=wd[:, f, h * 512:(h + 1) * 512],
                    start=(f == 0), stop=(f == NF - 1)), chain=not first)
                first = False
        P.op('act', lambda e, iv: e.mul(out=xa, in_=xs, mul=ALPHA))
        for h in range(2):
            P.op('dve', lambda e, iv, h=h: e.scalar_tensor_tensor(
                out=u[:, h * 512:(h + 1) * 512], in0=R.ps[3 + h][:], scalar=0.5, in1=xa[:, h * 512:(h + 1) * 512],
                op0=ALU.mult, op1=ALU.add))
        ln_ops(P, R, u, gbc, bbc, yo)
        P.dma('sp', lambda e, iv: e.dma_start(out=x_out[bass.ts(iv[L], 128), :], in_=yo).then_inc(dso, 16))
        P.dma('sp', lambda e, iv: None, waits=[(dso, 16)])


QK_COLS = [0, 128, 256, 384, 512, 640,
           1152, 1280, 1408, 1536, 1664, 1792,
           2304, 2432, 2560, 2688, 2816, 2944]
V_COLS = [768, 1920, 3072]
G_COL = 3456


def stage_win(P, R, x_in, win_d, bgate_d, qkT_out, qm32_out, kmean_out, v_out, gates_out, nsb=16):
    ntiles = nsb * 4
    win = R.WBUF[:, 0:52224].rearrange("p (c n) -> p c n", c=8)
    xs = R.SCRF[:, 0:1024]
    gt = R.SCRF[:, 1024:4096]
    q32 = R.SCRF[:, 4096:4480].rearrange("p (r t) -> p r t", r=3)
    bgb = R.SCRF[:, 5632:8704]
    xT = R.SCRB[:, 0:4096].rearrange("p (c t) -> p c t", c=8)
    qkst = R.SCRB[:, 4096:6400].rearrange("p (r t) -> p r t", r=18)
    vst = R.SCRB[:, 13312:14464]
    km = R.stats[:, 0:3]
    wsem = new_sem(R, "w")
    dsx = new_sem(R, "dsx")
    dsq = new_sem(R, "dsq")

    def loadw(e, iv):
        for c in range(8):
            e.dma_start(out=win[:, c, :].rearrange("p (h n) -> p h n", h=4),
                        in_=win_d[c * 128:(c + 1) * 128, :].rearrange("p (h n) -> p h n", h=4)).then_inc(wsem, 16)
    P.dma('pool', loadw)
    P.dma('pool', lambda e, iv: None, waits=[(wsem, 16 * 8)])
    bsem = new_sem(R, "bg")
    P.dma('sp', lambda e, iv: e.dma_start(
        out=bgb, in_=bgate_d.rearrange("a d -> (a d)").partition_broadcast(128)).then_inc(bsem, 16))
    P.dma('sp', lambda e, iv: None, waits=[(bsem, 16)])

    with P.loop(ntiles, clear=(dsx, dsq)) as L:
        def loader(j):
            P.dma('act', lambda e, iv: e.dma_start(out=xs, in_=x_in[bass.ts(iv[L], 128), :]).then_inc(dsx, 16))
            P.dma('act', lambda e, iv: None, waits=[(dsx, 16)])
            return xs
        transposes_to_xT(P, R, loader, xT, nj=1)
        for r in range(18):
            co = QK_COLS[r]
            for c in range(8):
                P.op('pe', lambda e, iv, c=c, co=co: e.matmul(R.ps[1][:, 0:128], lhsT=win[:, c, co:co + 128], rhs=xT[:, c, 0:128],
                                                              start=(c == 0), stop=(c == 7)), chain=(c > 0))
            P.op('act', lambda e, iv, r=r: e.copy(out=qkst[:, r, :], in_=R.ps[1][:, 0:128]))
            if r < 3:
                P.op('dve', lambda e, iv, r=r: e.tensor_copy(out=q32[:, r, :], in_=R.ps[1][:, 0:128]))
            if 3 <= r < 6:
                P.op('dve', lambda e, iv, r=r: e.tensor_reduce(out=km[:, r - 3:r - 2], in_=R.ps[1][:, 0:128], axis=AX.X, op=ALU.add))

        def stq(e, iv):
            e.dma_start(out=qkT_out[:, bass.ts(iv[L], 128)].rearrange("(r p) t -> p r t", p=128),
                        in_=qkst).then_inc(dsq, 16)
            e.dma_start(out=qm32_out[:, bass.ts(iv[L], 128)].rearrange("(r p) t -> p r t", p=128),
                        in_=q32).then_inc(dsq, 16)
            e.dma_start(out=kmean_out[:, bass.ts(iv[L], 1)].rearrange("(r p) b -> p r b", p=128),
                        in_=km.rearrange("p (r b) -> p r b", b=1), allow_slow_non_contiguous=True).then_inc(dsq, 16)
        P.dma('act', stq)
        for vi, co in enumerate(V_COLS):
            for c in range(8):
                P.op('pe', lambda e, iv, c=c, co=co: e.matmul(
                    R.ps[2][:, 0:384], lhsT=xT[:, c, 0:128], rhs=win[:, c, co:co + 384],
                    start=(c == 0), stop=(c == 7)), chain=(c > 0))
            P.op('act', lambda e, iv, vi=vi: e.copy(out=vst[:, vi * 384:(vi + 1) * 384], in_=R.ps[2][:, 0:384]))
        for gi in range(6):
            co = G_COL + gi * 512
            for c in range(8):
                P.op('pe', lambda e, iv, c=c, co=co: e.matmul(
                    R.ps[3][:], lhsT=xT[:, c, 0:128], rhs=win[:, c, co:co + 512],
                    start=(c == 0), stop=(c == 7)), chain=(c > 0))
            P.op('dve', lambda e, iv, gi=gi: e.tensor_tensor(out=gt[:, gi * 512:(gi + 1) * 512], in0=R.ps[3][:],
                                                             in1=bgb[:, gi * 512:(gi + 1) * 512], op=ALU.add))
        P.op('act', lambda e, iv: e.activation(out=gt, in_=gt, func=AF.Sigmoid))
        P.dma('act', lambda e, iv: e.dma_start(out=v_out[bass.ts(iv[L], 128), :], in_=vst).then_inc(dsq, 16))
        P.dma('act', lambda e, iv: e.dma_start(out=gates_out[bass.ts(iv[L], 128), :], in_=gt).then_inc(dsq, 16))
        P.dma('act', lambda e, iv: None, waits=[(dsq, 80)])


BIG = 30000.0
SLOPES = (2.0 ** (-8.0 * np.arange(1, 13, dtype=np.float32) / 12)).astype(np.float32)
SC_M = 64 ** -0.5
SC_D = 32 ** -0.5
SC_S = 64 ** -0.5
SUBLN_EPS = 1e-5


def att_tables():
    bf = ml_dtypes.bfloat16
    i = np.arange(SEQ)
    ib = (i // 128).astype(np.float32)
    il = (i % 128).astype(np.float32)
    one = np.ones(SEQ, np.float32)

    def hl(v):
        hi = np.float32(np.asarray(v, np.float32).astype(bf).astype(np.float32))
        lo = np.float32(np.asarray(v - hi, np.float32).astype(bf).astype(np.float32))
        return hi, lo
    qaug = np.zeros((12, 8, SEQ), np.float32)
    kaug = np.zeros((12, 8, SEQ), np.float32)
    for slot in range(12):
        scale = SC_M if slot % 2 == 0 else SC_D
        sig = float(SLOPES[slot]) / scale
        Sh, Sl = hl(128.0 * sig)
        sh, sl = hl(sig)
        qaug[slot] = np.stack([Sh * one, Sl * one, sh * one, sl * one, -ib, -ib, -il, -il])
        kaug[slot] = np.stack([ib, ib, il, il, Sh * one, Sl * one, sh * one, sl * one])
    onehot = (np.arange(32)[:, None] == (i // 256)[None, :]).astype(np.float32)
    p = np.arange(128)[:, None]
    c = np.arange(512)[None, :]
    maskT = np.concatenate([np.where((r * 128 + p) > c, -BIG, 0.0) for r in range(4)], axis=1)
    maskS = np.concatenate([np.where(c >= (r * 128 + p), -BIG, 0.0) for r in range(4)], axis=1)
    o = np.arange(32)[:, None]
    n = np.arange(32)[None, :]
    pb = np.stack([np.where(n < o, 0.0, -BIG), np.where(n == o, 0.0, -BIG), np.where(n <= o, 0.0, -BIG)]).astype(np.float32)
    return dict(qaug=qaug.reshape(96, SEQ).astype(bf), kaug=kaug.reshape(96, SEQ).astype(bf), onehot=onehot.astype(bf),
                maskT=maskT.astype(bf), maskS=maskS.astype(bf), pb=pb.reshape(-1))


def stage_att(P, R, T, qkT_d, qm32_d, kmean_d, v_d, o_d, lam_d, subg_d, one_m_linit, nheads=6, do=('m', 'd', 's')):
    W = R.WBUF
    QA = [W[:, 0:8192], W[:, 8192:16384]]
    KA = [W[:, 16384:24576], W[:, 24576:32768]]
    VA = W[:, 32768:40960].rearrange("p (k d) -> p k d", k=64)
    VAf = W[:, 32768:40960]
    maskT = W[:, 40960:43008]
    maskS = W[:, 43008:45056]
    pt = [W[:, 45056:45568], W[:, 45568:46080]]
    wb = W[:, 46080:46592]
    wT = W[:, 46592:47104].rearrange("p (k q) -> p k q", k=4)
    FW = W[:, 47360:67584].bitcast(F32)
    pbt = FW[:, 0:3072].rearrange("p (a o n) -> p a o n", a=3, o=32)
    osb = [FW[:, 3072:3584], FW[:, 3584:4096]]
    gm = FW[:, 4096:4128]
    sel = FW[:, 4128:4160]
    m8 = FW[:, 4160:4168]
    rinv = [FW[:, 4168:4172], FW[:, 4172:4176]]
    ss = FW[:, 4176:4180]
    carry = FW[:, 4180:4181]
    negc = FW[:, 4181:4182]
    tot = FW[:, 4182:4183]
    lamt = FW[:, 4183:4184]
    lsc = FW[:, 4184:4186]
    t1 = FW[:, 4192:4256]
    od = FW[:, 4256:4512].rearrange("p (j d) -> p j d", j=4)
    gsc = FW[:, 4512:4576]
    lamw = FW[:, 4576:4704]
    lamw2 = FW[:, 4704:4768]
    ones = FW[:, 4768:5280]
    tt = FW[:, 5280:5792]
    spb = FW[:, 5792:6305]
    E1 = FW[:, 6336:6848]
    aa = FW[:, 6848:7360]
    kme = FW[:, 7360:7392]
    kmr = FW[:, 9472:9536]
    ost_all = FW[:, 7424:9472].bitcast(BF16).rearrange("p (t d) -> p t d", t=64)
    qm32 = R.SCRF[:, 0:8192]
    ps_s = [R.ps[0], R.ps[1]]
    ps_o = [R.ps[2], R.ps[3]]
    ps_t = [R.ps[4], R.ps[5]]
    psb = R.psb[0]
    tsem = new_sem(R, "att_t")
    hsem = new_sem(R, "att_h")
    osem = new_sem(R, "att_o")
    cnt = {'h': 0, 'o': 0}

    def ldc(e, iv):
        e.dma_start(out=maskT, in_=T['maskT'][:, :]).then_inc(tsem, 16)
        e.dma_start(out=maskS, in_=T['maskS'][:, :]).then_inc(tsem, 16)
        e.dma_start(out=FW[:, 0:3072], in_=T['pb'].partition_broadcast(128)).then_inc(tsem, 16)
        e.dma_start(out=gsc, in_=subg_d.partition_broadcast(128)).then_inc(tsem, 16)
        e.dma_start(out=lamw, in_=lam_d.rearrange("a d -> (a d)").partition_broadcast(128)).then_inc(tsem, 16)
    P.dma('sp', ldc)
    P.dma('sp', lambda e, iv: None, waits=[(tsem, 80)])
    P.op('dve', lambda e, iv: e.memset(ones, 1.0))
    P.op('dve', lambda e, iv: e.memset(spb[:, 0:1], 0.0))
    P.op('dve', lambda e, iv: e.memset(VA[:, :, 64:65], 1.0))
    linit = 1.0 - one_m_linit
    P.op('dve', lambda e, iv: e.tensor_tensor(out=lamw2[:, 0:32], in0=lamw[:, 0:32], in1=lamw[:, 32:64], op=ALU.mult))
    P.op('dve', lambda e, iv: e.tensor_tensor(out=lamw2[:, 32:64], in0=lamw[:, 64:96], in1=lamw[:, 96:128], op=ALU.mult))
    P.op('dve', lambda e, iv: e.tensor_reduce(out=lsc, in_=lamw2.rearrange("p (a d) -> p a d", a=2), axis=AX.X, op=ALU.add))
    P.op('act', lambda e, iv: e.activation(out=lsc, in_=lsc, func=AF.Exp))
    P.op('dve', lambda e, iv: e.tensor_tensor(out=lamt, in0=lsc[:, 0:1], in1=lsc[:, 1:2], op=ALU.subtract))
    P.op('dve', lambda e, iv: e.tensor_scalar(out=lamt, in0=lamt, scalar1=linit, scalar2=-1.0, op0=ALU.add, op1=ALU.mult))
    P.op('act', lambda e, iv: e.mul(out=gsc, in_=gsc, mul=one_m_linit))

    def dense_tile(s, Kd, scale, kb_ap_K, kb_ap_V, Q, mask_r, first):
        P.op('pe', lambda e, iv: e.matmul(ps_s[s][:], lhsT=kb_ap_K(iv), rhs=QA[s][0:Kd, Q * 512:(Q + 1) * 512],
                                          start=True, stop=(mask_r is None)))
        if mask_r is not None:
            P.op('pe', lambda e, iv: e.matmul(ps_s[s][:], lhsT=R.identb[:], rhs=maskT[:, mask_r * 512:(mask_r + 1) * 512],
                                              start=False, stop=True), chain=True)
        P.op('act', lambda e, iv: e.activation(out=pt[s], in_=ps_s[s][:], func=AF.Exp, scale=scale))
        P.op('pe', lambda e, iv: e.matmul(ps_o[s][0:65, :], lhsT=kb_ap_V(iv), rhs=pt[s], start=first, stop=True))

    def dense_Q(streams, Q):
        for r in range(4):
            kb = 4 * Q + r
            for (s, Kd, scale) in streams:
                dense_tile(s, Kd, scale, lambda iv, s=s, Kd=Kd, kb=kb: KA[s][0:Kd, kb * 128:(kb + 1) * 128],
                           lambda iv, kb=kb: VA[:, kb, 0:65], Q, r, r == 0)
        for kb in range(4 * Q):
            for (s, Kd, scale) in streams:
                dense_tile(s, Kd, scale, lambda iv, s=s, Kd=Kd, kb=kb: KA[s][0:Kd, kb * 128:(kb + 1) * 128],
                           lambda iv, kb=kb: VA[:, kb, 0:65], Q, None, False)
        for (s, Kd, scale) in streams:
            P.op('act', lambda e, iv, s=s: e.copy(out=osb[s][0:65, :], in_=ps_o[s][0:65, :]))
            for j in range(4):
                P.op('pe', lambda e, iv, s=s, j=j: e.transpose(out=ps_t[s][:, j * 65:(j + 1) * 65],
                                                               in_=osb[s][0:65, j * 128:(j + 1) * 128],
                                                               identity=R.identf[0:65, 0:65]), chain=(j > 0))
            P.op('dve', lambda e, iv, s=s: e.reciprocal(
                out=rinv[s], in_=ps_t[s][:, 0:260].rearrange("p (j d) -> p j d", j=4)[:, :, 64]))

    def store_head(hfn, dq):
        P.dma(dq, lambda e, iv: e.dma_start(
            out=o_d[bass.ts(hfn(iv), 1), :, :].rearrange("h (t p) d -> p (h t) d", p=128),
            in_=ost_all).then_inc(osem, 16))
        P.dma(dq, lambda e, iv: None, waits=[(osem, 16)])

    NTD = 3
    Q_ASSIGN = [[15, 10, 9, 4, 3], [14, 11, 8, 5, 2], [13, 12, 7, 6, 1, 0]]
    SB16 = R.SCRB
    d_pt = [SB16[:, 2624 * i:2624 * i + 512] for i in range(NTD)]
    d_f = [SB16[:, 2624 * i + 512:2624 * i + 2624].bitcast(F32) for i in range(NTD)]
    d_osb = [d_f[i][:, 0:512] for i in range(NTD)]
    d_t1 = [d_f[i][:, 512:768].rearrange("p (j d) -> p j d", j=4) for i in range(NTD)]
    d_od = [d_f[i][:, 768:1024].rearrange("p (j d) -> p j d", j=4) for i in range(NTD)]
    d_r0 = [d_f[i][:, 1024:1028] for i in range(NTD)]
    d_r1 = [d_f[i][:, 1028:1032] for i in range(NTD)]
    d_ss = [d_f[i][:, 1032:1036] for i in range(NTD)]
    d_sq = [d_f[i][:, 1036:1040] for i in range(NTD)]
    d_pss = [R.ps[0], R.ps[1], R.ps[2]]
    d_pso = [R.ps[3], R.ps[4], R.ps[5]]

    def dense_map_ops(th, m, Kd, scale, Q):
        ops = []
        kbs = [(4 * Q + r, r) for r in range(4)] + [(kb, None) for kb in range(4 * Q)]
        for idx, (kb, mr) in enumerate(kbs):
            ops.append(('pe', lambda e, iv, kb=kb, mr=mr: e.matmul(
                d_pss[th][:], lhsT=KA[m][0:Kd, kb * 128:(kb + 1) * 128], rhs=QA[m][0:Kd, Q * 512:(Q + 1) * 512],
                start=True, stop=(mr is None)), False))
            if mr is not None:
                ops.append(('pe', lambda e, iv, mr=mr: e.matmul(d_pss[th][:], lhsT=R.identb[:],
                                                                rhs=maskT[:, mr * 512:(mr + 1) * 512],
                                                                start=False, stop=True), True))
            ops.append(('act', lambda e, iv: e.activation(out=d_pt[th], in_=d_pss[th][:], func=AF.Exp, scale=scale), False))
            ops.append(('pe', lambda e, iv, kb=kb, idx=idx: e.matmul(d_pso[th][0:65, :], lhsT=VA[:, kb, 0:65], rhs=d_pt[th],
                                                                     start=(idx == 0), stop=True), False))
        ops.append(('act', lambda e, iv: e.copy(out=d_osb[th][0:65, :], in_=d_pso[th][0:65, :]), False))
        for j in range(4):
            ops.append(('pe', lambda e, iv, j=j: e.transpose(out=d_pss[th][:, j * 65:(j + 1) * 65],
                                                             in_=d_osb[th][0:65, j * 128:(j + 1) * 128],
                                                             identity=R.identf[0:65, 0:65]), j > 0))
        return ops

    def psT(th, j):
        return d_pss[th][:, j * 65:j * 65 + 64]

    def rsum(th):
        return d_pss[th][:, 0:260].rearrange("p (j d) -> p j d", j=4)[:, :, 64]

    def emit_threads(lists):
        for idx in range(max(len(x) for x in lists)):
            for th in range(len(lists)):
                if idx < len(lists[th]):
                    eng, fn, ch = lists[th][idx]
                    P.op(eng, fn, chain=ch, thread=th)

    if 'm' in do:
        with P.loop(nheads, clear=(hsem, osem)) as LHM:
            hbaseM = 0

            def ldm(e, iv):
                h = iv[LHM]
                e.dma_start(out=QA[0][32:96, :], in_=qkT_d[0:384, :][bass.ts(h, 64), :]).then_inc(hsem, 16)
                e.dma_start(out=KA[0][32:96, :], in_=qkT_d[384:768, :][bass.ts(h, 64), :]).then_inc(hsem, 16)
                e.dma_start(out=KA[0][0:32, :], in_=T['onehot'][:, :]).then_inc(hsem, 16)
                e.dma_start(out=QA[0][96:104, :], in_=T['qaug'][bass.ts(h * 2, 8), :]).then_inc(hsem, 16)
                e.dma_start(out=KA[0][96:104, :], in_=T['kaug'][bass.ts(h * 2, 8), :]).then_inc(hsem, 16)
                e.dma_start(out=VA[:, :, 0:64],
                            in_=v_d[:, 0:384][:, bass.ts(h, 64)].rearrange("(k p) d -> p k d", p=128)).then_inc(hsem, 16)
                e.dma_start(out=qm32[0:64, :], in_=qm32_d[bass.ts(h, 64), :]).then_inc(hsem, 16)
                e.dma_start(out=kmr[0:64, :], in_=kmean_d[bass.ts(h, 64), :]).then_inc(hsem, 16)
            P.dma('sp', ldm)
            P.dma('sp', lambda e, iv: None, waits=[(hsem, lambda iv: (hbaseM + (iv[LHM] + 1) * 8) * 16)])
            P.op('dve', lambda e, iv: e.tensor_reduce(out=kme[0:64, :], in_=kmr[0:64, :].rearrange("p (n b) -> p n b", b=2),
                                                      axis=AX.X, op=ALU.add))
            P.op('dve', lambda e, iv: e.tensor_scalar(out=kme[0:64, :], in0=kme[0:64, :], scalar1=1.0 / 256.0, scalar2=None,
                                                      op0=ALU.mult))
            for t4 in range(16):
                for tj in range(4):
                    t = t4 * 4 + tj
                    own = t // 2
                    P.op('pe', lambda e, iv, t=t: e.matmul(ps_s[1][:, 0:32], lhsT=qm32[0:64, t * 128:(t + 1) * 128],
                                                           rhs=kme[0:64, :], start=True, stop=True))
                    P.op('dve', lambda e, iv, own=own: e.tensor_tensor(out=gm, in0=ps_s[1][:, 0:32], in1=pbt[:, 0, own, :],
                                                                       op=ALU.add))
                    P.op('dve', lambda e, iv: e.max(out=m8, in_=gm))
                    P.op('dve', lambda e, iv: e.tensor_scalar(out=sel, in0=gm, scalar1=m8[:, 2:3], scalar2=1.0,
                                                              op0=ALU.is_ge, op1=ALU.subtract))
                    P.op('dve', lambda e, iv, own=own: e.scalar_tensor_tensor(out=sel, in0=sel, scalar=BIG,
                                                                              in1=pbt[:, 1, own, :], op0=ALU.mult, op1=ALU.max))
                    P.op('dve', lambda e, iv, own=own: e.tensor_tensor(out=sel, in0=sel, in1=pbt[:, 2, own, :], op=ALU.add))
                    P.op('pe', lambda e, iv, tj=tj: e.transpose(out=ps_t[0][0:32, tj * 128:(tj + 1) * 128], in_=sel,
                                                                identity=R.identf[:]))
                P.op('act', lambda e, iv, t4=t4: e.copy(out=QA[0][0:32, t4 * 512:(t4 + 1) * 512], in_=ps_t[0][0:32, :]))
            for th in range(1, NTD):
                P.join('pe', th, 0)
            lists = []
            for th in range(NTD):
                ops = []
                for Q in Q_ASSIGN[th]:
                    ops += dense_map_ops(th, 0, 104, SC_M, Q)
                    ops.append(('dve', (lambda e, iv, th=th: e.reciprocal(out=d_r0[th], in_=rsum(th))), False))
                    for j in range(4):
                        ops.append(('dve', (lambda e, iv, th=th, j=j, Q=Q: e.tensor_scalar(
                            out=ost_all[:, Q * 4 + j, :], in0=psT(th, j), scalar1=d_r0[th][:, j:j + 1], scalar2=None,
                            op0=ALU.mult)), False))
                lists.append(ops)
            emit_threads(lists)
            for th in range(1, NTD):
                P.join('sp', 0, th)
            store_head(lambda iv: iv[LHM], 'sp')
    obase_d = 0

    if 'd' in do:
        with P.loop(nheads, clear=(hsem, osem)) as LHD:
            hbaseD = 0

            def ldd(e, iv):
                h = iv[LHD]
                for m in range(2):
                    e.dma_start(out=QA[m][0:32, :], in_=qkT_d[768:1152, :][bass.ts(h * 2 + m, 32), :]).then_inc(hsem, 16)
                    e.dma_start(out=KA[m][0:32, :], in_=qkT_d[1152:1536, :][bass.ts(h * 2 + m, 32), :]).then_inc(hsem, 16)
                    e.dma_start(out=QA[m][32:40, :], in_=T['qaug'][bass.ts(h * 2 + 1, 8), :]).then_inc(hsem, 16)
                    e.dma_start(out=KA[m][32:40, :], in_=T['kaug'][bass.ts(h * 2 + 1, 8), :]).then_inc(hsem, 16)
                e.dma_start(out=VA[:, :, 0:64],
                            in_=v_d[:, 384:768][:, bass.ts(h, 64)].rearrange("(k p) d -> p k d", p=128)).then_inc(hsem, 16)
            P.dma('act', ldd)
            P.dma('act', lambda e, iv: None, waits=[(hsem, lambda iv: (hbaseD + (iv[LHD] + 1) * 9) * 16)])
            for th in range(1, NTD):
                P.join('pe', th, 0)
            lists = []
            for th in range(NTD):
                ops = []
                for Q in Q_ASSIGN[th]:
                    ops += dense_map_ops(th, 0, 40, SC_D, Q)
                    ops.append(('dve', (lambda e, iv, th=th: e.reciprocal(out=d_r0[th], in_=rsum(th))), False))
                    for j in range(4):
                        ops.append(('dve', (lambda e, iv, th=th, j=j: e.tensor_scalar(
                            out=d_t1[th][:, j, :], in0=psT(th, j), scalar1=d_r0[th][:, j:j + 1], scalar2=None,
                            op0=ALU.mult)), False))
                    ops += dense_map_ops(th, 1, 40, SC_D, Q)
                    ops.append(('dve', (lambda e, iv, th=th: e.reciprocal(out=d_r1[th], in_=rsum(th))), False))
                    ops.append(('dve', (lambda e, iv, th=th: e.tensor_scalar(out=d_r1[th], in0=d_r1[th], scalar1=lamt[:, 0:1],
                                                                             scalar2=None, op0=ALU.mult)), False))
                    for j in range(4):
                        ops.append(('dve', (lambda e, iv, th=th, j=j: e.scalar_tensor_tensor(
                            out=d_od[th][:, j, :], in0=psT(th, j), scalar=d_r1[th][:, j:j + 1], in1=d_t1[th][:, j, :],
                            op0=ALU.mult, op1=ALU.add)), False))
                    ops.append(('dve', (lambda e, iv, th=th: e.tensor_tensor(out=d_t1[th], in0=d_od[th], in1=d_od[th],
                                                                             op=ALU.mult)), False))
                    ops.append(('dve', (lambda e, iv, th=th: e.tensor_reduce(out=d_ss[th], in_=d_t1[th], axis=AX.X,
                                                                             op=ALU.add)), False))
                    ops.append(('dve', (lambda e, iv, th=th: e.tensor_scalar(out=d_ss[th], in0=d_ss[th], scalar1=1.0 / 64.0,
                                                                             scalar2=SUBLN_EPS, op0=ALU.mult, op1=ALU.add)), False))
                    ops.append(('act', (lambda e, iv, th=th: e.activation(out=d_ss[th], in_=d_ss[th], func=AF.Sqrt)), False))
                    ops.append(('dve', (lambda e, iv, th=th: e.reciprocal(out=d_ss[th], in_=d_ss[th])), False))
                    for j in range(4):
                        ops.append(('dve', (lambda e, iv, th=th, j=j, Q=Q: e.scalar_tensor_tensor(
                            out=ost_all[:, Q * 4 + j, :], in0=d_od[th][:, j, :], scalar=d_ss[th][:, j:j + 1], in1=gsc,
                            op0=ALU.mult, op1=ALU.mult)), False))
                lists.append(ops)
            emit_threads(lists)
            for th in range(1, NTD):
                P.join('sp', 0, th)
            store_head(lambda iv: iv[LHD] + 6, 'act')
    obase_s = 0

    if 's' in do:
        ones13 = FW[:, 4768:5281]
        SF = R.SCRF
        tts = [FW[:, 5312:5824]] + [SF[:, 2176 * i:2176 * i + 512] for i in range(3)]
        spbs = [FW[:, 5824:6337]] + [SF[:, 2176 * i + 512:2176 * i + 1025] for i in range(3)]
        Efs = [FW[:, 6368:6881]] + [SF[:, 2176 * i + 1056:2176 * i + 1569] for i in range(3)]
        aas = [FW[:, 6912:7424]] + [SF[:, 2176 * i + 1600:2176 * i + 2112] for i in range(3)]
        negcs = [FW[:, 4181:4182]] + [SF[:, 2176 * i + 2112:2176 * i + 2113] for i in range(3)]
        wbs_ = [wb] + [R.SCRB[:, 1024 * i:1024 * i + 512] for i in range(3)]
        wTs = [wT] + [R.SCRB[:, 1024 * i + 512:1024 * i + 1024].rearrange("p (k q) -> p k q", k=4) for i in range(3)]
        psbs = [R.psb[0][:, 0:512], R.psb[0][:, 512:1024], R.psb[1][:, 0:512], R.psb[1][:, 512:1024]]
        pszs = [R.ps[0], R.ps[1], R.ps[2]]
        psos = [R.ps[3][:, 0:64], R.ps[4][:, 0:64], R.ps[5][:, 0:64]]
        NTS = 3
        P.op('dve', lambda e, iv: e.memset(ones13, 1.0))
        for th in range(NTS):
            P.op('dve', lambda e, iv, th=th: e.memset(spbs[th][:, 0:1], 0.0))
        with P.loop(nheads, clear=(hsem, osem)) as LHS:
            def lds(e, iv):
                h = iv[LHS]
                e.dma_start(out=QA[0][0:64, :], in_=qkT_d[1536:1920, :][bass.ts(h, 64), :]).then_inc(hsem, 16)
                e.dma_start(out=KA[0][0:64, :], in_=qkT_d[1920:2304, :][bass.ts(h, 64), :]).then_inc(hsem, 16)
                e.dma_start(out=VA[:, :, 0:64],
                            in_=v_d[:, 768:1152][:, bass.ts(h, 64)].rearrange("(k p) d -> p k d", p=128)).then_inc(hsem, 16)
            P.dma('sp', lds)
            P.dma('sp', lambda e, iv: None, waits=[(hsem, 48)])
            for th in range(1, NTS):
                P.join('pe', th, 0)

            def sb_tile_ops(th, t, kt, mask_r, first):
                ops = []
                ops.append(('pe', lambda e, iv: e.matmul(pszs[th][:], lhsT=QA[0][0:64, t * 128:(t + 1) * 128],
                                                         rhs=KA[0][0:64, kt * 512:(kt + 1) * 512],
                                                         start=True, stop=(mask_r is None)), False))
                if mask_r is not None:
                    ops.append(('pe', lambda e, iv: e.matmul(pszs[th][:], lhsT=R.identb[:],
                                                             rhs=maskS[:, mask_r * 512:(mask_r + 1) * 512],
                                                             start=False, stop=True), True))
                ops.append(('act', lambda e, iv: e.activation(out=tts[th], in_=pszs[th][:], func=AF.Exp, scale=SC_S), False))
                ops.append(('act', lambda e, iv: e.activation(out=spbs[th][:, 1:513], in_=tts[th], func=AF.Ln, bias=1.0), False))
                ops.append(('dve', lambda e, iv: e.tensor_tensor_scan(out=Efs[th], data0=ones13, data1=spbs[th][:, 0:513],
                                                                      initial=0.0, op0=ALU.mult, op1=ALU.add), False))
                ops.append(('dve', lambda e, iv: e.scalar_tensor_tensor(out=aas[th], in0=pszs[th][:], scalar=SC_S,
                                                                        in1=Efs[th][:, 0:512], op0=ALU.mult, op1=ALU.add), False))
                if first:
                    ops.append(('dve', lambda e, iv: e.tensor_scalar(out=negcs[th], in0=Efs[th][:, 512:513], scalar1=-1.0,
                                                                     scalar2=None, op0=ALU.mult), False))
                else:
                    ops.append(('dve', lambda e, iv: e.tensor_tensor(out=negcs[th], in0=negcs[th], in1=Efs[th][:, 512:513],
                                                                     op=ALU.subtract), False))
                ops.append(('act', lambda e, iv: e.activation(out=wbs_[th], in_=aas[th], func=AF.Exp, bias=negcs[th][:, 0:1]), False))
                for k4 in range(4):
                    ops.append(('pe', lambda e, iv, k4=k4: e.transpose(out=psbs[th][:, k4 * 128:(k4 + 1) * 128],
                                                                       in_=wbs_[th][:, k4 * 128:(k4 + 1) * 128],
                                                                       identity=R.identb[:]), k4 > 0))
                ops.append(('act', lambda e, iv: e.copy(out=wTs[th], in_=psbs[th].rearrange("p (k q) -> p k q", k=4)), False))
                for k4 in range(4):
                    ops.append(('pe', lambda e, iv, k4=k4: e.matmul(psos[th], lhsT=wTs[th][:, k4, :],
                                                                    rhs=VA[:, kt * 4 + k4, 0:64],
                                                                    start=(first and k4 == 0), stop=True), k4 > 0))
                return ops

            for base in range(0, 64, NTS):
                lists = []
                for th in range(NTS):
                    t = base + th
                    if t >= 64:
                        lists.append([])
                        continue
                    ktd = t // 4
                    ops = sb_tile_ops(th, t, ktd, t % 4, True)
                    for kt in range(ktd - 1, -1, -1):
                        ops += sb_tile_ops(th, t, kt, None, False)
                    ops.append(('act', (lambda e, iv, t=t, th=th: e.copy(out=ost_all[:, t, :], in_=psos[th])), False))
                    lists.append(ops)
                for idx in range(max(len(x) for x in lists)):
                    for th in range(NTS):
                        if idx < len(lists[th]):
                            eng, fn, ch = lists[th][idx]
                            P.op(eng, fn, chain=ch, thread=th)
            for th in range(1, NTS):
                P.join('sp', 0, th)
            store_head(lambda iv: iv[LHS] + 12, 'sp')
def stage_mixpost(P, R, x_in, o_d, gates_d, x_out, wbm_d, wbd_d, wbs_d, wout_d, g_d, b_d, ntiles=64):
    W = R.WBUF
    wbr = W[:, 0:9216].rearrange("p (k n) -> p k n", k=9)
    wout = W[:, 9216:17408].rearrange("p (c n) -> p c n", c=8)
    gts = W[:, 17408:23552].bitcast(F32)
    xs = R.SCRF[:, 0:1024]
    u = R.SCRF[:, 1024:2048]
    xa = R.SCRF[:, 2048:3072]
    yo = R.SCRF[:, 3072:4096]
    gbc = R.SCRF[:, 4608:5632]
    bbc = R.SCRF[:, 5632:6656]
    merged = R.SCRF[:, 6656:7680]
    tmp = R.SCRF[:, 7680:8704]
    xT = R.SCRB[:, 0:4096].rearrange("p (c t) -> p c t", c=8)
    ot = R.SCRB[:, 4096:5248]
    oT = R.SCRB[:, 5248:6400].rearrange("p (k t) -> p k t", k=9)
    wsem = new_sem(R, "w")
    dsi = new_sem(R, "dsi")
    dso = new_sem(R, "dso")

    def loadw(e, iv):
        for bi, wd_ in enumerate((wbm_d, wbd_d, wbs_d)):
            for i in range(3):
                e.dma_start(out=wbr[:, bi * 3 + i, :], in_=wd_[i * 128:(i + 1) * 128, :]).then_inc(wsem, 16)
        for c in range(8):
            e.dma_start(out=wout[:, c, :], in_=wout_d[c * 128:(c + 1) * 128, :]).then_inc(wsem, 16)
    P.dma('pool', loadw)
    P.dma('pool', lambda e, iv: None, waits=[(wsem, 16 * 17)])
    load_ln_params(P, R, g_d, b_d, gbc, bbc)
    with P.loop(ntiles, clear=(dsi, dso)) as L:
        def ld(e, iv):
            e.dma_start(out=gts, in_=gates_d[bass.ts(iv[L], 128), :]).then_inc(dsi, 16)
            e.dma_start(out=xs, in_=x_in[bass.ts(iv[L], 128), :]).then_inc(dsi, 16)
        P.dma('sp', lambda e, iv: e.dma_start(out=ot.rearrange("p (h d) -> p h d", h=18),
                                              in_=o_d.rearrange("h s d -> s h d")[bass.ts(iv[L], 128), :, :]).then_inc(dsi, 16))
        P.dma('act', ld)
        P.dma('act', lambda e, iv: None, waits=[(dsi, lambda iv: (iv[L] + 1) * 48)])
        for k in range(9):
            P.op('pe', lambda e, iv, k=k: e.transpose(
                out=(R.psb[0][:, k * 128:(k + 1) * 128] if k < 8 else R.psb[1][:, 0:128]),
                in_=ot[:, k * 128:(k + 1) * 128], identity=R.identb[:]), chain=(k > 0))
        P.op('act', lambda e, iv: e.copy(out=oT[:, 0:8, :], in_=R.psb[0][:, :].rearrange("p (k t) -> p k t", k=8)))
        P.op('act', lambda e, iv: e.copy(out=oT[:, 8, :], in_=R.psb[1][:, 0:128]))
        for br in range(3):
            for h in range(2):
                for i in range(3):
                    P.op('pe', lambda e, iv, br=br, h=h, i=i: e.matmul(
                        R.ps[1][:], lhsT=oT[:, br * 3 + i, :], rhs=wbr[:, br * 3 + i, h * 512:(h + 1) * 512],
                        start=(i == 0), stop=(i == 2)), chain=(i > 0))
                gsl = gts[:, br * 1024 + h * 512: br * 1024 + (h + 1) * 512]
                if br == 0:
                    P.op('dve', lambda e, iv, h=h, gsl=gsl: e.tensor_tensor(out=merged[:, h * 512:(h + 1) * 512], in0=gsl,
                                                                            in1=R.ps[1][:], op=ALU.mult))
                else:
                    P.op('dve', lambda e, iv, h=h, gsl=gsl: e.tensor_tensor(out=tmp[:, h * 512:(h + 1) * 512], in0=gsl,
                                                                            in1=R.ps[1][:], op=ALU.mult))
                    P.op('dve', lambda e, iv, h=h: e.tensor_tensor(out=merged[:, h * 512:(h + 1) * 512],
                                                                   in0=merged[:, h * 512:(h + 1) * 512],
                                                                   in1=tmp[:, h * 512:(h + 1) * 512], op=ALU.add))
        transposes_to_xT(P, R, lambda j: merged, xT, nj=1)
        first = True
        for h in range(2):
            for c in range(8):
                P.op('pe', lambda e, iv, h=h, c=c: e.matmul(R.ps[3 + h][:], lhsT=xT[:, c, 0:128],
                                                            rhs=wout[:, c, h * 512:(h + 1) * 512],
                                                            start=(c == 0), stop=(c == 7)), chain=not first)
                first = False
        P.op('act', lambda e, iv: e.mul(out=xa, in_=xs, mul=ALPHA))
        for h in range(2):
            P.op('dve', lambda e, iv, h=h: e.tensor_tensor(out=u[:, h * 512:(h + 1) * 512], in0=xa[:, h * 512:(h + 1) * 512],
                                                           in1=R.ps[3 + h][:], op=ALU.add))
        ln_ops(P, R, u, gbc, bbc, yo, pre_waits=[(dso, lambda iv: iv[L] * 16)])
        P.dma('act', lambda e, iv: e.dma_start(out=x_out[bass.ts(iv[L], 128), :], in_=yo).then_inc(dso, 16))
        P.dma('act', lambda e, iv: None, waits=[(dso, 16)])


NACT = 4


def build_layer(tb, l):
    bf = ml_dtypes.bfloat16
    nc = bass.Bass("TRN2", target_bir_lowering=False)

    def dt(n, s, d=F32, k="ExternalInput"):
        return nc.dram_tensor(n, list(s), d, kind=k).ap()
    x = dt("x", [SEQ, D])
    ln_g = dt("ln_g", [3, D])
    ln_b = dt("ln_b", [3, D])
    wg = dt("ffn_w_gate", [2, D, DFF])
    wu = dt("ffn_w_up", [2, D, DFF])
    wd = dt("ffn_w_down", [2, DFF, D])
    win = dt("w_in", [D, NIN])
    bg = dt("b_gate", [3, D])
    dl = dt("diff_lambda", [4, 32])
    dg = dt("diff_subln_g", [64])
    wbm = dt("w_br_moba", [384, D])
    wbd = dt("w_br_diff", [384, D])
    wbs = dt("w_br_sb", [384, D])
    wo = dt("w_out", [D, D])
    idf = dt("idf", [128, 128])
    idb = dt("idb", [128, 128], BF16)
    T = {k: dt("t_" + k, a.shape, BF16 if a.dtype == bf else F32) for k, a in tb.items()}
    y = dt("y", [SEQ, D], F32, "ExternalOutput")
    xa_d = dt("xa_d", [SEQ, D], F32, "Internal")
    xb_d = dt("xb_d", [SEQ, D], F32, "Internal")
    qkT = dt("qkT_d", [2304, SEQ], BF16, "Internal")
    qm32 = dt("qm32_d", [384, SEQ], F32, "Internal")
    kmean = dt("kmean_d", [384, 64], F32, "Internal")
    v = dt("v_d", [SEQ, 1152], BF16, "Internal")
    gates = dt("gates_d", [SEQ, 3072], F32, "Internal")
    o = dt("o_d", [18, SEQ, 64], BF16, "Internal")
    with ExitStack() as ctx:
        R = rl_alloc(nc, ctx)
        P = Prog()
        rl_consts(P, R, idf, idb)
        linit = 0.8 - 0.6 * math.exp(-0.3 * l)
        stage_ffnln(P, R, x, xa_d, wg[0], wu[0], wd[0], ln_g[0], ln_b[0], nsb=16)
        stage_win(P, R, xa_d, win, bg, qkT, qm32, kmean, v, gates, nsb=16)
        stage_att(P, R, T, qkT, qm32, kmean, v, o, dl, dg, 1.0 - linit)
        stage_mixpost(P, R, xa_d, o, gates, xb_d, wbm, wbd, wbs, wo, ln_g[1], ln_b[1])
        stage_ffnln(P, R, xb_d, y, wg[1], wu[1], wd[1], ln_g[2], ln_b[2], nsb=16)
        run_prog(nc, P, R.G)
    return nc


def kernel(x, ln_g, ln_b, ffn_w_gate, ffn_w_up, ffn_w_down, w_in, b_gate, diff_lambda, diff_subln_g,
           w_br_moba, w_br_diff, w_br_sb, w_out):
    bf = ml_dtypes.bfloat16
    tb = att_tables()
    f = lambda a: np.ascontiguousarray(np.asarray(a, dtype=np.float32))
    cur = f(x)
    for l in range(2):
        nc = build_layer(tb, l)
        shared = dict(ln_g=f(ln_g[l]), ln_b=f(ln_b[l]), ffn_w_gate=f(ffn_w_gate[l]), ffn_w_up=f(ffn_w_up[l]),
                      ffn_w_down=f(ffn_w_down[l]), w_in=f(w_in[l]), b_gate=f(b_gate[l]), diff_lambda=f(diff_lambda[l]),
                      diff_subln_g=f(diff_subln_g[l]), w_br_moba=f(w_br_moba[l]), w_br_diff=f(w_br_diff[l]),
                      w_br_sb=f(w_br_sb[l]), w_out=f(w_out[l]),
                      idf=np.eye(128, dtype=np.float32), idb=np.eye(128).astype(bf))
        for k, a in tb.items():
            shared["t_" + k] = a
        in_maps = [dict(shared, x=cur[b]) for b in range(NACT)]
        res = run_bass_kernel_spmd(nc, in_maps, core_ids=list(range(NACT)))
        cur = np.stack([np.asarray(r["y"], dtype=np.float32) for r in res.results], axis=0)
    return cur
```

```python
import math
from contextlib import contextmanager, ExitStack
import numpy as np
import ml_dtypes
import concourse.bass as bass
import concourse.mybir as mybir
from concourse.bass_utils import run_bass_kernel_spmd

F32 = mybir.dt.float32
BF16 = mybir.dt.bfloat16
AF = mybir.ActivationFunctionType
ALU = mybir.AluOpType
AX = mybir.AxisListType

D = 1024
DFF = 2816
NF = DFF // 128
NTOK = 4096
SEQ = 8192
NIN = 6528
ALPHA = 4.0 ** 0.25
LN_EPS = 1e-5
NCORES = 8


@contextmanager
def my_fori(nc, e, regs, start, end):
    loop_id = nc.next_id()
    name = f"myfori_{loop_id}"
    ls, le = name + "_loop", name + "_end"
    engines = bass.OrderedEngineSet([e.engine])
    nc.regs_mov(regs, start)
    nc.br(ls, engines=engines)
    with nc.body(ls, valid_engines=engines):
        yield nc.snap(regs, min_val=start, max_val=end - 1)
        nc.regs_alu(regs, regs, 1, op=mybir.AluOpType.add)
        nc.br_lt(regs, end, on_true=ls, on_false=le, engines=engines)
    nc.switch_bb(le)


class Prog:
    def __init__(self):
        self.root = []
        self.stack = [self.root]
        self.nloops = 0

    def op(self, eng, fn, chain=False, waits=(), thread=0):
        self.stack[-1].append(['op', eng, fn, chain, tuple(waits), 'c', thread])

    def dma(self, eng, fn, waits=(), thread=0):
        self.stack[-1].append(['op', eng, fn, False, tuple(waits), 'd', thread])

    def join(self, eng, thread, other):
        self.stack[-1].append(['op', eng, None, False, (), 'j', thread, other])

    @contextmanager
    def loop(self, n, clear=()):
        body = []
        lid = self.nloops
        self.nloops += 1
        self.stack[-1].append(['loop', n, body, lid, tuple(clear)])
        self.stack.append(body)
        try:
            yield lid
        finally:
            self.stack.pop()

    @staticmethod
    def _size(node):
        if node[0] == 'op':
            return 1
        return node[1] * sum(Prog._size(b) for b in node[2])

    @staticmethod
    def _has(node, ename):
        if node[0] == 'op':
            return node[1] == ename
        return any(Prog._has(b, ename) for b in node[2])

    def total(self):
        return sum(self._size(n) for n in self.root)

    def emit(self, ename, e, G, semfn, nc, GX):
        k = 0
        lregs = nc.alloc_registers(f"lr_{ename}", engines=bass.OrderedEngineSet([e.engine]))
        GS = (G,) + tuple(GX)
        NTH = len(GS)
        for node in self.root:
            if node[0] == 'op' or node[1] == 1:
                subs = [node] if node[0] == 'op' else node[2]
                iv = {} if node[0] == 'op' else {node[3]: 0}
                for sub in subs:
                    eng, fn, chain, waits, kind = sub[1:6]
                    assert kind != 'j' and sub[6] == 0
                    if eng == ename:
                        e.wait_ge(G, k)
                        for (sem, vf) in waits:
                            e.wait_ge(sem, vf(iv) if callable(vf) else vf)
                        if kind == 'c':
                            fn(e, iv).then_inc(G, 1)
                        else:
                            fn(e, iv)
                            e.sem_inc(G, 1)
                    k += 1
                continue
            _, n, body, lid, clr = node
            E, S2 = semfn(lid)
            NT = [sum(1 for sub in body if sub[6] == th) for th in range(NTH)]

            def release(first):
                if first:
                    e.wait_ge(G, k)
                else:
                    for th in range(NTH):
                        if NT[th]:
                            e.wait_ge(GS[th], NT[th])
                e.wait_ge(E, 5)
                for gs in GS:
                    e.sem_clear(gs)
                e.sem_clear(E)
                for cs in clr:
                    e.sem_clear(cs)
                e.sem_inc(S2, 1)
            e.sem_inc(E, 1)
            if ename == 'sp':
                release(True)
            with my_fori(nc, e, lregs, 1, n + 1) as i:
                e.wait_ge(S2, i)
                iv = {lid: i - 1}
                prev = [None] * NTH
                cnt = [0] * NTH
                for sub in body:
                    assert sub[0] == 'op'
                    eng, fn, chain, waits, kind, th = sub[1:7]
                    j = cnt[th]
                    if eng == ename:
                        if j > 0 and not (chain and prev[th] is not None and prev[th][1] == ename):
                            e.wait_ge(GS[th], j)
                        if kind == 'j':
                            e.wait_ge(GS[sub[7]], cnt[sub[7]])
                            e.sem_inc(GS[th], 1)
                        else:
                            for (sem, vf) in waits:
                                e.wait_ge(sem, vf({lid: 0}) if callable(vf) else vf)
                            if kind == 'c':
                                fn(e, iv).then_inc(GS[th], 1)
                            else:
                                fn(e, iv)
                                e.sem_inc(GS[th], 1)
                    prev[th] = sub
                    cnt[th] += 1
                e.sem_inc(E, 1)
                if ename == 'sp':
                    release(False)
            e.wait_ge(S2, n + 1)
            if ename == 'sp':
                e.sem_inc(G, k + 1)
            k += 1


def run_prog(nc, P, G):
    engs = {'pe': 'tensor', 'act': 'scalar', 'dve': 'vector', 'pool': 'gpsimd', 'sp': 'sync'}
    with ExitStack() as sctx:
        GX = [sctx.enter_context(nc.semaphore(f"GX{i}")) for i in range(3)]
        lsems = {}
        for node in P.root:
            if node[0] == 'loop' and node[1] > 1:
                lid = node[3]
                lsems[lid] = tuple(sctx.enter_context(nc.semaphore(f"L{lid}_{nm}")) for nm in ("E", "S"))
        with nc.Block() as block:
            for ename, bname in engs.items():
                def mk(ename=ename):
                    def f(e):
                        P.emit(ename, e, G, lambda lid: lsems[lid], nc, GX)
                    return f
                getattr(block, bname)(mk())


class RL:
    pass


def rl_alloc(nc, ctx):
    R = RL()
    R.nc = nc
    R.WBUF = ctx.enter_context(nc.sbuf_tensor("WBUF", [128, 67584], BF16))
    R.SCRF = ctx.enter_context(nc.sbuf_tensor("SCRF", [128, 8704], F32))
    R.SCRB = ctx.enter_context(nc.sbuf_tensor("SCRB", [128, 15360], BF16))
    R.identf = ctx.enter_context(nc.sbuf_tensor("identf", [128, 128], F32))
    R.identb = ctx.enter_context(nc.sbuf_tensor("identb", [128, 128], BF16))
    R.stats = ctx.enter_context(nc.sbuf_tensor("stats", [128, 12], F32))
    R.mv = ctx.enter_context(nc.sbuf_tensor("mv", [128, 2], F32))
    R.rstd = ctx.enter_context(nc.sbuf_tensor("rstd", [128, 1], F32))
    R.ps = [ctx.enter_context(nc.psum_tensor(f"ps{i}", [128, 512], F32)) for i in range(6)]
    R.psb = [ctx.enter_context(nc.psum_tensor(f"psb{i}", [128, 1024], BF16)) for i in range(2)]
    R.G = ctx.enter_context(nc.semaphore("G"))
    R.csem = ctx.enter_context(nc.semaphore("csem"))
    R.nsem = 0
    R.ctx = ctx
    return R


def new_sem(R, name):
    R.nsem += 1
    return R.ctx.enter_context(R.nc.semaphore(f"{name}_{R.nsem}"))


def rl_consts(P, R, identf_d, identb_d):
    def f(e, iv):
        e.dma_start(out=R.identf[:], in_=identf_d[:, :]).then_inc(R.csem, 16)
        e.dma_start(out=R.identb[:], in_=identb_d[:, :]).then_inc(R.csem, 16)
    P.dma('sp', f)
    P.dma('sp', lambda e, iv: None, waits=[(R.csem, 32)])


def ln_ops(P, R, u, gbc, bbc, yo, pre_waits=()):
    for h in range(2):
        P.op('dve', lambda e, iv, h=h: e.bn_stats(out=R.stats[:, h * 6:(h + 1) * 6], in_=u[:, h * 512:(h + 1) * 512]))
    P.op('dve', lambda e, iv: e.bn_aggr(out=R.mv[:], in_=R.stats[:]))
    P.op('dve', lambda e, iv: e.tensor_scalar(out=R.rstd[:], in0=R.mv[:, 1:2], scalar1=LN_EPS, scalar2=None, op0=ALU.add))
    P.op('act', lambda e, iv: e.activation(out=R.rstd[:], in_=R.rstd[:], func=AF.Sqrt))
    P.op('dve', lambda e, iv: e.reciprocal(out=R.rstd[:], in_=R.rstd[:]))
    P.op('dve', lambda e, iv: e.tensor_scalar(out=u, in0=u, scalar1=R.mv[:, 0:1], scalar2=R.rstd[:, 0:1],
                                              op0=ALU.subtract, op1=ALU.mult))
    P.op('pool', lambda e, iv: e.tensor_tensor(out=u, in0=u, in1=gbc, op=ALU.mult))
    P.op('pool', lambda e, iv: e.tensor_tensor(out=yo, in0=u, in1=bbc, op=ALU.add), waits=pre_waits)


def load_ln_params(P, R, g_d, b_d, gbc, bbc):
    sem = new_sem(R, "lnp")

    def f(e, iv):
        e.dma_start(out=gbc, in_=g_d.partition_broadcast(128)).then_inc(sem, 16)
        e.dma_start(out=bbc, in_=b_d.partition_broadcast(128)).then_inc(sem, 16)
    P.dma('sp', f)
    P.dma('sp', lambda e, iv: None, waits=[(sem, 32)])


def transposes_to_xT(P, R, xs4_loader, xT, nj=4):
    for j in range(nj):
        xs = xs4_loader(j)
        for c2 in range(2):
            for c in range(4):
                cc = c2 * 4 + c
                P.op('pe', lambda e, iv, cc=cc, c=c, xs=xs: e.transpose(
                    out=R.ps[0][:, c * 128:(c + 1) * 128], in_=xs[:, cc * 128:(cc + 1) * 128], identity=R.identf[:]),
                    chain=(c > 0))
            P.op('act', lambda e, iv, c2=c2, j=j: e.activation(
                out=xT[:, c2 * 4:(c2 + 1) * 4, j * 128:(j + 1) * 128],
                in_=R.ps[0][:, :].rearrange("p (c t) -> p c t", c=4), func=AF.Copy))


def stage_ffnln(P, R, x_in, x_out, wg_d, wu_d, wd_d, g_d, b_d, nsb=16):
    ntiles = nsb * 4
    wg = R.WBUF[:, 0:22528].rearrange("p (c f) -> p c f", c=8)
    wu = R.WBUF[:, 22528:45056].rearrange("p (c f) -> p c f", c=8)
    wd = R.WBUF[:, 45056:67584].rearrange("p (c n) -> p c n", c=NF)
    xs = R.SCRF[:, 0:1024]
    u = R.SCRF[:, 1024:2048]
    xa = R.SCRF[:, 2048:3072]
    yo = R.SCRF[:, 3072:4096]
    sg = R.SCRF[:, 4096:4608]
    gbc = R.SCRF[:, 4608:5632]
    bbc = R.SCRF[:, 5632:6656]
    xT = R.SCRB[:, 0:4096].rearrange("p (c t) -> p c t", c=8)
    aT = R.SCRB[:, 4096:15360].rearrange("p (f t) -> p f t", f=NF)
    wsem = new_sem(R, "w")
    dsx = new_sem(R, "dsx")
    dso = new_sem(R, "dso")

    def loadw(e, iv):
        for c in range(8):
            e.dma_start(out=wg[:, c, :].rearrange("p (h n) -> p h n", h=2),
                        in_=wg_d[c * 128:(c + 1) * 128, :].rearrange("p (h n) -> p h n", h=2)).then_inc(wsem, 16)
            e.dma_start(out=wu[:, c, :].rearrange("p (h n) -> p h n", h=2),
                        in_=wu_d[c * 128:(c + 1) * 128, :].rearrange("p (h n) -> p h n", h=2)).then_inc(wsem, 16)
        for f in range(NF):
            e.dma_start(out=wd[:, f, :], in_=wd_d[f * 128:(f + 1) * 128, :]).then_inc(wsem, 16)
    P.dma('pool', loadw)
    P.dma('pool', lambda e, iv: None, waits=[(wsem, 16 * (16 + NF))])
    load_ln_params(P, R, g_d, b_d, gbc, bbc)

    with P.loop(ntiles, clear=(dsx, dso)) as L:
        def loader(j):
            P.dma('sp', lambda e, iv: e.dma_start(out=xs, in_=x_in[bass.ts(iv[L], 128), :]).then_inc(dsx, 16))
            P.dma('sp', lambda e, iv: None, waits=[(dsx, 16)])
            return xs
        transposes_to_xT(P, R, loader, xT, nj=1)
        for g0 in range(0, NF, 4):
            gw = min(4, NF - g0)
            first = True
            for gi in range(gw):
                f = g0 + gi
                for c in range(8):
                    P.op('pe', lambda e, iv, f=f, c=c, gi=gi: e.matmul(
                        R.ps[1][:, gi * 128:(gi + 1) * 128], lhsT=wg[:, c, f * 128:(f + 1) * 128], rhs=xT[:, c, 0:128],
                        start=(c == 0), stop=(c == 7)), chain=not first)
                    first = False
            for gi in range(gw):
                f = g0 + gi
                for c in range(8):
                    P.op('pe', lambda e, iv, f=f, c=c, gi=gi: e.matmul(
                        R.ps[2][:, gi * 128:(gi + 1) * 128], lhsT=wu[:, c, f * 128:(f + 1) * 128], rhs=xT[:, c, 0:128],
                        start=(c == 0), stop=(c == 7)), chain=True)
            P.op('act', lambda e, iv, gw=gw: e.activation(out=sg[:, 0:gw * 128], in_=R.ps[1][:, 0:gw * 128], func=AF.Silu))
            P.op('dve', lambda e, iv, gw=gw, g0=g0: e.tensor_tensor(
                out=aT[:, g0:g0 + gw, 0:128], in0=sg[:, 0:gw * 128].rearrange("p (g t) -> p g t", g=gw),
                in1=R.ps[2][:, 0:gw * 128].rearrange("p (g t) -> p g t", g=gw), op=ALU.mult))
        first = True
        for h in range(2):
            for f in range(NF):
                P.op('pe', lambda e, iv, f=f, h=h: e.matmul(
                    R.ps[3 + h][:], lhsT=aT[:, f, 0:128], rhs=wd[:, f, h * 512:(h + 1) * 512],
                    start=(f == 0), stop=(f == NF - 1)), chain=not first)
                first = False
        P.op('act', lambda e, iv: e.mul(out=xa, in_=xs, mul=ALPHA))
        for h in range(2):
            P.op('dve', lambda e, iv, h=h: e.scalar_tensor_tensor(
                out=u[:, h * 512:(h + 1) * 512], in0=R.ps[3 + h][:], scalar=0.5, in1=xa[:, h * 512:(h + 1) * 512],
                op0=ALU.mult, op1=ALU.add))
        ln_ops(P, R, u, gbc, bbc, yo)
        P.dma('sp', lambda e, iv: e.dma_start(out=x_out[bass.ts(iv[L], 128), :], in_=yo).then_inc(dso, 16))
        P.dma('sp', lambda e, iv: None, waits=[(dso, 16)])


QK_COLS = [0, 128, 256, 384, 512, 640,
           1152, 1280, 1408, 1536, 1664, 1792,
           2304, 2432, 2560, 2688, 2816, 2944]
V_COLS = [768, 1920, 3072]
G_COL = 3456


def stage_win(P, R, x_in, win_d, bgate_d, qkT_out, qm32_out, kmean_out, v_out, gates_out, nsb=16):
    ntiles = nsb * 4
    win = R.WBUF[:, 0:52224].rearrange("p (c n) -> p c n", c=8)
    xs = R.SCRF[:, 0:1024]
    gt = R.SCRF[:, 1024:4096]
    q32 = R.SCRF[:, 4096:4480].rearrange("p (r t) -> p r t", r=3)
    bgb = R.SCRF[:, 5632:8704]
    xT = R.SCRB[:, 0:4096].rearrange("p (c t) -> p c t", c=8)
    qkst = R.SCRB[:, 4096:6400].rearrange("p (r t) -> p r t", r=18)
    vst = R.SCRB[:, 13312:14464]
    km = R.stats[:, 0:3]
    wsem = new_sem(R, "w")
    dsx = new_sem(R, "dsx")
    dsq = new_sem(R, "dsq")

    def loadw(e, iv):
        for c in range(8):
            e.dma_start(out=win[:, c, :].rearrange("p (h n) -> p h n", h=4),
                        in_=win_d[c * 128:(c + 1) * 128, :].rearrange("p (h n) -> p h n", h=4)).then_inc(wsem, 16)
    P.dma('pool', loadw)
    P.dma('pool', lambda e, iv: None, waits=[(wsem, 16 * 8)])
    bsem = new_sem(R, "bg")
    P.dma('sp', lambda e, iv: e.dma_start(
        out=bgb, in_=bgate_d.rearrange("a d -> (a d)").partition_broadcast(128)).then_inc(bsem, 16))
    P.dma('sp', lambda e, iv: None, waits=[(bsem, 16)])

    with P.loop(ntiles, clear=(dsx, dsq)) as L:
        def loader(j):
            P.dma('act', lambda e, iv: e.dma_start(out=xs, in_=x_in[bass.ts(iv[L], 128), :]).then_inc(dsx, 16))
            P.dma('act', lambda e, iv: None, waits=[(dsx, 16)])
            return xs
        transposes_to_xT(P, R, loader, xT, nj=1)
        for r in range(18):
            co = QK_COLS[r]
            for c in range(8):
                P.op('pe', lambda e, iv, c=c, co=co: e.matmul(R.ps[1][:, 0:128], lhsT=win[:, c, co:co + 128], rhs=xT[:, c, 0:128],
                                                              start=(c == 0), stop=(c == 7)), chain=(c > 0))
            P.op('act', lambda e, iv, r=r: e.copy(out=qkst[:, r, :], in_=R.ps[1][:, 0:128]))
            if r < 3:
                P.op('dve', lambda e, iv, r=r: e.tensor_copy(out=q32[:, r, :], in_=R.ps[1][:, 0:128]))
            if 3 <= r < 6:
                P.op('dve', lambda e, iv, r=r: e.tensor_reduce(out=km[:, r - 3:r - 2], in_=R.ps[1][:, 0:128], axis=AX.X, op=ALU.add))

        def stq(e, iv):
            e.dma_start(out=qkT_out[:, bass.ts(iv[L], 128)].rearrange("(r p) t -> p r t", p=128),
                        in_=qkst).then_inc(dsq, 16)
            e.dma_start(out=qm32_out[:, bass.ts(iv[L], 128)].rearrange("(r p) t -> p r t", p=128),
                        in_=q32).then_inc(dsq, 16)
            e.dma_start(out=kmean_out[:, bass.ts(iv[L], 1)].rearrange("(r p) b -> p r b", p=128),
                        in_=km.rearrange("p (r b) -> p r b", b=1), allow_slow_non_contiguous=True).then_inc(dsq, 16)
        P.dma('act', stq)
        for vi, co in enumerate(V_COLS):
            for c in range(8):
                P.op('pe', lambda e, iv, c=c, co=co: e.matmul(
                    R.ps[2][:, 0:384], lhsT=xT[:, c, 0:128], rhs=win[:, c, co:co + 384],
                    start=(c == 0), stop=(c == 7)), chain=(c > 0))
            P.op('act', lambda e, iv, vi=vi: e.copy(out=vst[:, vi * 384:(vi + 1) * 384], in_=R.ps[2][:, 0:384]))
        for gi in range(6):
            co = G_COL + gi * 512
            for c in range(8):
                P.op('pe', lambda e, iv, c=c, co=co: e.matmul(
                    R.ps[3][:], lhsT=xT[:, c, 0:128], rhs=win[:, c, co:co + 512],
                    start=(c == 0), stop=(c == 7)), chain=(c > 0))
            P.op('dve', lambda e, iv, gi=gi: e.tensor_tensor(out=gt[:, gi * 512:(gi + 1) * 512], in0=R.ps[3][:],
                                                             in1=bgb[:, gi * 512:(gi + 1) * 512], op=ALU.add))
        P.op('act', lambda e, iv: e.activation(out=gt, in_=gt, func=AF.Sigmoid))
        P.dma('act', lambda e, iv: e.dma_start(out=v_out[bass.ts(iv[L], 128), :], in_=vst).then_inc(dsq, 16))
        P.dma('act', lambda e, iv: e.dma_start(out=gates_out[bass.ts(iv[L], 128), :], in_=gt).then_inc(dsq, 16))
        P.dma('act', lambda e, iv: None, waits=[(dsq, 80)])


BIG = 30000.0
SLOPES = (2.0 ** (-8.0 * np.arange(1, 13, dtype=np.float32) / 12)).astype(np.float32)
SC_M = 64 ** -0.5
SC_D = 32 ** -0.5
SC_S = 64 ** -0.5
SUBLN_EPS = 1e-5


def att_tables():
    bf = ml_dtypes.bfloat16
    i = np.arange(SEQ)
    ib = (i // 128).astype(np.float32)
    il = (i % 128).astype(np.float32)
    one = np.ones(SEQ, np.float32)

    def hl(v):
        hi = np.float32(np.asarray(v, np.float32).astype(bf).astype(np.float32))
        lo = np.float32(np.asarray(v - hi, np.float32).astype(bf).astype(np.float32))
        return hi, lo
    qaug = np.zeros((12, 8, SEQ), np.float32)
    kaug = np.zeros((12, 8, SEQ), np.float32)
    for slot in range(12):
        scale = SC_M if slot % 2 == 0 else SC_D
        sig = float(SLOPES[slot]) / scale
        Sh, Sl = hl(128.0 * sig)
        sh, sl = hl(sig)
        qaug[slot] = np.stack([Sh * one, Sl * one, sh * one, sl * one, -ib, -ib, -il, -il])
        kaug[slot] = np.stack([ib, ib, il, il, Sh * one, Sl * one, sh * one, sl * one])
    onehot = (np.arange(32)[:, None] == (i // 256)[None, :]).astype(np.float32)
    p = np.arange(128)[:, None]
    c = np.arange(512)[None, :]
    maskT = np.concatenate([np.where((r * 128 + p) > c, -BIG, 0.0) for r in range(4)], axis=1)
    maskS = np.concatenate([np.where(c >= (r * 128 + p), -BIG, 0.0) for r in range(4)], axis=1)
    o = np.arange(32)[:, None]
    n = np.arange(32)[None, :]
    pb = np.stack([np.where(n < o, 0.0, -BIG), np.where(n == o, 0.0, -BIG), np.where(n <= o, 0.0, -BIG)]).astype(np.float32)
    return dict(qaug=qaug.reshape(96, SEQ).astype(bf), kaug=kaug.reshape(96, SEQ).astype(bf), onehot=onehot.astype(bf),
                maskT=maskT.astype(bf), maskS=maskS.astype(bf), pb=pb.reshape(-1))


def stage_att(P, R, T, qkT_d, qm32_d, kmean_d, v_d, o_d, lam_d, subg_d, one_m_linit, nheads=6, do=('m', 'd', 's')):
    W = R.WBUF
    GW = nheads * 64
    QA = [W[:, 0:8192], W[:, 8192:16384]]
    KA = [W[:, 16384:24576], W[:, 24576:32768]]
    VA = W[:, 32768:40960].rearrange("p (k d) -> p k d", k=64)
    VAf = W[:, 32768:40960]
    maskT = W[:, 40960:43008]
    maskS = W[:, 43008:45056]
    pt = [W[:, 45056:45568], W[:, 45568:46080]]
    wb = W[:, 46080:46592]
    wT = W[:, 46592:47104].rearrange("p (k q) -> p k q", k=4)
    FW = W[:, 47360:67584].bitcast(F32)
    pbt = FW[:, 0:3072].rearrange("p (a o n) -> p a o n", a=3, o=32)
    osb = [FW[:, 3072:3584], FW[:, 3584:4096]]
    gm = FW[:, 4096:4128]
    sel = FW[:, 4128:4160]
    m8 = FW[:, 4160:4168]
    rinv = [FW[:, 4168:4172], FW[:, 4172:4176]]
    ss = FW[:, 4176:4180]
    carry = FW[:, 4180:4181]
    negc = FW[:, 4181:4182]
    tot = FW[:, 4182:4183]
    lamt = FW[:, 4183:4184]
    lsc = FW[:, 4184:4186]
    t1 = FW[:, 4192:4256]
    od = FW[:, 4256:4512].rearrange("p (j d) -> p j d", j=4)
    gsc = FW[:, 4512:4576]
    lamw = FW[:, 4576:4704]
    lamw2 = FW[:, 4704:4768]
    ones = FW[:, 4768:5280]
    tt = FW[:, 5280:5792]
    spb = FW[:, 5792:6305]
    E1 = FW[:, 6336:6848]
    aa = FW[:, 6848:7360]
    kme = FW[:, 7360:7392]
    kmr = FW[:, 9472:9536]
    ost_all = FW[:, 7424:9472].bitcast(BF16).rearrange("p (t d) -> p t d", t=64)
    qm32 = R.SCRF[:, 0:8192]
    ps_s = [R.ps[0], R.ps[1]]
    ps_o = [R.ps[2], R.ps[3]]
    ps_t = [R.ps[4], R.ps[5]]
    psb = R.psb[0]
    tsem = new_sem(R, "att_t")
    hsem = new_sem(R, "att_h")
    osem = new_sem(R, "att_o")
    cnt = {'h': 0, 'o': 0}

    def ldc(e, iv):
        e.dma_start(out=maskT, in_=T['maskT'][:, :]).then_inc(tsem, 16)
        e.dma_start(out=maskS, in_=T['maskS'][:, :]).then_inc(tsem, 16)
        e.dma_start(out=FW[:, 0:3072], in_=T['pb'].partition_broadcast(128)).then_inc(tsem, 16)
        e.dma_start(out=gsc, in_=subg_d.partition_broadcast(128)).then_inc(tsem, 16)
        e.dma_start(out=lamw, in_=lam_d.rearrange("a d -> (a d)").partition_broadcast(128)).then_inc(tsem, 16)
    P.dma('sp', ldc)
    P.dma('sp', lambda e, iv: None, waits=[(tsem, 80)])
    P.op('dve', lambda e, iv: e.memset(ones, 1.0))
    P.op('dve', lambda e, iv: e.memset(spb[:, 0:1], 0.0))
    P.op('dve', lambda e, iv: e.memset(VA[:, :, 64:65], 1.0))
    linit = 1.0 - one_m_linit
    P.op('dve', lambda e, iv: e.tensor_tensor(out=lamw2[:, 0:32], in0=lamw[:, 0:32], in1=lamw[:, 32:64], op=ALU.mult))
    P.op('dve', lambda e, iv: e.tensor_tensor(out=lamw2[:, 32:64], in0=lamw[:, 64:96], in1=lamw[:, 96:128], op=ALU.mult))
    P.op('dve', lambda e, iv: e.tensor_reduce(out=lsc, in_=lamw2.rearrange("p (a d) -> p a d", a=2), axis=AX.X, op=ALU.add))
    P.op('act', lambda e, iv: e.activation(out=lsc, in_=lsc, func=AF.Exp))
    P.op('dve', lambda e, iv: e.tensor_tensor(out=lamt, in0=lsc[:, 0:1], in1=lsc[:, 1:2], op=ALU.subtract))
    P.op('dve', lambda e, iv: e.tensor_scalar(out=lamt, in0=lamt, scalar1=linit, scalar2=-1.0, op0=ALU.add, op1=ALU.mult))
    P.op('act', lambda e, iv: e.mul(out=gsc, in_=gsc, mul=one_m_linit))

    def dense_tile(s, Kd, scale, kb_ap_K, kb_ap_V, Q, mask_r, first):
        P.op('pe', lambda e, iv: e.matmul(ps_s[s][:], lhsT=kb_ap_K(iv), rhs=QA[s][0:Kd, Q * 512:(Q + 1) * 512],
                                          start=True, stop=(mask_r is None)))
        if mask_r is not None:
            P.op('pe', lambda e, iv: e.matmul(ps_s[s][:], lhsT=R.identb[:], rhs=maskT[:, mask_r * 512:(mask_r + 1) * 512],
                                              start=False, stop=True), chain=True)
        P.op('act', lambda e, iv: e.activation(out=pt[s], in_=ps_s[s][:], func=AF.Exp, scale=scale))
        P.op('pe', lambda e, iv: e.matmul(ps_o[s][0:65, :], lhsT=kb_ap_V(iv), rhs=pt[s], start=first, stop=True))

    def dense_Q(streams, Q):
        for r in range(4):
            kb = 4 * Q + r
            for (s, Kd, scale) in streams:
                dense_tile(s, Kd, scale, lambda iv, s=s, Kd=Kd, kb=kb: KA[s][0:Kd, kb * 128:(kb + 1) * 128],
                           lambda iv, kb=kb: VA[:, kb, 0:65], Q, r, r == 0)
        for kb in range(4 * Q):
            for (s, Kd, scale) in streams:
                dense_tile(s, Kd, scale, lambda iv, s=s, Kd=Kd, kb=kb: KA[s][0:Kd, kb * 128:(kb + 1) * 128],
                           lambda iv, kb=kb: VA[:, kb, 0:65], Q, None, False)
        for (s, Kd, scale) in streams:
            P.op('act', lambda e, iv, s=s: e.copy(out=osb[s][0:65, :], in_=ps_o[s][0:65, :]))
            for j in range(4):
                P.op('pe', lambda e, iv, s=s, j=j: e.transpose(out=ps_t[s][:, j * 65:(j + 1) * 65],
                                                               in_=osb[s][0:65, j * 128:(j + 1) * 128],
                                                               identity=R.identf[0:65, 0:65]), chain=(j > 0))
            P.op('dve', lambda e, iv, s=s: e.reciprocal(
                out=rinv[s], in_=ps_t[s][:, 0:260].rearrange("p (j d) -> p j d", j=4)[:, :, 64]))

    def store_head(hfn, dq):
        P.dma(dq, lambda e, iv: e.dma_start(
            out=o_d[bass.ts(hfn(iv), 1), :, :].rearrange("h (t p) d -> p (h t) d", p=128),
            in_=ost_all).then_inc(osem, 16))
        P.dma(dq, lambda e, iv: None, waits=[(osem, 16)])

    NTD = 3
    Q_ASSIGN = [[15, 10, 9, 4, 3], [14, 11, 8, 5, 2], [13, 12, 7, 6, 1, 0]]
    SB16 = R.SCRB
    d_pt = [SB16[:, 2624 * i:2624 * i + 512] for i in range(NTD)]
    d_f = [SB16[:, 2624 * i + 512:2624 * i + 2624].bitcast(F32) for i in range(NTD)]
    d_osb = [d_f[i][:, 0:512] for i in range(NTD)]
    d_t1 = [d_f[i][:, 512:768].rearrange("p (j d) -> p j d", j=4) for i in range(NTD)]
    d_od = [d_f[i][:, 768:1024].rearrange("p (j d) -> p j d", j=4) for i in range(NTD)]
    d_r0 = [d_f[i][:, 1024:1028] for i in range(NTD)]
    d_r1 = [d_f[i][:, 1028:1032] for i in range(NTD)]
    d_ss = [d_f[i][:, 1032:1036] for i in range(NTD)]
    d_sq = [d_f[i][:, 1036:1040] for i in range(NTD)]
    d_pss = [R.ps[0], R.ps[1], R.ps[2]]
    d_pso = [R.ps[3], R.ps[4], R.ps[5]]

    def dense_map_ops(th, m, Kd, scale, Q):
        ops = []
        kbs = [(4 * Q + r, r) for r in range(4)] + [(kb, None) for kb in range(4 * Q)]
        for idx, (kb, mr) in enumerate(kbs):
            ops.append(('pe', lambda e, iv, kb=kb, mr=mr: e.matmul(
                d_pss[th][:], lhsT=KA[m][0:Kd, kb * 128:(kb + 1) * 128], rhs=QA[m][0:Kd, Q * 512:(Q + 1) * 512],
                start=True, stop=(mr is None)), False))
            if mr is not None:
                ops.append(('pe', lambda e, iv, mr=mr: e.matmul(d_pss[th][:], lhsT=R.identb[:],
                                                                rhs=maskT[:, mr * 512:(mr + 1) * 512],
                                                                start=False, stop=True), True))
            ops.append(('act', lambda e, iv: e.activation(out=d_pt[th], in_=d_pss[th][:], func=AF.Exp, scale=scale), False))
            ops.append(('pe', lambda e, iv, kb=kb, idx=idx: e.matmul(d_pso[th][0:65, :], lhsT=VA[:, kb, 0:65], rhs=d_pt[th],
                                                                     start=(idx == 0), stop=True), False))
        ops.append(('act', lambda e, iv: e.copy(out=d_osb[th][0:65, :], in_=d_pso[th][0:65, :]), False))
        for j in range(4):
            ops.append(('pe', lambda e, iv, j=j: e.transpose(out=d_pss[th][:, j * 65:(j + 1) * 65],
                                                             in_=d_osb[th][0:65, j * 128:(j + 1) * 128],
                                                             identity=R.identf[0:65, 0:65]), j > 0))
        return ops

    def psT(th, j):
        return d_pss[th][:, j * 65:j * 65 + 64]

    def rsum(th):
        return d_pss[th][:, 0:260].rearrange("p (j d) -> p j d", j=4)[:, :, 64]

    def emit_threads(lists):
        for idx in range(max(len(x) for x in lists)):
            for th in range(len(lists)):
                if idx < len(lists[th]):
                    eng, fn, ch = lists[th][idx]
                    P.op(eng, fn, chain=ch, thread=th)

    if 'm' in do:
        with P.loop(nheads, clear=(hsem, osem)) as LHM:
            hbaseM = 0

            def ldm(e, iv):
                h = iv[LHM]
                e.dma_start(out=QA[0][32:96, :], in_=qkT_d[0:GW, :][bass.ts(h, 64), :]).then_inc(hsem, 16)
                e.dma_start(out=KA[0][32:96, :], in_=qkT_d[GW:2 * GW, :][bass.ts(h, 64), :]).then_inc(hsem, 16)
                e.dma_start(out=KA[0][0:32, :], in_=T['onehot'][:, :]).then_inc(hsem, 16)
                e.dma_start(out=QA[0][96:104, :], in_=T['qaug'][bass.ts(h * 2, 8), :]).then_inc(hsem, 16)
                e.dma_start(out=KA[0][96:104, :], in_=T['kaug'][bass.ts(h * 2, 8), :]).then_inc(hsem, 16)
                e.dma_start(out=VA[:, :, 0:64],
                            in_=v_d[:, 0:GW][:, bass.ts(h, 64)].rearrange("(k p) d -> p k d", p=128)).then_inc(hsem, 16)
                e.dma_start(out=qm32[0:64, :], in_=qm32_d[bass.ts(h, 64), :]).then_inc(hsem, 16)
                e.dma_start(out=kmr[0:64, :], in_=kmean_d[bass.ts(h, 64), :]).then_inc(hsem, 16)
            P.dma('sp', ldm)
            P.dma('sp', lambda e, iv: None, waits=[(hsem, lambda iv: (hbaseM + (iv[LHM] + 1) * 8) * 16)])
            P.op('dve', lambda e, iv: e.tensor_reduce(out=kme[0:64, :], in_=kmr[0:64, :].rearrange("p (n b) -> p n b", b=2),
                                                      axis=AX.X, op=ALU.add))
            P.op('dve', lambda e, iv: e.tensor_scalar(out=kme[0:64, :], in0=kme[0:64, :], scalar1=1.0 / 256.0, scalar2=None,
                                                      op0=ALU.mult))
            for t4 in range(16):
                for tj in range(4):
                    t = t4 * 4 + tj
                    own = t // 2
                    P.op('pe', lambda e, iv, t=t: e.matmul(ps_s[1][:, 0:32], lhsT=qm32[0:64, t * 128:(t + 1) * 128],
                                                           rhs=kme[0:64, :], start=True, stop=True))
                    P.op('dve', lambda e, iv, own=own: e.tensor_tensor(out=gm, in0=ps_s[1][:, 0:32], in1=pbt[:, 0, own, :],
                                                                       op=ALU.add))
                    P.op('dve', lambda e, iv: e.max(out=m8, in_=gm))
                    P.op('dve', lambda e, iv: e.tensor_scalar(out=sel, in0=gm, scalar1=m8[:, 2:3], scalar2=1.0,
                                                              op0=ALU.is_ge, op1=ALU.subtract))
                    P.op('dve', lambda e, iv, own=own: e.scalar_tensor_tensor(out=sel, in0=sel, scalar=BIG,
                                                                              in1=pbt[:, 1, own, :], op0=ALU.mult, op1=ALU.max))
                    P.op('dve', lambda e, iv, own=own: e.tensor_tensor(out=sel, in0=sel, in1=pbt[:, 2, own, :], op=ALU.add))
                    P.op('pe', lambda e, iv, tj=tj: e.transpose(out=ps_t[0][0:32, tj * 128:(tj + 1) * 128], in_=sel,
                                                                identity=R.identf[:]))
                P.op('act', lambda e, iv, t4=t4: e.copy(out=QA[0][0:32, t4 * 512:(t4 + 1) * 512], in_=ps_t[0][0:32, :]))
            for th in range(1, NTD):
                P.join('pe', th, 0)
            lists = []
            for th in range(NTD):
                ops = []
                for Q in Q_ASSIGN[th]:
                    ops += dense_map_ops(th, 0, 104, SC_M, Q)
                    ops.append(('dve', (lambda e, iv, th=th: e.reciprocal(out=d_r0[th], in_=rsum(th))), False))
                    for j in range(4):
                        ops.append(('dve', (lambda e, iv, th=th, j=j, Q=Q: e.tensor_scalar(
                            out=ost_all[:, Q * 4 + j, :], in0=psT(th, j), scalar1=d_r0[th][:, j:j + 1], scalar2=None,
                            op0=ALU.mult)), False))
                lists.append(ops)
            emit_threads(lists)
            for th in range(1, NTD):
                P.join('sp', 0, th)
            store_head(lambda iv: iv[LHM], 'sp')
    obase_d = 0

    if 'd' in do:
        with P.loop(nheads, clear=(hsem, osem)) as LHD:
            hbaseD = 0

            def ldd(e, iv):
                h = iv[LHD]
                for m in range(2):
                    e.dma_start(out=QA[m][0:32, :], in_=qkT_d[2 * GW:3 * GW, :][bass.ts(h * 2 + m, 32), :]).then_inc(hsem, 16)
                    e.dma_start(out=KA[m][0:32, :], in_=qkT_d[3 * GW:4 * GW, :][bass.ts(h * 2 + m, 32), :]).then_inc(hsem, 16)
                    e.dma_start(out=QA[m][32:40, :], in_=T['qaug'][bass.ts(h * 2 + 1, 8), :]).then_inc(hsem, 16)
                    e.dma_start(out=KA[m][32:40, :], in_=T['kaug'][bass.ts(h * 2 + 1, 8), :]).then_inc(hsem, 16)
                e.dma_start(out=VA[:, :, 0:64],
                            in_=v_d[:, GW:2 * GW][:, bass.ts(h, 64)].rearrange("(k p) d -> p k d", p=128)).then_inc(hsem, 16)
            P.dma('act', ldd)
            P.dma('act', lambda e, iv: None, waits=[(hsem, lambda iv: (hbaseD + (iv[LHD] + 1) * 9) * 16)])
            for th in range(1, NTD):
                P.join('pe', th, 0)
            lists = []
            for th in range(NTD):
                ops = []
                for Q in Q_ASSIGN[th]:
                    ops += dense_map_ops(th, 0, 40, SC_D, Q)
                    ops.append(('dve', (lambda e, iv, th=th: e.reciprocal(out=d_r0[th], in_=rsum(th))), False))
                    for j in range(4):
                        ops.append(('dve', (lambda e, iv, th=th, j=j: e.tensor_scalar(
                            out=d_t1[th][:, j, :], in0=psT(th, j), scalar1=d_r0[th][:, j:j + 1], scalar2=None,
                            op0=ALU.mult)), False))
                    ops += dense_map_ops(th, 1, 40, SC_D, Q)
                    ops.append(('dve', (lambda e, iv, th=th: e.reciprocal(out=d_r1[th], in_=rsum(th))), False))
                    ops.append(('dve', (lambda e, iv, th=th: e.tensor_scalar(out=d_r1[th], in0=d_r1[th], scalar1=lamt[:, 0:1],
                                                                             scalar2=None, op0=ALU.mult)), False))
                    for j in range(4):
                        ops.append(('dve', (lambda e, iv, th=th, j=j: e.scalar_tensor_tensor(
                            out=d_od[th][:, j, :], in0=psT(th, j), scalar=d_r1[th][:, j:j + 1], in1=d_t1[th][:, j, :],
                            op0=ALU.mult, op1=ALU.add)), False))
                    ops.append(('dve', (lambda e, iv, th=th: e.tensor_tensor(out=d_t1[th], in0=d_od[th], in1=d_od[th],
                                                                             op=ALU.mult)), False))
                    ops.append(('dve', (lambda e, iv, th=th: e.tensor_reduce(out=d_ss[th], in_=d_t1[th], axis=AX.X,
                                                                             op=ALU.add)), False))
                    ops.append(('dve', (lambda e, iv, th=th: e.tensor_scalar(out=d_ss[th], in0=d_ss[th], scalar1=1.0 / 64.0,
                                                                             scalar2=SUBLN_EPS, op0=ALU.mult, op1=ALU.add)), False))
                    ops.append(('act', (lambda e, iv, th=th: e.activation(out=d_ss[th], in_=d_ss[th], func=AF.Sqrt)), False))
                    ops.append(('dve', (lambda e, iv, th=th: e.reciprocal(out=d_ss[th], in_=d_ss[th])), False))
                    for j in range(4):
                        ops.append(('dve', (lambda e, iv, th=th, j=j, Q=Q: e.scalar_tensor_tensor(
                            out=ost_all[:, Q * 4 + j, :], in0=d_od[th][:, j, :], scalar=d_ss[th][:, j:j + 1], in1=gsc,
                            op0=ALU.mult, op1=ALU.mult)), False))
                lists.append(ops)
            emit_threads(lists)
            for th in range(1, NTD):
                P.join('sp', 0, th)
            store_head(lambda iv: iv[LHD] + nheads, 'act')
    obase_s = 0

    if 's' in do:
        ones13 = FW[:, 4768:5281]
        SF = R.SCRF
        tts = [FW[:, 5312:5824]] + [SF[:, 2176 * i:2176 * i + 512] for i in range(3)]
        spbs = [FW[:, 5824:6337]] + [SF[:, 2176 * i + 512:2176 * i + 1025] for i in range(3)]
        Efs = [FW[:, 6368:6881]] + [SF[:, 2176 * i + 1056:2176 * i + 1569] for i in range(3)]
        aas = [FW[:, 6912:7424]] + [SF[:, 2176 * i + 1600:2176 * i + 2112] for i in range(3)]
        negcs = [FW[:, 4181:4182]] + [SF[:, 2176 * i + 2112:2176 * i + 2113] for i in range(3)]
        wbs_ = [wb] + [R.SCRB[:, 1024 * i:1024 * i + 512] for i in range(3)]
        wTs = [wT] + [R.SCRB[:, 1024 * i + 512:1024 * i + 1024].rearrange("p (k q) -> p k q", k=4) for i in range(3)]
        psbs = [R.psb[0][:, 0:512], R.psb[0][:, 512:1024], R.psb[1][:, 0:512], R.psb[1][:, 512:1024]]
        pszs = [R.ps[0], R.ps[1], R.ps[2]]
        psos = [R.ps[3][:, 0:64], R.ps[4][:, 0:64], R.ps[5][:, 0:64]]
        NTS = 3
        P.op('dve', lambda e, iv: e.memset(ones13, 1.0))
        for th in range(NTS):
            P.op('dve', lambda e, iv, th=th: e.memset(spbs[th][:, 0:1], 0.0))
        with P.loop(nheads, clear=(hsem, osem)) as LHS:
            def lds(e, iv):
                h = iv[LHS]
                e.dma_start(out=QA[0][0:64, :], in_=qkT_d[4 * GW:5 * GW, :][bass.ts(h, 64), :]).then_inc(hsem, 16)
                e.dma_start(out=KA[0][0:64, :], in_=qkT_d[5 * GW:6 * GW, :][bass.ts(h, 64), :]).then_inc(hsem, 16)
                e.dma_start(out=VA[:, :, 0:64],
                            in_=v_d[:, 2 * GW:3 * GW][:, bass.ts(h, 64)].rearrange("(k p) d -> p k d", p=128)).then_inc(hsem, 16)
            P.dma('sp', lds)
            P.dma('sp', lambda e, iv: None, waits=[(hsem, 48)])
            for th in range(1, NTS):
                P.join('pe', th, 0)

            def sb_tile_ops(th, t, kt, mask_r, first):
                ops = []
                ops.append(('pe', lambda e, iv: e.matmul(pszs[th][:], lhsT=QA[0][0:64, t * 128:(t + 1) * 128],
                                                         rhs=KA[0][0:64, kt * 512:(kt + 1) * 512],
                                                         start=True, stop=(mask_r is None)), False))
                if mask_r is not None:
                    ops.append(('pe', lambda e, iv: e.matmul(pszs[th][:], lhsT=R.identb[:],
                                                             rhs=maskS[:, mask_r * 512:(mask_r + 1) * 512],
                                                             start=False, stop=True), True))
                ops.append(('act', lambda e, iv: e.activation(out=tts[th], in_=pszs[th][:], func=AF.Exp, scale=SC_S), False))
                ops.append(('act', lambda e, iv: e.activation(out=spbs[th][:, 1:513], in_=tts[th], func=AF.Ln, bias=1.0), False))
                ops.append(('dve', lambda e, iv: e.tensor_tensor_scan(out=Efs[th], data0=ones13, data1=spbs[th][:, 0:513],
                                                                      initial=0.0, op0=ALU.mult, op1=ALU.add), False))
                ops.append(('dve', lambda e, iv: e.scalar_tensor_tensor(out=aas[th], in0=pszs[th][:], scalar=SC_S,
                                                                        in1=Efs[th][:, 0:512], op0=ALU.mult, op1=ALU.add), False))
                if first:
                    ops.append(('dve', lambda e, iv: e.tensor_scalar(out=negcs[th], in0=Efs[th][:, 512:513], scalar1=-1.0,
                                                                     scalar2=None, op0=ALU.mult), False))
                else:
                    ops.append(('dve', lambda e, iv: e.tensor_tensor(out=negcs[th], in0=negcs[th], in1=Efs[th][:, 512:513],
                                                                     op=ALU.subtract), False))
                ops.append(('act', lambda e, iv: e.activation(out=wbs_[th], in_=aas[th], func=AF.Exp, bias=negcs[th][:, 0:1]), False))
                for k4 in range(4):
                    ops.append(('pe', lambda e, iv, k4=k4: e.transpose(out=psbs[th][:, k4 * 128:(k4 + 1) * 128],
                                                                       in_=wbs_[th][:, k4 * 128:(k4 + 1) * 128],
                                                                       identity=R.identb[:]), k4 > 0))
                ops.append(('act', lambda e, iv: e.copy(out=wTs[th], in_=psbs[th].rearrange("p (k q) -> p k q", k=4)), False))
                for k4 in range(4):
                    ops.append(('pe', lambda e, iv, k4=k4: e.matmul(psos[th], lhsT=wTs[th][:, k4, :],
                                                                    rhs=VA[:, kt * 4 + k4, 0:64],
                                                                    start=(first and k4 == 0), stop=True), k4 > 0))
                return ops

            for base in range(0, 64, NTS):
                lists = []
                for th in range(NTS):
                    t = base + th
                    if t >= 64:
                        lists.append([])
                        continue
                    ktd = t // 4
                    ops = sb_tile_ops(th, t, ktd, t % 4, True)
                    for kt in range(ktd - 1, -1, -1):
                        ops += sb_tile_ops(th, t, kt, None, False)
                    ops.append(('act', (lambda e, iv, t=t, th=th: e.copy(out=ost_all[:, t, :], in_=psos[th])), False))
                    lists.append(ops)
                for idx in range(max(len(x) for x in lists)):
                    for th in range(NTS):
                        if idx < len(lists[th]):
                            eng, fn, ch = lists[th][idx]
                            P.op(eng, fn, chain=ch, thread=th)
            for th in range(1, NTS):
                P.join('sp', 0, th)
            store_head(lambda iv: iv[LHS] + 2 * nheads, 'sp')
def stage_mixpost(P, R, x_in, o_d, gates_d, x_out, wbm_d, wbd_d, wbs_d, wout_d, g_d, b_d, ntiles=64):
    W = R.WBUF
    wbr = W[:, 0:9216].rearrange("p (k n) -> p k n", k=9)
    wout = W[:, 9216:17408].rearrange("p (c n) -> p c n", c=8)
    gts = W[:, 17408:23552].bitcast(F32)
    xs = R.SCRF[:, 0:1024]
    u = R.SCRF[:, 1024:2048]
    xa = R.SCRF[:, 2048:3072]
    yo = R.SCRF[:, 3072:4096]
    gbc = R.SCRF[:, 4608:5632]
    bbc = R.SCRF[:, 5632:6656]
    merged = R.SCRF[:, 6656:7680]
    tmp = R.SCRF[:, 7680:8704]
    xT = R.SCRB[:, 0:4096].rearrange("p (c t) -> p c t", c=8)
    ot = R.SCRB[:, 4096:5248]
    oT = R.SCRB[:, 5248:6400].rearrange("p (k t) -> p k t", k=9)
    wsem = new_sem(R, "w")
    dsi = new_sem(R, "dsi")
    dso = new_sem(R, "dso")

    def loadw(e, iv):
        for bi, wd_ in enumerate((wbm_d, wbd_d, wbs_d)):
            for i in range(3):
                e.dma_start(out=wbr[:, bi * 3 + i, :], in_=wd_[i * 128:(i + 1) * 128, :]).then_inc(wsem, 16)
        for c in range(8):
            e.dma_start(out=wout[:, c, :], in_=wout_d[c * 128:(c + 1) * 128, :]).then_inc(wsem, 16)
    P.dma('pool', loadw)
    P.dma('pool', lambda e, iv: None, waits=[(wsem, 16 * 17)])
    load_ln_params(P, R, g_d, b_d, gbc, bbc)
    with P.loop(ntiles, clear=(dsi, dso)) as L:
        def ld(e, iv):
            e.dma_start(out=gts, in_=gates_d[bass.ts(iv[L], 128), :]).then_inc(dsi, 16)
            e.dma_start(out=xs, in_=x_in[bass.ts(iv[L], 128), :]).then_inc(dsi, 16)
        P.dma('sp', lambda e, iv: e.dma_start(out=ot.rearrange("p (h d) -> p h d", h=18),
                                              in_=o_d.rearrange("h s d -> s h d")[bass.ts(iv[L], 128), :, :]).then_inc(dsi, 16))
        P.dma('act', ld)
        P.dma('act', lambda e, iv: None, waits=[(dsi, lambda iv: (iv[L] + 1) * 48)])
        for k in range(9):
            P.op('pe', lambda e, iv, k=k: e.transpose(
                out=(R.psb[0][:, k * 128:(k + 1) * 128] if k < 8 else R.psb[1][:, 0:128]),
                in_=ot[:, k * 128:(k + 1) * 128], identity=R.identb[:]), chain=(k > 0))
        P.op('act', lambda e, iv: e.copy(out=oT[:, 0:8, :], in_=R.psb[0][:, :].rearrange("p (k t) -> p k t", k=8)))
        P.op('act', lambda e, iv: e.copy(out=oT[:, 8, :], in_=R.psb[1][:, 0:128]))
        for br in range(3):
            for h in range(2):
                for i in range(3):
                    P.op('pe', lambda e, iv, br=br, h=h, i=i: e.matmul(
                        R.ps[1][:], lhsT=oT[:, br * 3 + i, :], rhs=wbr[:, br * 3 + i, h * 512:(h + 1) * 512],
                        start=(i == 0), stop=(i == 2)), chain=(i > 0))
                gsl = gts[:, br * 1024 + h * 512: br * 1024 + (h + 1) * 512]
                if br == 0:
                    P.op('dve', lambda e, iv, h=h, gsl=gsl: e.tensor_tensor(out=merged[:, h * 512:(h + 1) * 512], in0=gsl,
                                                                            in1=R.ps[1][:], op=ALU.mult))
                else:
                    P.op('dve', lambda e, iv, h=h, gsl=gsl: e.tensor_tensor(out=tmp[:, h * 512:(h + 1) * 512], in0=gsl,
                                                                            in1=R.ps[1][:], op=ALU.mult))
                    P.op('dve', lambda e, iv, h=h: e.tensor_tensor(out=merged[:, h * 512:(h + 1) * 512],
                                                                   in0=merged[:, h * 512:(h + 1) * 512],
                                                                   in1=tmp[:, h * 512:(h + 1) * 512], op=ALU.add))
        transposes_to_xT(P, R, lambda j: merged, xT, nj=1)
        first = True
        for h in range(2):
            for c in range(8):
                P.op('pe', lambda e, iv, h=h, c=c: e.matmul(R.ps[3 + h][:], lhsT=xT[:, c, 0:128],
                                                            rhs=wout[:, c, h * 512:(h + 1) * 512],
                                                            start=(c == 0), stop=(c == 7)), chain=not first)
                first = False
        P.op('act', lambda e, iv: e.mul(out=xa, in_=xs, mul=ALPHA))
        for h in range(2):
            P.op('dve', lambda e, iv, h=h: e.tensor_tensor(out=u[:, h * 512:(h + 1) * 512], in0=xa[:, h * 512:(h + 1) * 512],
                                                           in1=R.ps[3 + h][:], op=ALU.add))
        ln_ops(P, R, u, gbc, bbc, yo, pre_waits=[(dso, lambda iv: iv[L] * 16)])
        P.dma('act', lambda e, iv: e.dma_start(out=x_out[bass.ts(iv[L], 128), :], in_=yo).then_inc(dso, 16))
        P.dma('act', lambda e, iv: None, waits=[(dso, 16)])


NACT = 4


NCORE = 8
HTOK = 4096


def _nc_dt():
    nc = bass.Bass("TRN2", target_bir_lowering=False)

    def dt(n, s, d=F32, k="ExternalInput"):
        return nc.dram_tensor(n, list(s), d, kind=k).ap()
    return nc, dt


def build_pre():
    nc, dt = _nc_dt()
    x = dt("x", [HTOK, D])
    g = dt("ln_g", [D]); b = dt("ln_b", [D])
    wg = dt("wg", [D, DFF]); wu = dt("wu", [D, DFF]); wd = dt("wd", [DFF, D])
    win = dt("w_in", [D, NIN]); bg = dt("b_gate", [3, D])
    idf = dt("idf", [128, 128]); idb = dt("idb", [128, 128], BF16)
    x1 = dt("x1", [HTOK, D], F32, "ExternalOutput")
    qkT = dt("qkT", [2304, HTOK], BF16, "ExternalOutput")
    qm32 = dt("qm32", [384, HTOK], F32, "ExternalOutput")
    kmean = dt("kmean", [384, HTOK // 128], F32, "ExternalOutput")
    v = dt("v", [HTOK, 1152], BF16, "ExternalOutput")
    gates = dt("gates", [HTOK, 3072], F32, "ExternalOutput")
    with ExitStack() as ctx:
        R = rl_alloc(nc, ctx)
        P = Prog()
        rl_consts(P, R, idf, idb)
        stage_ffnln(P, R, x, x1, wg, wu, wd, g, b, nsb=HTOK // 512)
        stage_win(P, R, x1, win, bg, qkT, qm32, kmean, v, gates, nsb=HTOK // 512)
        run_prog(nc, P, R.G)
    return nc


def build_att(tb_shapes, l):
    bf = ml_dtypes.bfloat16
    nc, dt = _nc_dt()
    NH = 3
    qkT = dt("qkT", [6 * NH * 64, SEQ], BF16)
    qm32 = dt("qm32", [NH * 64, SEQ])
    kmean = dt("kmean", [NH * 64, SEQ // 128])
    v = dt("v", [SEQ, 3 * NH * 64], BF16)
    dl = dt("diff_lambda", [4, 32]); dg = dt("diff_subln_g", [64])
    idf = dt("idf", [128, 128]); idb = dt("idb", [128, 128], BF16)
    T = {k: dt("t_" + k, shp, BF16 if dty == bf else F32) for k, (shp, dty) in tb_shapes.items()}
    o = dt("o", [3 * NH, SEQ, 64], BF16, "ExternalOutput")
    with ExitStack() as ctx:
        R = rl_alloc(nc, ctx)
        P = Prog()
        rl_consts(P, R, idf, idb)
        linit = 0.8 - 0.6 * math.exp(-0.3 * l)
        stage_att(P, R, T, qkT, qm32, kmean, v, o, dl, dg, 1.0 - linit, nheads=NH)
        run_prog(nc, P, R.G)
    return nc


def build_post():
    nc, dt = _nc_dt()
    x1 = dt("x1", [HTOK, D])
    o = dt("o", [18, HTOK, 64], BF16)
    gates = dt("gates", [HTOK, 3072])
    wbm = dt("w_br_moba", [384, D]); wbd = dt("w_br_diff", [384, D]); wbs = dt("w_br_sb", [384, D]); wo = dt("w_out", [D, D])
    g1 = dt("ln_g1", [D]); b1 = dt("ln_b1", [D]); g2 = dt("ln_g2", [D]); b2 = dt("ln_b2", [D])
    wg = dt("wg", [D, DFF]); wu = dt("wu", [D, DFF]); wd = dt("wd", [DFF, D])
    idf = dt("idf", [128, 128]); idb = dt("idb", [128, 128], BF16)
    y = dt("y", [HTOK, D], F32, "ExternalOutput")
    xb = dt("xb_d", [HTOK, D], F32, "Internal")
    with ExitStack() as ctx:
        R = rl_alloc(nc, ctx)
        P = Prog()
        rl_consts(P, R, idf, idb)
        stage_mixpost(P, R, x1, o, gates, xb, wbm, wbd, wbs, wo, g1, b1, ntiles=HTOK // 128)
        stage_ffnln(P, R, xb, y, wg, wu, wd, g2, b2, nsb=HTOK // 512)
        run_prog(nc, P, R.G)
    return nc


def kernel(x, ln_g, ln_b, ffn_w_gate, ffn_w_up, ffn_w_down, w_in, b_gate, diff_lambda, diff_subln_g,
           w_br_moba, w_br_diff, w_br_sb, w_out):
    bf = ml_dtypes.bfloat16
    tb = att_tables()
    f = lambda a: np.ascontiguousarray(np.asarray(a, dtype=np.float32))
    cc = np.ascontiguousarray
    ident = dict(idf=np.eye(128, dtype=np.float32), idb=np.eye(128).astype(bf))
    cores = list(range(NCORE))
    cur = f(x).reshape(NCORE, HTOK, D)
    for l in range(2):
        nc = build_pre()
        sh = dict(ident, ln_g=f(ln_g[l, 0]), ln_b=f(ln_b[l, 0]), wg=f(ffn_w_gate[l, 0]), wu=f(ffn_w_up[l, 0]),
                  wd=f(ffn_w_down[l, 0]), w_in=f(w_in[l]), b_gate=f(b_gate[l]))
        r1 = run_bass_kernel_spmd(nc, [dict(sh, x=cc(cur[c])) for c in cores], core_ids=cores).results
        att_in = []
        for c in cores:
            b_, hh = c // 2, c % 2
            qk_b = np.concatenate([r1[2 * b_]["qkT"], r1[2 * b_ + 1]["qkT"]], axis=1)
            qm_b = np.concatenate([r1[2 * b_]["qm32"], r1[2 * b_ + 1]["qm32"]], axis=1)
            km_b = np.concatenate([r1[2 * b_]["kmean"], r1[2 * b_ + 1]["kmean"]], axis=1)
            v_b = np.concatenate([r1[2 * b_]["v"], r1[2 * b_ + 1]["v"]], axis=0)
            qk_h = np.concatenate([qk_b[g * 384 + hh * 192:g * 384 + hh * 192 + 192] for g in range(6)], axis=0)
            v_h = np.concatenate([v_b[:, g * 384 + hh * 192:g * 384 + hh * 192 + 192] for g in range(3)], axis=1)
            d_ = dict(ident, qkT=cc(qk_h), qm32=cc(qm_b[hh * 192:(hh + 1) * 192]), kmean=cc(km_b[hh * 192:(hh + 1) * 192]),
                      v=cc(v_h), diff_lambda=f(diff_lambda[l]), diff_subln_g=f(diff_subln_g[l]))
            for k_, a_ in tb.items():
                if k_ in ("qaug", "kaug"):
                    d_["t_" + k_] = cc(a_[hh * 48:(hh + 1) * 48])
                else:
                    d_["t_" + k_] = a_
            att_in.append(d_)
        tb_shapes = {k_: (list(att_in[0]["t_" + k_].shape), att_in[0]["t_" + k_].dtype) for k_ in tb}
        nc = build_att(tb_shapes, l)
        r2 = run_bass_kernel_spmd(nc, att_in, core_ids=cores).results
        nc = build_post()
        sh = dict(ident, w_br_moba=f(w_br_moba[l]), w_br_diff=f(w_br_diff[l]), w_br_sb=f(w_br_sb[l]), w_out=f(w_out[l]),
                  ln_g1=f(ln_g[l, 1]), ln_b1=f(ln_b[l, 1]), ln_g2=f(ln_g[l, 2]), ln_b2=f(ln_b[l, 2]),
                  wg=f(ffn_w_gate[l, 1]), wu=f(ffn_w_up[l, 1]), wd=f(ffn_w_down[l, 1]))
        post_in = []
        for c in cores:
            b_, hf = c // 2, c % 2
            o_b = np.empty((18, SEQ, 64), dtype=r2[0]["o"].dtype)
            for g in range(3):
                for hh in range(2):
                    o_b[g * 6 + hh * 3:g * 6 + hh * 3 + 3] = r2[2 * b_ + hh]["o"][g * 3:g * 3 + 3]
            post_in.append(dict(sh, x1=cc(r1[c]["x1"]), gates=cc(r1[c]["gates"]), o=cc(o_b[:, hf * HTOK:(hf + 1) * HTOK])))
        r3 = run_bass_kernel_spmd(nc, post_in, core_ids=cores).results
        cur = np.stack([np.asarray(r["y"], dtype=np.float32) for r in r3], axis=0)
    return cur.reshape(4, SEQ, D)
```

```python
import math
from contextlib import contextmanager, ExitStack
import numpy as np
import ml_dtypes
import concourse.bass as bass
import concourse.mybir as mybir
from concourse.bass_utils import run_bass_kernel_spmd

F32 = mybir.dt.float32
BF16 = mybir.dt.bfloat16
AF = mybir.ActivationFunctionType
ALU = mybir.AluOpType
AX = mybir.AxisListType

D = 1024
DFF = 2816
NF = DFF // 128
NTOK = 4096
SEQ = 8192
NIN = 6528
ALPHA = 4.0 ** 0.25
LN_EPS = 1e-5
NCORES = 8


@contextmanager
def my_fori(nc, e, regs, start, end):
    loop_id = nc.next_id()
    name = f"myfori_{loop_id}"
    ls, le = name + "_loop", name + "_end"
    engines = bass.OrderedEngineSet([e.engine])
    nc.regs_mov(regs, start)
    nc.br(ls, engines=engines)
    with nc.body(ls, valid_engines=engines):
        yield nc.snap(regs, min_val=start, max_val=end - 1)
        nc.regs_alu(regs, regs, 1, op=mybir.AluOpType.add)
        nc.br_lt(regs, end, on_true=ls, on_false=le, engines=engines)
    nc.switch_bb(le)


class Prog:
    def __init__(self):
        self.root = []
        self.stack = [self.root]
        self.nloops = 0

    def op(self, eng, fn, chain=False, waits=(), thread=0):
        self.stack[-1].append(['op', eng, fn, chain, tuple(waits), 'c', thread])

    def dma(self, eng, fn, waits=(), thread=0):
        self.stack[-1].append(['op', eng, fn, False, tuple(waits), 'd', thread])

    def join(self, eng, thread, other):
        self.stack[-1].append(['op', eng, None, False, (), 'j', thread, other])

    @contextmanager
    def loop(self, n, clear=()):
        body = []
        lid = self.nloops
        self.nloops += 1
        self.stack[-1].append(['loop', n, body, lid, tuple(clear)])
        self.stack.append(body)
        try:
            yield lid
        finally:
            self.stack.pop()

    @staticmethod
    def _size(node):
        if node[0] == 'op':
            return 1
        return node[1] * sum(Prog._size(b) for b in node[2])

    @staticmethod
    def _has(node, ename):
        if node[0] == 'op':
            return node[1] == ename
        return any(Prog._has(b, ename) for b in node[2])

    def total(self):
        return sum(self._size(n) for n in self.root)

    def emit(self, ename, e, G, semfn, nc, GX):
        k = 0
        lregs = nc.alloc_registers(f"lr_{ename}", engines=bass.OrderedEngineSet([e.engine]))
        GS = (G,) + tuple(GX)
        NTH = len(GS)
        for node in self.root:
            if node[0] == 'op' or node[1] == 1:
                subs = [node] if node[0] == 'op' else node[2]
                iv = {} if node[0] == 'op' else {node[3]: 0}
                for sub in subs:
                    eng, fn, chain, waits, kind = sub[1:6]
                    assert kind != 'j' and sub[6] == 0
                    if eng == ename:
                        e.wait_ge(G, k)
                        for (sem, vf) in waits:
                            e.wait_ge(sem, vf(iv) if callable(vf) else vf)
                        if kind == 'c':
                            fn(e, iv).then_inc(G, 1)
                        else:
                            fn(e, iv)
                            e.sem_inc(G, 1)
                    k += 1
                continue
            _, n, body, lid, clr = node
            E, S2 = semfn(lid)
            NT = [sum(1 for sub in body if sub[6] == th) for th in range(NTH)]

            def release(first):
                if first:
                    e.wait_ge(G, k)
                else:
                    for th in range(NTH):
                        if NT[th]:
                            e.wait_ge(GS[th], NT[th])
                e.wait_ge(E, 5)
                for gs in GS:
                    e.sem_clear(gs)
                e.sem_clear(E)
                for cs in clr:
                    e.sem_clear(cs)
                e.sem_inc(S2, 1)
            e.sem_inc(E, 1)
            if ename == 'sp':
                release(True)
            with my_fori(nc, e, lregs, 1, n + 1) as i:
                e.wait_ge(S2, i)
                iv = {lid: i - 1}
                prev = [None] * NTH
                cnt = [0] * NTH
                for sub in body:
                    assert sub[0] == 'op'
                    eng, fn, chain, waits, kind, th = sub[1:7]
                    j = cnt[th]
                    if eng == ename:
                        if j > 0 and not (chain and prev[th] is not None and prev[th][1] == ename):
                            e.wait_ge(GS[th], j)
                        if kind == 'j':
                            e.wait_ge(GS[sub[7]], cnt[sub[7]])
                            e.sem_inc(GS[th], 1)
                        else:
                            for (sem, vf) in waits:
                                e.wait_ge(sem, vf({lid: 0}) if callable(vf) else vf)
                            if kind == 'c':
                                fn(e, iv).then_inc(GS[th], 1)
                            else:
                                fn(e, iv)
                                e.sem_inc(GS[th], 1)
                    prev[th] = sub
                    cnt[th] += 1
                e.sem_inc(E, 1)
                if ename == 'sp':
                    release(False)
            e.wait_ge(S2, n + 1)
            if ename == 'sp':
                e.sem_inc(G, k + 1)
            k += 1


def run_prog(nc, P, G):
    engs = {'pe': 'tensor', 'act': 'scalar', 'dve': 'vector', 'pool': 'gpsimd', 'sp': 'sync'}
    with ExitStack() as sctx:
        GX = [sctx.enter_context(nc.semaphore(f"GX{i}")) for i in range(3)]
        lsems = {}
        for node in P.root:
            if node[0] == 'loop' and node[1] > 1:
                lid = node[3]
                lsems[lid] = tuple(sctx.enter_context(nc.semaphore(f"L{lid}_{nm}")) for nm in ("E", "S"))
        with nc.Block() as block:
            for ename, bname in engs.items():
                def mk(ename=ename):
                    def f(e):
                        P.emit(ename, e, G, lambda lid: lsems[lid], nc, GX)
                    return f
                getattr(block, bname)(mk())


class RL:
    pass


def rl_alloc(nc, ctx):
    R = RL()
    R.nc = nc
    R.WBUF = ctx.enter_context(nc.sbuf_tensor("WBUF", [128, 67584], BF16))
    R.SCRF = ctx.enter_context(nc.sbuf_tensor("SCRF", [128, 8704], F32))
    R.SCRB = ctx.enter_context(nc.sbuf_tensor("SCRB", [128, 15360], BF16))
    R.identf = ctx.enter_context(nc.sbuf_tensor("identf", [128, 128], F32))
    R.identb = ctx.enter_context(nc.sbuf_tensor("identb", [128, 128], BF16))
    R.stats = ctx.enter_context(nc.sbuf_tensor("stats", [128, 12], F32))
    R.mv = ctx.enter_context(nc.sbuf_tensor("mv", [128, 2], F32))
    R.rstd = ctx.enter_context(nc.sbuf_tensor("rstd", [128, 1], F32))
    R.psw = [ctx.enter_context(nc.psum_tensor(f"psw{i}", [128, 1024], F32)) for i in range(3)]
    R.ps = []
    for i in range(3):
        R.ps += [R.psw[i][:, 0:512], R.psw[i][:, 512:1024]]
    R.psb = [ctx.enter_context(nc.psum_tensor(f"psb{i}", [128, 1024], BF16)) for i in range(2)]
    R.G = ctx.enter_context(nc.semaphore("G"))
    R.csem = ctx.enter_context(nc.semaphore("csem"))
    R.nsem = 0
    R.ctx = ctx
    return R


def new_sem(R, name):
    R.nsem += 1
    return R.ctx.enter_context(R.nc.semaphore(f"{name}_{R.nsem}"))


def rl_consts(P, R, identf_d, identb_d):
    def f(e, iv):
        e.dma_start(out=R.identf[:], in_=identf_d[:, :]).then_inc(R.csem, 16)
        e.dma_start(out=R.identb[:], in_=identb_d[:, :]).then_inc(R.csem, 16)
    P.dma('sp', f)
    P.dma('sp', lambda e, iv: None, waits=[(R.csem, 32)])


def ln_ops(P, R, u, gbc, bbc, yo, pre_waits=()):
    for h in range(2):
        P.op('dve', lambda e, iv, h=h: e.bn_stats(out=R.stats[:, h * 6:(h + 1) * 6], in_=u[:, h * 512:(h + 1) * 512]))
    P.op('dve', lambda e, iv: e.bn_aggr(out=R.mv[:], in_=R.stats[:]))
    P.op('dve', lambda e, iv: e.tensor_scalar(out=R.rstd[:], in0=R.mv[:, 1:2], scalar1=LN_EPS, scalar2=None, op0=ALU.add))
    P.op('act', lambda e, iv: e.activation(out=R.rstd[:], in_=R.rstd[:], func=AF.Sqrt))
    P.op('dve', lambda e, iv: e.reciprocal(out=R.rstd[:], in_=R.rstd[:]))
    P.op('dve', lambda e, iv: e.tensor_scalar(out=u, in0=u, scalar1=R.mv[:, 0:1], scalar2=R.rstd[:, 0:1],
                                              op0=ALU.subtract, op1=ALU.mult))
    P.op('pool', lambda e, iv: e.tensor_tensor(out=u, in0=u, in1=gbc, op=ALU.mult))
    P.op('pool', lambda e, iv: e.tensor_tensor(out=yo, in0=u, in1=bbc, op=ALU.add), waits=pre_waits)


def load_ln_params(P, R, g_d, b_d, gbc, bbc):
    sem = new_sem(R, "lnp")

    def f(e, iv):
        e.dma_start(out=gbc, in_=g_d.partition_broadcast(128)).then_inc(sem, 16)
        e.dma_start(out=bbc, in_=b_d.partition_broadcast(128)).then_inc(sem, 16)
    P.dma('sp', f)
    P.dma('sp', lambda e, iv: None, waits=[(sem, 32)])


def transposes_to_xT(P, R, xs4_loader, xT, nj=4):
    for j in range(nj):
        xs = xs4_loader(j)
        for c2 in range(2):
            for c in range(4):
                cc = c2 * 4 + c
                P.op('pe', lambda e, iv, cc=cc, c=c, xs=xs: e.transpose(
                    out=R.ps[0][:, c * 128:(c + 1) * 128], in_=xs[:, cc * 128:(cc + 1) * 128], identity=R.identf[:]),
                    chain=(c > 0))
            P.op('act', lambda e, iv, c2=c2, j=j: e.activation(
                out=xT[:, c2 * 4:(c2 + 1) * 4, j * 128:(j + 1) * 128],
                in_=R.ps[0][:, :].rearrange("p (c t) -> p c t", c=4), func=AF.Copy))


def stage_ffnln(P, R, x_in, x_out, wg_d, wu_d, wd_d, g_d, b_d, nsb=16):
    ntiles = nsb * 4
    wg = R.WBUF[:, 0:22528].rearrange("p (c f) -> p c f", c=8)
    wu = R.WBUF[:, 22528:45056].rearrange("p (c f) -> p c f", c=8)
    wd = R.WBUF[:, 45056:67584].rearrange("p (c n) -> p c n", c=NF)
    xs = R.SCRF[:, 0:1024]
    u = R.SCRF[:, 1024:2048]
    xa = R.SCRF[:, 2048:3072]
    yo = R.SCRF[:, 3072:4096]
    sg = R.SCRF[:, 4096:4608]
    gbc = R.SCRF[:, 4608:5632]
    bbc = R.SCRF[:, 5632:6656]
    xT = R.SCRB[:, 0:4096].rearrange("p (c t) -> p c t", c=8)
    aT = R.SCRB[:, 4096:15360].rearrange("p (f t) -> p f t", f=NF)
    wsem = new_sem(R, "w")
    dsx = new_sem(R, "dsx")
    dso = new_sem(R, "dso")

    def loadw(e, iv):
        for c in range(8):
            e.dma_start(out=wg[:, c, :].rearrange("p (h n) -> p h n", h=2),
                        in_=wg_d[c * 128:(c + 1) * 128, :].rearrange("p (h n) -> p h n", h=2)).then_inc(wsem, 16)
            e.dma_start(out=wu[:, c, :].rearrange("p (h n) -> p h n", h=2),
                        in_=wu_d[c * 128:(c + 1) * 128, :].rearrange("p (h n) -> p h n", h=2)).then_inc(wsem, 16)
        for f in range(NF):
            e.dma_start(out=wd[:, f, :], in_=wd_d[f * 128:(f + 1) * 128, :]).then_inc(wsem, 16)
    P.dma('pool', loadw)
    P.dma('pool', lambda e, iv: None, waits=[(wsem, 16 * (16 + NF))])
    load_ln_params(P, R, g_d, b_d, gbc, bbc)

    with P.loop(ntiles, clear=(dsx, dso)) as L:
        def loader(j):
            P.dma('sp', lambda e, iv: e.dma_start(out=xs, in_=x_in[bass.ts(iv[L], 128), :]).then_inc(dsx, 16))
            P.dma('sp', lambda e, iv: None, waits=[(dsx, 16)])
            return xs
        transposes_to_xT(P, R, loader, xT, nj=1)
        for g0 in range(0, NF, 4):
            gw = min(4, NF - g0)
            first = True
            for gi in range(gw):
                f = g0 + gi
                for c in range(8):
                    P.op('pe', lambda e, iv, f=f, c=c, gi=gi: e.matmul(
                        R.ps[1][:, gi * 128:(gi + 1) * 128], lhsT=wg[:, c, f * 128:(f + 1) * 128], rhs=xT[:, c, 0:128],
                        start=(c == 0), stop=(c == 7)), chain=not first)
                    first = False
            for gi in range(gw):
                f = g0 + gi
                for c in range(8):
                    P.op('pe', lambda e, iv, f=f, c=c, gi=gi: e.matmul(
                        R.ps[2][:, gi * 128:(gi + 1) * 128], lhsT=wu[:, c, f * 128:(f + 1) * 128], rhs=xT[:, c, 0:128],
                        start=(c == 0), stop=(c == 7)), chain=True)
            P.op('act', lambda e, iv, gw=gw: e.activation(out=sg[:, 0:gw * 128], in_=R.ps[1][:, 0:gw * 128], func=AF.Silu))
            P.op('dve', lambda e, iv, gw=gw, g0=g0: e.tensor_tensor(
                out=aT[:, g0:g0 + gw, 0:128], in0=sg[:, 0:gw * 128].rearrange("p (g t) -> p g t", g=gw),
                in1=R.ps[2][:, 0:gw * 128].rearrange("p (g t) -> p g t", g=gw), op=ALU.mult))
        first = True
        for h in range(2):
            for f in range(NF):
                P.op('pe', lambda e, iv, f=f, h=h: e.matmul(
                    R.ps[3 + h][:], lhsT=aT[:, f, 0:128], rhs=wd[:, f, h * 512:(h + 1) * 512],
                    start=(f == 0), stop=(f == NF - 1)), chain=not first)
                first = False
        P.op('act', lambda e, iv: e.mul(out=xa, in_=xs, mul=ALPHA))
        for h in range(2):
            P.op('dve', lambda e, iv, h=h: e.scalar_tensor_tensor(
                out=u[:, h * 512:(h + 1) * 512], in0=R.ps[3 + h][:], scalar=0.5, in1=xa[:, h * 512:(h + 1) * 512],
                op0=ALU.mult, op1=ALU.add))
        ln_ops(P, R, u, gbc, bbc, yo)
        P.dma('sp', lambda e, iv: e.dma_start(out=x_out[bass.ts(iv[L], 128), :], in_=yo).then_inc(dso, 16))
        P.dma('sp', lambda e, iv: None, waits=[(dso, 16)])


QK_COLS = [0, 128, 256, 384, 512, 640,
           1152, 1280, 1408, 1536, 1664, 1792,
           2304, 2432, 2560, 2688, 2816, 2944]
V_COLS = [768, 1920, 3072]
G_COL = 3456


def stage_win(P, R, x_in, win_d, bgate_d, qkT_out, qm32_out, kmean_out, v_out, gates_out, nsb=16):
    ntiles = nsb * 4
    win = R.WBUF[:, 0:52224].rearrange("p (c n) -> p c n", c=8)
    xs = R.SCRF[:, 0:1024]
    gt = R.SCRF[:, 1024:4096]
    q32 = R.SCRF[:, 4096:4480].rearrange("p (r t) -> p r t", r=3)
    bgb = R.SCRF[:, 5632:8704]
    xT = R.SCRB[:, 0:4096].rearrange("p (c t) -> p c t", c=8)
    qkst = R.SCRB[:, 4096:6400].rearrange("p (r t) -> p r t", r=18)
    vst = R.SCRB[:, 13312:14464]
    km = R.stats[:, 0:3]
    wsem = new_sem(R, "w")
    dsx = new_sem(R, "dsx")
    dsq = new_sem(R, "dsq")

    def loadw(e, iv):
        for c in range(8):
            e.dma_start(out=win[:, c, :].rearrange("p (h n) -> p h n", h=4),
                        in_=win_d[c * 128:(c + 1) * 128, :].rearrange("p (h n) -> p h n", h=4)).then_inc(wsem, 16)
    P.dma('pool', loadw)
    P.dma('pool', lambda e, iv: None, waits=[(wsem, 16 * 8)])
    bsem = new_sem(R, "bg")
    P.dma('sp', lambda e, iv: e.dma_start(
        out=bgb, in_=bgate_d.rearrange("a d -> (a d)").partition_broadcast(128)).then_inc(bsem, 16))
    P.dma('sp', lambda e, iv: None, waits=[(bsem, 16)])

    with P.loop(ntiles, clear=(dsx, dsq)) as L:
        def loader(j):
            P.dma('act', lambda e, iv: e.dma_start(out=xs, in_=x_in[bass.ts(iv[L], 128), :]).then_inc(dsx, 16))
            P.dma('act', lambda e, iv: None, waits=[(dsx, 16)])
            return xs
        transposes_to_xT(P, R, loader, xT, nj=1)
        for r in range(18):
            co = QK_COLS[r]
            for c in range(8):
                P.op('pe', lambda e, iv, c=c, co=co: e.matmul(R.ps[1][:, 0:128], lhsT=win[:, c, co:co + 128], rhs=xT[:, c, 0:128],
                                                              start=(c == 0), stop=(c == 7)), chain=(c > 0))
            P.op('act', lambda e, iv, r=r: e.copy(out=qkst[:, r, :], in_=R.ps[1][:, 0:128]))
            if r < 3:
                P.op('dve', lambda e, iv, r=r: e.tensor_copy(out=q32[:, r, :], in_=R.ps[1][:, 0:128]))
            if 3 <= r < 6:
                P.op('dve', lambda e, iv, r=r: e.tensor_reduce(out=km[:, r - 3:r - 2], in_=R.ps[1][:, 0:128], axis=AX.X, op=ALU.add))

        def stq(e, iv):
            e.dma_start(out=qkT_out[:, bass.ts(iv[L], 128)].rearrange("(r p) t -> p r t", p=128),
                        in_=qkst).then_inc(dsq, 16)
            e.dma_start(out=qm32_out[:, bass.ts(iv[L], 128)].rearrange("(r p) t -> p r t", p=128),
                        in_=q32).then_inc(dsq, 16)
            e.dma_start(out=kmean_out[:, bass.ts(iv[L], 1)].rearrange("(r p) b -> p r b", p=128),
                        in_=km.rearrange("p (r b) -> p r b", b=1), allow_slow_non_contiguous=True).then_inc(dsq, 16)
        P.dma('act', stq)
        for vi, co in enumerate(V_COLS):
            for c in range(8):
                P.op('pe', lambda e, iv, c=c, co=co: e.matmul(
                    R.ps[2][:, 0:384], lhsT=xT[:, c, 0:128], rhs=win[:, c, co:co + 384],
                    start=(c == 0), stop=(c == 7)), chain=(c > 0))
            P.op('act', lambda e, iv, vi=vi: e.copy(out=vst[:, vi * 384:(vi + 1) * 384], in_=R.ps[2][:, 0:384]))
        for gi in range(6):
            co = G_COL + gi * 512
            for c in range(8):
                P.op('pe', lambda e, iv, c=c, co=co: e.matmul(
                    R.ps[3][:], lhsT=xT[:, c, 0:128], rhs=win[:, c, co:co + 512],
                    start=(c == 0), stop=(c == 7)), chain=(c > 0))
            P.op('dve', lambda e, iv, gi=gi: e.tensor_tensor(out=gt[:, gi * 512:(gi + 1) * 512], in0=R.ps[3][:],
                                                             in1=bgb[:, gi * 512:(gi + 1) * 512], op=ALU.add))
        P.op('act', lambda e, iv: e.activation(out=gt, in_=gt, func=AF.Sigmoid))
        P.dma('act', lambda e, iv: e.dma_start(out=v_out[bass.ts(iv[L], 128), :], in_=vst).then_inc(dsq, 16))
        P.dma('act', lambda e, iv: e.dma_start(out=gates_out[bass.ts(iv[L], 128), :], in_=gt).then_inc(dsq, 16))
        P.dma('act', lambda e, iv: None, waits=[(dsq, 80)])


BIG = 30000.0
SLOPES = (2.0 ** (-8.0 * np.arange(1, 13, dtype=np.float32) / 12)).astype(np.float32)
SC_M = 64 ** -0.5
SC_D = 32 ** -0.5
SC_S = 64 ** -0.5
SUBLN_EPS = 1e-5


def att_tables():
    bf = ml_dtypes.bfloat16
    i = np.arange(SEQ)
    ib = (i // 128).astype(np.float32)
    il = (i % 128).astype(np.float32)
    one = np.ones(SEQ, np.float32)

    def hl(v):
        hi = np.float32(np.asarray(v, np.float32).astype(bf).astype(np.float32))
        lo = np.float32(np.asarray(v - hi, np.float32).astype(bf).astype(np.float32))
        return hi, lo
    qaug = np.zeros((12, 8, SEQ), np.float32)
    kaug = np.zeros((12, 8, SEQ), np.float32)
    for slot in range(12):
        scale = SC_M if slot % 2 == 0 else SC_D
        sig = float(SLOPES[slot]) / scale
        Sh, Sl = hl(128.0 * sig)
        sh, sl = hl(sig)
        qaug[slot] = np.stack([Sh * one, Sl * one, sh * one, sl * one, -ib, -ib, -il, -il])
        kaug[slot] = np.stack([ib, ib, il, il, Sh * one, Sl * one, sh * one, sl * one])
    onehot = (np.arange(32)[:, None] == (i // 256)[None, :]).astype(np.float32)
    p = np.arange(128)[:, None]
    c = np.arange(512)[None, :]
    maskT = np.concatenate([np.where((r * 128 + p) > c, -BIG, 0.0) for r in range(4)], axis=1)
    maskS = np.concatenate([np.where(c >= (r * 128 + p), -BIG, 0.0) for r in range(4)], axis=1)
    o = np.arange(32)[:, None]
    n = np.arange(32)[None, :]
    pb = np.stack([np.where(n < o, 0.0, -BIG), np.where(n == o, 0.0, -BIG), np.where(n <= o, 0.0, -BIG)]).astype(np.float32)
    return dict(qaug=qaug.reshape(96, SEQ).astype(bf), kaug=kaug.reshape(96, SEQ).astype(bf), onehot=onehot.astype(bf),
                maskT=maskT.astype(bf), maskS=maskS.astype(bf), pb=pb.reshape(-1))


def stage_att(P, R, T, qkT_d, qm32_d, kmean_d, v_d, o_d, lam_d, subg_d, one_m_linit, nheads=6, do=('m', 'd', 's')):
    W = R.WBUF
    GW = nheads * 64
    QA = [W[:, 0:8192], W[:, 8192:16384]]
    KA = [W[:, 16384:24576], W[:, 24576:32768]]
    VA = W[:, 32768:40960].rearrange("p (k d) -> p k d", k=64)
    VAf = W[:, 32768:40960]
    maskT = W[:, 40960:43008]
    maskS = W[:, 43008:45056]
    pt = [W[:, 45056:45568], W[:, 45568:46080]]
    wb = W[:, 46080:46592]
    wT = W[:, 46592:47104].rearrange("p (k q) -> p k q", k=4)
    FW = W[:, 47360:67584].bitcast(F32)
    pbt = FW[:, 0:3072].rearrange("p (a o n) -> p a o n", a=3, o=32)
    osb = [FW[:, 3072:3584], FW[:, 3584:4096]]
    gm = FW[:, 4096:4128]
    sel = FW[:, 4128:4160]
    m8 = FW[:, 4160:4168]
    rinv = [FW[:, 4168:4172], FW[:, 4172:4176]]
    ss = FW[:, 4176:4180]
    carry = FW[:, 4180:4181]
    negc = FW[:, 4181:4182]
    tot = FW[:, 4182:4183]
    lamt = FW[:, 4183:4184]
    lsc = FW[:, 4184:4186]
    t1 = FW[:, 4192:4256]
    od = FW[:, 4256:4512].rearrange("p (j d) -> p j d", j=4)
    gsc = FW[:, 4512:4576]
    lamw = FW[:, 4576:4704]
    lamw2 = FW[:, 4704:4768]
    ones = FW[:, 4768:5280]
    tt = FW[:, 5280:5792]
    spb = FW[:, 5792:6305]
    E1 = FW[:, 6336:6848]
    aa = FW[:, 6848:7360]
    kme = FW[:, 7360:7392]
    kmr = FW[:, 9472:9536]
    ost_all = FW[:, 7424:9472].bitcast(BF16).rearrange("p (t d) -> p t d", t=64)
    qm32 = R.SCRF[:, 0:8192]
    ps_s = [R.ps[0], R.ps[1]]
    ps_o = [R.ps[2], R.ps[3]]
    ps_t = [R.ps[4], R.ps[5]]
    psb = R.psb[0]
    tsem = new_sem(R, "att_t")
    hsem = new_sem(R, "att_h")
    osem = new_sem(R, "att_o")
    cnt = {'h': 0, 'o': 0}

    def ldc(e, iv):
        e.dma_start(out=maskT, in_=T['maskT'][:, :]).then_inc(tsem, 16)
        e.dma_start(out=maskS, in_=T['maskS'][:, :]).then_inc(tsem, 16)
        e.dma_start(out=FW[:, 0:3072], in_=T['pb'].partition_broadcast(128)).then_inc(tsem, 16)
        e.dma_start(out=gsc, in_=subg_d.partition_broadcast(128)).then_inc(tsem, 16)
        e.dma_start(out=lamw, in_=lam_d.rearrange("a d -> (a d)").partition_broadcast(128)).then_inc(tsem, 16)
    P.dma('sp', ldc)
    P.dma('sp', lambda e, iv: None, waits=[(tsem, 80)])
    P.op('dve', lambda e, iv: e.memset(ones, 1.0))
    P.op('dve', lambda e, iv: e.memset(spb[:, 0:1], 0.0))
    P.op('dve', lambda e, iv: e.memset(VA[:, :, 64:65], 1.0))
    linit = 1.0 - one_m_linit
    P.op('dve', lambda e, iv: e.tensor_tensor(out=lamw2[:, 0:32], in0=lamw[:, 0:32], in1=lamw[:, 32:64], op=ALU.mult))
    P.op('dve', lambda e, iv: e.tensor_tensor(out=lamw2[:, 32:64], in0=lamw[:, 64:96], in1=lamw[:, 96:128], op=ALU.mult))
    P.op('dve', lambda e, iv: e.tensor_reduce(out=lsc, in_=lamw2.rearrange("p (a d) -> p a d", a=2), axis=AX.X, op=ALU.add))
    P.op('act', lambda e, iv: e.activation(out=lsc, in_=lsc, func=AF.Exp))
    P.op('dve', lambda e, iv: e.tensor_tensor(out=lamt, in0=lsc[:, 0:1], in1=lsc[:, 1:2], op=ALU.subtract))
    P.op('dve', lambda e, iv: e.tensor_scalar(out=lamt, in0=lamt, scalar1=linit, scalar2=-1.0, op0=ALU.add, op1=ALU.mult))
    P.op('act', lambda e, iv: e.mul(out=gsc, in_=gsc, mul=one_m_linit))

    def dense_tile(s, Kd, scale, kb_ap_K, kb_ap_V, Q, mask_r, first):
        P.op('pe', lambda e, iv: e.matmul(ps_s[s][:], lhsT=kb_ap_K(iv), rhs=QA[s][0:Kd, Q * 512:(Q + 1) * 512],
                                          start=True, stop=(mask_r is None)))
        if mask_r is not None:
            P.op('pe', lambda e, iv: e.matmul(ps_s[s][:], lhsT=R.identb[:], rhs=maskT[:, mask_r * 512:(mask_r + 1) * 512],
                                              start=False, stop=True), chain=True)
        P.op('act', lambda e, iv: e.activation(out=pt[s], in_=ps_s[s][:], func=AF.Exp, scale=scale))
        P.op('pe', lambda e, iv: e.matmul(ps_o[s][0:65, :], lhsT=kb_ap_V(iv), rhs=pt[s], start=first, stop=True))

    def dense_Q(streams, Q):
        for r in range(4):
            kb = 4 * Q + r
            for (s, Kd, scale) in streams:
                dense_tile(s, Kd, scale, lambda iv, s=s, Kd=Kd, kb=kb: KA[s][0:Kd, kb * 128:(kb + 1) * 128],
                           lambda iv, kb=kb: VA[:, kb, 0:65], Q, r, r == 0)
        for kb in range(4 * Q):
            for (s, Kd, scale) in streams:
                dense_tile(s, Kd, scale, lambda iv, s=s, Kd=Kd, kb=kb: KA[s][0:Kd, kb * 128:(kb + 1) * 128],
                           lambda iv, kb=kb: VA[:, kb, 0:65], Q, None, False)
        for (s, Kd, scale) in streams:
            P.op('act', lambda e, iv, s=s: e.copy(out=osb[s][0:65, :], in_=ps_o[s][0:65, :]))
            for j in range(4):
                P.op('pe', lambda e, iv, s=s, j=j: e.transpose(out=ps_t[s][:, j * 65:(j + 1) * 65],
                                                               in_=osb[s][0:65, j * 128:(j + 1) * 128],
                                                               identity=R.identf[0:65, 0:65]), chain=(j > 0))
            P.op('dve', lambda e, iv, s=s: e.reciprocal(
                out=rinv[s], in_=ps_t[s][:, 0:260].rearrange("p (j d) -> p j d", j=4)[:, :, 64]))

    def store_head(hfn, dq):
        P.dma(dq, lambda e, iv: e.dma_start(
            out=o_d[bass.ts(hfn(iv), 1), :, :].rearrange("h (t p) d -> p (h t) d", p=128),
            in_=ost_all).then_inc(osem, 16))
        P.dma(dq, lambda e, iv: None, waits=[(osem, 16)])

    NTD = 3
    Q_ASSIGN = [[15, 10, 9, 4, 3], [14, 11, 8, 5, 2], [13, 12, 7, 6, 1, 0]]
    SB16 = R.SCRB
    d_pt = [SB16[:, 2624 * i:2624 * i + 512] for i in range(NTD)]
    d_f = [SB16[:, 2624 * i + 512:2624 * i + 2624].bitcast(F32) for i in range(NTD)]
    d_osb = [d_f[i][:, 0:512] for i in range(NTD)]
    d_t1 = [d_f[i][:, 512:768].rearrange("p (j d) -> p j d", j=4) for i in range(NTD)]
    d_od = [d_f[i][:, 768:1024].rearrange("p (j d) -> p j d", j=4) for i in range(NTD)]
    d_r0 = [d_f[i][:, 1024:1028] for i in range(NTD)]
    d_r1 = [d_f[i][:, 1028:1032] for i in range(NTD)]
    d_ss = [d_f[i][:, 1032:1036] for i in range(NTD)]
    d_sq = [d_f[i][:, 1036:1040] for i in range(NTD)]
    d_pss = [R.ps[0], R.ps[1], R.ps[2]]
    d_pso = [R.ps[3], R.ps[4], R.ps[5]]

    def dense_map_ops(th, m, Kd, scale, Q):
        ops = []
        kbs = [(4 * Q + r, r) for r in range(4)] + [(kb, None) for kb in range(4 * Q)]
        for idx, (kb, mr) in enumerate(kbs):
            ops.append(('pe', lambda e, iv, kb=kb, mr=mr: e.matmul(
                d_pss[th][:], lhsT=KA[m][0:Kd, kb * 128:(kb + 1) * 128], rhs=QA[m][0:Kd, Q * 512:(Q + 1) * 512],
                start=True, stop=(mr is None)), False))
            if mr is not None:
                ops.append(('pe', lambda e, iv, mr=mr: e.matmul(d_pss[th][:], lhsT=R.identb[:],
                                                                rhs=maskT[:, mr * 512:(mr + 1) * 512],
                                                                start=False, stop=True), True))
            ops.append(('act', lambda e, iv: e.activation(out=d_pt[th], in_=d_pss[th][:], func=AF.Exp, scale=scale), False))
            ops.append(('pe', lambda e, iv, kb=kb, idx=idx: e.matmul(d_pso[th][0:65, :], lhsT=VA[:, kb, 0:65], rhs=d_pt[th],
                                                                     start=(idx == 0), stop=True), False))
        ops.append(('act', lambda e, iv: e.copy(out=d_osb[th][0:65, :], in_=d_pso[th][0:65, :]), False))
        for j in range(4):
            ops.append(('pe', lambda e, iv, j=j: e.transpose(out=d_pss[th][:, j * 65:(j + 1) * 65],
                                                             in_=d_osb[th][0:65, j * 128:(j + 1) * 128],
                                                             identity=R.identf[0:65, 0:65]), j > 0))
        return ops

    def psT(th, j):
        return d_pss[th][:, j * 65:j * 65 + 64]

    def rsum(th):
        return d_pss[th][:, 0:260].rearrange("p (j d) -> p j d", j=4)[:, :, 64]

    def emit_threads(lists, shift=0):
        n = len(lists)
        for idx in range(max(len(x) for x in lists) + shift * (n - 1)):
            for th in range(n):
                j = idx - th * shift
                if 0 <= j < len(lists[th]):
                    eng, fn, ch = lists[th][j]
                    P.op(eng, fn, chain=ch, thread=th)

    if 'm' in do:
        with P.loop(nheads, clear=(hsem, osem)) as LHM:
            hbaseM = 0

            def ldm(e, iv):
                h = iv[LHM]
                e.dma_start(out=QA[0][32:96, :], in_=qkT_d[0:GW, :][bass.ts(h, 64), :]).then_inc(hsem, 16)
                e.dma_start(out=KA[0][32:96, :], in_=qkT_d[GW:2 * GW, :][bass.ts(h, 64), :]).then_inc(hsem, 16)
                e.dma_start(out=KA[0][0:32, :], in_=T['onehot'][:, :]).then_inc(hsem, 16)
                e.dma_start(out=QA[0][96:104, :], in_=T['qaug'][bass.ts(h * 2, 8), :]).then_inc(hsem, 16)
                e.dma_start(out=KA[0][96:104, :], in_=T['kaug'][bass.ts(h * 2, 8), :]).then_inc(hsem, 16)
                e.dma_start(out=VA[:, :, 0:64],
                            in_=v_d[:, 0:GW][:, bass.ts(h, 64)].rearrange("(k p) d -> p k d", p=128)).then_inc(hsem, 16)
                e.dma_start(out=qm32[0:64, :], in_=qm32_d[bass.ts(h, 64), :]).then_inc(hsem, 16)
                e.dma_start(out=kmr[0:64, :], in_=kmean_d[bass.ts(h, 64), :]).then_inc(hsem, 16)
            P.dma('sp', ldm)
            P.dma('sp', lambda e, iv: None, waits=[(hsem, lambda iv: (hbaseM + (iv[LHM] + 1) * 8) * 16)])
            P.op('dve', lambda e, iv: e.tensor_reduce(out=kme[0:64, :], in_=kmr[0:64, :].rearrange("p (n b) -> p n b", b=2),
                                                      axis=AX.X, op=ALU.add))
            P.op('dve', lambda e, iv: e.tensor_scalar(out=kme[0:64, :], in0=kme[0:64, :], scalar1=1.0 / 256.0, scalar2=None,
                                                      op0=ALU.mult))
            for t4 in range(16):
                for tj in range(4):
                    t = t4 * 4 + tj
                    own = t // 2
                    P.op('pe', lambda e, iv, t=t: e.matmul(ps_s[1][:, 0:32], lhsT=qm32[0:64, t * 128:(t + 1) * 128],
                                                           rhs=kme[0:64, :], start=True, stop=True))
                    P.op('dve', lambda e, iv, own=own: e.tensor_tensor(out=gm, in0=ps_s[1][:, 0:32], in1=pbt[:, 0, own, :],
                                                                       op=ALU.add))
                    P.op('dve', lambda e, iv: e.max(out=m8, in_=gm))
                    P.op('dve', lambda e, iv: e.tensor_scalar(out=sel, in0=gm, scalar1=m8[:, 2:3], scalar2=1.0,
                                                              op0=ALU.is_ge, op1=ALU.subtract))
                    P.op('dve', lambda e, iv, own=own: e.scalar_tensor_tensor(out=sel, in0=sel, scalar=BIG,
                                                                              in1=pbt[:, 1, own, :], op0=ALU.mult, op1=ALU.max))
                    P.op('dve', lambda e, iv, own=own: e.tensor_tensor(out=sel, in0=sel, in1=pbt[:, 2, own, :], op=ALU.add))
                    P.op('pe', lambda e, iv, tj=tj: e.transpose(out=ps_t[0][0:32, tj * 128:(tj + 1) * 128], in_=sel,
                                                                identity=R.identf[:]))
                P.op('act', lambda e, iv, t4=t4: e.copy(out=QA[0][0:32, t4 * 512:(t4 + 1) * 512], in_=ps_t[0][0:32, :]))
            for th in range(1, NTD):
                P.join('pe', th, 0)
            lists = []
            for th in range(NTD):
                ops = []
                for Q in Q_ASSIGN[th]:
                    ops += dense_map_ops(th, 0, 104, SC_M, Q)
                    ops.append(('dve', (lambda e, iv, th=th: e.reciprocal(out=d_r0[th], in_=rsum(th))), False))
                    for j in range(4):
                        ops.append(('dve', (lambda e, iv, th=th, j=j, Q=Q: e.tensor_scalar(
                            out=ost_all[:, Q * 4 + j, :], in0=psT(th, j), scalar1=d_r0[th][:, j:j + 1], scalar2=None,
                            op0=ALU.mult)), False))
                lists.append(ops)
            emit_threads(lists, 2)
            for th in range(1, NTD):
                P.join('sp', 0, th)
            store_head(lambda iv: iv[LHM], 'sp')
    obase_d = 0

    if 'd' in do:
        with P.loop(nheads, clear=(hsem, osem)) as LHD:
            hbaseD = 0

            def ldd(e, iv):
                h = iv[LHD]
                for m in range(2):
                    e.dma_start(out=QA[m][0:32, :], in_=qkT_d[2 * GW:3 * GW, :][bass.ts(h * 2 + m, 32), :]).then_inc(hsem, 16)
                    e.dma_start(out=KA[m][0:32, :], in_=qkT_d[3 * GW:4 * GW, :][bass.ts(h * 2 + m, 32), :]).then_inc(hsem, 16)
                    e.dma_start(out=QA[m][32:40, :], in_=T['qaug'][bass.ts(h * 2 + 1, 8), :]).then_inc(hsem, 16)
                    e.dma_start(out=KA[m][32:40, :], in_=T['kaug'][bass.ts(h * 2 + 1, 8), :]).then_inc(hsem, 16)
                e.dma_start(out=VA[:, :, 0:64],
                            in_=v_d[:, GW:2 * GW][:, bass.ts(h, 64)].rearrange("(k p) d -> p k d", p=128)).then_inc(hsem, 16)
            P.dma('act', ldd)
            P.dma('act', lambda e, iv: None, waits=[(hsem, lambda iv: (hbaseD + (iv[LHD] + 1) * 9) * 16)])
            for th in range(1, NTD):
                P.join('pe', th, 0)
            lists = []
            for th in range(NTD):
                ops = []
                for Q in Q_ASSIGN[th]:
                    ops += dense_map_ops(th, 0, 40, SC_D, Q)
                    ops.append(('dve', (lambda e, iv, th=th: e.reciprocal(out=d_r0[th], in_=rsum(th))), False))
                    for j in range(4):
                        ops.append(('dve', (lambda e, iv, th=th, j=j: e.tensor_scalar(
                            out=d_t1[th][:, j, :], in0=psT(th, j), scalar1=d_r0[th][:, j:j + 1], scalar2=None,
                            op0=ALU.mult)), False))
                    ops += dense_map_ops(th, 1, 40, SC_D, Q)
                    ops.append(('dve', (lambda e, iv, th=th: e.reciprocal(out=d_r1[th], in_=rsum(th))), False))
                    ops.append(('dve', (lambda e, iv, th=th: e.tensor_scalar(out=d_r1[th], in0=d_r1[th], scalar1=lamt[:, 0:1],
                                                                             scalar2=None, op0=ALU.mult)), False))
                    for j in range(4):
                        ops.append(('dve', (lambda e, iv, th=th, j=j: e.scalar_tensor_tensor(
                            out=d_od[th][:, j, :], in0=psT(th, j), scalar=d_r1[th][:, j:j + 1], in1=d_t1[th][:, j, :],
                            op0=ALU.mult, op1=ALU.add)), False))
                    ops.append(('dve', (lambda e, iv, th=th: e.tensor_tensor(out=d_t1[th], in0=d_od[th], in1=d_od[th],
                                                                             op=ALU.mult)), False))
                    ops.append(('dve', (lambda e, iv, th=th: e.tensor_reduce(out=d_ss[th], in_=d_t1[th], axis=AX.X,
                                                                             op=ALU.add)), False))
                    ops.append(('dve', (lambda e, iv, th=th: e.tensor_scalar(out=d_ss[th], in0=d_ss[th], scalar1=1.0 / 64.0,
                                                                             scalar2=SUBLN_EPS, op0=ALU.mult, op1=ALU.add)), False))
                    ops.append(('act', (lambda e, iv, th=th: e.activation(out=d_ss[th], in_=d_ss[th], func=AF.Sqrt)), False))
                    ops.append(('dve', (lambda e, iv, th=th: e.reciprocal(out=d_ss[th], in_=d_ss[th])), False))
                    for j in range(4):
                        ops.append(('dve', (lambda e, iv, th=th, j=j, Q=Q: e.scalar_tensor_tensor(
                            out=ost_all[:, Q * 4 + j, :], in0=d_od[th][:, j, :], scalar=d_ss[th][:, j:j + 1], in1=gsc,
                            op0=ALU.mult, op1=ALU.mult)), False))
                lists.append(ops)
            emit_threads(lists, 2)
            for th in range(1, NTD):
                P.join('sp', 0, th)
            store_head(lambda iv: iv[LHD] + nheads, 'act')
    obase_s = 0

    if 's' in do:
        NTS = 2
        SHIFT_S = 0
        ones13 = FW[:, 4768:5793]
        SF = R.SCRF
        tts = [SF[:, 4200 * i:4200 * i + 1024] for i in range(NTS)]
        spbs = [SF[:, 4200 * i + 1024:4200 * i + 2049] for i in range(NTS)]
        Efs = [SF[:, 4200 * i + 2080:4200 * i + 3105] for i in range(NTS)]
        aas = [SF[:, 4200 * i + 3136:4200 * i + 4160] for i in range(NTS)]
        negcs = [SF[:, 4200 * i + 4160:4200 * i + 4161] for i in range(NTS)]
        wbs_ = [R.SCRB[:, 2048 * i:2048 * i + 1024] for i in range(NTS)]
        wTs = [R.SCRB[:, 2048 * i + 1024:2048 * i + 2048].rearrange("p (k q) -> p k q", k=8) for i in range(NTS)]
        psbs = [R.psb[0], R.psb[1]]
        pszs = [R.psw[0], R.psw[1]]
        psos = [R.psw[2][:, 0:64], R.psw[2][:, 512:576]]
        P.op('dve', lambda e, iv: e.memset(ones13, 1.0))
        for th in range(NTS):
            P.op('dve', lambda e, iv, th=th: e.memset(spbs[th][:, 0:1], 0.0))
        with P.loop(nheads, clear=(hsem, osem)) as LHS:
            def lds(e, iv):
                h = iv[LHS]
                e.dma_start(out=QA[0][0:64, :], in_=qkT_d[4 * GW:5 * GW, :][bass.ts(h, 64), :]).then_inc(hsem, 16)
                e.dma_start(out=KA[0][0:64, :], in_=qkT_d[5 * GW:6 * GW, :][bass.ts(h, 64), :]).then_inc(hsem, 16)
                e.dma_start(out=VA[:, :, 0:64],
                            in_=v_d[:, 2 * GW:3 * GW][:, bass.ts(h, 64)].rearrange("(k p) d -> p k d", p=128)).then_inc(hsem, 16)
            P.dma('sp', lds)
            P.dma('sp', lambda e, iv: None, waits=[(hsem, 48)])
            for th in range(1, NTS):
                P.join('pe', th, 0)

            def sb_step_ops(th, t, k0, nb, mask_r, first):
                Wd = nb * 512
                ops = []
                for bi in range(nb):
                    kt = k0 + bi
                    is_diag = (mask_r is not None and bi == nb - 1)
                    ops.append(('pe', lambda e, iv, bi=bi, kt=kt, is_diag=is_diag: e.matmul(
                        pszs[th][:, bi * 512:(bi + 1) * 512], lhsT=QA[0][0:64, t * 128:(t + 1) * 128],
                        rhs=KA[0][0:64, kt * 512:(kt + 1) * 512], start=True, stop=(not is_diag)), bi > 0))
                    if is_diag:
                        ops.append(('pe', lambda e, iv, bi=bi: e.matmul(
                            pszs[th][:, bi * 512:(bi + 1) * 512], lhsT=R.identb[:],
                            rhs=maskS[:, mask_r * 512:(mask_r + 1) * 512], start=False, stop=True), True))
                ops.append(('act', lambda e, iv: e.activation(out=tts[th][:, 0:Wd], in_=pszs[th][:, 0:Wd], func=AF.Exp, scale=SC_S), False))
                ops.append(('act', lambda e, iv: e.activation(out=spbs[th][:, 1:Wd + 1], in_=tts[th][:, 0:Wd], func=AF.Ln, bias=1.0), False))
                ops.append(('dve', lambda e, iv: e.tensor_tensor_scan(out=Efs[th][:, 0:Wd + 1], data0=ones13[:, 0:Wd + 1],
                                                                      data1=spbs[th][:, 0:Wd + 1], initial=0.0,
                                                                      op0=ALU.mult, op1=ALU.add), False))
                ops.append(('dve', lambda e, iv: e.scalar_tensor_tensor(out=aas[th][:, 0:Wd], in0=pszs[th][:, 0:Wd], scalar=SC_S,
                                                                        in1=Efs[th][:, 0:Wd], op0=ALU.mult, op1=ALU.add), False))
                if first:
                    ops.append(('dve', lambda e, iv: e.tensor_scalar(out=negcs[th], in0=Efs[th][:, Wd:Wd + 1], scalar1=-1.0,
                                                                     scalar2=None, op0=ALU.mult), False))
                else:
                    ops.append(('dve', lambda e, iv: e.tensor_tensor(out=negcs[th], in0=negcs[th], in1=Efs[th][:, Wd:Wd + 1],
                                                                     op=ALU.subtract), False))
                ops.append(('act', lambda e, iv: e.activation(out=wbs_[th][:, 0:Wd], in_=aas[th][:, 0:Wd], func=AF.Exp,
                                                              bias=negcs[th][:, 0:1]), False))
                nk = nb * 4
                for k4 in range(nk):
                    ops.append(('pe', lambda e, iv, k4=k4: e.transpose(out=psbs[th][:, k4 * 128:(k4 + 1) * 128],
                                                                       in_=wbs_[th][:, k4 * 128:(k4 + 1) * 128],
                                                                       identity=R.identb[:]), k4 > 0))
                ops.append(('dve', lambda e, iv: e.tensor_copy(out=wTs[th][:, 0:nk, :],
                                                               in_=psbs[th][:, 0:Wd].rearrange("p (k q) -> p k q", k=nk)), False))
                for k4 in range(nk):
                    ops.append(('pe', lambda e, iv, k4=k4: e.matmul(psos[th], lhsT=wTs[th][:, k4, :],
                                                                    rhs=VA[:, k0 * 4 + k4, 0:64],
                                                                    start=(first and k4 == 0), stop=True), k4 > 0))
                return ops

            for base in range(0, 64, NTS):
                lists = []
                for th in range(NTS):
                    t = base + th
                    kd = t // 4
                    if kd % 2 == 1:
                        ops = sb_step_ops(th, t, kd - 1, 2, t % 4, True)
                        rem = kd - 1
                    else:
                        ops = sb_step_ops(th, t, kd, 1, t % 4, True)
                        rem = kd
                    for k0 in range(rem - 2, -1, -2):
                        ops += sb_step_ops(th, t, k0, 2, None, False)
                    ops.append(('act', (lambda e, iv, t=t, th=th: e.copy(out=ost_all[:, t, :], in_=psos[th])), False))
                    lists.append(ops)
                emit_threads(lists, SHIFT_S)
            for th in range(1, NTS):
                P.join('sp', 0, th)
            store_head(lambda iv: iv[LHS] + 2 * nheads, 'sp')
def stage_mixpost(P, R, x_in, o_d, gates_d, x_out, wbm_d, wbd_d, wbs_d, wout_d, g_d, b_d, ntiles=64):
    W = R.WBUF
    wbr = W[:, 0:9216].rearrange("p (k n) -> p k n", k=9)
    wout = W[:, 9216:17408].rearrange("p (c n) -> p c n", c=8)
    gts = W[:, 17408:23552].bitcast(F32)
    xs = R.SCRF[:, 0:1024]
    u = R.SCRF[:, 1024:2048]
    xa = R.SCRF[:, 2048:3072]
    yo = R.SCRF[:, 3072:4096]
    gbc = R.SCRF[:, 4608:5632]
    bbc = R.SCRF[:, 5632:6656]
    merged = R.SCRF[:, 6656:7680]
    tmp = R.SCRF[:, 7680:8704]
    xT = R.SCRB[:, 0:4096].rearrange("p (c t) -> p c t", c=8)
    ot = R.SCRB[:, 4096:5248]
    oT = R.SCRB[:, 5248:6400].rearrange("p (k t) -> p k t", k=9)
    wsem = new_sem(R, "w")
    dsi = new_sem(R, "dsi")
    dso = new_sem(R, "dso")

    def loadw(e, iv):
        for bi, wd_ in enumerate((wbm_d, wbd_d, wbs_d)):
            for i in range(3):
                e.dma_start(out=wbr[:, bi * 3 + i, :], in_=wd_[i * 128:(i + 1) * 128, :]).then_inc(wsem, 16)
        for c in range(8):
            e.dma_start(out=wout[:, c, :], in_=wout_d[c * 128:(c + 1) * 128, :]).then_inc(wsem, 16)
    P.dma('pool', loadw)
    P.dma('pool', lambda e, iv: None, waits=[(wsem, 16 * 17)])
    load_ln_params(P, R, g_d, b_d, gbc, bbc)
    with P.loop(ntiles, clear=(dsi, dso)) as L:
        def ld(e, iv):
            e.dma_start(out=gts, in_=gates_d[bass.ts(iv[L], 128), :]).then_inc(dsi, 16)
            e.dma_start(out=xs, in_=x_in[bass.ts(iv[L], 128), :]).then_inc(dsi, 16)
        P.dma('sp', lambda e, iv: e.dma_start(out=ot.rearrange("p (h d) -> p h d", h=18),
                                              in_=o_d.rearrange("h s d -> s h d")[bass.ts(iv[L], 128), :, :]).then_inc(dsi, 16))
        P.dma('act', ld)
        P.dma('act', lambda e, iv: None, waits=[(dsi, lambda iv: (iv[L] + 1) * 48)])
        for k in range(9):
            P.op('pe', lambda e, iv, k=k: e.transpose(
                out=(R.psb[0][:, k * 128:(k + 1) * 128] if k < 8 else R.psb[1][:, 0:128]),
                in_=ot[:, k * 128:(k + 1) * 128], identity=R.identb[:]), chain=(k > 0))
        P.op('act', lambda e, iv: e.copy(out=oT[:, 0:8, :], in_=R.psb[0][:, :].rearrange("p (k t) -> p k t", k=8)))
        P.op('act', lambda e, iv: e.copy(out=oT[:, 8, :], in_=R.psb[1][:, 0:128]))
        for br in range(3):
            for h in range(2):
                for i in range(3):
                    P.op('pe', lambda e, iv, br=br, h=h, i=i: e.matmul(
                        R.ps[1][:], lhsT=oT[:, br * 3 + i, :], rhs=wbr[:, br * 3 + i, h * 512:(h + 1) * 512],
                        start=(i == 0), stop=(i == 2)), chain=(i > 0))
                gsl = gts[:, br * 1024 + h * 512: br * 1024 + (h + 1) * 512]
                if br == 0:
                    P.op('dve', lambda e, iv, h=h, gsl=gsl: e.tensor_tensor(out=merged[:, h * 512:(h + 1) * 512], in0=gsl,
                                                                            in1=R.ps[1][:], op=ALU.mult))
                else:
                    P.op('dve', lambda e, iv, h=h, gsl=gsl: e.tensor_tensor(out=tmp[:, h * 512:(h + 1) * 512], in0=gsl,
                                                                            in1=R.ps[1][:], op=ALU.mult))
                    P.op('dve', lambda e, iv, h=h: e.tensor_tensor(out=merged[:, h * 512:(h + 1) * 512],
                                                                   in0=merged[:, h * 512:(h + 1) * 512],
                                                                   in1=tmp[:, h * 512:(h + 1) * 512], op=ALU.add))
        transposes_to_xT(P, R, lambda j: merged, xT, nj=1)
        first = True
        for h in range(2):
            for c in range(8):
                P.op('pe', lambda e, iv, h=h, c=c: e.matmul(R.ps[3 + h][:], lhsT=xT[:, c, 0:128],
                                                            rhs=wout[:, c, h * 512:(h + 1) * 512],
                                                            start=(c == 0), stop=(c == 7)), chain=not first)
                first = False
        P.op('act', lambda e, iv: e.mul(out=xa, in_=xs, mul=ALPHA))
        for h in range(2):
            P.op('dve', lambda e, iv, h=h: e.tensor_tensor(out=u[:, h * 512:(h + 1) * 512], in0=xa[:, h * 512:(h + 1) * 512],
                                                           in1=R.ps[3 + h][:], op=ALU.add))
        ln_ops(P, R, u, gbc, bbc, yo, pre_waits=[(dso, lambda iv: iv[L] * 16)])
        P.dma('act', lambda e, iv: e.dma_start(out=x_out[bass.ts(iv[L], 128), :], in_=yo).then_inc(dso, 16))
        P.dma('act', lambda e, iv: None, waits=[(dso, 16)])


NACT = 4


NCORE = 8
HTOK = 4096


def _nc_dt():
    nc = bass.Bass("TRN2", target_bir_lowering=False)

    def dt(n, s, d=F32, k="ExternalInput"):
        return nc.dram_tensor(n, list(s), d, kind=k).ap()
    return nc, dt


def build_pre():
    nc, dt = _nc_dt()
    x = dt("x", [HTOK, D])
    g = dt("ln_g", [D]); b = dt("ln_b", [D])
    wg = dt("wg", [D, DFF]); wu = dt("wu", [D, DFF]); wd = dt("wd", [DFF, D])
    win = dt("w_in", [D, NIN]); bg = dt("b_gate", [3, D])
    idf = dt("idf", [128, 128]); idb = dt("idb", [128, 128], BF16)
    x1 = dt("x1", [HTOK, D], F32, "ExternalOutput")
    qkT = dt("qkT", [2304, HTOK], BF16, "ExternalOutput")
    qm32 = dt("qm32", [384, HTOK], F32, "ExternalOutput")
    kmean = dt("kmean", [384, HTOK // 128], F32, "ExternalOutput")
    v = dt("v", [HTOK, 1152], BF16, "ExternalOutput")
    gates = dt("gates", [HTOK, 3072], F32, "ExternalOutput")
    with ExitStack() as ctx:
        R = rl_alloc(nc, ctx)
        P = Prog()
        rl_consts(P, R, idf, idb)
        stage_ffnln(P, R, x, x1, wg, wu, wd, g, b, nsb=HTOK // 512)
        stage_win(P, R, x1, win, bg, qkT, qm32, kmean, v, gates, nsb=HTOK // 512)
        run_prog(nc, P, R.G)
    return nc


def build_att(tb_shapes, l):
    bf = ml_dtypes.bfloat16
    nc, dt = _nc_dt()
    NH = 3
    qkT = dt("qkT", [6 * NH * 64, SEQ], BF16)
    qm32 = dt("qm32", [NH * 64, SEQ])
    kmean = dt("kmean", [NH * 64, SEQ // 128])
    v = dt("v", [SEQ, 3 * NH * 64], BF16)
    dl = dt("diff_lambda", [4, 32]); dg = dt("diff_subln_g", [64])
    idf = dt("idf", [128, 128]); idb = dt("idb", [128, 128], BF16)
    T = {k: dt("t_" + k, shp, BF16 if dty == bf else F32) for k, (shp, dty) in tb_shapes.items()}
    o = dt("o", [3 * NH, SEQ, 64], BF16, "ExternalOutput")
    with ExitStack() as ctx:
        R = rl_alloc(nc, ctx)
        P = Prog()
        rl_consts(P, R, idf, idb)
        linit = 0.8 - 0.6 * math.exp(-0.3 * l)
        stage_att(P, R, T, qkT, qm32, kmean, v, o, dl, dg, 1.0 - linit, nheads=NH)
        run_prog(nc, P, R.G)
    return nc


def build_post():
    nc, dt = _nc_dt()
    x1 = dt("x1", [HTOK, D])
    o = dt("o", [18, HTOK, 64], BF16)
    gates = dt("gates", [HTOK, 3072])
    wbm = dt("w_br_moba", [384, D]); wbd = dt("w_br_diff", [384, D]); wbs = dt("w_br_sb", [384, D]); wo = dt("w_out", [D, D])
    g1 = dt("ln_g1", [D]); b1 = dt("ln_b1", [D]); g2 = dt("ln_g2", [D]); b2 = dt("ln_b2", [D])
    wg = dt("wg", [D, DFF]); wu = dt("wu", [D, DFF]); wd = dt("wd", [DFF, D])
    idf = dt("idf", [128, 128]); idb = dt("idb", [128, 128], BF16)
    y = dt("y", [HTOK, D], F32, "ExternalOutput")
    xb = dt("xb_d", [HTOK, D], F32, "Internal")
    with ExitStack() as ctx:
        R = rl_alloc(nc, ctx)
        P = Prog()
        rl_consts(P, R, idf, idb)
        stage_mixpost(P, R, x1, o, gates, xb, wbm, wbd, wbs, wo, g1, b1, ntiles=HTOK // 128)
        stage_ffnln(P, R, xb, y, wg, wu, wd, g2, b2, nsb=HTOK // 512)
        run_prog(nc, P, R.G)
    return nc


def kernel(x, ln_g, ln_b, ffn_w_gate, ffn_w_up, ffn_w_down, w_in, b_gate, diff_lambda, diff_subln_g,
           w_br_moba, w_br_diff, w_br_sb, w_out):
    bf = ml_dtypes.bfloat16
    tb = att_tables()
    f = lambda a: np.ascontiguousarray(np.asarray(a, dtype=np.float32))
    cc = np.ascontiguousarray
    ident = dict(idf=np.eye(128, dtype=np.float32), idb=np.eye(128).astype(bf))
    cores = list(range(NCORE))
    cur = f(x).reshape(NCORE, HTOK, D)
    for l in range(2):
        nc = build_pre()
        sh = dict(ident, ln_g=f(ln_g[l, 0]), ln_b=f(ln_b[l, 0]), wg=f(ffn_w_gate[l, 0]), wu=f(ffn_w_up[l, 0]),
                  wd=f(ffn_w_down[l, 0]), w_in=f(w_in[l]), b_gate=f(b_gate[l]))
        r1 = run_bass_kernel_spmd(nc, [dict(sh, x=cc(cur[c])) for c in cores], core_ids=cores).results
        att_in = []
        for c in cores:
            b_, hh = c // 2, c % 2
            qk_b = np.concatenate([r1[2 * b_]["qkT"], r1[2 * b_ + 1]["qkT"]], axis=1)
            qm_b = np.concatenate([r1[2 * b_]["qm32"], r1[2 * b_ + 1]["qm32"]], axis=1)
            km_b = np.concatenate([r1[2 * b_]["kmean"], r1[2 * b_ + 1]["kmean"]], axis=1)
            v_b = np.concatenate([r1[2 * b_]["v"], r1[2 * b_ + 1]["v"]], axis=0)
            qk_h = np.concatenate([qk_b[g * 384 + hh * 192:g * 384 + hh * 192 + 192] for g in range(6)], axis=0)
            v_h = np.concatenate([v_b[:, g * 384 + hh * 192:g * 384 + hh * 192 + 192] for g in range(3)], axis=1)
            d_ = dict(ident, qkT=cc(qk_h), qm32=cc(qm_b[hh * 192:(hh + 1) * 192]), kmean=cc(km_b[hh * 192:(hh + 1) * 192]),
                      v=cc(v_h), diff_lambda=f(diff_lambda[l]), diff_subln_g=f(diff_subln_g[l]))
            for k_, a_ in tb.items():
                if k_ in ("qaug", "kaug"):
                    d_["t_" + k_] = cc(a_[hh * 48:(hh + 1) * 48])
                else:
                    d_["t_" + k_] = a_
            att_in.append(d_)
        tb_shapes = {k_: (list(att_in[0]["t_" + k_].shape), att_in[0]["t_" + k_].dtype) for k_ in tb}
        nc = build_att(tb_shapes, l)
        r2 = run_bass_kernel_spmd(nc, att_in, core_ids=cores).results
        nc = build_post()
        sh = dict(ident, w_br_moba=f(w_br_moba[l]), w_br_diff=f(w_br_diff[l]), w_br_sb=f(w_br_sb[l]), w_out=f(w_out[l]),
                  ln_g1=f(ln_g[l, 1]), ln_b1=f(ln_b[l, 1]), ln_g2=f(ln_g[l, 2]), ln_b2=f(ln_b[l, 2]),
                  wg=f(ffn_w_gate[l, 1]), wu=f(ffn_w_up[l, 1]), wd=f(ffn_w_down[l, 1]))
        post_in = []
        for c in cores:
            b_, hf = c // 2, c % 2
            o_b = np.empty((18, SEQ, 64), dtype=r2[0]["o"].dtype)
            for g in range(3):
                for hh in range(2):
                    o_b[g * 6 + hh * 3:g * 6 + hh * 3 + 3] = r2[2 * b_ + hh]["o"][g * 3:g * 3 + 3]
            post_in.append(dict(sh, x1=cc(r1[c]["x1"]), gates=cc(r1[c]["gates"]), o=cc(o_b[:, hf * HTOK:(hf + 1) * HTOK])))
        r3 = run_bass_kernel_spmd(nc, post_in, core_ids=cores).results
        cur = np.stack([np.asarray(r["y"], dtype=np.float32) for r in r3], axis=0)
    return cur.reshape(4, SEQ, D)
```

```python
import math
from contextlib import contextmanager, ExitStack
import numpy as np
import ml_dtypes
import concourse.bass as bass
import concourse.mybir as mybir
from concourse.bass_utils import run_bass_kernel_spmd

F32 = mybir.dt.float32
BF16 = mybir.dt.bfloat16
AF = mybir.ActivationFunctionType
ALU = mybir.AluOpType
AX = mybir.AxisListType

D = 1024
DFF = 2816
NF = DFF // 128
NTOK = 4096
SEQ = 8192
NIN = 6528
ALPHA = 4.0 ** 0.25
LN_EPS = 1e-5
NCORES = 8


@contextmanager
def my_fori(nc, e, regs, start, end):
    loop_id = nc.next_id()
    name = f"myfori_{loop_id}"
    ls, le = name + "_loop", name + "_end"
    engines = bass.OrderedEngineSet([e.engine])
    nc.regs_mov(regs, start)
    nc.br(ls, engines=engines)
    with nc.body(ls, valid_engines=engines):
        yield nc.snap(regs, min_val=start, max_val=end - 1)
        nc.regs_alu(regs, regs, 1, op=mybir.AluOpType.add)
        nc.br_lt(regs, end, on_true=ls, on_false=le, engines=engines)
    nc.switch_bb(le)


class Prog:
    def __init__(self):
        self.root = []
        self.stack = [self.root]
        self.nloops = 0

    def op(self, eng, fn, chain=False, waits=(), thread=0):
        self.stack[-1].append(['op', eng, fn, chain, tuple(waits), 'c', thread])

    def dma(self, eng, fn, waits=(), thread=0):
        self.stack[-1].append(['op', eng, fn, False, tuple(waits), 'd', thread])

    def join(self, eng, thread, other):
        self.stack[-1].append(['op', eng, None, False, (), 'j', thread, other])

    @contextmanager
    def loop(self, n, clear=()):
        body = []
        lid = self.nloops
        self.nloops += 1
        self.stack[-1].append(['loop', n, body, lid, tuple(clear)])
        self.stack.append(body)
        try:
            yield lid
        finally:
            self.stack.pop()

    @staticmethod
    def _size(node):
        if node[0] == 'op':
            return 1
        return node[1] * sum(Prog._size(b) for b in node[2])

    @staticmethod
    def _has(node, ename):
        if node[0] == 'op':
            return node[1] == ename
        return any(Prog._has(b, ename) for b in node[2])

    def total(self):
        return sum(self._size(n) for n in self.root)

    def emit(self, ename, e, G, semfn, nc, GX):
        k = 0
        lregs = nc.alloc_registers(f"lr_{ename}", engines=bass.OrderedEngineSet([e.engine]))
        GS = (G,) + tuple(GX)
        NTH = len(GS)
        for node in self.root:
            if node[0] == 'op' or node[1] == 1:
                subs = [node] if node[0] == 'op' else node[2]
                iv = {} if node[0] == 'op' else {node[3]: 0}
                for sub in subs:
                    eng, fn, chain, waits, kind = sub[1:6]
                    assert kind != 'j' and sub[6] == 0
                    if eng == ename:
                        e.wait_ge(G, k)
                        for (sem, vf) in waits:
                            e.wait_ge(sem, vf(iv) if callable(vf) else vf)
                        if kind == 'c':
                            fn(e, iv).then_inc(G, 1)
                        else:
                            fn(e, iv)
                            e.sem_inc(G, 1)
                    k += 1
                continue
            _, n, body, lid, clr = node
            E, S2 = semfn(lid)
            NT = [sum(1 for sub in body if sub[6] == th) for th in range(NTH)]

            def release(first):
                if first:
                    e.wait_ge(G, k)
                else:
                    for th in range(NTH):
                        if NT[th]:
                            e.wait_ge(GS[th], NT[th])
                e.wait_ge(E, 5)
                for gs in GS:
                    e.sem_clear(gs)
                e.sem_clear(E)
                for cs in clr:
                    e.sem_clear(cs)
                e.sem_inc(S2, 1)
            e.sem_inc(E, 1)
            if ename == 'sp':
                release(True)
            with my_fori(nc, e, lregs, 1, n + 1) as i:
                e.wait_ge(S2, i)
                iv = {lid: i - 1}
                prev = [None] * NTH
                cnt = [0] * NTH
                for sub in body:
                    assert sub[0] == 'op'
                    eng, fn, chain, waits, kind, th = sub[1:7]
                    j = cnt[th]
                    if eng == ename:
                        if j > 0 and not (chain and prev[th] is not None and prev[th][1] == ename):
                            e.wait_ge(GS[th], j)
                        if kind == 'j':
                            e.wait_ge(GS[sub[7]], cnt[sub[7]])
                            e.sem_inc(GS[th], 1)
                        else:
                            for (sem, vf) in waits:
                                e.wait_ge(sem, vf({lid: 0}) if callable(vf) else vf)
                            if kind == 'c':
                                fn(e, iv).then_inc(GS[th], 1)
                            else:
                                fn(e, iv)
                                e.sem_inc(GS[th], 1)
                    prev[th] = sub
                    cnt[th] += 1
                e.sem_inc(E, 1)
                if ename == 'sp':
                    release(False)
            e.wait_ge(S2, n + 1)
            if ename == 'sp':
                e.sem_inc(G, k + 1)
            k += 1


def run_prog(nc, P, G):
    engs = {'pe': 'tensor', 'act': 'scalar', 'dve': 'vector', 'pool': 'gpsimd', 'sp': 'sync'}
    with ExitStack() as sctx:
        GX = [sctx.enter_context(nc.semaphore(f"GX{i}")) for i in range(3)]
        lsems = {}
        for node in P.root:
            if node[0] == 'loop' and node[1] > 1:
                lid = node[3]
                lsems[lid] = tuple(sctx.enter_context(nc.semaphore(f"L{lid}_{nm}")) for nm in ("E", "S"))
        with nc.Block() as block:
            for ename, bname in engs.items():
                def mk(ename=ename):
                    def f(e):
                        P.emit(ename, e, G, lambda lid: lsems[lid], nc, GX)
                    return f
                getattr(block, bname)(mk())


class RL:
    pass


def rl_alloc(nc, ctx):
    R = RL()
    R.nc = nc
    R.WBUF = ctx.enter_context(nc.sbuf_tensor("WBUF", [128, 67584], BF16))
    R.SCRF = ctx.enter_context(nc.sbuf_tensor("SCRF", [128, 8704], F32))
    R.SCRB = ctx.enter_context(nc.sbuf_tensor("SCRB", [128, 15360], BF16))
    R.identf = ctx.enter_context(nc.sbuf_tensor("identf", [128, 128], F32))
    R.identb = ctx.enter_context(nc.sbuf_tensor("identb", [128, 128], BF16))
    R.stats = ctx.enter_context(nc.sbuf_tensor("stats", [128, 12], F32))
    R.mv = ctx.enter_context(nc.sbuf_tensor("mv", [128, 2], F32))
    R.rstd = ctx.enter_context(nc.sbuf_tensor("rstd", [128, 1], F32))
    R.psw = [ctx.enter_context(nc.psum_tensor(f"psw{i}", [128, 1024], F32)) for i in range(3)]
    R.ps = []
    for i in range(3):
        R.ps += [R.psw[i][:, 0:512], R.psw[i][:, 512:1024]]
    R.psb = [ctx.enter_context(nc.psum_tensor(f"psb{i}", [128, 1024], BF16)) for i in range(2)]
    R.G = ctx.enter_context(nc.semaphore("G"))
    R.csem = ctx.enter_context(nc.semaphore("csem"))
    R.nsem = 0
    R.ctx = ctx
    return R


def new_sem(R, name):
    R.nsem += 1
    return R.ctx.enter_context(R.nc.semaphore(f"{name}_{R.nsem}"))


def rl_consts(P, R, identf_d, identb_d):
    def f(e, iv):
        e.dma_start(out=R.identf[:], in_=identf_d[:, :]).then_inc(R.csem, 16)
        e.dma_start(out=R.identb[:], in_=identb_d[:, :]).then_inc(R.csem, 16)
    P.dma('sp', f)
    P.dma('sp', lambda e, iv: None, waits=[(R.csem, 32)])


def ln_ops(P, R, u, gbc, bbc, yo, pre_waits=()):
    for h in range(2):
        P.op('dve', lambda e, iv, h=h: e.bn_stats(out=R.stats[:, h * 6:(h + 1) * 6], in_=u[:, h * 512:(h + 1) * 512]))
    P.op('dve', lambda e, iv: e.bn_aggr(out=R.mv[:], in_=R.stats[:]))
    P.op('dve', lambda e, iv: e.tensor_scalar(out=R.rstd[:], in0=R.mv[:, 1:2], scalar1=LN_EPS, scalar2=None, op0=ALU.add))
    P.op('act', lambda e, iv: e.activation(out=R.rstd[:], in_=R.rstd[:], func=AF.Sqrt))
    P.op('dve', lambda e, iv: e.reciprocal(out=R.rstd[:], in_=R.rstd[:]))
    P.op('dve', lambda e, iv: e.tensor_scalar(out=u, in0=u, scalar1=R.mv[:, 0:1], scalar2=R.rstd[:, 0:1],
                                              op0=ALU.subtract, op1=ALU.mult))
    P.op('pool', lambda e, iv: e.tensor_tensor(out=u, in0=u, in1=gbc, op=ALU.mult))
    P.op('pool', lambda e, iv: e.tensor_tensor(out=yo, in0=u, in1=bbc, op=ALU.add), waits=pre_waits)


def load_ln_params(P, R, g_d, b_d, gbc, bbc):
    sem = new_sem(R, "lnp")

    def f(e, iv):
        e.dma_start(out=gbc, in_=g_d.partition_broadcast(128)).then_inc(sem, 16)
        e.dma_start(out=bbc, in_=b_d.partition_broadcast(128)).then_inc(sem, 16)
    P.dma('sp', f)
    P.dma('sp', lambda e, iv: None, waits=[(sem, 32)])


def transposes_to_xT(P, R, xs4_loader, xT, nj=4):
    for j in range(nj):
        xs = xs4_loader(j)
        for c2 in range(2):
            for c in range(4):
                cc = c2 * 4 + c
                P.op('pe', lambda e, iv, cc=cc, c=c, xs=xs: e.transpose(
                    out=R.ps[0][:, c * 128:(c + 1) * 128], in_=xs[:, cc * 128:(cc + 1) * 128], identity=R.identf[:]),
                    chain=(c > 0))
            P.op('act', lambda e, iv, c2=c2, j=j: e.activation(
                out=xT[:, c2 * 4:(c2 + 1) * 4, j * 128:(j + 1) * 128],
                in_=R.ps[0][:, :].rearrange("p (c t) -> p c t", c=4), func=AF.Copy))


def stage_ffnln(P, R, x_in, x_out, wg_d, wu_d, wd_d, g_d, b_d, nsb=16):
    ntiles = nsb * 4
    wg = R.WBUF[:, 0:22528].rearrange("p (c f) -> p c f", c=8)
    wu = R.WBUF[:, 22528:45056].rearrange("p (c f) -> p c f", c=8)
    wd = R.WBUF[:, 45056:67584].rearrange("p (c n) -> p c n", c=NF)
    xs = R.SCRF[:, 0:1024]
    u = R.SCRF[:, 1024:2048]
    xa = R.SCRF[:, 2048:3072]
    yo = R.SCRF[:, 3072:4096]
    sg = R.SCRF[:, 4096:4608]
    gbc = R.SCRF[:, 4608:5632]
    bbc = R.SCRF[:, 5632:6656]
    xT = R.SCRB[:, 0:4096].rearrange("p (c t) -> p c t", c=8)
    aT = R.SCRB[:, 4096:15360].rearrange("p (f t) -> p f t", f=NF)
    wsem = new_sem(R, "w")
    dsx = new_sem(R, "dsx")
    dso = new_sem(R, "dso")

    def loadw(e, iv):
        for c in range(8):
            e.dma_start(out=wg[:, c, :].rearrange("p (h n) -> p h n", h=2),
                        in_=wg_d[c * 128:(c + 1) * 128, :].rearrange("p (h n) -> p h n", h=2)).then_inc(wsem, 16)
            e.dma_start(out=wu[:, c, :].rearrange("p (h n) -> p h n", h=2),
                        in_=wu_d[c * 128:(c + 1) * 128, :].rearrange("p (h n) -> p h n", h=2)).then_inc(wsem, 16)
        for f in range(NF):
            e.dma_start(out=wd[:, f, :], in_=wd_d[f * 128:(f + 1) * 128, :]).then_inc(wsem, 16)
    P.dma('pool', loadw)
    P.dma('pool', lambda e, iv: None, waits=[(wsem, 16 * (16 + NF))])
    load_ln_params(P, R, g_d, b_d, gbc, bbc)

    with P.loop(nsb, clear=(dsx, dso)) as L:
        def loader(j):
            P.dma('sp', lambda e, iv, j=j: e.dma_start(
                out=xs, in_=x_in[bass.ts(iv[L], 512), :][j * 128:(j + 1) * 128, :]).then_inc(dsx, 16))
            P.dma('sp', lambda e, iv: None, waits=[(dsx, (j + 1) * 16)])
            return xs
        transposes_to_xT(P, R, loader, xT, nj=4)
        for f in range(NF):
            for c in range(8):
                P.op('pe', lambda e, iv, f=f, c=c: e.matmul(R.ps[1][:], lhsT=wg[:, c, f * 128:(f + 1) * 128], rhs=xT[:, c, :],
                                                            start=(c == 0), stop=(c == 7)), chain=(c > 0))
            for c in range(8):
                P.op('pe', lambda e, iv, f=f, c=c: e.matmul(R.ps[2][:], lhsT=wu[:, c, f * 128:(f + 1) * 128], rhs=xT[:, c, :],
                                                            start=(c == 0), stop=(c == 7)), chain=True)
            P.op('act', lambda e, iv: e.activation(out=sg, in_=R.ps[1][:], func=AF.Silu))
            P.op('dve', lambda e, iv, f=f: e.tensor_tensor(out=aT[:, f, :], in0=sg, in1=R.ps[2][:], op=ALU.mult))
        for j in range(4):
            first = True
            for h in range(2):
                for f in range(NF):
                    P.op('pe', lambda e, iv, f=f, h=h, j=j: e.matmul(
                        R.ps[3 + h][:], lhsT=aT[:, f, j * 128:(j + 1) * 128], rhs=wd[:, f, h * 512:(h + 1) * 512],
                        start=(f == 0), stop=(f == NF - 1)), chain=not first)
                    first = False
            P.dma('sp', lambda e, iv, j=j: e.dma_start(
                out=xs, in_=x_in[bass.ts(iv[L], 512), :][j * 128:(j + 1) * 128, :]).then_inc(dsx, 16))
            P.op('act', lambda e, iv: e.mul(out=xa, in_=xs, mul=ALPHA), waits=[(dsx, (4 + j + 1) * 16)])
            for h in range(2):
                P.op('dve', lambda e, iv, h=h: e.scalar_tensor_tensor(
                    out=u[:, h * 512:(h + 1) * 512], in0=R.ps[3 + h][:], scalar=0.5, in1=xa[:, h * 512:(h + 1) * 512],
                    op0=ALU.mult, op1=ALU.add))
            ln_ops(P, R, u, gbc, bbc, yo, pre_waits=[(dso, j * 16)])
            P.dma('sp', lambda e, iv, j=j: e.dma_start(
                out=x_out[bass.ts(iv[L], 512), :][j * 128:(j + 1) * 128, :], in_=yo).then_inc(dso, 16))
        P.dma('sp', lambda e, iv: None, waits=[(dso, 64)])


QK_COLS = [0, 128, 256, 384, 512, 640,
           1152, 1280, 1408, 1536, 1664, 1792,
           2304, 2432, 2560, 2688, 2816, 2944]
V_COLS = [768, 1920, 3072]
G_COL = 3456


def stage_win(P, R, x_in, win_d, bgate_d, qkT_out, qm32_out, kmean_out, v_out, gates_out, nsb=16):
    ntiles = nsb * 4
    win = R.WBUF[:, 0:52224].rearrange("p (c n) -> p c n", c=8)
    xs = R.SCRF[:, 0:1024]
    gt = R.SCRF[:, 1024:4096]
    q32 = R.SCRF[:, 4096:4480].rearrange("p (r t) -> p r t", r=3)
    bgb = R.SCRF[:, 5632:8704]
    xT = R.SCRB[:, 0:4096].rearrange("p (c t) -> p c t", c=8)
    qkst = R.SCRB[:, 4096:6400].rearrange("p (r t) -> p r t", r=18)
    vst = R.SCRB[:, 13312:14464]
    km = R.stats[:, 0:3]
    wsem = new_sem(R, "w")
    dsx = new_sem(R, "dsx")
    dsq = new_sem(R, "dsq")

    def loadw(e, iv):
        for c in range(8):
            e.dma_start(out=win[:, c, :].rearrange("p (h n) -> p h n", h=4),
                        in_=win_d[c * 128:(c + 1) * 128, :].rearrange("p (h n) -> p h n", h=4)).then_inc(wsem, 16)
    P.dma('pool', loadw)
    P.dma('pool', lambda e, iv: None, waits=[(wsem, 16 * 8)])
    bsem = new_sem(R, "bg")
    P.dma('sp', lambda e, iv: e.dma_start(
        out=bgb, in_=bgate_d.rearrange("a d -> (a d)").partition_broadcast(128)).then_inc(bsem, 16))
    P.dma('sp', lambda e, iv: None, waits=[(bsem, 16)])

    with P.loop(ntiles, clear=(dsx, dsq)) as L:
        def loader(j):
            P.dma('act', lambda e, iv: e.dma_start(out=xs, in_=x_in[bass.ts(iv[L], 128), :]).then_inc(dsx, 16))
            P.dma('act', lambda e, iv: None, waits=[(dsx, 16)])
            return xs
        transposes_to_xT(P, R, loader, xT, nj=1)
        for r in range(18):
            co = QK_COLS[r]
            for c in range(8):
                P.op('pe', lambda e, iv, c=c, co=co: e.matmul(R.ps[1][:, 0:128], lhsT=win[:, c, co:co + 128], rhs=xT[:, c, 0:128],
                                                              start=(c == 0), stop=(c == 7)), chain=(c > 0))
            P.op('act', lambda e, iv, r=r: e.copy(out=qkst[:, r, :], in_=R.ps[1][:, 0:128]))
            if r < 3:
                P.op('dve', lambda e, iv, r=r: e.tensor_copy(out=q32[:, r, :], in_=R.ps[1][:, 0:128]))
            if 3 <= r < 6:
                P.op('dve', lambda e, iv, r=r: e.tensor_reduce(out=km[:, r - 3:r - 2], in_=R.ps[1][:, 0:128], axis=AX.X, op=ALU.add))

        def stq(e, iv):
            e.dma_start(out=qkT_out[:, bass.ts(iv[L], 128)].rearrange("(r p) t -> p r t", p=128),
                        in_=qkst).then_inc(dsq, 16)
            e.dma_start(out=qm32_out[:, bass.ts(iv[L], 128)].rearrange("(r p) t -> p r t", p=128),
                        in_=q32).then_inc(dsq, 16)
            e.dma_start(out=kmean_out[:, bass.ts(iv[L], 1)].rearrange("(r p) b -> p r b", p=128),
                        in_=km.rearrange("p (r b) -> p r b", b=1), allow_slow_non_contiguous=True).then_inc(dsq, 16)
        P.dma('act', stq)
        for vi, co in enumerate(V_COLS):
            for c in range(8):
                P.op('pe', lambda e, iv, c=c, co=co: e.matmul(
                    R.ps[2][:, 0:384], lhsT=xT[:, c, 0:128], rhs=win[:, c, co:co + 384],
                    start=(c == 0), stop=(c == 7)), chain=(c > 0))
            P.op('act', lambda e, iv, vi=vi: e.copy(out=vst[:, vi * 384:(vi + 1) * 384], in_=R.ps[2][:, 0:384]))
        for gi in range(6):
            co = G_COL + gi * 512
            for c in range(8):
                P.op('pe', lambda e, iv, c=c, co=co: e.matmul(
                    R.ps[3][:], lhsT=xT[:, c, 0:128], rhs=win[:, c, co:co + 512],
                    start=(c == 0), stop=(c == 7)), chain=(c > 0))
            P.op('dve', lambda e, iv, gi=gi: e.tensor_tensor(out=gt[:, gi * 512:(gi + 1) * 512], in0=R.ps[3][:],
                                                             in1=bgb[:, gi * 512:(gi + 1) * 512], op=ALU.add))
        P.op('act', lambda e, iv: e.activation(out=gt, in_=gt, func=AF.Sigmoid))
        P.dma('act', lambda e, iv: e.dma_start(out=v_out[bass.ts(iv[L], 128), :], in_=vst).then_inc(dsq, 16))
        P.dma('act', lambda e, iv: e.dma_start(out=gates_out[bass.ts(iv[L], 128), :], in_=gt).then_inc(dsq, 16))
        P.dma('act', lambda e, iv: None, waits=[(dsq, 80)])


BIG = 30000.0
SLOPES = (2.0 ** (-8.0 * np.arange(1, 13, dtype=np.float32) / 12)).astype(np.float32)
SC_M = 64 ** -0.5
SC_D = 32 ** -0.5
SC_S = 64 ** -0.5
SUBLN_EPS = 1e-5


def att_tables():
    bf = ml_dtypes.bfloat16
    i = np.arange(SEQ)
    ib = (i // 128).astype(np.float32)
    il = (i % 128).astype(np.float32)
    one = np.ones(SEQ, np.float32)

    def hl(v):
        hi = np.float32(np.asarray(v, np.float32).astype(bf).astype(np.float32))
        lo = np.float32(np.asarray(v - hi, np.float32).astype(bf).astype(np.float32))
        return hi, lo
    qaug = np.zeros((12, 8, SEQ), np.float32)
    kaug = np.zeros((12, 8, SEQ), np.float32)
    for slot in range(12):
        scale = SC_M if slot % 2 == 0 else SC_D
        sig = float(SLOPES[slot]) / scale
        Sh, Sl = hl(128.0 * sig)
        sh, sl = hl(sig)
        qaug[slot] = np.stack([Sh * one, Sl * one, sh * one, sl * one, -ib, -ib, -il, -il])
        kaug[slot] = np.stack([ib, ib, il, il, Sh * one, Sl * one, sh * one, sl * one])
    onehot = (np.arange(32)[:, None] == (i // 256)[None, :]).astype(np.float32)
    p = np.arange(128)[:, None]
    c = np.arange(512)[None, :]
    maskT = np.concatenate([np.where((r * 128 + p) > c, -BIG, 0.0) for r in range(4)], axis=1)
    maskS = np.concatenate([np.where(c >= (r * 128 + p), -BIG, 0.0) for r in range(4)], axis=1)
    o = np.arange(32)[:, None]
    n = np.arange(32)[None, :]
    pb = np.stack([np.where(n < o, 0.0, -BIG), np.where(n == o, 0.0, -BIG), np.where(n <= o, 0.0, -BIG)]).astype(np.float32)
    return dict(qaug=qaug.reshape(96, SEQ).astype(bf), kaug=kaug.reshape(96, SEQ).astype(bf), onehot=onehot.astype(bf),
                maskT=maskT.astype(bf), maskS=maskS.astype(bf), pb=pb.reshape(-1))


def stage_att(P, R, T, qkT_d, qm32_d, kmean_d, v_d, o_d, lam_d, subg_d, one_m_linit, nheads=6, do=('m', 'd', 's')):
    W = R.WBUF
    GW = nheads * 64
    QA = [W[:, 0:8192], W[:, 8192:16384]]
    KA = [W[:, 16384:24576], W[:, 24576:32768]]
    VA = W[:, 32768:40960].rearrange("p (k d) -> p k d", k=64)
    VAf = W[:, 32768:40960]
    maskT = W[:, 40960:43008]
    maskS = W[:, 43008:45056]
    pt = [W[:, 45056:45568], W[:, 45568:46080]]
    wb = W[:, 46080:46592]
    wT = W[:, 46592:47104].rearrange("p (k q) -> p k q", k=4)
    FW = W[:, 47360:67584].bitcast(F32)
    pbt = FW[:, 0:3072].rearrange("p (a o n) -> p a o n", a=3, o=32)
    osb = [FW[:, 3072:3584], FW[:, 3584:4096]]
    gm = FW[:, 4096:4128]
    sel = FW[:, 4128:4160]
    m8 = FW[:, 4160:4168]
    rinv = [FW[:, 4168:4172], FW[:, 4172:4176]]
    ss = FW[:, 4176:4180]
    carry = FW[:, 4180:4181]
    negc = FW[:, 4181:4182]
    tot = FW[:, 4182:4183]
    lamt = FW[:, 4183:4184]
    lsc = FW[:, 4184:4186]
    t1 = FW[:, 4192:4256]
    od = FW[:, 4256:4512].rearrange("p (j d) -> p j d", j=4)
    gsc = FW[:, 4512:4576]
    lamw = FW[:, 4576:4704]
    lamw2 = FW[:, 4704:4768]
    ones = FW[:, 4768:5280]
    tt = FW[:, 5280:5792]
    spb = FW[:, 5792:6305]
    E1 = FW[:, 6336:6848]
    aa = FW[:, 6848:7360]
    kme = FW[:, 7360:7392]
    kmr = FW[:, 9472:9536]
    ost_all = FW[:, 7424:9472].bitcast(BF16).rearrange("p (t d) -> p t d", t=64)
    qm32 = R.SCRF[:, 0:8192]
    ps_s = [R.ps[0], R.ps[1]]
    ps_o = [R.ps[2], R.ps[3]]
    ps_t = [R.ps[4], R.ps[5]]
    psb = R.psb[0]
    tsem = new_sem(R, "att_t")
    hsem = new_sem(R, "att_h")
    osem = new_sem(R, "att_o")
    cnt = {'h': 0, 'o': 0}

    def ldc(e, iv):
        e.dma_start(out=maskT, in_=T['maskT'][:, :]).then_inc(tsem, 16)
        e.dma_start(out=maskS, in_=T['maskS'][:, :]).then_inc(tsem, 16)
        e.dma_start(out=FW[:, 0:3072], in_=T['pb'].partition_broadcast(128)).then_inc(tsem, 16)
        e.dma_start(out=gsc, in_=subg_d.partition_broadcast(128)).then_inc(tsem, 16)
        e.dma_start(out=lamw, in_=lam_d.rearrange("a d -> (a d)").partition_broadcast(128)).then_inc(tsem, 16)
    P.dma('sp', ldc)
    P.dma('sp', lambda e, iv: None, waits=[(tsem, 80)])
    P.op('dve', lambda e, iv: e.memset(ones, 1.0))
    P.op('dve', lambda e, iv: e.memset(spb[:, 0:1], 0.0))
    P.op('dve', lambda e, iv: e.memset(VA[:, :, 64:65], 1.0))
    linit = 1.0 - one_m_linit
    P.op('dve', lambda e, iv: e.tensor_tensor(out=lamw2[:, 0:32], in0=lamw[:, 0:32], in1=lamw[:, 32:64], op=ALU.mult))
    P.op('dve', lambda e, iv: e.tensor_tensor(out=lamw2[:, 32:64], in0=lamw[:, 64:96], in1=lamw[:, 96:128], op=ALU.mult))
    P.op('dve', lambda e, iv: e.tensor_reduce(out=lsc, in_=lamw2.rearrange("p (a d) -> p a d", a=2), axis=AX.X, op=ALU.add))
    P.op('act', lambda e, iv: e.activation(out=lsc, in_=lsc, func=AF.Exp))
    P.op('dve', lambda e, iv: e.tensor_tensor(out=lamt, in0=lsc[:, 0:1], in1=lsc[:, 1:2], op=ALU.subtract))
    P.op('dve', lambda e, iv: e.tensor_scalar(out=lamt, in0=lamt, scalar1=linit, scalar2=-1.0, op0=ALU.add, op1=ALU.mult))
    P.op('act', lambda e, iv: e.mul(out=gsc, in_=gsc, mul=one_m_linit))

    def dense_tile(s, Kd, scale, kb_ap_K, kb_ap_V, Q, mask_r, first):
        P.op('pe', lambda e, iv: e.matmul(ps_s[s][:], lhsT=kb_ap_K(iv), rhs=QA[s][0:Kd, Q * 512:(Q + 1) * 512],
                                          start=True, stop=(mask_r is None)))
        if mask_r is not None:
            P.op('pe', lambda e, iv: e.matmul(ps_s[s][:], lhsT=R.identb[:], rhs=maskT[:, mask_r * 512:(mask_r + 1) * 512],
                                              start=False, stop=True), chain=True)
        P.op('act', lambda e, iv: e.activation(out=pt[s], in_=ps_s[s][:], func=AF.Exp, scale=scale))
        P.op('pe', lambda e, iv: e.matmul(ps_o[s][0:65, :], lhsT=kb_ap_V(iv), rhs=pt[s], start=first, stop=True))

    def dense_Q(streams, Q):
        for r in range(4):
            kb = 4 * Q + r
            for (s, Kd, scale) in streams:
                dense_tile(s, Kd, scale, lambda iv, s=s, Kd=Kd, kb=kb: KA[s][0:Kd, kb * 128:(kb + 1) * 128],
                           lambda iv, kb=kb: VA[:, kb, 0:65], Q, r, r == 0)
        for kb in range(4 * Q):
            for (s, Kd, scale) in streams:
                dense_tile(s, Kd, scale, lambda iv, s=s, Kd=Kd, kb=kb: KA[s][0:Kd, kb * 128:(kb + 1) * 128],
                           lambda iv, kb=kb: VA[:, kb, 0:65], Q, None, False)
        for (s, Kd, scale) in streams:
            P.op('act', lambda e, iv, s=s: e.copy(out=osb[s][0:65, :], in_=ps_o[s][0:65, :]))
            for j in range(4):
                P.op('pe', lambda e, iv, s=s, j=j: e.transpose(out=ps_t[s][:, j * 65:(j + 1) * 65],
                                                               in_=osb[s][0:65, j * 128:(j + 1) * 128],
                                                               identity=R.identf[0:65, 0:65]), chain=(j > 0))
            P.op('dve', lambda e, iv, s=s: e.reciprocal(
                out=rinv[s], in_=ps_t[s][:, 0:260].rearrange("p (j d) -> p j d", j=4)[:, :, 64]))

    def store_head(hfn, dq):
        P.dma(dq, lambda e, iv: e.dma_start(
            out=o_d[bass.ts(hfn(iv), 1), :, :].rearrange("h (t p) d -> p (h t) d", p=128),
            in_=ost_all).then_inc(osem, 16))
        P.dma(dq, lambda e, iv: None, waits=[(osem, 16)])

    NTD = 3
    Q_ASSIGN = [[15, 10, 9, 4, 3], [14, 11, 8, 5, 2], [13, 12, 7, 6, 1, 0]]
    SB16 = R.SCRB
    d_pt = [SB16[:, 2624 * i:2624 * i + 512] for i in range(NTD)]
    d_f = [SB16[:, 2624 * i + 512:2624 * i + 2624].bitcast(F32) for i in range(NTD)]
    d_osb = [d_f[i][:, 0:512] for i in range(NTD)]
    d_t1 = [d_f[i][:, 512:768].rearrange("p (j d) -> p j d", j=4) for i in range(NTD)]
    d_od = [d_f[i][:, 768:1024].rearrange("p (j d) -> p j d", j=4) for i in range(NTD)]
    d_r0 = [d_f[i][:, 1024:1028] for i in range(NTD)]
    d_r1 = [d_f[i][:, 1028:1032] for i in range(NTD)]
    d_ss = [d_f[i][:, 1032:1036] for i in range(NTD)]
    d_sq = [d_f[i][:, 1036:1040] for i in range(NTD)]
    d_pss = [R.ps[0], R.ps[1], R.ps[2]]
    d_pso = [R.ps[3], R.ps[4], R.ps[5]]

    def dense_map_ops(th, m, Kd, scale, Q):
        ops = []
        kbs = [(4 * Q + r, r) for r in range(4)] + [(kb, None) for kb in range(4 * Q)]
        for idx, (kb, mr) in enumerate(kbs):
            ops.append(('pe', lambda e, iv, kb=kb, mr=mr: e.matmul(
                d_pss[th][:], lhsT=KA[m][0:Kd, kb * 128:(kb + 1) * 128], rhs=QA[m][0:Kd, Q * 512:(Q + 1) * 512],
                start=True, stop=(mr is None)), False))
            if mr is not None:
                ops.append(('pe', lambda e, iv, mr=mr: e.matmul(d_pss[th][:], lhsT=R.identb[:],
                                                                rhs=maskT[:, mr * 512:(mr + 1) * 512],
                                                                start=False, stop=True), True))
            ops.append(('act', lambda e, iv: e.activation(out=d_pt[th], in_=d_pss[th][:], func=AF.Exp, scale=scale), False))
            ops.append(('pe', lambda e, iv, kb=kb, idx=idx: e.matmul(d_pso[th][0:65, :], lhsT=VA[:, kb, 0:65], rhs=d_pt[th],
                                                                     start=(idx == 0), stop=True), False))
        ops.append(('act', lambda e, iv: e.copy(out=d_osb[th][0:65, :], in_=d_pso[th][0:65, :]), False))
        for j in range(4):
            ops.append(('pe', lambda e, iv, j=j: e.transpose(out=d_pss[th][:, j * 65:(j + 1) * 65],
                                                             in_=d_osb[th][0:65, j * 128:(j + 1) * 128],
                                                             identity=R.identf[0:65, 0:65]), j > 0))
        return ops

    def psT(th, j):
        return d_pss[th][:, j * 65:j * 65 + 64]

    def rsum(th):
        return d_pss[th][:, 0:260].rearrange("p (j d) -> p j d", j=4)[:, :, 64]

    def emit_threads(lists, shift=0):
        n = len(lists)
        for idx in range(max(len(x) for x in lists) + shift * (n - 1)):
            for th in range(n):
                j = idx - th * shift
                if 0 <= j < len(lists[th]):
                    eng, fn, ch = lists[th][j]
                    P.op(eng, fn, chain=ch, thread=th)

    if 'm' in do:
        with P.loop(nheads, clear=(hsem, osem)) as LHM:
            hbaseM = 0

            def ldm(e, iv):
                h = iv[LHM]
                e.dma_start(out=QA[0][32:96, :], in_=qkT_d[0:GW, :][bass.ts(h, 64), :]).then_inc(hsem, 16)
                e.dma_start(out=KA[0][32:96, :], in_=qkT_d[GW:2 * GW, :][bass.ts(h, 64), :]).then_inc(hsem, 16)
                e.dma_start(out=KA[0][0:32, :], in_=T['onehot'][:, :]).then_inc(hsem, 16)
                e.dma_start(out=QA[0][96:104, :], in_=T['qaug'][bass.ts(h * 2, 8), :]).then_inc(hsem, 16)
                e.dma_start(out=KA[0][96:104, :], in_=T['kaug'][bass.ts(h * 2, 8), :]).then_inc(hsem, 16)
                e.dma_start(out=VA[:, :, 0:64],
                            in_=v_d[:, 0:GW][:, bass.ts(h, 64)].rearrange("(k p) d -> p k d", p=128)).then_inc(hsem, 16)
                e.dma_start(out=qm32[0:64, :], in_=qm32_d[bass.ts(h, 64), :]).then_inc(hsem, 16)
                e.dma_start(out=kmr[0:64, :], in_=kmean_d[bass.ts(h, 64), :]).then_inc(hsem, 16)
            P.dma('sp', ldm)
            P.dma('sp', lambda e, iv: None, waits=[(hsem, lambda iv: (hbaseM + (iv[LHM] + 1) * 8) * 16)])
            P.op('dve', lambda e, iv: e.tensor_reduce(out=kme[0:64, :], in_=kmr[0:64, :].rearrange("p (n b) -> p n b", b=2),
                                                      axis=AX.X, op=ALU.add))
            P.op('dve', lambda e, iv: e.tensor_scalar(out=kme[0:64, :], in0=kme[0:64, :], scalar1=1.0 / 256.0, scalar2=None,
                                                      op0=ALU.mult))
            for t4 in range(16):
                for tj in range(4):
                    t = t4 * 4 + tj
                    own = t // 2
                    P.op('pe', lambda e, iv, t=t: e.matmul(ps_s[1][:, 0:32], lhsT=qm32[0:64, t * 128:(t + 1) * 128],
                                                           rhs=kme[0:64, :], start=True, stop=True))
                    P.op('dve', lambda e, iv, own=own: e.tensor_tensor(out=gm, in0=ps_s[1][:, 0:32], in1=pbt[:, 0, own, :],
                                                                       op=ALU.add))
                    P.op('dve', lambda e, iv: e.max(out=m8, in_=gm))
                    P.op('dve', lambda e, iv: e.tensor_scalar(out=sel, in0=gm, scalar1=m8[:, 2:3], scalar2=1.0,
                                                              op0=ALU.is_ge, op1=ALU.subtract))
                    P.op('dve', lambda e, iv, own=own: e.scalar_tensor_tensor(out=sel, in0=sel, scalar=BIG,
                                                                              in1=pbt[:, 1, own, :], op0=ALU.mult, op1=ALU.max))
                    P.op('dve', lambda e, iv, own=own: e.tensor_tensor(out=sel, in0=sel, in1=pbt[:, 2, own, :], op=ALU.add))
                    P.op('pe', lambda e, iv, tj=tj: e.transpose(out=ps_t[0][0:32, tj * 128:(tj + 1) * 128], in_=sel,
                                                                identity=R.identf[:]))
                P.op('act', lambda e, iv, t4=t4: e.copy(out=QA[0][0:32, t4 * 512:(t4 + 1) * 512], in_=ps_t[0][0:32, :]))
            for th in range(1, NTD):
                P.join('pe', th, 0)
            lists = []
            for th in range(NTD):
                ops = []
                for Q in Q_ASSIGN[th]:
                    ops += dense_map_ops(th, 0, 104, SC_M, Q)
                    ops.append(('dve', (lambda e, iv, th=th: e.reciprocal(out=d_r0[th], in_=rsum(th))), False))
                    for j in range(4):
                        ops.append(('dve', (lambda e, iv, th=th, j=j, Q=Q: e.tensor_scalar(
                            out=ost_all[:, Q * 4 + j, :], in0=psT(th, j), scalar1=d_r0[th][:, j:j + 1], scalar2=None,
                            op0=ALU.mult)), False))
                lists.append(ops)
            emit_threads(lists, 2)
            for th in range(1, NTD):
                P.join('sp', 0, th)
            store_head(lambda iv: iv[LHM], 'sp')
    obase_d = 0

    if 'd' in do:
        with P.loop(nheads, clear=(hsem, osem)) as LHD:
            hbaseD = 0

            def ldd(e, iv):
                h = iv[LHD]
                for m in range(2):
                    e.dma_start(out=QA[m][0:32, :], in_=qkT_d[2 * GW:3 * GW, :][bass.ts(h * 2 + m, 32), :]).then_inc(hsem, 16)
                    e.dma_start(out=KA[m][0:32, :], in_=qkT_d[3 * GW:4 * GW, :][bass.ts(h * 2 + m, 32), :]).then_inc(hsem, 16)
                    e.dma_start(out=QA[m][32:40, :], in_=T['qaug'][bass.ts(h * 2 + 1, 8), :]).then_inc(hsem, 16)
                    e.dma_start(out=KA[m][32:40, :], in_=T['kaug'][bass.ts(h * 2 + 1, 8), :]).then_inc(hsem, 16)
                e.dma_start(out=VA[:, :, 0:64],
                            in_=v_d[:, GW:2 * GW][:, bass.ts(h, 64)].rearrange("(k p) d -> p k d", p=128)).then_inc(hsem, 16)
            P.dma('act', ldd)
            P.dma('act', lambda e, iv: None, waits=[(hsem, lambda iv: (hbaseD + (iv[LHD] + 1) * 9) * 16)])
            for th in range(1, NTD):
                P.join('pe', th, 0)
            lists = []
            for th in range(NTD):
                ops = []
                for Q in Q_ASSIGN[th]:
                    ops += dense_map_ops(th, 0, 40, SC_D, Q)
                    ops.append(('dve', (lambda e, iv, th=th: e.reciprocal(out=d_r0[th], in_=rsum(th))), False))
                    for j in range(4):
                        ops.append(('dve', (lambda e, iv, th=th, j=j: e.tensor_scalar(
                            out=d_t1[th][:, j, :], in0=psT(th, j), scalar1=d_r0[th][:, j:j + 1], scalar2=None,
                            op0=ALU.mult)), False))
                    ops += dense_map_ops(th, 1, 40, SC_D, Q)
                    ops.append(('dve', (lambda e, iv, th=th: e.reciprocal(out=d_r1[th], in_=rsum(th))), False))
                    ops.append(('dve', (lambda e, iv, th=th: e.tensor_scalar(out=d_r1[th], in0=d_r1[th], scalar1=lamt[:, 0:1],
                                                                             scalar2=None, op0=ALU.mult)), False))
                    for j in range(4):
                        ops.append(('dve', (lambda e, iv, th=th, j=j: e.scalar_tensor_tensor(
                            out=d_od[th][:, j, :], in0=psT(th, j), scalar=d_r1[th][:, j:j + 1], in1=d_t1[th][:, j, :],
                            op0=ALU.mult, op1=ALU.add)), False))
                    ops.append(('dve', (lambda e, iv, th=th: e.tensor_tensor(out=d_t1[th], in0=d_od[th], in1=d_od[th],
                                                                             op=ALU.mult)), False))
                    ops.append(('dve', (lambda e, iv, th=th: e.tensor_reduce(out=d_ss[th], in_=d_t1[th], axis=AX.X,
                                                                             op=ALU.add)), False))
                    ops.append(('dve', (lambda e, iv, th=th: e.tensor_scalar(out=d_ss[th], in0=d_ss[th], scalar1=1.0 / 64.0,
                                                                             scalar2=SUBLN_EPS, op0=ALU.mult, op1=ALU.add)), False))
                    ops.append(('act', (lambda e, iv, th=th: e.activation(out=d_ss[th], in_=d_ss[th], func=AF.Sqrt)), False))
                    ops.append(('dve', (lambda e, iv, th=th: e.reciprocal(out=d_ss[th], in_=d_ss[th])), False))
                    for j in range(4):
                        ops.append(('dve', (lambda e, iv, th=th, j=j, Q=Q: e.scalar_tensor_tensor(
                            out=ost_all[:, Q * 4 + j, :], in0=d_od[th][:, j, :], scalar=d_ss[th][:, j:j + 1], in1=gsc,
                            op0=ALU.mult, op1=ALU.mult)), False))
                lists.append(ops)
            emit_threads(lists, 2)
            for th in range(1, NTD):
                P.join('sp', 0, th)
            store_head(lambda iv: iv[LHD] + nheads, 'act')
    obase_s = 0

    if 's' in do:
        NTS = 2
        SHIFT_S = 0
        ones13 = FW[:, 4768:5793]
        SF = R.SCRF
        tts = [SF[:, 4200 * i:4200 * i + 1024] for i in range(NTS)]
        spbs = [SF[:, 4200 * i + 1024:4200 * i + 2049] for i in range(NTS)]
        Efs = [SF[:, 4200 * i + 2080:4200 * i + 3105] for i in range(NTS)]
        aas = [SF[:, 4200 * i + 3136:4200 * i + 4160] for i in range(NTS)]
        negcs = [SF[:, 4200 * i + 4160:4200 * i + 4161] for i in range(NTS)]
        wbs_ = [R.SCRB[:, 2048 * i:2048 * i + 1024] for i in range(NTS)]
        wTs = [R.SCRB[:, 2048 * i + 1024:2048 * i + 2048].rearrange("p (k q) -> p k q", k=8) for i in range(NTS)]
        psbs = [R.psb[0], R.psb[1]]
        pszs = [R.psw[0], R.psw[1]]
        psos = [R.psw[2][:, 0:64], R.psw[2][:, 512:576]]
        P.op('dve', lambda e, iv: e.memset(ones13, 1.0))
        for th in range(NTS):
            P.op('dve', lambda e, iv, th=th: e.memset(spbs[th][:, 0:1], 0.0))
        with P.loop(nheads, clear=(hsem, osem)) as LHS:
            def lds(e, iv):
                h = iv[LHS]
                e.dma_start(out=QA[0][0:64, :], in_=qkT_d[4 * GW:5 * GW, :][bass.ts(h, 64), :]).then_inc(hsem, 16)
                e.dma_start(out=KA[0][0:64, :], in_=qkT_d[5 * GW:6 * GW, :][bass.ts(h, 64), :]).then_inc(hsem, 16)
                e.dma_start(out=VA[:, :, 0:64],
                            in_=v_d[:, 2 * GW:3 * GW][:, bass.ts(h, 64)].rearrange("(k p) d -> p k d", p=128)).then_inc(hsem, 16)
            P.dma('sp', lds)
            P.dma('sp', lambda e, iv: None, waits=[(hsem, 48)])
            for th in range(1, NTS):
                P.join('pe', th, 0)

            def sb_step_ops(th, t, k0, nb, mask_r, first):
                Wd = nb * 512
                ops = []
                for bi in range(nb):
                    kt = k0 + bi
                    is_diag = (mask_r is not None and bi == nb - 1)
                    ops.append(('pe', lambda e, iv, bi=bi, kt=kt, is_diag=is_diag: e.matmul(
                        pszs[th][:, bi * 512:(bi + 1) * 512], lhsT=QA[0][0:64, t * 128:(t + 1) * 128],
                        rhs=KA[0][0:64, kt * 512:(kt + 1) * 512], start=True, stop=(not is_diag)), bi > 0))
                    if is_diag:
                        ops.append(('pe', lambda e, iv, bi=bi: e.matmul(
                            pszs[th][:, bi * 512:(bi + 1) * 512], lhsT=R.identb[:],
                            rhs=maskS[:, mask_r * 512:(mask_r + 1) * 512], start=False, stop=True), True))
                ops.append(('act', lambda e, iv: e.activation(out=tts[th][:, 0:Wd], in_=pszs[th][:, 0:Wd], func=AF.Exp, scale=SC_S), False))
                ops.append(('act', lambda e, iv: e.activation(out=spbs[th][:, 1:Wd + 1], in_=tts[th][:, 0:Wd], func=AF.Ln, bias=1.0), False))
                ops.append(('dve', lambda e, iv: e.tensor_tensor_scan(out=Efs[th][:, 0:Wd + 1], data0=ones13[:, 0:Wd + 1],
                                                                      data1=spbs[th][:, 0:Wd + 1], initial=0.0,
                                                                      op0=ALU.mult, op1=ALU.add), False))
                ops.append(('dve', lambda e, iv: e.scalar_tensor_tensor(out=aas[th][:, 0:Wd], in0=pszs[th][:, 0:Wd], scalar=SC_S,
                                                                        in1=Efs[th][:, 0:Wd], op0=ALU.mult, op1=ALU.add), False))
                if first:
                    ops.append(('dve', lambda e, iv: e.tensor_scalar(out=negcs[th], in0=Efs[th][:, Wd:Wd + 1], scalar1=-1.0,
                                                                     scalar2=None, op0=ALU.mult), False))
                else:
                    ops.append(('dve', lambda e, iv: e.tensor_tensor(out=negcs[th], in0=negcs[th], in1=Efs[th][:, Wd:Wd + 1],
                                                                     op=ALU.subtract), False))
                ops.append(('act', lambda e, iv: e.activation(out=wbs_[th][:, 0:Wd], in_=aas[th][:, 0:Wd], func=AF.Exp,
                                                              bias=negcs[th][:, 0:1]), False))
                nk = nb * 4
                for k4 in range(nk):
                    ops.append(('pe', lambda e, iv, k4=k4: e.transpose(out=psbs[th][:, k4 * 128:(k4 + 1) * 128],
                                                                       in_=wbs_[th][:, k4 * 128:(k4 + 1) * 128],
                                                                       identity=R.identb[:]), k4 > 0))
                ops.append(('dve', lambda e, iv: e.tensor_copy(out=wTs[th][:, 0:nk, :],
                                                               in_=psbs[th][:, 0:Wd].rearrange("p (k q) -> p k q", k=nk)), False))
                for k4 in range(nk):
                    ops.append(('pe', lambda e, iv, k4=k4: e.matmul(psos[th], lhsT=wTs[th][:, k4, :],
                                                                    rhs=VA[:, k0 * 4 + k4, 0:64],
                                                                    start=(first and k4 == 0), stop=True), k4 > 0))
                return ops

            for base in range(0, 64, NTS):
                lists = []
                for th in range(NTS):
                    t = base + th
                    kd = t // 4
                    if kd % 2 == 1:
                        ops = sb_step_ops(th, t, kd - 1, 2, t % 4, True)
                        rem = kd - 1
                    else:
                        ops = sb_step_ops(th, t, kd, 1, t % 4, True)
                        rem = kd
                    for k0 in range(rem - 2, -1, -2):
                        ops += sb_step_ops(th, t, k0, 2, None, False)
                    ops.append(('act', (lambda e, iv, t=t, th=th: e.copy(out=ost_all[:, t, :], in_=psos[th])), False))
                    lists.append(ops)
                emit_threads(lists, SHIFT_S)
            for th in range(1, NTS):
                P.join('sp', 0, th)
            store_head(lambda iv: iv[LHS] + 2 * nheads, 'sp')
def stage_mixpost(P, R, x_in, o_d, gates_d, x_out, wbm_d, wbd_d, wbs_d, wout_d, g_d, b_d, ntiles=64):
    W = R.WBUF
    wbr = W[:, 0:9216].rearrange("p (k n) -> p k n", k=9)
    wout = W[:, 9216:17408].rearrange("p (c n) -> p c n", c=8)
    gts = W[:, 17408:23552].bitcast(F32)
    xs = R.SCRF[:, 0:1024]
    u = R.SCRF[:, 1024:2048]
    xa = R.SCRF[:, 2048:3072]
    yo = R.SCRF[:, 3072:4096]
    gbc = R.SCRF[:, 4608:5632]
    bbc = R.SCRF[:, 5632:6656]
    merged = R.SCRF[:, 6656:7680]
    tmp = R.SCRF[:, 7680:8704]
    xT = R.SCRB[:, 0:4096].rearrange("p (c t) -> p c t", c=8)
    ot = R.SCRB[:, 4096:5248]
    oT = R.SCRB[:, 5248:6400].rearrange("p (k t) -> p k t", k=9)
    wsem = new_sem(R, "w")
    dsi = new_sem(R, "dsi")
    dso = new_sem(R, "dso")

    def loadw(e, iv):
        for bi, wd_ in enumerate((wbm_d, wbd_d, wbs_d)):
            for i in range(3):
                e.dma_start(out=wbr[:, bi * 3 + i, :], in_=wd_[i * 128:(i + 1) * 128, :]).then_inc(wsem, 16)
        for c in range(8):
            e.dma_start(out=wout[:, c, :], in_=wout_d[c * 128:(c + 1) * 128, :]).then_inc(wsem, 16)
    P.dma('pool', loadw)
    P.dma('pool', lambda e, iv: None, waits=[(wsem, 16 * 17)])
    load_ln_params(P, R, g_d, b_d, gbc, bbc)
    with P.loop(ntiles, clear=(dsi, dso)) as L:
        def ld(e, iv):
            e.dma_start(out=gts, in_=gates_d[bass.ts(iv[L], 128), :]).then_inc(dsi, 16)
            e.dma_start(out=xs, in_=x_in[bass.ts(iv[L], 128), :]).then_inc(dsi, 16)
        P.dma('sp', lambda e, iv: e.dma_start(out=ot.rearrange("p (h d) -> p h d", h=18),
                                              in_=o_d.rearrange("h s d -> s h d")[bass.ts(iv[L], 128), :, :]).then_inc(dsi, 16))
        P.dma('act', ld)
        P.dma('act', lambda e, iv: None, waits=[(dsi, lambda iv: (iv[L] + 1) * 48)])
        for k in range(9):
            P.op('pe', lambda e, iv, k=k: e.transpose(
                out=(R.psb[0][:, k * 128:(k + 1) * 128] if k < 8 else R.psb[1][:, 0:128]),
                in_=ot[:, k * 128:(k + 1) * 128], identity=R.identb[:]), chain=(k > 0))
        P.op('act', lambda e, iv: e.copy(out=oT[:, 0:8, :], in_=R.psb[0][:, :].rearrange("p (k t) -> p k t", k=8)))
        P.op('act', lambda e, iv: e.copy(out=oT[:, 8, :], in_=R.psb[1][:, 0:128]))
        for br in range(3):
            for h in range(2):
                for i in range(3):
                    P.op('pe', lambda e, iv, br=br, h=h, i=i: e.matmul(
                        R.ps[1][:], lhsT=oT[:, br * 3 + i, :], rhs=wbr[:, br * 3 + i, h * 512:(h + 1) * 512],
                        start=(i == 0), stop=(i == 2)), chain=(i > 0))
                gsl = gts[:, br * 1024 + h * 512: br * 1024 + (h + 1) * 512]
                if br == 0:
                    P.op('dve', lambda e, iv, h=h, gsl=gsl: e.tensor_tensor(out=merged[:, h * 512:(h + 1) * 512], in0=gsl,
                                                                            in1=R.ps[1][:], op=ALU.mult))
                else:
                    P.op('dve', lambda e, iv, h=h, gsl=gsl: e.tensor_tensor(out=tmp[:, h * 512:(h + 1) * 512], in0=gsl,
                                                                            in1=R.ps[1][:], op=ALU.mult))
                    P.op('dve', lambda e, iv, h=h: e.tensor_tensor(out=merged[:, h * 512:(h + 1) * 512],
                                                                   in0=merged[:, h * 512:(h + 1) * 512],
                                                                   in1=tmp[:, h * 512:(h + 1) * 512], op=ALU.add))
        transposes_to_xT(P, R, lambda j: merged, xT, nj=1)
        first = True
        for h in range(2):
            for c in range(8):
                P.op('pe', lambda e, iv, h=h, c=c: e.matmul(R.ps[3 + h][:], lhsT=xT[:, c, 0:128],
                                                            rhs=wout[:, c, h * 512:(h + 1) * 512],
                                                            start=(c == 0), stop=(c == 7)), chain=not first)
                first = False
        P.op('act', lambda e, iv: e.mul(out=xa, in_=xs, mul=ALPHA))
        for h in range(2):
            P.op('dve', lambda e, iv, h=h: e.tensor_tensor(out=u[:, h * 512:(h + 1) * 512], in0=xa[:, h * 512:(h + 1) * 512],
                                                           in1=R.ps[3 + h][:], op=ALU.add))
        ln_ops(P, R, u, gbc, bbc, yo, pre_waits=[(dso, lambda iv: iv[L] * 16)])
        P.dma('act', lambda e, iv: e.dma_start(out=x_out[bass.ts(iv[L], 128), :], in_=yo).then_inc(dso, 16))
        P.dma('act', lambda e, iv: None, waits=[(dso, 16)])


NACT = 4


NCORE = 8
HTOK = 4096


def _nc_dt():
    nc = bass.Bass("TRN2", target_bir_lowering=False)

    def dt(n, s, d=F32, k="ExternalInput"):
        return nc.dram_tensor(n, list(s), d, kind=k).ap()
    return nc, dt


def build_pre():
    nc, dt = _nc_dt()
    x = dt("x", [HTOK, D])
    g = dt("ln_g", [D]); b = dt("ln_b", [D])
    wg = dt("wg", [D, DFF]); wu = dt("wu", [D, DFF]); wd = dt("wd", [DFF, D])
    win = dt("w_in", [D, NIN]); bg = dt("b_gate", [3, D])
    idf = dt("idf", [128, 128]); idb = dt("idb", [128, 128], BF16)
    x1 = dt("x1", [HTOK, D], F32, "ExternalOutput")
    qkT = dt("qkT", [2304, HTOK], BF16, "ExternalOutput")
    qm32 = dt("qm32", [384, HTOK], F32, "ExternalOutput")
    kmean = dt("kmean", [384, HTOK // 128], F32, "ExternalOutput")
    v = dt("v", [HTOK, 1152], BF16, "ExternalOutput")
    gates = dt("gates", [HTOK, 3072], F32, "ExternalOutput")
    with ExitStack() as ctx:
        R = rl_alloc(nc, ctx)
        P = Prog()
        rl_consts(P, R, idf, idb)
        stage_ffnln(P, R, x, x1, wg, wu, wd, g, b, nsb=HTOK // 512)
        stage_win(P, R, x1, win, bg, qkT, qm32, kmean, v, gates, nsb=HTOK // 512)
        run_prog(nc, P, R.G)
    return nc


def build_att(tb_shapes, l):
    bf = ml_dtypes.bfloat16
    nc, dt = _nc_dt()
    NH = 3
    qkT = dt("qkT", [6 * NH * 64, SEQ], BF16)
    qm32 = dt("qm32", [NH * 64, SEQ])
    kmean = dt("kmean", [NH * 64, SEQ // 128])
    v = dt("v", [SEQ, 3 * NH * 64], BF16)
    dl = dt("diff_lambda", [4, 32]); dg = dt("diff_subln_g", [64])
    idf = dt("idf", [128, 128]); idb = dt("idb", [128, 128], BF16)
    T = {k: dt("t_" + k, shp, BF16 if dty == bf else F32) for k, (shp, dty) in tb_shapes.items()}
    o = dt("o", [3 * NH, SEQ, 64], BF16, "ExternalOutput")
    with ExitStack() as ctx:
        R = rl_alloc(nc, ctx)
        P = Prog()
        rl_consts(P, R, idf, idb)
        linit = 0.8 - 0.6 * math.exp(-0.3 * l)
        stage_att(P, R, T, qkT, qm32, kmean, v, o, dl, dg, 1.0 - linit, nheads=NH)
        run_prog(nc, P, R.G)
    return nc


def build_post():
    nc, dt = _nc_dt()
    x1 = dt("x1", [HTOK, D])
    o = dt("o", [18, HTOK, 64], BF16)
    gates = dt("gates", [HTOK, 3072])
    wbm = dt("w_br_moba", [384, D]); wbd = dt("w_br_diff", [384, D]); wbs = dt("w_br_sb", [384, D]); wo = dt("w_out", [D, D])
    g1 = dt("ln_g1", [D]); b1 = dt("ln_b1", [D]); g2 = dt("ln_g2", [D]); b2 = dt("ln_b2", [D])
    wg = dt("wg", [D, DFF]); wu = dt("wu", [D, DFF]); wd = dt("wd", [DFF, D])
    idf = dt("idf", [128, 128]); idb = dt("idb", [128, 128], BF16)
    y = dt("y", [HTOK, D], F32, "ExternalOutput")
    xb = dt("xb_d", [HTOK, D], F32, "Internal")
    with ExitStack() as ctx:
        R = rl_alloc(nc, ctx)
        P = Prog()
        rl_consts(P, R, idf, idb)
        stage_mixpost(P, R, x1, o, gates, xb, wbm, wbd, wbs, wo, g1, b1, ntiles=HTOK // 128)
        stage_ffnln(P, R, xb, y, wg, wu, wd, g2, b2, nsb=HTOK // 512)
        run_prog(nc, P, R.G)
    return nc


def kernel(x, ln_g, ln_b, ffn_w_gate, ffn_w_up, ffn_w_down, w_in, b_gate, diff_lambda, diff_subln_g,
           w_br_moba, w_br_diff, w_br_sb, w_out):
    bf = ml_dtypes.bfloat16
    tb = att_tables()
    f = lambda a: np.ascontiguousarray(np.asarray(a, dtype=np.float32))
    cc = np.ascontiguousarray
    ident = dict(idf=np.eye(128, dtype=np.float32), idb=np.eye(128).astype(bf))
    cores = list(range(NCORE))
    cur = f(x).reshape(NCORE, HTOK, D)
    for l in range(2):
        nc = build_pre()
        sh = dict(ident, ln_g=f(ln_g[l, 0]), ln_b=f(ln_b[l, 0]), wg=f(ffn_w_gate[l, 0]), wu=f(ffn_w_up[l, 0]),
                  wd=f(ffn_w_down[l, 0]), w_in=f(w_in[l]), b_gate=f(b_gate[l]))
        r1 = run_bass_kernel_spmd(nc, [dict(sh, x=cc(cur[c])) for c in cores], core_ids=cores).results
        att_in = []
        for c in cores:
            b_, hh = c // 2, c % 2
            qk_b = np.concatenate([r1[2 * b_]["qkT"], r1[2 * b_ + 1]["qkT"]], axis=1)
            qm_b = np.concatenate([r1[2 * b_]["qm32"], r1[2 * b_ + 1]["qm32"]], axis=1)
            km_b = np.concatenate([r1[2 * b_]["kmean"], r1[2 * b_ + 1]["kmean"]], axis=1)
            v_b = np.concatenate([r1[2 * b_]["v"], r1[2 * b_ + 1]["v"]], axis=0)
            qk_h = np.concatenate([qk_b[g * 384 + hh * 192:g * 384 + hh * 192 + 192] for g in range(6)], axis=0)
            v_h = np.concatenate([v_b[:, g * 384 + hh * 192:g * 384 + hh * 192 + 192] for g in range(3)], axis=1)
            d_ = dict(ident, qkT=cc(qk_h), qm32=cc(qm_b[hh * 192:(hh + 1) * 192]), kmean=cc(km_b[hh * 192:(hh + 1) * 192]),
                      v=cc(v_h), diff_lambda=f(diff_lambda[l]), diff_subln_g=f(diff_subln_g[l]))
            for k_, a_ in tb.items():
                if k_ in ("qaug", "kaug"):
                    d_["t_" + k_] = cc(a_[hh * 48:(hh + 1) * 48])
                else:
                    d_["t_" + k_] = a_
            att_in.append(d_)
        tb_shapes = {k_: (list(att_in[0]["t_" + k_].shape), att_in[0]["t_" + k_].dtype) for k_ in tb}
        nc = build_att(tb_shapes, l)
        r2 = run_bass_kernel_spmd(nc, att_in, core_ids=cores).results
        nc = build_post()
        sh = dict(ident, w_br_moba=f(w_br_moba[l]), w_br_diff=f(w_br_diff[l]), w_br_sb=f(w_br_sb[l]), w_out=f(w_out[l]),
                  ln_g1=f(ln_g[l, 1]), ln_b1=f(ln_b[l, 1]), ln_g2=f(ln_g[l, 2]), ln_b2=f(ln_b[l, 2]),
                  wg=f(ffn_w_gate[l, 1]), wu=f(ffn_w_up[l, 1]), wd=f(ffn_w_down[l, 1]))
        post_in = []
        for c in cores:
            b_, hf = c // 2, c % 2
            o_b = np.empty((18, SEQ, 64), dtype=r2[0]["o"].dtype)
            for g in range(3):
                for hh in range(2):
                    o_b[g * 6 + hh * 3:g * 6 + hh * 3 + 3] = r2[2 * b_ + hh]["o"][g * 3:g * 3 + 3]
            post_in.append(dict(sh, x1=cc(r1[c]["x1"]), gates=cc(r1[c]["gates"]), o=cc(o_b[:, hf * HTOK:(hf + 1) * HTOK])))
        r3 = run_bass_kernel_spmd(nc, post_in, core_ids=cores).results
        cur = np.stack([np.asarray(r["y"], dtype=np.float32) for r in r3], axis=0)
    return cur.reshape(4, SEQ, D)
```

```python
import math
from contextlib import contextmanager, ExitStack
import numpy as np
import ml_dtypes
import concourse.bass as bass
import concourse.mybir as mybir
from concourse.bass_utils import run_bass_kernel_spmd

F32 = mybir.dt.float32
BF16 = mybir.dt.bfloat16
AF = mybir.ActivationFunctionType
ALU = mybir.AluOpType
AX = mybir.AxisListType

D = 1024
DFF = 2816
NF = DFF // 128
NTOK = 4096
SEQ = 8192
NIN = 6528
ALPHA = 4.0 ** 0.25
LN_EPS = 1e-5
NCORES = 8


@contextmanager
def my_fori(nc, e, regs, start, end):
    loop_id = nc.next_id()
    name = f"myfori_{loop_id}"
    ls, le = name + "_loop", name + "_end"
    engines = bass.OrderedEngineSet([e.engine])
    nc.regs_mov(regs, start)
    nc.br(ls, engines=engines)
    with nc.body(ls, valid_engines=engines):
        yield nc.snap(regs, min_val=start, max_val=end - 1)
        nc.regs_alu(regs, regs, 1, op=mybir.AluOpType.add)
        nc.br_lt(regs, end, on_true=ls, on_false=le, engines=engines)
    nc.switch_bb(le)


class Prog:
    def __init__(self):
        self.root = []
        self.stack = [self.root]
        self.nloops = 0

    def op(self, eng, fn, chain=False, waits=(), thread=0):
        self.stack[-1].append(['op', eng, fn, chain, tuple(waits), 'c', thread])

    def dma(self, eng, fn, waits=(), thread=0):
        self.stack[-1].append(['op', eng, fn, False, tuple(waits), 'd', thread])

    def join(self, eng, thread, other):
        self.stack[-1].append(['op', eng, None, False, (), 'j', thread, other])

    @contextmanager
    def loop(self, n, clear=()):
        body = []
        lid = self.nloops
        self.nloops += 1
        self.stack[-1].append(['loop', n, body, lid, tuple(clear)])
        self.stack.append(body)
        try:
            yield lid
        finally:
            self.stack.pop()

    @staticmethod
    def _size(node):
        if node[0] == 'op':
            return 1
        return node[1] * sum(Prog._size(b) for b in node[2])

    @staticmethod
    def _has(node, ename):
        if node[0] == 'op':
            return node[1] == ename
        return any(Prog._has(b, ename) for b in node[2])

    def total(self):
        return sum(self._size(n) for n in self.root)

    def emit(self, ename, e, G, semfn, nc, GX):
        k = 0
        lregs = nc.alloc_registers(f"lr_{ename}", engines=bass.OrderedEngineSet([e.engine]))
        GS = (G,) + tuple(GX)
        NTH = len(GS)
        for node in self.root:
            if node[0] == 'op' or node[1] == 1:
                subs = [node] if node[0] == 'op' else node[2]
                iv = {} if node[0] == 'op' else {node[3]: 0}
                for sub in subs:
                    eng, fn, chain, waits, kind = sub[1:6]
                    assert kind != 'j' and sub[6] == 0
                    if eng == ename:
                        e.wait_ge(G, k)
                        for (sem, vf) in waits:
                            e.wait_ge(sem, vf(iv) if callable(vf) else vf)
                        if kind == 'c':
                            fn(e, iv).then_inc(G, 1)
                        else:
                            fn(e, iv)
                            e.sem_inc(G, 1)
                    k += 1
                continue
            _, n, body, lid, clr = node
            E, S2 = semfn(lid)
            NT = [sum(1 for sub in body if sub[6] == th) for th in range(NTH)]

            def release(first):
                if first:
                    e.wait_ge(G, k)
                else:
                    for th in range(NTH):
                        if NT[th]:
                            e.wait_ge(GS[th], NT[th])
                e.wait_ge(E, 5)
                for gs in GS:
                    e.sem_clear(gs)
                e.sem_clear(E)
                for cs in clr:
                    e.sem_clear(cs)
                e.sem_inc(S2, 1)
            e.sem_inc(E, 1)
            if ename == 'sp':
                release(True)
            with my_fori(nc, e, lregs, 1, n + 1) as i:
                e.wait_ge(S2, i)
                iv = {lid: i - 1}
                prev = [None] * NTH
                cnt = [0] * NTH
                for sub in body:
                    assert sub[0] == 'op'
                    eng, fn, chain, waits, kind, th = sub[1:7]
                    j = cnt[th]
                    if eng == ename:
                        if j > 0 and not (chain and prev[th] is not None and prev[th][1] == ename):
                            e.wait_ge(GS[th], j)
                        if kind == 'j':
                            e.wait_ge(GS[sub[7]], cnt[sub[7]])
                            e.sem_inc(GS[th], 1)
                        else:
                            for (sem, vf) in waits:
                                e.wait_ge(sem, vf({lid: 0}) if callable(vf) else vf)
                            if kind == 'c':
                                fn(e, iv).then_inc(GS[th], 1)
                            else:
                                fn(e, iv)
                                e.sem_inc(GS[th], 1)
                    prev[th] = sub
                    cnt[th] += 1
                e.sem_inc(E, 1)
                if ename == 'sp':
                    release(False)
            e.wait_ge(S2, n + 1)
            if ename == 'sp':
                e.sem_inc(G, k + 1)
            k += 1


def run_prog(nc, P, G):
    engs = {'pe': 'tensor', 'act': 'scalar', 'dve': 'vector', 'pool': 'gpsimd', 'sp': 'sync'}
    with ExitStack() as sctx:
        GX = [sctx.enter_context(nc.semaphore(f"GX{i}")) for i in range(3)]
        lsems = {}
        for node in P.root:
            if node[0] == 'loop' and node[1] > 1:
                lid = node[3]
                lsems[lid] = tuple(sctx.enter_context(nc.semaphore(f"L{lid}_{nm}")) for nm in ("E", "S"))
        with nc.Block() as block:
            for ename, bname in engs.items():
                def mk(ename=ename):
                    def f(e):
                        P.emit(ename, e, G, lambda lid: lsems[lid], nc, GX)
                    return f
                getattr(block, bname)(mk())


class RL:
    pass


def rl_alloc(nc, ctx):
    R = RL()
    R.nc = nc
    R.WBUF = ctx.enter_context(nc.sbuf_tensor("WBUF", [128, 67584], BF16))
    R.SCRF = ctx.enter_context(nc.sbuf_tensor("SCRF", [128, 8704], F32))
    R.SCRB = ctx.enter_context(nc.sbuf_tensor("SCRB", [128, 15360], BF16))
    R.identf = ctx.enter_context(nc.sbuf_tensor("identf", [128, 128], F32))
    R.identb = ctx.enter_context(nc.sbuf_tensor("identb", [128, 128], BF16))
    R.stats = ctx.enter_context(nc.sbuf_tensor("stats", [128, 12], F32))
    R.mv = ctx.enter_context(nc.sbuf_tensor("mv", [128, 2], F32))
    R.rstd = ctx.enter_context(nc.sbuf_tensor("rstd", [128, 1], F32))
    R.psw = [ctx.enter_context(nc.psum_tensor(f"psw{i}", [128, 1024], F32)) for i in range(3)]
    R.ps = []
    for i in range(3):
        R.ps += [R.psw[i][:, 0:512], R.psw[i][:, 512:1024]]
    R.psb = [ctx.enter_context(nc.psum_tensor(f"psb{i}", [128, 1024], BF16)) for i in range(2)]
    R.G = ctx.enter_context(nc.semaphore("G"))
    R.csem = ctx.enter_context(nc.semaphore("csem"))
    R.nsem = 0
    R.ctx = ctx
    return R


def new_sem(R, name):
    R.nsem += 1
    return R.ctx.enter_context(R.nc.semaphore(f"{name}_{R.nsem}"))


def rl_consts(P, R, identf_d, identb_d):
    def f(e, iv):
        e.dma_start(out=R.identf[:], in_=identf_d[:, :]).then_inc(R.csem, 16)
        e.dma_start(out=R.identb[:], in_=identb_d[:, :]).then_inc(R.csem, 16)
    P.dma('sp', f)
    P.dma('sp', lambda e, iv: None, waits=[(R.csem, 32)])


def ln_ops(P, R, u, gbc, bbc, yo, pre_waits=()):
    for h in range(2):
        P.op('dve', lambda e, iv, h=h: e.bn_stats(out=R.stats[:, h * 6:(h + 1) * 6], in_=u[:, h * 512:(h + 1) * 512]))
    P.op('dve', lambda e, iv: e.bn_aggr(out=R.mv[:], in_=R.stats[:]))
    P.op('dve', lambda e, iv: e.tensor_scalar(out=R.rstd[:], in0=R.mv[:, 1:2], scalar1=LN_EPS, scalar2=None, op0=ALU.add))
    P.op('act', lambda e, iv: e.activation(out=R.rstd[:], in_=R.rstd[:], func=AF.Sqrt))
    P.op('dve', lambda e, iv: e.reciprocal(out=R.rstd[:], in_=R.rstd[:]))
    P.op('dve', lambda e, iv: e.tensor_scalar(out=u, in0=u, scalar1=R.mv[:, 0:1], scalar2=R.rstd[:, 0:1],
                                              op0=ALU.subtract, op1=ALU.mult))
    P.op('pool', lambda e, iv: e.tensor_tensor(out=u, in0=u, in1=gbc, op=ALU.mult))
    P.op('pool', lambda e, iv: e.tensor_tensor(out=yo, in0=u, in1=bbc, op=ALU.add), waits=pre_waits)


def load_ln_params(P, R, g_d, b_d, gbc, bbc):
    sem = new_sem(R, "lnp")

    def f(e, iv):
        e.dma_start(out=gbc, in_=g_d.partition_broadcast(128)).then_inc(sem, 16)
        e.dma_start(out=bbc, in_=b_d.partition_broadcast(128)).then_inc(sem, 16)
    P.dma('sp', f)
    P.dma('sp', lambda e, iv: None, waits=[(sem, 32)])


def transposes_to_xT(P, R, xs4_loader, xT, nj=4):
    for j in range(nj):
        xs = xs4_loader(j)
        for c2 in range(2):
            for c in range(4):
                cc = c2 * 4 + c
                P.op('pe', lambda e, iv, cc=cc, c=c, xs=xs: e.transpose(
                    out=R.ps[0][:, c * 128:(c + 1) * 128], in_=xs[:, cc * 128:(cc + 1) * 128], identity=R.identf[:]),
                    chain=(c > 0))
            P.op('act', lambda e, iv, c2=c2, j=j: e.activation(
                out=xT[:, c2 * 4:(c2 + 1) * 4, j * 128:(j + 1) * 128],
                in_=R.ps[0][:, :].rearrange("p (c t) -> p c t", c=4), func=AF.Copy))


def stage_ffnln(P, R, x_in, x_out, wg_d, wu_d, wd_d, g_d, b_d, nsb=16):
    ntiles = nsb * 4
    wg = R.WBUF[:, 0:22528].rearrange("p (c f) -> p c f", c=8)
    wu = R.WBUF[:, 22528:45056].rearrange("p (c f) -> p c f", c=8)
    wd = R.WBUF[:, 45056:67584].rearrange("p (c n) -> p c n", c=NF)
    xs = R.SCRF[:, 0:1024]
    u = R.SCRF[:, 1024:2048]
    xa = R.SCRF[:, 2048:3072]
    yo = R.SCRF[:, 3072:4096]
    sg = R.SCRF[:, 4096:4608]
    gbc = R.SCRF[:, 4608:5632]
    bbc = R.SCRF[:, 5632:6656]
    xT = R.SCRB[:, 0:4096].rearrange("p (c t) -> p c t", c=8)
    aT = R.SCRB[:, 4096:15360].rearrange("p (f t) -> p f t", f=NF)
    wsem = new_sem(R, "w")
    dsx = new_sem(R, "dsx")
    dso = new_sem(R, "dso")

    def loadw(e, iv):
        for c in range(8):
            e.dma_start(out=wg[:, c, :].rearrange("p (h n) -> p h n", h=2),
                        in_=wg_d[c * 128:(c + 1) * 128, :].rearrange("p (h n) -> p h n", h=2)).then_inc(wsem, 16)
            e.dma_start(out=wu[:, c, :].rearrange("p (h n) -> p h n", h=2),
                        in_=wu_d[c * 128:(c + 1) * 128, :].rearrange("p (h n) -> p h n", h=2)).then_inc(wsem, 16)
        for f in range(NF):
            e.dma_start(out=wd[:, f, :], in_=wd_d[f * 128:(f + 1) * 128, :]).then_inc(wsem, 16)
    P.dma('pool', loadw)
    P.dma('pool', lambda e, iv: None, waits=[(wsem, 16 * (16 + NF))])
    load_ln_params(P, R, g_d, b_d, gbc, bbc)

    with P.loop(nsb, clear=(dsx, dso)) as L:
        def loader(j):
            P.dma('sp', lambda e, iv, j=j: e.dma_start(
                out=xs, in_=x_in[bass.ts(iv[L], 512), :][j * 128:(j + 1) * 128, :]).then_inc(dsx, 16))
            P.dma('sp', lambda e, iv: None, waits=[(dsx, (j + 1) * 16)])
            return xs
        transposes_to_xT(P, R, loader, xT, nj=4)
        P.join('pe', 1, 0)
        sgs = [sg, R.SCRF[:, 6656:7168]]
        pgs = [(R.ps[1], R.ps[2]), (R.ps[4], R.ps[5])]
        lists = [[], []]
        for f in range(NF):
            th = f % 2
            pg, pu = pgs[th]
            for c in range(8):
                lists[th].append(('pe', (lambda e, iv, f=f, c=c, pg=pg: e.matmul(
                    pg[:], lhsT=wg[:, c, f * 128:(f + 1) * 128], rhs=xT[:, c, :], start=(c == 0), stop=(c == 7))), c > 0))
            for c in range(8):
                lists[th].append(('pe', (lambda e, iv, f=f, c=c, pu=pu: e.matmul(
                    pu[:], lhsT=wu[:, c, f * 128:(f + 1) * 128], rhs=xT[:, c, :], start=(c == 0), stop=(c == 7))), True))
            lists[th].append(('act', (lambda e, iv, th=th, pg=pg: e.activation(out=sgs[th], in_=pg[:], func=AF.Silu)), False))
            lists[th].append(('dve', (lambda e, iv, f=f, th=th, pu=pu: e.tensor_tensor(out=aT[:, f, :], in0=sgs[th], in1=pu[:],
                                                                                 op=ALU.mult)), False))
        n0 = max(len(lists[0]), len(lists[1]))
        for idx in range(n0 + 9):
            for th in range(2):
                j_ = idx - th * 9
                if 0 <= j_ < len(lists[th]):
                    eng, fn, ch = lists[th][j_]
                    P.op(eng, fn, chain=ch, thread=th)
        P.join('pe', 0, 1)
        for j in range(4):
            first = True
            for h in range(2):
                for f in range(NF):
                    P.op('pe', lambda e, iv, f=f, h=h, j=j: e.matmul(
                        R.ps[3 + h][:], lhsT=aT[:, f, j * 128:(j + 1) * 128], rhs=wd[:, f, h * 512:(h + 1) * 512],
                        start=(f == 0), stop=(f == NF - 1)), chain=not first)
                    first = False
            P.dma('sp', lambda e, iv, j=j: e.dma_start(
                out=xs, in_=x_in[bass.ts(iv[L], 512), :][j * 128:(j + 1) * 128, :]).then_inc(dsx, 16))
            P.op('act', lambda e, iv: e.mul(out=xa, in_=xs, mul=ALPHA), waits=[(dsx, (4 + j + 1) * 16)])
            for h in range(2):
                P.op('dve', lambda e, iv, h=h: e.scalar_tensor_tensor(
                    out=u[:, h * 512:(h + 1) * 512], in0=R.ps[3 + h][:], scalar=0.5, in1=xa[:, h * 512:(h + 1) * 512],
                    op0=ALU.mult, op1=ALU.add))
            ln_ops(P, R, u, gbc, bbc, yo, pre_waits=[(dso, j * 16)])
            P.dma('sp', lambda e, iv, j=j: e.dma_start(
                out=x_out[bass.ts(iv[L], 512), :][j * 128:(j + 1) * 128, :], in_=yo).then_inc(dso, 16))
        P.dma('sp', lambda e, iv: None, waits=[(dso, 64)])


QK_COLS = [0, 128, 256, 384, 512, 640,
           1152, 1280, 1408, 1536, 1664, 1792,
           2304, 2432, 2560, 2688, 2816, 2944]
V_COLS = [768, 1920, 3072]
G_COL = 3456


def stage_win(P, R, x_in, win_d, bgate_d, qkT_out, qm32_out, kmean_out, v_out, gates_out, nsb=16):
    ntiles = nsb * 4
    win = R.WBUF[:, 0:52224].rearrange("p (c n) -> p c n", c=8)
    xs = R.SCRF[:, 0:1024]
    gt = R.SCRF[:, 1024:4096]
    q32 = R.SCRF[:, 4096:4480].rearrange("p (r t) -> p r t", r=3)
    bgb = R.SCRF[:, 5632:8704]
    xT = R.SCRB[:, 0:4096].rearrange("p (c t) -> p c t", c=8)
    qkst = R.SCRB[:, 4096:6400].rearrange("p (r t) -> p r t", r=18)
    vst = R.SCRB[:, 13312:14464]
    km = R.stats[:, 0:3]
    wsem = new_sem(R, "w")
    dsx = new_sem(R, "dsx")
    dsq = new_sem(R, "dsq")

    def loadw(e, iv):
        for c in range(8):
            e.dma_start(out=win[:, c, :].rearrange("p (h n) -> p h n", h=4),
                        in_=win_d[c * 128:(c + 1) * 128, :].rearrange("p (h n) -> p h n", h=4)).then_inc(wsem, 16)
    P.dma('pool', loadw)
    P.dma('pool', lambda e, iv: None, waits=[(wsem, 16 * 8)])
    bsem = new_sem(R, "bg")
    P.dma('sp', lambda e, iv: e.dma_start(
        out=bgb, in_=bgate_d.rearrange("a d -> (a d)").partition_broadcast(128)).then_inc(bsem, 16))
    P.dma('sp', lambda e, iv: None, waits=[(bsem, 16)])

    with P.loop(ntiles, clear=(dsx, dsq)) as L:
        def loader(j):
            P.dma('act', lambda e, iv: e.dma_start(out=xs, in_=x_in[bass.ts(iv[L], 128), :]).then_inc(dsx, 16))
            P.dma('act', lambda e, iv: None, waits=[(dsx, 16)])
            return xs
        transposes_to_xT(P, R, loader, xT, nj=1)
        for r in range(18):
            co = QK_COLS[r]
            for c in range(8):
                P.op('pe', lambda e, iv, c=c, co=co: e.matmul(R.ps[1][:, 0:128], lhsT=win[:, c, co:co + 128], rhs=xT[:, c, 0:128],
                                                              start=(c == 0), stop=(c == 7)), chain=(c > 0))
            P.op('act', lambda e, iv, r=r: e.copy(out=qkst[:, r, :], in_=R.ps[1][:, 0:128]))
            if r < 3:
                P.op('dve', lambda e, iv, r=r: e.tensor_copy(out=q32[:, r, :], in_=R.ps[1][:, 0:128]))
            if 3 <= r < 6:
                P.op('dve', lambda e, iv, r=r: e.tensor_reduce(out=km[:, r - 3:r - 2], in_=R.ps[1][:, 0:128], axis=AX.X, op=ALU.add))

        def stq(e, iv):
            e.dma_start(out=qkT_out[:, bass.ts(iv[L], 128)].rearrange("(r p) t -> p r t", p=128),
                        in_=qkst).then_inc(dsq, 16)
            e.dma_start(out=qm32_out[:, bass.ts(iv[L], 128)].rearrange("(r p) t -> p r t", p=128),
                        in_=q32).then_inc(dsq, 16)
            e.dma_start(out=kmean_out[:, bass.ts(iv[L], 1)].rearrange("(r p) b -> p r b", p=128),
                        in_=km.rearrange("p (r b) -> p r b", b=1), allow_slow_non_contiguous=True).then_inc(dsq, 16)
        P.dma('act', stq)
        for vi, co in enumerate(V_COLS):
            for c in range(8):
                P.op('pe', lambda e, iv, c=c, co=co: e.matmul(
                    R.ps[2][:, 0:384], lhsT=xT[:, c, 0:128], rhs=win[:, c, co:co + 384],
                    start=(c == 0), stop=(c == 7)), chain=(c > 0))
            P.op('act', lambda e, iv, vi=vi: e.copy(out=vst[:, vi * 384:(vi + 1) * 384], in_=R.ps[2][:, 0:384]))
        for gi in range(6):
            co = G_COL + gi * 512
            for c in range(8):
                P.op('pe', lambda e, iv, c=c, co=co: e.matmul(
                    R.ps[3][:], lhsT=xT[:, c, 0:128], rhs=win[:, c, co:co + 512],
                    start=(c == 0), stop=(c == 7)), chain=(c > 0))
            P.op('dve', lambda e, iv, gi=gi: e.tensor_tensor(out=gt[:, gi * 512:(gi + 1) * 512], in0=R.ps[3][:],
                                                             in1=bgb[:, gi * 512:(gi + 1) * 512], op=ALU.add))
        P.op('act', lambda e, iv: e.activation(out=gt, in_=gt, func=AF.Sigmoid))
        P.dma('act', lambda e, iv: e.dma_start(out=v_out[bass.ts(iv[L], 128), :], in_=vst).then_inc(dsq, 16))
        P.dma('act', lambda e, iv: e.dma_start(out=gates_out[bass.ts(iv[L], 128), :], in_=gt).then_inc(dsq, 16))
        P.dma('act', lambda e, iv: None, waits=[(dsq, 80)])


BIG = 30000.0
SLOPES = (2.0 ** (-8.0 * np.arange(1, 13, dtype=np.float32) / 12)).astype(np.float32)
SC_M = 64 ** -0.5
SC_D = 32 ** -0.5
SC_S = 64 ** -0.5
SUBLN_EPS = 1e-5


def att_tables():
    bf = ml_dtypes.bfloat16
    i = np.arange(SEQ)
    ib = (i // 128).astype(np.float32)
    il = (i % 128).astype(np.float32)
    one = np.ones(SEQ, np.float32)

    def hl(v):
        hi = np.float32(np.asarray(v, np.float32).astype(bf).astype(np.float32))
        lo = np.float32(np.asarray(v - hi, np.float32).astype(bf).astype(np.float32))
        return hi, lo
    qaug = np.zeros((12, 8, SEQ), np.float32)
    kaug = np.zeros((12, 8, SEQ), np.float32)
    for slot in range(12):
        scale = SC_M if slot % 2 == 0 else SC_D
        sig = float(SLOPES[slot]) / scale
        Sh, Sl = hl(128.0 * sig)
        sh, sl = hl(sig)
        qaug[slot] = np.stack([Sh * one, Sl * one, sh * one, sl * one, -ib, -ib, -il, -il])
        kaug[slot] = np.stack([ib, ib, il, il, Sh * one, Sl * one, sh * one, sl * one])
    onehot = (np.arange(32)[:, None] == (i // 256)[None, :]).astype(np.float32)
    p = np.arange(128)[:, None]
    c = np.arange(512)[None, :]
    maskT = np.concatenate([np.where((r * 128 + p) > c, -BIG, 0.0) for r in range(4)], axis=1)
    maskS = np.concatenate([np.where(c >= (r * 128 + p), -BIG, 0.0) for r in range(4)], axis=1)
    o = np.arange(32)[:, None]
    n = np.arange(32)[None, :]
    pb = np.stack([np.where(n < o, 0.0, -BIG), np.where(n == o, 0.0, -BIG), np.where(n <= o, 0.0, -BIG)]).astype(np.float32)
    return dict(qaug=qaug.reshape(96, SEQ).astype(bf), kaug=kaug.reshape(96, SEQ).astype(bf), onehot=onehot.astype(bf),
                maskT=maskT.astype(bf), maskS=maskS.astype(bf), pb=pb.reshape(-1))


def stage_att(P, R, T, qkT_d, qm32_d, kmean_d, v_d, o_d, lam_d, subg_d, one_m_linit, nheads=6, do=('m', 'd', 's')):
    W = R.WBUF
    GW = nheads * 64
    QA = [W[:, 0:8192], W[:, 8192:16384]]
    KA = [W[:, 16384:24576], W[:, 24576:32768]]
    VA = W[:, 32768:40960].rearrange("p (k d) -> p k d", k=64)
    VAf = W[:, 32768:40960]
    maskT = W[:, 40960:43008]
    maskS = W[:, 43008:45056]
    pt = [W[:, 45056:45568], W[:, 45568:46080]]
    wb = W[:, 46080:46592]
    wT = W[:, 46592:47104].rearrange("p (k q) -> p k q", k=4)
    FW = W[:, 47360:67584].bitcast(F32)
    pbt = FW[:, 0:3072].rearrange("p (a o n) -> p a o n", a=3, o=32)
    osb = [FW[:, 3072:3584], FW[:, 3584:4096]]
    gm = FW[:, 4096:4128]
    sel = FW[:, 4128:4160]
    m8 = FW[:, 4160:4168]
    rinv = [FW[:, 4168:4172], FW[:, 4172:4176]]
    ss = FW[:, 4176:4180]
    carry = FW[:, 4180:4181]
    negc = FW[:, 4181:4182]
    tot = FW[:, 4182:4183]
    lamt = FW[:, 4183:4184]
    lsc = FW[:, 4184:4186]
    t1 = FW[:, 4192:4256]
    od = FW[:, 4256:4512].rearrange("p (j d) -> p j d", j=4)
    gsc = FW[:, 4512:4576]
    lamw = FW[:, 4576:4704]
    lamw2 = FW[:, 4704:4768]
    ones = FW[:, 4768:5280]
    tt = FW[:, 5280:5792]
    spb = FW[:, 5792:6305]
    E1 = FW[:, 6336:6848]
    aa = FW[:, 6848:7360]
    kme = FW[:, 7360:7392]
    kmr = FW[:, 9472:9536]
    ost_all = FW[:, 7424:9472].bitcast(BF16).rearrange("p (t d) -> p t d", t=64)
    qm32 = R.SCRF[:, 0:8192]
    ps_s = [R.ps[0], R.ps[1]]
    ps_o = [R.ps[2], R.ps[3]]
    ps_t = [R.ps[4], R.ps[5]]
    psb = R.psb[0]
    tsem = new_sem(R, "att_t")
    hsem = new_sem(R, "att_h")
    osem = new_sem(R, "att_o")
    cnt = {'h': 0, 'o': 0}

    def ldc(e, iv):
        e.dma_start(out=maskT, in_=T['maskT'][:, :]).then_inc(tsem, 16)
        e.dma_start(out=maskS, in_=T['maskS'][:, :]).then_inc(tsem, 16)
        e.dma_start(out=FW[:, 0:3072], in_=T['pb'].partition_broadcast(128)).then_inc(tsem, 16)
        e.dma_start(out=gsc, in_=subg_d.partition_broadcast(128)).then_inc(tsem, 16)
        e.dma_start(out=lamw, in_=lam_d.rearrange("a d -> (a d)").partition_broadcast(128)).then_inc(tsem, 16)
    P.dma('sp', ldc)
    P.dma('sp', lambda e, iv: None, waits=[(tsem, 80)])
    P.op('dve', lambda e, iv: e.memset(ones, 1.0))
    P.op('dve', lambda e, iv: e.memset(spb[:, 0:1], 0.0))
    P.op('dve', lambda e, iv: e.memset(VA[:, :, 64:65], 1.0))
    linit = 1.0 - one_m_linit
    P.op('dve', lambda e, iv: e.tensor_tensor(out=lamw2[:, 0:32], in0=lamw[:, 0:32], in1=lamw[:, 32:64], op=ALU.mult))
    P.op('dve', lambda e, iv: e.tensor_tensor(out=lamw2[:, 32:64], in0=lamw[:, 64:96], in1=lamw[:, 96:128], op=ALU.mult))
    P.op('dve', lambda e, iv: e.tensor_reduce(out=lsc, in_=lamw2.rearrange("p (a d) -> p a d", a=2), axis=AX.X, op=ALU.add))
    P.op('act', lambda e, iv: e.activation(out=lsc, in_=lsc, func=AF.Exp))
    P.op('dve', lambda e, iv: e.tensor_tensor(out=lamt, in0=lsc[:, 0:1], in1=lsc[:, 1:2], op=ALU.subtract))
    P.op('dve', lambda e, iv: e.tensor_scalar(out=lamt, in0=lamt, scalar1=linit, scalar2=-1.0, op0=ALU.add, op1=ALU.mult))
    P.op('act', lambda e, iv: e.mul(out=gsc, in_=gsc, mul=one_m_linit))

    def dense_tile(s, Kd, scale, kb_ap_K, kb_ap_V, Q, mask_r, first):
        P.op('pe', lambda e, iv: e.matmul(ps_s[s][:], lhsT=kb_ap_K(iv), rhs=QA[s][0:Kd, Q * 512:(Q + 1) * 512],
                                          start=True, stop=(mask_r is None)))
        if mask_r is not None:
            P.op('pe', lambda e, iv: e.matmul(ps_s[s][:], lhsT=R.identb[:], rhs=maskT[:, mask_r * 512:(mask_r + 1) * 512],
                                              start=False, stop=True), chain=True)
        P.op('act', lambda e, iv: e.activation(out=pt[s], in_=ps_s[s][:], func=AF.Exp, scale=scale))
        P.op('pe', lambda e, iv: e.matmul(ps_o[s][0:65, :], lhsT=kb_ap_V(iv), rhs=pt[s], start=first, stop=True))

    def dense_Q(streams, Q):
        for r in range(4):
            kb = 4 * Q + r
            for (s, Kd, scale) in streams:
                dense_tile(s, Kd, scale, lambda iv, s=s, Kd=Kd, kb=kb: KA[s][0:Kd, kb * 128:(kb + 1) * 128],
                           lambda iv, kb=kb: VA[:, kb, 0:65], Q, r, r == 0)
        for kb in range(4 * Q):
            for (s, Kd, scale) in streams:
                dense_tile(s, Kd, scale, lambda iv, s=s, Kd=Kd, kb=kb: KA[s][0:Kd, kb * 128:(kb + 1) * 128],
                           lambda iv, kb=kb: VA[:, kb, 0:65], Q, None, False)
        for (s, Kd, scale) in streams:
            P.op('act', lambda e, iv, s=s: e.copy(out=osb[s][0:65, :], in_=ps_o[s][0:65, :]))
            for j in range(4):
                P.op('pe', lambda e, iv, s=s, j=j: e.transpose(out=ps_t[s][:, j * 65:(j + 1) * 65],
                                                               in_=osb[s][0:65, j * 128:(j + 1) * 128],
                                                               identity=R.identf[0:65, 0:65]), chain=(j > 0))
            P.op('dve', lambda e, iv, s=s: e.reciprocal(
                out=rinv[s], in_=ps_t[s][:, 0:260].rearrange("p (j d) -> p j d", j=4)[:, :, 64]))

    def store_head(hfn, dq):
        P.dma(dq, lambda e, iv: e.dma_start(
            out=o_d[bass.ts(hfn(iv), 1), :, :].rearrange("h (t p) d -> p (h t) d", p=128),
            in_=ost_all).then_inc(osem, 16))
        P.dma(dq, lambda e, iv: None, waits=[(osem, 16)])

    NTD = 3
    Q_ASSIGN = [[15, 10, 9, 4, 3], [14, 11, 8, 5, 2], [13, 12, 7, 6, 1, 0]]
    SB16 = R.SCRB
    d_pt = [SB16[:, 2624 * i:2624 * i + 512] for i in range(NTD)]
    d_f = [SB16[:, 2624 * i + 512:2624 * i + 2624].bitcast(F32) for i in range(NTD)]
    d_osb = [d_f[i][:, 0:512] for i in range(NTD)]
    d_t1 = [d_f[i][:, 512:768].rearrange("p (j d) -> p j d", j=4) for i in range(NTD)]
    d_od = [d_f[i][:, 768:1024].rearrange("p (j d) -> p j d", j=4) for i in range(NTD)]
    d_r0 = [d_f[i][:, 1024:1028] for i in range(NTD)]
    d_r1 = [d_f[i][:, 1028:1032] for i in range(NTD)]
    d_ss = [d_f[i][:, 1032:1036] for i in range(NTD)]
    d_sq = [d_f[i][:, 1036:1040] for i in range(NTD)]
    d_pss = [R.ps[0], R.ps[1], R.ps[2]]
    d_pso = [R.ps[3], R.ps[4], R.ps[5]]

    def dense_map_ops(th, m, Kd, scale, Q):
        ops = []
        kbs = [(4 * Q + r, r) for r in range(4)] + [(kb, None) for kb in range(4 * Q)]
        for idx, (kb, mr) in enumerate(kbs):
            ops.append(('pe', lambda e, iv, kb=kb, mr=mr: e.matmul(
                d_pss[th][:], lhsT=KA[m][0:Kd, kb * 128:(kb + 1) * 128], rhs=QA[m][0:Kd, Q * 512:(Q + 1) * 512],
                start=True, stop=(mr is None)), False))
            if mr is not None:
                ops.append(('pe', lambda e, iv, mr=mr: e.matmul(d_pss[th][:], lhsT=R.identb[:],
                                                                rhs=maskT[:, mr * 512:(mr + 1) * 512],
                                                                start=False, stop=True), True))
            ops.append(('act', lambda e, iv: e.activation(out=d_pt[th], in_=d_pss[th][:], func=AF.Exp, scale=scale), False))
            ops.append(('pe', lambda e, iv, kb=kb, idx=idx: e.matmul(d_pso[th][0:65, :], lhsT=VA[:, kb, 0:65], rhs=d_pt[th],
                                                                     start=(idx == 0), stop=True), False))
        ops.append(('act', lambda e, iv: e.copy(out=d_osb[th][0:65, :], in_=d_pso[th][0:65, :]), False))
        for j in range(4):
            ops.append(('pe', lambda e, iv, j=j: e.transpose(out=d_pss[th][:, j * 65:(j + 1) * 65],
                                                             in_=d_osb[th][0:65, j * 128:(j + 1) * 128],
                                                             identity=R.identf[0:65, 0:65]), j > 0))
        return ops

    def psT(th, j):
        return d_pss[th][:, j * 65:j * 65 + 64]

    def rsum(th):
        return d_pss[th][:, 0:260].rearrange("p (j d) -> p j d", j=4)[:, :, 64]

    def emit_threads(lists, shift=0):
        n = len(lists)
        for idx in range(max(len(x) for x in lists) + shift * (n - 1)):
            for th in range(n):
                j = idx - th * shift
                if 0 <= j < len(lists[th]):
                    eng, fn, ch = lists[th][j]
                    P.op(eng, fn, chain=ch, thread=th)

    if 'm' in do:
        with P.loop(nheads, clear=(hsem, osem)) as LHM:
            hbaseM = 0

            def ldm(e, iv):
                h = iv[LHM]
                e.dma_start(out=QA[0][32:96, :], in_=qkT_d[0:GW, :][bass.ts(h, 64), :]).then_inc(hsem, 16)
                e.dma_start(out=KA[0][32:96, :], in_=qkT_d[GW:2 * GW, :][bass.ts(h, 64), :]).then_inc(hsem, 16)
                e.dma_start(out=KA[0][0:32, :], in_=T['onehot'][:, :]).then_inc(hsem, 16)
                e.dma_start(out=QA[0][96:104, :], in_=T['qaug'][bass.ts(h * 2, 8), :]).then_inc(hsem, 16)
                e.dma_start(out=KA[0][96:104, :], in_=T['kaug'][bass.ts(h * 2, 8), :]).then_inc(hsem, 16)
                e.dma_start(out=VA[:, :, 0:64],
                            in_=v_d[:, 0:GW][:, bass.ts(h, 64)].rearrange("(k p) d -> p k d", p=128)).then_inc(hsem, 16)
                e.dma_start(out=qm32[0:64, :], in_=qm32_d[bass.ts(h, 64), :]).then_inc(hsem, 16)
                e.dma_start(out=kmr[0:64, :], in_=kmean_d[bass.ts(h, 64), :]).then_inc(hsem, 16)
            P.dma('sp', ldm)
            P.dma('sp', lambda e, iv: None, waits=[(hsem, lambda iv: (hbaseM + (iv[LHM] + 1) * 8) * 16)])
            P.op('dve', lambda e, iv: e.tensor_reduce(out=kme[0:64, :], in_=kmr[0:64, :].rearrange("p (n b) -> p n b", b=2),
                                                      axis=AX.X, op=ALU.add))
            P.op('dve', lambda e, iv: e.tensor_scalar(out=kme[0:64, :], in0=kme[0:64, :], scalar1=1.0 / 256.0, scalar2=None,
                                                      op0=ALU.mult))
            for t4 in range(16):
                for tj in range(4):
                    t = t4 * 4 + tj
                    own = t // 2
                    P.op('pe', lambda e, iv, t=t: e.matmul(ps_s[1][:, 0:32], lhsT=qm32[0:64, t * 128:(t + 1) * 128],
                                                           rhs=kme[0:64, :], start=True, stop=True))
                    P.op('dve', lambda e, iv, own=own: e.tensor_tensor(out=gm, in0=ps_s[1][:, 0:32], in1=pbt[:, 0, own, :],
                                                                       op=ALU.add))
                    P.op('dve', lambda e, iv: e.max(out=m8, in_=gm))
                    P.op('dve', lambda e, iv: e.tensor_scalar(out=sel, in0=gm, scalar1=m8[:, 2:3], scalar2=1.0,
                                                              op0=ALU.is_ge, op1=ALU.subtract))
                    P.op('dve', lambda e, iv, own=own: e.scalar_tensor_tensor(out=sel, in0=sel, scalar=BIG,
                                                                              in1=pbt[:, 1, own, :], op0=ALU.mult, op1=ALU.max))
                    P.op('dve', lambda e, iv, own=own: e.tensor_tensor(out=sel, in0=sel, in1=pbt[:, 2, own, :], op=ALU.add))
                    P.op('pe', lambda e, iv, tj=tj: e.transpose(out=ps_t[0][0:32, tj * 128:(tj + 1) * 128], in_=sel,
                                                                identity=R.identf[:]))
                P.op('act', lambda e, iv, t4=t4: e.copy(out=QA[0][0:32, t4 * 512:(t4 + 1) * 512], in_=ps_t[0][0:32, :]))
            for th in range(1, NTD):
                P.join('pe', th, 0)
            lists = []
            for th in range(NTD):
                ops = []
                for Q in Q_ASSIGN[th]:
                    ops += dense_map_ops(th, 0, 104, SC_M, Q)
                    ops.append(('dve', (lambda e, iv, th=th: e.reciprocal(out=d_r0[th], in_=rsum(th))), False))
                    for j in range(4):
                        ops.append(('dve', (lambda e, iv, th=th, j=j, Q=Q: e.tensor_scalar(
                            out=ost_all[:, Q * 4 + j, :], in0=psT(th, j), scalar1=d_r0[th][:, j:j + 1], scalar2=None,
                            op0=ALU.mult)), False))
                lists.append(ops)
            emit_threads(lists, 2)
            for th in range(1, NTD):
                P.join('sp', 0, th)
            store_head(lambda iv: iv[LHM], 'sp')
    obase_d = 0

    if 'd' in do:
        with P.loop(nheads, clear=(hsem, osem)) as LHD:
            hbaseD = 0

            def ldd(e, iv):
                h = iv[LHD]
                for m in range(2):
                    e.dma_start(out=QA[m][0:32, :], in_=qkT_d[2 * GW:3 * GW, :][bass.ts(h * 2 + m, 32), :]).then_inc(hsem, 16)
                    e.dma_start(out=KA[m][0:32, :], in_=qkT_d[3 * GW:4 * GW, :][bass.ts(h * 2 + m, 32), :]).then_inc(hsem, 16)
                    e.dma_start(out=QA[m][32:40, :], in_=T['qaug'][bass.ts(h * 2 + 1, 8), :]).then_inc(hsem, 16)
                    e.dma_start(out=KA[m][32:40, :], in_=T['kaug'][bass.ts(h * 2 + 1, 8), :]).then_inc(hsem, 16)
                e.dma_start(out=VA[:, :, 0:64],
                            in_=v_d[:, GW:2 * GW][:, bass.ts(h, 64)].rearrange("(k p) d -> p k d", p=128)).then_inc(hsem, 16)
            P.dma('act', ldd)
            P.dma('act', lambda e, iv: None, waits=[(hsem, lambda iv: (hbaseD + (iv[LHD] + 1) * 9) * 16)])
            for th in range(1, NTD):
                P.join('pe', th, 0)
            lists = []
            for th in range(NTD):
                ops = []
                for Q in Q_ASSIGN[th]:
                    ops += dense_map_ops(th, 0, 40, SC_D, Q)
                    ops.append(('dve', (lambda e, iv, th=th: e.reciprocal(out=d_r0[th], in_=rsum(th))), False))
                    for j in range(4):
                        ops.append(('dve', (lambda e, iv, th=th, j=j: e.tensor_scalar(
                            out=d_t1[th][:, j, :], in0=psT(th, j), scalar1=d_r0[th][:, j:j + 1], scalar2=None,
                            op0=ALU.mult)), False))
                    ops += dense_map_ops(th, 1, 40, SC_D, Q)
                    ops.append(('dve', (lambda e, iv, th=th: e.reciprocal(out=d_r1[th], in_=rsum(th))), False))
                    ops.append(('dve', (lambda e, iv, th=th: e.tensor_scalar(out=d_r1[th], in0=d_r1[th], scalar1=lamt[:, 0:1],
                                                                             scalar2=None, op0=ALU.mult)), False))
                    for j in range(4):
                        ops.append(('dve', (lambda e, iv, th=th, j=j: e.scalar_tensor_tensor(
                            out=d_od[th][:, j, :], in0=psT(th, j), scalar=d_r1[th][:, j:j + 1], in1=d_t1[th][:, j, :],
                            op0=ALU.mult, op1=ALU.add)), False))
                    ops.append(('dve', (lambda e, iv, th=th: e.tensor_tensor(out=d_t1[th], in0=d_od[th], in1=d_od[th],
                                                                             op=ALU.mult)), False))
                    ops.append(('dve', (lambda e, iv, th=th: e.tensor_reduce(out=d_ss[th], in_=d_t1[th], axis=AX.X,
                                                                             op=ALU.add)), False))
                    ops.append(('dve', (lambda e, iv, th=th: e.tensor_scalar(out=d_ss[th], in0=d_ss[th], scalar1=1.0 / 64.0,
                                                                             scalar2=SUBLN_EPS, op0=ALU.mult, op1=ALU.add)), False))
                    ops.append(('act', (lambda e, iv, th=th: e.activation(out=d_ss[th], in_=d_ss[th], func=AF.Sqrt)), False))
                    ops.append(('dve', (lambda e, iv, th=th: e.reciprocal(out=d_ss[th], in_=d_ss[th])), False))
                    for j in range(4):
                        ops.append(('dve', (lambda e, iv, th=th, j=j, Q=Q: e.scalar_tensor_tensor(
                            out=ost_all[:, Q * 4 + j, :], in0=d_od[th][:, j, :], scalar=d_ss[th][:, j:j + 1], in1=gsc,
                            op0=ALU.mult, op1=ALU.mult)), False))
                lists.append(ops)
            emit_threads(lists, 2)
            for th in range(1, NTD):
                P.join('sp', 0, th)
            store_head(lambda iv: iv[LHD] + nheads, 'act')
    obase_s = 0

    if 's' in do:
        NTS = 2
        SHIFT_S = 0
        ones13 = FW[:, 4768:5793]
        SF = R.SCRF
        tts = [SF[:, 4200 * i:4200 * i + 1024] for i in range(NTS)]
        spbs = [SF[:, 4200 * i + 1024:4200 * i + 2049] for i in range(NTS)]
        Efs = [SF[:, 4200 * i + 2080:4200 * i + 3105] for i in range(NTS)]
        aas = [SF[:, 4200 * i + 3136:4200 * i + 4160] for i in range(NTS)]
        negcs = [SF[:, 4200 * i + 4160:4200 * i + 4161] for i in range(NTS)]
        wbs_ = [R.SCRB[:, 2048 * i:2048 * i + 1024] for i in range(NTS)]
        wTs = [R.SCRB[:, 2048 * i + 1024:2048 * i + 2048].rearrange("p (k q) -> p k q", k=8) for i in range(NTS)]
        psbs = [R.psb[0], R.psb[1]]
        pszs = [R.psw[0], R.psw[1]]
        psos = [R.psw[2][:, 0:64], R.psw[2][:, 512:576]]
        P.op('dve', lambda e, iv: e.memset(ones13, 1.0))
        for th in range(NTS):
            P.op('dve', lambda e, iv, th=th: e.memset(spbs[th][:, 0:1], 0.0))
        with P.loop(nheads, clear=(hsem, osem)) as LHS:
            def lds(e, iv):
                h = iv[LHS]
                e.dma_start(out=QA[0][0:64, :], in_=qkT_d[4 * GW:5 * GW, :][bass.ts(h, 64), :]).then_inc(hsem, 16)
                e.dma_start(out=KA[0][0:64, :], in_=qkT_d[5 * GW:6 * GW, :][bass.ts(h, 64), :]).then_inc(hsem, 16)
                e.dma_start(out=VA[:, :, 0:64],
                            in_=v_d[:, 2 * GW:3 * GW][:, bass.ts(h, 64)].rearrange("(k p) d -> p k d", p=128)).then_inc(hsem, 16)
            P.dma('sp', lds)
            P.dma('sp', lambda e, iv: None, waits=[(hsem, 48)])
            for th in range(1, NTS):
                P.join('pe', th, 0)

            def sb_step_ops(th, t, k0, nb, mask_r, first):
                Wd = nb * 512
                ops = []
                for bi in range(nb):
                    kt = k0 + bi
                    is_diag = (mask_r is not None and bi == nb - 1)
                    ops.append(('pe', lambda e, iv, bi=bi, kt=kt, is_diag=is_diag: e.matmul(
                        pszs[th][:, bi * 512:(bi + 1) * 512], lhsT=QA[0][0:64, t * 128:(t + 1) * 128],
                        rhs=KA[0][0:64, kt * 512:(kt + 1) * 512], start=True, stop=(not is_diag)), bi > 0))
                    if is_diag:
                        ops.append(('pe', lambda e, iv, bi=bi: e.matmul(
                            pszs[th][:, bi * 512:(bi + 1) * 512], lhsT=R.identb[:],
                            rhs=maskS[:, mask_r * 512:(mask_r + 1) * 512], start=False, stop=True), True))
                ops.append(('act', lambda e, iv: e.activation(out=tts[th][:, 0:Wd], in_=pszs[th][:, 0:Wd], func=AF.Exp, scale=SC_S), False))
                ops.append(('act', lambda e, iv: e.activation(out=spbs[th][:, 1:Wd + 1], in_=tts[th][:, 0:Wd], func=AF.Ln, bias=1.0), False))
                ops.append(('dve', lambda e, iv: e.tensor_tensor_scan(out=Efs[th][:, 0:Wd + 1], data0=ones13[:, 0:Wd + 1],
                                                                      data1=spbs[th][:, 0:Wd + 1], initial=0.0,
                                                                      op0=ALU.mult, op1=ALU.add), False))
                ops.append(('dve', lambda e, iv: e.scalar_tensor_tensor(out=aas[th][:, 0:Wd], in0=pszs[th][:, 0:Wd], scalar=SC_S,
                                                                        in1=Efs[th][:, 0:Wd], op0=ALU.mult, op1=ALU.add), False))
                if first:
                    ops.append(('dve', lambda e, iv: e.tensor_scalar(out=negcs[th], in0=Efs[th][:, Wd:Wd + 1], scalar1=-1.0,
                                                                     scalar2=None, op0=ALU.mult), False))
                else:
                    ops.append(('dve', lambda e, iv: e.tensor_tensor(out=negcs[th], in0=negcs[th], in1=Efs[th][:, Wd:Wd + 1],
                                                                     op=ALU.subtract), False))
                ops.append(('act', lambda e, iv: e.activation(out=wbs_[th][:, 0:Wd], in_=aas[th][:, 0:Wd], func=AF.Exp,
                                                              bias=negcs[th][:, 0:1]), False))
                nk = nb * 4
                for k4 in range(nk):
                    ops.append(('pe', lambda e, iv, k4=k4: e.transpose(out=psbs[th][:, k4 * 128:(k4 + 1) * 128],
                                                                       in_=wbs_[th][:, k4 * 128:(k4 + 1) * 128],
                                                                       identity=R.identb[:]), k4 > 0))
                ops.append(('dve', lambda e, iv: e.tensor_copy(out=wTs[th][:, 0:nk, :],
                                                               in_=psbs[th][:, 0:Wd].rearrange("p (k q) -> p k q", k=nk)), False))
                for k4 in range(nk):
                    ops.append(('pe', lambda e, iv, k4=k4: e.matmul(psos[th], lhsT=wTs[th][:, k4, :],
                                                                    rhs=VA[:, k0 * 4 + k4, 0:64],
                                                                    start=(first and k4 == 0), stop=True), k4 > 0))
                return ops

            for base in range(0, 64, NTS):
                lists = []
                for th in range(NTS):
                    t = base + th
                    kd = t // 4
                    if kd % 2 == 1:
                        ops = sb_step_ops(th, t, kd - 1, 2, t % 4, True)
                        rem = kd - 1
                    else:
                        ops = sb_step_ops(th, t, kd, 1, t % 4, True)
                        rem = kd
                    for k0 in range(rem - 2, -1, -2):
                        ops += sb_step_ops(th, t, k0, 2, None, False)
                    ops.append(('act', (lambda e, iv, t=t, th=th: e.copy(out=ost_all[:, t, :], in_=psos[th])), False))
                    lists.append(ops)
                emit_threads(lists, SHIFT_S)
            for th in range(1, NTS):
                P.join('sp', 0, th)
            store_head(lambda iv: iv[LHS] + 2 * nheads, 'sp')
def stage_mixpost(P, R, x_in, o_d, gates_d, x_out, wbm_d, wbd_d, wbs_d, wout_d, g_d, b_d, ntiles=64):
    W = R.WBUF
    wbr = W[:, 0:9216].rearrange("p (k n) -> p k n", k=9)
    wout = W[:, 9216:17408].rearrange("p (c n) -> p c n", c=8)
    gts = W[:, 17408:23552].bitcast(F32)
    xs = R.SCRF[:, 0:1024]
    u = R.SCRF[:, 1024:2048]
    xa = R.SCRF[:, 2048:3072]
    yo = R.SCRF[:, 3072:4096]
    gbc = R.SCRF[:, 4608:5632]
    bbc = R.SCRF[:, 5632:6656]
    merged = R.SCRF[:, 6656:7680]
    tmp = R.SCRF[:, 7680:8704]
    xT = R.SCRB[:, 0:4096].rearrange("p (c t) -> p c t", c=8)
    ot = R.SCRB[:, 4096:5248]
    oT = R.SCRB[:, 5248:6400].rearrange("p (k t) -> p k t", k=9)
    wsem = new_sem(R, "w")
    dsi = new_sem(R, "dsi")
    dso = new_sem(R, "dso")

    def loadw(e, iv):
        for bi, wd_ in enumerate((wbm_d, wbd_d, wbs_d)):
            for i in range(3):
                e.dma_start(out=wbr[:, bi * 3 + i, :], in_=wd_[i * 128:(i + 1) * 128, :]).then_inc(wsem, 16)
        for c in range(8):
            e.dma_start(out=wout[:, c, :], in_=wout_d[c * 128:(c + 1) * 128, :]).then_inc(wsem, 16)
    P.dma('pool', loadw)
    P.dma('pool', lambda e, iv: None, waits=[(wsem, 16 * 17)])
    load_ln_params(P, R, g_d, b_d, gbc, bbc)
    with P.loop(ntiles, clear=(dsi, dso)) as L:
        def ld(e, iv):
            e.dma_start(out=gts, in_=gates_d[bass.ts(iv[L], 128), :]).then_inc(dsi, 16)
            e.dma_start(out=xs, in_=x_in[bass.ts(iv[L], 128), :]).then_inc(dsi, 16)
        P.dma('sp', lambda e, iv: e.dma_start(out=ot.rearrange("p (h d) -> p h d", h=18),
                                              in_=o_d.rearrange("h s d -> s h d")[bass.ts(iv[L], 128), :, :]).then_inc(dsi, 16))
        P.dma('act', ld)
        P.dma('act', lambda e, iv: None, waits=[(dsi, lambda iv: (iv[L] + 1) * 48)])
        for k in range(9):
            P.op('pe', lambda e, iv, k=k: e.transpose(
                out=(R.psb[0][:, k * 128:(k + 1) * 128] if k < 8 else R.psb[1][:, 0:128]),
                in_=ot[:, k * 128:(k + 1) * 128], identity=R.identb[:]), chain=(k > 0))
        P.op('act', lambda e, iv: e.copy(out=oT[:, 0:8, :], in_=R.psb[0][:, :].rearrange("p (k t) -> p k t", k=8)))
        P.op('act', lambda e, iv: e.copy(out=oT[:, 8, :], in_=R.psb[1][:, 0:128]))
        for br in range(3):
            for h in range(2):
                for i in range(3):
                    P.op('pe', lambda e, iv, br=br, h=h, i=i: e.matmul(
                        R.ps[1][:], lhsT=oT[:, br * 3 + i, :], rhs=wbr[:, br * 3 + i, h * 512:(h + 1) * 512],
                        start=(i == 0), stop=(i == 2)), chain=(i > 0))
                gsl = gts[:, br * 1024 + h * 512: br * 1024 + (h + 1) * 512]
                if br == 0:
                    P.op('dve', lambda e, iv, h=h, gsl=gsl: e.tensor_tensor(out=merged[:, h * 512:(h + 1) * 512], in0=gsl,
                                                                            in1=R.ps[1][:], op=ALU.mult))
                else:
                    P.op('dve', lambda e, iv, h=h, gsl=gsl: e.tensor_tensor(out=tmp[:, h * 512:(h + 1) * 512], in0=gsl,
                                                                            in1=R.ps[1][:], op=ALU.mult))
                    P.op('dve', lambda e, iv, h=h: e.tensor_tensor(out=merged[:, h * 512:(h + 1) * 512],
                                                                   in0=merged[:, h * 512:(h + 1) * 512],
                                                                   in1=tmp[:, h * 512:(h + 1) * 512], op=ALU.add))
        transposes_to_xT(P, R, lambda j: merged, xT, nj=1)
        first = True
        for h in range(2):
            for c in range(8):
                P.op('pe', lambda e, iv, h=h, c=c: e.matmul(R.ps[3 + h][:], lhsT=xT[:, c, 0:128],
                                                            rhs=wout[:, c, h * 512:(h + 1) * 512],
                                                            start=(c == 0), stop=(c == 7)), chain=not first)
                first = False
        P.op('act', lambda e, iv: e.mul(out=xa, in_=xs, mul=ALPHA))
        for h in range(2):
            P.op('dve', lambda e, iv, h=h: e.tensor_tensor(out=u[:, h * 512:(h + 1) * 512], in0=xa[:, h * 512:(h + 1) * 512],
                                                           in1=R.ps[3 + h][:], op=ALU.add))
        ln_ops(P, R, u, gbc, bbc, yo, pre_waits=[(dso, lambda iv: iv[L] * 16)])
        P.dma('act', lambda e, iv: e.dma_start(out=x_out[bass.ts(iv[L], 128), :], in_=yo).then_inc(dso, 16))
        P.dma('act', lambda e, iv: None, waits=[(dso, 16)])


NACT = 4


NCORE = 8
HTOK = 4096


def _nc_dt():
    nc = bass.Bass("TRN2", target_bir_lowering=False)

    def dt(n, s, d=F32, k="ExternalInput"):
        return nc.dram_tensor(n, list(s), d, kind=k).ap()
    return nc, dt


def build_pre():
    nc, dt = _nc_dt()
    x = dt("x", [HTOK, D])
    g = dt("ln_g", [D]); b = dt("ln_b", [D])
    wg = dt("wg", [D, DFF]); wu = dt("wu", [D, DFF]); wd = dt("wd", [DFF, D])
    win = dt("w_in", [D, NIN]); bg = dt("b_gate", [3, D])
    idf = dt("idf", [128, 128]); idb = dt("idb", [128, 128], BF16)
    x1 = dt("x1", [HTOK, D], F32, "ExternalOutput")
    qkT = dt("qkT", [2304, HTOK], BF16, "ExternalOutput")
    qm32 = dt("qm32", [384, HTOK], F32, "ExternalOutput")
    kmean = dt("kmean", [384, HTOK // 128], F32, "ExternalOutput")
    v = dt("v", [HTOK, 1152], BF16, "ExternalOutput")
    gates = dt("gates", [HTOK, 3072], F32, "ExternalOutput")
    with ExitStack() as ctx:
        R = rl_alloc(nc, ctx)
        P = Prog()
        rl_consts(P, R, idf, idb)
        stage_ffnln(P, R, x, x1, wg, wu, wd, g, b, nsb=HTOK // 512)
        stage_win(P, R, x1, win, bg, qkT, qm32, kmean, v, gates, nsb=HTOK // 512)
        run_prog(nc, P, R.G)
    return nc


def build_att(tb_shapes, l):
    bf = ml_dtypes.bfloat16
    nc, dt = _nc_dt()
    NH = 3
    qkT = dt("qkT", [6 * NH * 64, SEQ], BF16)
    qm32 = dt("qm32", [NH * 64, SEQ])
    kmean = dt("kmean", [NH * 64, SEQ // 128])
    v = dt("v", [SEQ, 3 * NH * 64], BF16)
    dl = dt("diff_lambda", [4, 32]); dg = dt("diff_subln_g", [64])
    idf = dt("idf", [128, 128]); idb = dt("idb", [128, 128], BF16)
    T = {k: dt("t_" + k, shp, BF16 if dty == bf else F32) for k, (shp, dty) in tb_shapes.items()}
    o = dt("o", [3 * NH, SEQ, 64], BF16, "ExternalOutput")
    with ExitStack() as ctx:
        R = rl_alloc(nc, ctx)
        P = Prog()
        rl_consts(P, R, idf, idb)
        linit = 0.8 - 0.6 * math.exp(-0.3 * l)
        stage_att(P, R, T, qkT, qm32, kmean, v, o, dl, dg, 1.0 - linit, nheads=NH)
        run_prog(nc, P, R.G)
    return nc


def build_post():
    nc, dt = _nc_dt()
    x1 = dt("x1", [HTOK, D])
    o = dt("o", [18, HTOK, 64], BF16)
    gates = dt("gates", [HTOK, 3072])
    wbm = dt("w_br_moba", [384, D]); wbd = dt("w_br_diff", [384, D]); wbs = dt("w_br_sb", [384, D]); wo = dt("w_out", [D, D])
    g1 = dt("ln_g1", [D]); b1 = dt("ln_b1", [D]); g2 = dt("ln_g2", [D]); b2 = dt("ln_b2", [D])
    wg = dt("wg", [D, DFF]); wu = dt("wu", [D, DFF]); wd = dt("wd", [DFF, D])
    idf = dt("idf", [128, 128]); idb = dt("idb", [128, 128], BF16)
    y = dt("y", [HTOK, D], F32, "ExternalOutput")
    xb = dt("xb_d", [HTOK, D], F32, "Internal")
    with ExitStack() as ctx:
        R = rl_alloc(nc, ctx)
        P = Prog()
        rl_consts(P, R, idf, idb)
        stage_mixpost(P, R, x1, o, gates, xb, wbm, wbd, wbs, wo, g1, b1, ntiles=HTOK // 128)
        stage_ffnln(P, R, xb, y, wg, wu, wd, g2, b2, nsb=HTOK // 512)
        run_prog(nc, P, R.G)
    return nc


def kernel(x, ln_g, ln_b, ffn_w_gate, ffn_w_up, ffn_w_down, w_in, b_gate, diff_lambda, diff_subln_g,
           w_br_moba, w_br_diff, w_br_sb, w_out):
    bf = ml_dtypes.bfloat16
    tb = att_tables()
    f = lambda a: np.ascontiguousarray(np.asarray(a, dtype=np.float32))
    cc = np.ascontiguousarray
    ident = dict(idf=np.eye(128, dtype=np.float32), idb=np.eye(128).astype(bf))
    cores = list(range(NCORE))
    cur = f(x).reshape(NCORE, HTOK, D)
    for l in range(2):
        nc = build_pre()
        sh = dict(ident, ln_g=f(ln_g[l, 0]), ln_b=f(ln_b[l, 0]), wg=f(ffn_w_gate[l, 0]), wu=f(ffn_w_up[l, 0]),
                  wd=f(ffn_w_down[l, 0]), w_in=f(w_in[l]), b_gate=f(b_gate[l]))
        r1 = run_bass_kernel_spmd(nc, [dict(sh, x=cc(cur[c])) for c in cores], core_ids=cores).results
        att_in = []
        for c in cores:
            b_, hh = c // 2, c % 2
            qk_b = np.concatenate([r1[2 * b_]["qkT"], r1[2 * b_ + 1]["qkT"]], axis=1)
            qm_b = np.concatenate([r1[2 * b_]["qm32"], r1[2 * b_ + 1]["qm32"]], axis=1)
            km_b = np.concatenate([r1[2 * b_]["kmean"], r1[2 * b_ + 1]["kmean"]], axis=1)
            v_b = np.concatenate([r1[2 * b_]["v"], r1[2 * b_ + 1]["v"]], axis=0)
            qk_h = np.concatenate([qk_b[g * 384 + hh * 192:g * 384 + hh * 192 + 192] for g in range(6)], axis=0)
            v_h = np.concatenate([v_b[:, g * 384 + hh * 192:g * 384 + hh * 192 + 192] for g in range(3)], axis=1)
            d_ = dict(ident, qkT=cc(qk_h), qm32=cc(qm_b[hh * 192:(hh + 1) * 192]), kmean=cc(km_b[hh * 192:(hh + 1) * 192]),
                      v=cc(v_h), diff_lambda=f(diff_lambda[l]), diff_subln_g=f(diff_subln_g[l]))
            for k_, a_ in tb.items():
                if k_ in ("qaug", "kaug"):
                    d_["t_" + k_] = cc(a_[hh * 48:(hh + 1) * 48])
                else:
                    d_["t_" + k_] = a_
            att_in.append(d_)
        tb_shapes = {k_: (list(att_in[0]["t_" + k_].shape), att_in[0]["t_" + k_].dtype) for k_ in tb}
        nc = build_att(tb_shapes, l)
        r2 = run_bass_kernel_spmd(nc, att_in, core_ids=cores).results
        nc = build_post()
        sh = dict(ident, w_br_moba=f(w_br_moba[l]), w_br_diff=f(w_br_diff[l]), w_br_sb=f(w_br_sb[l]), w_out=f(w_out[l]),
                  ln_g1=f(ln_g[l, 1]), ln_b1=f(ln_b[l, 1]), ln_g2=f(ln_g[l, 2]), ln_b2=f(ln_b[l, 2]),
                  wg=f(ffn_w_gate[l, 1]), wu=f(ffn_w_up[l, 1]), wd=f(ffn_w_down[l, 1]))
        post_in = []
        for c in cores:
            b_, hf = c // 2, c % 2
            o_b = np.empty((18, SEQ, 64), dtype=r2[0]["o"].dtype)
            for g in range(3):
                for hh in range(2):
                    o_b[g * 6 + hh * 3:g * 6 + hh * 3 + 3] = r2[2 * b_ + hh]["o"][g * 3:g * 3 + 3]
            post_in.append(dict(sh, x1=cc(r1[c]["x1"]), gates=cc(r1[c]["gates"]), o=cc(o_b[:, hf * HTOK:(hf + 1) * HTOK])))
        r3 = run_bass_kernel_spmd(nc, post_in, core_ids=cores).results
        cur = np.stack([np.asarray(r["y"], dtype=np.float32) for r in r3], axis=0)
    return cur.reshape(4, SEQ, D)
```
